# Optimizing a Trainium2 kernel written in Bass

```python
import math
import jax, jax.numpy as jnp
from jax import lax
import numpy as np

D_MODEL = 1024
BATCH = 8
SEQ = 4096
DEPTH = 2

HEAD_DIM = 64
NSA_HEADS = 6
NSA_KV_HEADS = 2
NSA_CMP_BLOCK = 32
NSA_CMP_STRIDE = 16
NSA_SEL_BLOCK = 64
NSA_N_SEL = 16
NSA_WINDOW = 512
NSA_CMP_HIDDEN = 256
NSA_Q_BLOCK = 64
MOBA_HEADS = 4
MOBA_BLOCK = 256
MOBA_TOPK = 3
MOBA_Q_BLOCK = 32
DIL_PATTERNS = ((128, 1), (512, 4), (2048, 16))
DIL_HEADS_PER_GROUP = 2
DIL_HEADS = DIL_HEADS_PER_GROUP * len(DIL_PATTERNS)
N_HEADS_TOTAL = NSA_HEADS + MOBA_HEADS + DIL_HEADS
N_BRANCHES = 3
REL_BUCKETS = 32
REL_MAX_EXACT = 16
REL_MAX_DIST = 2048
D_FF = 2816
NORM_EPS = 1e-6
NEG_INF = -1e30
FORCE_SCORE = 1e4

KV_W = NSA_KV_HEADS * HEAD_DIM
IN_LAYOUT = (
    ('nsa_q', NSA_HEADS * HEAD_DIM),
    ('nsa_k_cmp', KV_W), ('nsa_v_cmp', KV_W),
    ('nsa_k_sel', KV_W), ('nsa_v_sel', KV_W),
    ('nsa_k_win', KV_W), ('nsa_v_win', KV_W),
    ('nsa_gate', NSA_HEADS * 3),
    ('moba_q', MOBA_HEADS * HEAD_DIM), ('moba_k', MOBA_HEADS * HEAD_DIM), ('moba_v', MOBA_HEADS * HEAD_DIM),
    ('dil_q', DIL_HEADS * HEAD_DIM), ('dil_k', DIL_HEADS * HEAD_DIM), ('dil_v', DIL_HEADS * HEAD_DIM),
    ('merge_gate', N_BRANCHES * D_MODEL),
)
D_IN = sum(w for _, w in IN_LAYOUT)

kernel_name = 'hybrid_nsa_moba_dilated_macaron'


def rms_norm(x, gain):
    xf = x.astype(jnp.float32)
    y = xf * lax.rsqrt(jnp.mean(xf * xf, axis=-1, keepdims=True) + NORM_EPS)
    return (y * gain.astype(jnp.float32)).astype(x.dtype)


def swiglu(x, w_gate, w_up, w_down):
    return (jax.nn.silu(x @ w_gate) * (x @ w_up)) @ w_down


def rel_bucket(dist):
    n = jnp.maximum(dist, 0)
    nf = jnp.maximum(n, REL_MAX_EXACT).astype(jnp.float32)
    large = REL_MAX_EXACT + (jnp.log(nf / REL_MAX_EXACT) / math.log(REL_MAX_DIST / REL_MAX_EXACT)
                             * (REL_BUCKETS - REL_MAX_EXACT)).astype(jnp.int32)
    large = jnp.minimum(large, REL_BUCKETS - 1)
    return jnp.where(n < REL_MAX_EXACT, n, large)


def masked_softmax(logits, mask):
    l = jnp.where(mask, logits.astype(jnp.float32), NEG_INF)
    m = jnp.max(l, axis=-1, keepdims=True)
    e = jnp.where(mask, jnp.exp(l - m), 0.0)
    s = jnp.maximum(jnp.sum(e, axis=-1, keepdims=True), 1e-30)
    return e / s, (m + jnp.log(s))[..., 0]


def split_columns(proj):
    offs = np.cumsum([w for _, w in IN_LAYOUT])[:-1]
    parts = jnp.split(proj, offs, axis=-1)
    return {name: p for (name, _), p in zip(IN_LAYOUT, parts)}


def nsa_compress(kv, pe, w1, w2):
    B, S, G, hd = kv.shape
    n_cmp = (S - NSA_CMP_BLOCK) // NSA_CMP_STRIDE + 1
    idx = np.arange(n_cmp)[:, None] * NSA_CMP_STRIDE + np.arange(NSA_CMP_BLOCK)[None, :]
    blocks = kv[:, idx] + pe[None, None, :, None, :]
    blocks = blocks.transpose(0, 3, 1, 2, 4).reshape(B, G, n_cmp, NSA_CMP_BLOCK * hd)
    return jax.nn.gelu(blocks @ w1) @ w2


def nsa_mixer(q, k_cmp, v_cmp, k_sel, v_sel, k_win, v_win, gate_logits, bias_tbl,
              pe_k, pe_v, phi_k1, phi_k2, phi_v1, phi_v2):
    B, S, _ = q.shape
    G, R, hd = NSA_KV_HEADS, NSA_HEADS // NSA_KV_HEADS, HEAD_DIM
    scale = hd ** -0.5
    Qb = NSA_Q_BLOCK
    qh = q.reshape(B, S, G, R, hd).transpose(0, 2, 3, 1, 4)
    gates = jax.nn.sigmoid(gate_logits.reshape(B, S, G, R, 3)).transpose(0, 2, 3, 1, 4)
    tbl = bias_tbl.T.reshape(G, R, REL_BUCKETS)
    heads = lambda t: t.reshape(B, S, G, hd)
    kc = nsa_compress(heads(k_cmp), pe_k, phi_k1, phi_k2)
    vc = nsa_compress(heads(v_cmp), pe_v, phi_v1, phi_v2)
    n_cmp = kc.shape[2]
    cmp_end = jnp.arange(n_cmp) * NSA_CMP_STRIDE + NSA_CMP_BLOCK - 1
    n_slc = S // NSA_SEL_BLOCK
    n_sel = min(NSA_N_SEL, n_slc)
    c_start = np.arange(n_cmp) * NSA_CMP_STRIDE
    s_start = np.arange(n_slc) * NSA_SEL_BLOCK
    overlap = (c_start[:, None] < s_start[None, :] + NSA_SEL_BLOCK) & (c_start[:, None] + NSA_CMP_BLOCK > s_start[None, :])
    cmp_to_slc = jnp.asarray(overlap, dtype=jnp.float32)
    ks_blocks = heads(k_sel).transpose(0, 2, 1, 3).reshape(B, G, n_slc, NSA_SEL_BLOCK, hd)
    vs_blocks = heads(v_sel).transpose(0, 2, 1, 3).reshape(B, G, n_slc, NSA_SEL_BLOCK, hd)
    wpad = ((0, 0), (0, 0), (NSA_WINDOW, 0), (0, 0))
    kw_pad = jnp.pad(heads(k_win).transpose(0, 2, 1, 3), wpad)
    vw_pad = jnp.pad(heads(v_win).transpose(0, 2, 1, 3), wpad)
    bi = jnp.arange(B)[:, None, None, None]
    gi = jnp.arange(G)[None, :, None, None]
    ri = jnp.arange(R)[None, None, :, None, None]
    blk = jnp.arange(n_slc)

    def block_step(j):
        q0 = j * Qb
        t = q0 + jnp.arange(Qb)
        qb = lax.dynamic_slice_in_dim(qh, q0, Qb, axis=3)
        gb = lax.dynamic_slice_in_dim(gates, q0, Qb, axis=3)
        dist_c = t[:, None] - cmp_end[None, :]
        logit_c = jnp.einsum('bgrqd,bgkd->bgrqk', qb, kc).astype(jnp.float32) * scale + tbl[:, :, rel_bucket(dist_c)]
        p_c, _ = masked_softmax(logit_c, dist_c >= 0)
        o_c = jnp.einsum('bgrqk,bgkd->bgrqd', p_c.astype(vc.dtype), vc)
        imp = jnp.einsum('bgrqk,ks->bgqs', p_c, cmp_to_slc)
        cur = t // NSA_SEL_BLOCK
        forced = (blk[None, :] == 0) | (blk[None, :] == cur[:, None]) | (blk[None, :] == cur[:, None] - 1)
        future = blk[None, :] * NSA_SEL_BLOCK > t[:, None]
        imp = jnp.where(future, NEG_INF, jnp.where(forced, FORCE_SCORE, imp))
        _, idx = lax.top_k(imp, n_sel)
        ks = ks_blocks[bi, gi, idx].reshape(B, G, Qb, n_sel * NSA_SEL_BLOCK, hd)
        vs = vs_blocks[bi, gi, idx].reshape(B, G, Qb, n_sel * NSA_SEL_BLOCK, hd)
        pos_s = (idx[..., None] * NSA_SEL_BLOCK + jnp.arange(NSA_SEL_BLOCK)).reshape(B, G, Qb, n_sel * NSA_SEL_BLOCK)
        dist_s = t[None, None, :, None] - pos_s
        bias_s = tbl[gi[..., None], ri, rel_bucket(dist_s)[:, :, None]]
        logit_s = jnp.einsum('bgrqd,bgqkd->bgrqk', qb, ks).astype(jnp.float32) * scale + bias_s
        p_s, _ = masked_softmax(logit_s, (dist_s >= 0)[:, :, None])
        o_s = jnp.einsum('bgrqk,bgqkd->bgrqd', p_s.astype(vs.dtype), vs)
        kw = lax.dynamic_slice_in_dim(kw_pad, q0, NSA_WINDOW + Qb, axis=2)
        vw = lax.dynamic_slice_in_dim(vw_pad, q0, NSA_WINDOW + Qb, axis=2)
        pos_w = q0 - NSA_WINDOW + jnp.arange(NSA_WINDOW + Qb)
        dist_w = t[:, None] - pos_w[None, :]
        mask_w = (pos_w[None, :] >= 0) & (dist_w >= 0) & (dist_w < NSA_WINDOW)
        logit_w = jnp.einsum('bgrqd,bgkd->bgrqk', qb, kw).astype(jnp.float32) * scale + tbl[:, :, rel_bucket(dist_w)]
        p_w, _ = masked_softmax(logit_w, mask_w)
        o_w = jnp.einsum('bgrqk,bgkd->bgrqd', p_w.astype(vw.dtype), vw)
        return gb[..., 0:1] * o_c + gb[..., 1:2] * o_s + gb[..., 2:3] * o_w

    out = lax.map(block_step, jnp.arange(S // Qb))
    return out.transpose(1, 0, 4, 2, 3, 5).reshape(B, S, NSA_HEADS * hd)


def moba_mixer(q, k, v, bias_tbl):
    B, S, _ = q.shape
    H, hd, Qb = MOBA_HEADS, HEAD_DIM, MOBA_Q_BLOCK
    scale = hd ** -0.5
    to_heads = lambda t: t.reshape(B, S, H, hd).transpose(0, 2, 1, 3)
    qh, kh, vh = to_heads(q), to_heads(k), to_heads(v)
    nb = -(-S // MOBA_BLOCK)
    pad = ((0, 0), (0, 0), (0, nb * MOBA_BLOCK - S), (0, 0))
    kp, vp = jnp.pad(kh, pad), jnp.pad(vh, pad)
    k_blocks = kp.reshape(B, H, nb, MOBA_BLOCK, hd)
    v_blocks = vp.reshape(B, H, nb, MOBA_BLOCK, hd)
    k_mean = jnp.mean(k_blocks.astype(jnp.float32), axis=3)
    n_top = min(MOBA_TOPK, nb - 1)
    tbl = bias_tbl.T
    bi = jnp.arange(B)[:, None, None, None]
    hi = jnp.arange(H)[None, :, None, None]

    def block_step(j):
        q0 = j * Qb
        c = q0 // MOBA_BLOCK
        t = q0 + jnp.arange(Qb)
        qb = lax.dynamic_slice_in_dim(qh, q0, Qb, axis=2)
        k_own = lax.dynamic_slice_in_dim(kp, c * MOBA_BLOCK, MOBA_BLOCK, axis=2)
        v_own = lax.dynamic_slice_in_dim(vp, c * MOBA_BLOCK, MOBA_BLOCK, axis=2)
        dist_own = t[:, None] - (c * MOBA_BLOCK + jnp.arange(MOBA_BLOCK))[None, :]
        logit_own = jnp.einsum('bhqd,bhkd->bhqk', qb, k_own).astype(jnp.float32) * scale + tbl[:, rel_bucket(dist_own)]
        mask_own = jnp.broadcast_to(dist_own >= 0, logit_own.shape)
        if n_top > 0:
            gate = jnp.einsum('bhqd,bhnd->bhqn', qb.astype(jnp.float32), k_mean)
            gate = jnp.where(jnp.arange(nb) < c, gate, NEG_INF)
            _, idx = lax.top_k(gate, n_top)
            sel_ok = idx < c
            n_k = n_top * MOBA_BLOCK
            ks = k_blocks[bi, hi, idx].reshape(B, H, Qb, n_k, hd)
            vs = v_blocks[bi, hi, idx].reshape(B, H, Qb, n_k, hd)
            pos = (idx[..., None] * MOBA_BLOCK + jnp.arange(MOBA_BLOCK)).reshape(B, H, Qb, n_k)
            dist_sel = t[None, None, :, None] - pos
            logit_sel = jnp.einsum('bhqd,bhqkd->bhqk', qb, ks).astype(jnp.float32) * scale + tbl[hi, rel_bucket(dist_sel)]
            mask_sel = jnp.repeat(sel_ok, MOBA_BLOCK, axis=-1)
            p, _ = masked_softmax(jnp.concatenate([logit_sel, logit_own], axis=-1),
                                  jnp.concatenate([mask_sel, mask_own], axis=-1))
            p = p.astype(v.dtype)
            return (jnp.einsum('bhqk,bhqkd->bhqd', p[..., :n_k], vs)
                    + jnp.einsum('bhqk,bhkd->bhqd', p[..., n_k:], v_own))
        p, _ = masked_softmax(logit_own, mask_own)
        return jnp.einsum('bhqk,bhkd->bhqd', p.astype(v.dtype), v_own)

    out = lax.map(block_step, jnp.arange(S // Qb))
    return out.transpose(1, 0, 3, 2, 4).reshape(B, S, H * hd)


def dilated_group(q, k, v, tbl, window, dilation):
    B, S, h, hd = q.shape
    scale = hd ** -0.5
    L = S // dilation
    wb = window // dilation
    n_blk = -(-L // wb)
    Lp = n_blk * wb

    def to_sub(t, front):
        t = t.reshape(B, L, dilation, h, hd).transpose(0, 2, 3, 1, 4)
        return jnp.pad(t, ((0, 0), (0, 0), (0, 0), (front, Lp - L), (0, 0)))

    def band(t):
        tp = to_sub(t, wb)
        prev = tp[:, :, :, :Lp].reshape(B, dilation, h, n_blk, wb, hd)
        cur = tp[:, :, :, wb:].reshape(B, dilation, h, n_blk, wb, hd)
        return jnp.concatenate([prev, cur], axis=4)

    qs = to_sub(q, 0).reshape(B, dilation, h, n_blk, wb, hd)
    kb, vb = band(k), band(v)
    qa, ka = np.arange(wb), np.arange(2 * wb)
    delta = wb + qa[:, None] - ka[None, :]
    key_idx = np.arange(n_blk)[:, None] * wb - wb + ka[None, :]
    mask = jnp.asarray(((delta >= 0) & (delta <= wb))[None] & (key_idx >= 0)[:, None, :])
    bias = tbl[:, rel_bucket(jnp.asarray(delta * dilation))]
    logits = jnp.einsum('bdhnqc,bdhnkc->bdhnqk', qs, kb).astype(jnp.float32) * scale + bias[None, None, :, None]
    p, lse = masked_softmax(logits, mask)
    o = jnp.einsum('bdhnqk,bdhnkc->bdhnqc', p.astype(v.dtype), vb)
    o = o.reshape(B, dilation, h, Lp, hd)[:, :, :, :L].transpose(0, 3, 1, 2, 4).reshape(B, S, h, hd)
    lse = lse.reshape(B, dilation, h, Lp)[:, :, :, :L].transpose(0, 3, 1, 2).reshape(B, S, h)
    return o, lse


def dilated_mixer(q, k, v, bias_tbl):
    B, S, _ = q.shape
    h, hd = DIL_HEADS_PER_GROUP, HEAD_DIM
    outs, lses = [], []
    for g, (window, dilation) in enumerate(DIL_PATTERNS):
        sl = slice(g * h * hd, (g + 1) * h * hd)
        o, lse = dilated_group(q[..., sl].reshape(B, S, h, hd), k[..., sl].reshape(B, S, h, hd),
                               v[..., sl].reshape(B, S, h, hd), bias_tbl[:, g * h:(g + 1) * h].T,
                               window, dilation)
        outs.append(o)
        lses.append(lse)
    alpha = jax.nn.softmax(jnp.stack(lses, axis=0), axis=0)
    o = jnp.sum(alpha[..., None].astype(q.dtype) * jnp.stack(outs, axis=0), axis=0)
    return o.reshape(B, S, h * hd)


def hybrid_mixer(h, w_in, rel_bias, pe_k, pe_v, phi_k1, phi_k2, phi_v1, phi_v2, w_up_a, w_up_b, w_up_c, w_o):
    B, S, D = h.shape
    cols = split_columns(h @ w_in)
    bias_a = rel_bias[:, :NSA_HEADS]
    bias_b = rel_bias[:, NSA_HEADS:NSA_HEADS + MOBA_HEADS]
    bias_c = rel_bias[:, NSA_HEADS + MOBA_HEADS:]
    y_a = nsa_mixer(cols['nsa_q'], cols['nsa_k_cmp'], cols['nsa_v_cmp'], cols['nsa_k_sel'], cols['nsa_v_sel'],
                    cols['nsa_k_win'], cols['nsa_v_win'], cols['nsa_gate'], bias_a,
                    pe_k, pe_v, phi_k1, phi_k2, phi_v1, phi_v2)
    y_b = moba_mixer(cols['moba_q'], cols['moba_k'], cols['moba_v'], bias_b)
    y_c = dilated_mixer(cols['dil_q'], cols['dil_k'], cols['dil_v'], bias_c)
    gates = jax.nn.sigmoid(cols['merge_gate']).reshape(B, S, N_BRANCHES, D)
    merged = gates[:, :, 0] * (y_a @ w_up_a) + gates[:, :, 1] * (y_b @ w_up_b) + gates[:, :, 2] * (y_c @ w_up_c)
    return merged @ w_o


def setup_inputs(seed: int = 0) -> dict:
    key = jax.random.key(seed)
    ks = jax.random.split(key, 24)
    f32 = jnp.float32
    nrm = lambda k, shape, fan_in: jax.random.normal(k, shape, f32) * fan_in ** -0.5
    gain = lambda k, shape: 1.0 + 0.01 * jax.random.normal(k, shape, f32)
    L_HD = NSA_CMP_BLOCK * HEAD_DIM
    return {
        'x': jax.random.normal(ks[0], (BATCH, SEQ, D_MODEL), f32),
        'rel_bias': 0.1 * jax.random.normal(ks[1], (REL_BUCKETS, N_HEADS_TOTAL), f32),
        'ffn1_norm': gain(ks[2], (DEPTH, D_MODEL)),
        'ffn1_w_gate': nrm(ks[3], (DEPTH, D_MODEL, D_FF), D_MODEL),
        'ffn1_w_up': nrm(ks[4], (DEPTH, D_MODEL, D_FF), D_MODEL),
        'ffn1_w_down': nrm(ks[5], (DEPTH, D_FF, D_MODEL), D_FF),
        'mix_norm': gain(ks[6], (DEPTH, D_MODEL)),
        'w_in': nrm(ks[7], (DEPTH, D_MODEL, D_IN), D_MODEL),
        'nsa_pe_k': 0.02 * jax.random.normal(ks[8], (DEPTH, NSA_CMP_BLOCK, HEAD_DIM), f32),
        'nsa_pe_v': 0.02 * jax.random.normal(ks[9], (DEPTH, NSA_CMP_BLOCK, HEAD_DIM), f32),
        'nsa_phi_k1': nrm(ks[10], (DEPTH, L_HD, NSA_CMP_HIDDEN), L_HD),
        'nsa_phi_k2': nrm(ks[11], (DEPTH, NSA_CMP_HIDDEN, HEAD_DIM), NSA_CMP_HIDDEN),
        'nsa_phi_v1': nrm(ks[12], (DEPTH, L_HD, NSA_CMP_HIDDEN), L_HD),
        'nsa_phi_v2': nrm(ks[13], (DEPTH, NSA_CMP_HIDDEN, HEAD_DIM), NSA_CMP_HIDDEN),
        'w_up_a': nrm(ks[14], (DEPTH, NSA_HEADS * HEAD_DIM, D_MODEL), NSA_HEADS * HEAD_DIM),
        'w_up_b': nrm(ks[15], (DEPTH, MOBA_HEADS * HEAD_DIM, D_MODEL), MOBA_HEADS * HEAD_DIM),
        'w_up_c': nrm(ks[16], (DEPTH, DIL_HEADS_PER_GROUP * HEAD_DIM, D_MODEL), DIL_HEADS_PER_GROUP * HEAD_DIM),
        'w_o': nrm(ks[17], (DEPTH, D_MODEL, D_MODEL), D_MODEL),
        'ffn2_norm': gain(ks[18], (DEPTH, D_MODEL)),
        'ffn2_w_gate': nrm(ks[19], (DEPTH, D_MODEL, D_FF), D_MODEL),
        'ffn2_w_up': nrm(ks[20], (DEPTH, D_MODEL, D_FF), D_MODEL),
        'ffn2_w_down': nrm(ks[21], (DEPTH, D_FF, D_MODEL), D_FF),
        'final_norm': gain(ks[22], (D_MODEL,)),
    }


def reference(x, rel_bias, ffn1_norm, ffn1_w_gate, ffn1_w_up, ffn1_w_down, mix_norm, w_in,
              nsa_pe_k, nsa_pe_v, nsa_phi_k1, nsa_phi_k2, nsa_phi_v1, nsa_phi_v2,
              w_up_a, w_up_b, w_up_c, w_o, ffn2_norm, ffn2_w_gate, ffn2_w_up, ffn2_w_down, final_norm):
    for l in range(DEPTH):
        x = x + 0.5 * swiglu(rms_norm(x, ffn1_norm[l]), ffn1_w_gate[l], ffn1_w_up[l], ffn1_w_down[l])
        x = x + hybrid_mixer(rms_norm(x, mix_norm[l]), w_in[l], rel_bias,
                             nsa_pe_k[l], nsa_pe_v[l], nsa_phi_k1[l], nsa_phi_k2[l], nsa_phi_v1[l], nsa_phi_v2[l],
                             w_up_a[l], w_up_b[l], w_up_c[l], w_o[l])
        x = x + 0.5 * swiglu(rms_norm(x, ffn2_norm[l]), ffn2_w_gate[l], ffn2_w_up[l], ffn2_w_down[l])
    return rms_norm(x, final_norm)
```

```python
import numpy as np
from contextlib import ExitStack
import concourse.bass as bass
import concourse.mybir as mybir
from concourse.bass_utils import run_bass_kernel_spmd

F32 = mybir.dt.float32
BF16 = mybir.dt.bfloat16
AF = mybir.ActivationFunctionType
ALU = mybir.AluOpType

S = 4096
D = 1024
DFF = 2816
NFC = DFF // 128
NT = S // 128
EPS = 1e-6
NEGM = -30000.0
NDS = 16


class _Op:
    __slots__ = ("eng", "fn", "deps", "dma", "signal", "sig", "dsem", "dval", "bar")


class Prog:
    ENG = ("pe", "act", "dve", "pool", "sp")

    def __init__(self, nc):
        self.nc = nc
        self.ops = []
        self.lastw = {}
        self.readers = {}
        self.last_on = {e: None for e in self.ENG}

    def op(self, eng, fn, reads=(), writes=(), dma=False):
        o = _Op()
        o.eng, o.fn, o.dma, o.signal, o.bar = eng, fn, dma, False, None
        deps = set()
        for r in reads:
            w = self.lastw.get(r)
            if w is not None:
                deps.add(w)
        for w_ in writes:
            w = self.lastw.get(w_)
            if w is not None:
                deps.add(w)
            for rd in self.readers.get(w_, ()):
                deps.add(rd)
        idx = len(self.ops)
        for w_ in writes:
            self.lastw[w_] = idx
            self.readers[w_] = []
        for r in reads:
            self.readers.setdefault(r, []).append(idx)
        deps.discard(idx)
        best = {}
        pruned = []
        for d in deps:
            p = self.ops[d]
            if p.dma:
                pruned.append(d)
            elif best.get(p.eng, -1) < d:
                best[p.eng] = d
        pruned.extend(best.values())
        o.deps = sorted(pruned)
        for d in o.deps:
            self.ops[d].signal = True
        self.ops.append(o)
        self.last_on[eng] = idx
        return idx

    def barrier(self):
        o = _Op()
        o.eng, o.fn, o.dma, o.signal, o.deps = None, None, False, False, []
        o.bar = dict(self.last_on)
        for e, i in o.bar.items():
            if i is not None and not self.ops[i].dma:
                self.ops[i].signal = True
        self.ops.append(o)

    def emit(self, es):
        nc = self.nc
        E = {"pe": nc.tensor, "act": nc.scalar, "dve": nc.vector, "pool": nc.gpsimd, "sp": nc.sync}
        sem = {e: es.enter_context(nc.semaphore("s_" + e)) for e in self.ENG}
        dsem = {q: [es.enter_context(nc.semaphore("d_%s%d" % (q, i))) for i in range(NDS)]
                for q in ("sp", "pool")}
        cnt = {e: 0 for e in self.ENG}
        dcnt = {"sp": 0, "pool": 0}
        seen = {e: {} for e in self.ENG}

        def wait(e, s, v):
            key = id(s)
            if seen[e].get(key, 0) < v:
                E[e].wait_ge(s, v)
                seen[e][key] = v

        for o in self.ops:
            if o.bar is not None:
                for q in ("sp", "pool"):
                    k = dcnt[q]
                    if k == 0:
                        continue
                    for i in range(min(k, NDS)):
                        uses = (k - 1 - i) // NDS + 1
                        wait(q, dsem[q][i], 16 * uses)
                    E[q].sem_inc(sem[q], 1)
                    cnt[q] += 1
                for f in self.ENG:
                    for e in self.ENG:
                        if e != f and cnt[e] > 0:
                            wait(f, sem[e], cnt[e])
                continue
            e = o.eng
            for d in o.deps:
                p = self.ops[d]
                if p.dma:
                    wait(e, p.dsem, p.dval)
                else:
                    if p.eng == e and e == "pe":
                        continue
                    wait(e, sem[p.eng], p.sig)
            if o.dma:
                k = dcnt[e]
                s = dsem[e][k % NDS]
                v = 16 * (k // NDS + 1)
                if k >= NDS:
                    wait(e, s, v - 16)
                ins = o.fn()
                ins.then_inc(s, 16)
                o.dsem, o.dval = s, v
                dcnt[e] = k + 1
            else:
                ins = o.fn()
                if o.signal:
                    cnt[e] += 1
                    o.sig = cnt[e]
                    ins.then_inc(sem[e], 1)
        self.ops = []


class K:
    def __init__(self, nc, P):
        self.nc, self.P = nc, P
        self.uid = 0

    def name(self, s):
        self.uid += 1
        return "%s_%d" % (s, self.uid)

    def sb(self, es, nm, shape, dt):
        return es.enter_context(self.nc.sbuf_tensor(self.name(nm), list(shape), dt))

    def ps(self, es, nm, shape, dt):
        return es.enter_context(self.nc.psum_tensor(self.name(nm), list(shape), dt))

    def dma(self, q, out, in_, reads=(), writes=(), slow=False):
        nc = self.nc
        eng = nc.sync if q == "sp" else nc.gpsimd
        if slow:
            return self.P.op(q, lambda: eng.dma_start(out=out, in_=in_, allow_slow_non_contiguous=True),
                             reads, writes, dma=True)
        return self.P.op(q, lambda: eng.dma_start(out=out, in_=in_), reads, writes, dma=True)

    def mm(self, out, lhsT, rhs, start, stop, reads=(), writes=()):
        nc = self.nc
        return self.P.op("pe", lambda: nc.tensor.matmul(out, lhsT=lhsT, rhs=rhs, start=start, stop=stop),
                         reads, writes)

    def tr(self, out, in_, ident, reads=(), writes=()):
        nc = self.nc
        return self.P.op("pe", lambda: nc.tensor.transpose(out, in_, ident), reads, writes)

    def act(self, out, in_, func, reads=(), writes=(), **kw):
        nc = self.nc
        return self.P.op("act", lambda: nc.scalar.activation(out=out, in_=in_, func=func, **kw), reads, writes)

    def v(self, eng, name, reads=(), writes=(), **kw):
        nc = self.nc
        e = nc.vector if eng == "dve" else nc.gpsimd
        return self.P.op(eng, lambda: getattr(e, name)(**kw), reads, writes)


def dap(t, offset, pattern):
    return bass.AP(tensor=t, offset=offset, ap=[list(p) for p in pattern])


def load_cast_weight(k, stg, stg_tok, dst_fn, src_fn, nchunks, width, q_alt=True):
    for c in range(nchunks):
        b = c % 2
        k.dma("sp" if (c % 2 == 0 or not q_alt) else "pool", stg[b][:, 0:width], src_fn(c),
              writes=[stg_tok[b]])
        k.v("pool", "tensor_copy", reads=[stg_tok[b]], writes=[("w", id(dst_fn), c)],
            out=dst_fn(c), in_=stg[b][:, 0:width])


def ffn_phase(k, xsrc, xdst, w_norm, w_gate, w_up, w_down, ident):
    nc, P = k.nc, k.P
    G = 512
    with ExitStack() as es:
        wg = k.sb(es, "wg", [128, 8, DFF], BF16)
        wu = k.sb(es, "wu", [128, 8, DFF], BF16)
        wd = k.sb(es, "wd", [128, NFC, D], BF16)
        HW = DFF // 2
        stg = [k.sb(es, "stg", [128, HW], F32) for _ in range(2)]
        gain = k.sb(es, "gain", [128, D], F32)
        xin = [k.sb(es, "xin", [128, D], F32) for _ in range(2)]
        junk = k.sb(es, "junk", [128, D], BF16)
        hn = [k.sb(es, "hn", [128, D], BF16) for _ in range(2)]
        hT = k.sb(es, "hT", [128, 8, G], BF16)
        aT = k.sb(es, "aT", [128, NFC, G], BF16)
        sg = [k.sb(es, "sg", [128, G], F32) for _ in range(2)]
        xr = [k.sb(es, "xr", [128, 512], F32) for _ in range(2)]
        ob = [k.sb(es, "ob", [128, 512], F32) for _ in range(2)]
        st = [k.sb(es, "st", [128, 4], F32) for _ in range(2)]
        pgu = [k.ps(es, "pgu", [128, 512], F32) for _ in range(4)]
        ptp = [k.ps(es, "ptp", [128, 1024], BF16) for _ in range(2)]
        pdn = [k.ps(es, "pdn", [128, 512], F32) for _ in range(2)]

        k.dma("sp", gain[:], dap(w_norm.tensor, w_norm.offset, [[0, 128], [1, D]]), writes=["gain"])
        wgv = w_gate.rearrange("(kc p) f -> p kc f", p=128)
        wuv = w_up.rearrange("(kc p) f -> p kc f", p=128)
        wdv = w_down.rearrange("(fc p) d -> p fc d", p=128)
        ci = 0
        for (dst, srcv, n, width) in ((wg, wgv, 8, DFF), (wu, wuv, 8, DFF)):
            for c in range(n):
                for hh in range(2):
                    b = ci % 2
                    ci += 1
                    k.dma("sp" if b == 0 else "pool", stg[b][:, 0:HW], srcv[:, c, hh * HW:(hh + 1) * HW],
                          writes=[("stg", b)])
                    k.v("pool", "tensor_copy", reads=[("stg", b)], writes=[("w", id(dst))],
                        out=dst[:, c, hh * HW:(hh + 1) * HW], in_=stg[b][:, 0:HW])
        for c in range(NFC):
            b = ci % 2
            ci += 1
            k.dma("sp" if b == 0 else "pool", stg[b][:, 0:D], wdv[:, c, :], writes=[("stg", b)])
            k.v("pool", "tensor_copy", reads=[("stg", b)], writes=[("w", id(wd))],
                out=wd[:, c, :], in_=stg[b][:, 0:D])

        gi = 0
        di = 0
        for g in range(S // G):
            for t in range(G // 128):
                r0 = g * G + t * 128
                b = t % 2
                k.dma("sp", xin[b][:], xsrc[r0:r0 + 128, :], writes=[("xin", b)])
                k.v("dve", "scalar_tensor_tensor", reads=[("xin", b)], writes=["junk", ("st", b)],
                    out=junk[:], in0=xin[b][:], scalar=1.0, in1=xin[b][:], op0=ALU.mult, op1=ALU.mult,
                    accum_out=st[b][:, 0:1])
                k.act(st[b][:, 1:2], st[b][:, 0:1], AF.Sqrt, reads=[("st", b)], writes=[("st1", b)],
                      scale=1.0 / D, bias=EPS)
                k.v("dve", "reciprocal", reads=[("st1", b)], writes=[("st2", b)],
                    out=st[b][:, 2:3], in_=st[b][:, 1:2])
                k.v("dve", "scalar_tensor_tensor", reads=[("xin", b), ("st2", b), "gain"], writes=[("hn", b)],
                    out=hn[b][:], in0=xin[b][:], scalar=st[b][:, 2:3], in1=gain[:], op0=ALU.mult, op1=ALU.mult)
                for kc in range(8):
                    k.tr(ptp[b][:, kc * 128:(kc + 1) * 128], hn[b][:, kc * 128:(kc + 1) * 128], ident[:],
                         reads=[("hn", b), "ident"], writes=[("ptp", b)])
                k.act(hT[:, :, t * 128:(t + 1) * 128], ptp[b][:].rearrange("p (a c) -> p a c", a=8), AF.Copy,
                      reads=[("ptp", b)], writes=[("hT", t)])
            hT_all = [("hT", t) for t in range(G // 128)]
            for fc in range(NFC):
                pb = (gi % 2) * 2
                gi += 1
                for kc in range(8):
                    k.mm(pgu[pb][:], wg[:, kc, fc * 128:(fc + 1) * 128], hT[:, kc, :], kc == 0, kc == 7,
                         reads=hT_all + [("w", id(wg))], writes=[("pgu", pb)])
                for kc in range(8):
                    k.mm(pgu[pb + 1][:], wu[:, kc, fc * 128:(fc + 1) * 128], hT[:, kc, :], kc == 0, kc == 7,
                         reads=hT_all + [("w", id(wu))], writes=[("pgu", pb + 1)])
                sb_ = fc % 2
                k.act(sg[sb_][:], pgu[pb][:], AF.Silu, reads=[("pgu", pb)], writes=[("sg", sb_)])
                k.v("dve", "tensor_tensor", reads=[("sg", sb_), ("pgu", pb + 1)], writes=[("aT", fc)],
                    out=aT[:, fc, :], in0=sg[sb_][:], in1=pgu[pb + 1][:], op=ALU.mult)
            aT_all = [("aT", fc) for fc in range(NFC)]
            for t in range(G // 128):
                r0 = g * G + t * 128
                for hh in range(2):
                    b = di % 2
                    di += 1
                    k.dma("pool", xr[b][:], xsrc[r0:r0 + 128, hh * 512:(hh + 1) * 512], writes=[("xr", b)])
                    for fc in range(NFC):
                        k.mm(pdn[b][:], aT[:, fc, t * 128:(t + 1) * 128], wd[:, fc, hh * 512:(hh + 1) * 512],
                             fc == 0, fc == NFC - 1, reads=aT_all + [("w", id(wd))], writes=[("pdn", b)])
                    k.v("dve", "scalar_tensor_tensor", reads=[("pdn", b), ("xr", b)], writes=[("ob", b)],
                        out=ob[b][:], in0=pdn[b][:], scalar=0.5, in1=xr[b][:], op0=ALU.mult, op1=ALU.add)
                    k.dma("sp", xdst[r0:r0 + 128, hh * 512:(hh + 1) * 512], ob[b][:], reads=[("ob", b)])
    P.barrier()


def final_norm_phase(k, xsrc, ydst, w_norm):
    P = k.P
    with ExitStack() as es:
        gain = k.sb(es, "fgain", [128, D], F32)
        xin = [k.sb(es, "fxin", [128, D], F32) for _ in range(2)]
        yo = [k.sb(es, "fyo", [128, D], F32) for _ in range(2)]
        junk = k.sb(es, "fjunk", [128, D], BF16)
        st = [k.sb(es, "fst", [128, 4], F32) for _ in range(2)]
        k.dma("sp", gain[:], dap(w_norm.tensor, w_norm.offset, [[0, 128], [1, D]]), writes=["fgain"])
        for t in range(NT):
            b = t % 2
            r0 = t * 128
            k.dma("sp", xin[b][:], xsrc[r0:r0 + 128, :], writes=[("fxin", b)])
            k.v("dve", "scalar_tensor_tensor", reads=[("fxin", b)], writes=["fjunk", ("fst", b)],
                out=junk[:], in0=xin[b][:], scalar=1.0, in1=xin[b][:], op0=ALU.mult, op1=ALU.mult,
                accum_out=st[b][:, 0:1])
            k.act(st[b][:, 1:2], st[b][:, 0:1], AF.Sqrt, reads=[("fst", b)], writes=[("fst1", b)],
                  scale=1.0 / D, bias=EPS)
            k.v("dve", "reciprocal", reads=[("fst1", b)], writes=[("fst2", b)],
                out=st[b][:, 2:3], in_=st[b][:, 1:2])
            k.v("dve", "scalar_tensor_tensor", reads=[("fxin", b), ("fst2", b), "fgain"], writes=[("fyo", b)],
                out=yo[b][:], in0=xin[b][:], scalar=st[b][:, 2:3], in1=gain[:], op0=ALU.mult, op1=ALU.mult)
            k.dma("pool", ydst[r0:r0 + 128, :], yo[b][:], reads=[("fyo", b)])
    P.barrier()


NQK = 2176
NVC = 896
NVG = NVC + 18
R_NSAQ, R_KCMP, R_VCMP, R_KSEL, R_KWIN, R_MQ, R_MK, R_DQ, R_DK = 0, 384, 512, 640, 768, 896, 1152, 1408, 1792
C_VSEL, C_VWIN, C_MV, C_DV = 0, 128, 256, 512
FAM = {
    "full": (2560, 2432, 1),
    "win": (1536, 1408, 1),
    "cmp": (6144, 4096, 16),
    "dil0": (1152, 1024, 1),
    "dil1": (1536, 1408, 1),
    "dil2": (3072, 2944, 1),
}
DIL = ((128, 1), (512, 4), (2048, 16))
SCALE = 0.125
NEGP = -240000.0


def np_rel_bucket(dist):
    n = np.maximum(dist, 0)
    nf = np.maximum(n, 16).astype(np.float32)
    lg = (np.log(nf / np.float32(16)) / np.float32(np.log(2048 / 16)) * np.float32(16)).astype(np.float32)
    large = 16 + lg.astype(np.int32)
    large = np.minimum(large, 31)
    return np.where(n < 16, n, large)


def host_consts():
    c = {}
    c["c_ident"] = np.eye(128, dtype=np.float32)
    c["c_flip"] = np.ascontiguousarray(np.eye(128, dtype=np.float32)[::-1])
    def fam_oh(length, off, valid_fn):
        w = np.arange(length)
        d = w - off
        ok = valid_fn(d)
        b = np_rel_bucket(d)
        oh = np.zeros((33, length), np.float32)
        oh[b[ok], w[ok]] = 1.0
        oh[32, ~ok] = 1.0
        return oh
    c["oh_full"] = fam_oh(2560, 511, lambda d: d >= 0)
    c["oh_win"] = fam_oh(1536, 511, lambda d: (d >= 0) & (d < 512))
    c["oh_cmp"] = fam_oh(6144, 2063, lambda d: d >= 0)
    for g, (W, dl) in enumerate(DIL):
        c["oh_dil%d" % g] = fam_oh(FAM["dil%d" % g][0], 511, lambda d: (d >= 0) & (d <= W) & (d % dl == 0))
    c["c_negrow"] = np.full((1, 16), NEGM, np.float32)
    n_cmp = 255
    c_start = np.arange(n_cmp) * 16
    s_start = np.arange(64) * 64
    ov = (c_start[:, None] < s_start[None, :] + 64) & (c_start[:, None] + 32 > s_start[None, :])
    cts = np.zeros((256, 65), np.float32)
    cts[:255, :64] = ov
    cts[:255, 64] = 1.0
    c["c_cts"] = cts
    t = np.arange(S)
    blk = np.arange(64)
    cur = t // 64
    keep = np.ones((S, 64), np.float32)
    add = np.zeros((S, 64), np.float32)
    f0 = np.broadcast_to(blk[None, :] == 0, (S, 64))
    f1 = blk[None, :] == cur[:, None]
    f2 = blk[None, :] == cur[:, None] - 1
    fut = blk[None, :] * 64 > t[:, None]
    for f, val in ((f0, 1e4), (f2, 3e4), (f1, 2e4)):
        keep[f] = 0.0
        add[f] = val
    keep[fut] = 0.0
    add[fut] = -1e30
    c["c_keep"] = keep
    c["c_add"] = add
    c["c_esel"] = (np.arange(S)[None, :] // 64 == np.arange(64)[:, None]).astype(np.float32)
    nb = np.arange(16)
    cb = t // 256
    valid = (nb[None, :] < cb[:, None]).astype(np.float32)
    own = (nb[None, :] == cb[:, None]).astype(np.float32)
    c["c_mvalid"] = valid
    c["c_maddm"] = np.where(valid > 0, 0.0, -1e30).astype(np.float32)
    c["c_mown"] = ((own - 1.0) * (-NEGP)).astype(np.float32)
    eb = np.zeros((16, 16, 128), np.float32)
    for n in range(16):
        eb[n, n, :] = 1.0
    c["c_eb"] = eb.reshape(16, 16 * 128)
    return c


CONST_SHAPES = {
    "c_ident": [128, 128], "c_flip": [128, 128], "oh_full": [33, 2560], "oh_win": [33, 1536],
    "oh_cmp": [33, 6144], "oh_dil0": [33, 1152], "oh_dil1": [33, 1536], "oh_dil2": [33, 3072],
    "c_negrow": [1, 16], "c_cts": [256, 65], "c_keep": [S, 64], "c_add": [S, 64], "c_esel": [64, S],
    "c_mvalid": [S, 16], "c_maddm": [S, 16], "c_mown": [S, 16], "c_eb": [16, 2048],
}


def bias_gen_phase(k, rel_bias, consts, gd):
    P = k.P
    with ExitStack() as es:
        tblx = k.sb(es, "tblx", [33, 16], F32)
        oh = k.sb(es, "oh", [33, 6144], F32)
        go = k.sb(es, "go", [16, 6144], F32)
        pb = [k.ps(es, "pbg", [128, 512], F32) for _ in range(2)]
        k.dma("sp", tblx[0:32, :], rel_bias, writes=["tblx"])
        k.dma("sp", tblx[32:33, :], consts["c_negrow"], writes=["tblx"])
        i = 0
        for fam, (L, _, _) in FAM.items():
            k.dma("sp", oh[:, 0:L], consts["oh_" + fam], writes=["oh"])
            for c0 in range(0, L, 512):
                b = i % 2
                i += 1
                k.mm(pb[b][0:16, :], tblx[:, :], oh[:, c0:c0 + 512], True, True,
                     reads=["tblx", "oh"], writes=[("pbg", b)])
                k.v("dve", "tensor_copy", reads=[("pbg", b)], writes=["go"], out=go[:, c0:c0 + 512],
                    in_=pb[b][0:16, :])
            k.dma("sp", gd[fam], go[:, 0:L], reads=["go"])
    P.barrier()


def make_tb(k, fam, gd, h, flipf, rev, tb, pflip, tok):
    L, W, st = FAM[fam]
    g = gd[fam]
    k.dma("sp", rev[:, 0:W], dap(g.tensor, g.offset + h * L, [[st, 128], [1, W]]), writes=["rev"])
    i = 0
    for c0 in range(0, W, 512):
        n = min(512, W - c0)
        b = i % 2
        i += 1
        k.mm(pflip[b][:, 0:n], flipf[:, :], rev[:, c0:c0 + n], True, True, reads=["flipf", "rev"],
             writes=[("pflip", b)])
        k.v("dve", "tensor_copy", reads=[("pflip", b)], writes=[tok], out=tb[:, c0:c0 + n], in_=pflip[b][:, 0:n])


def proj_phase(k, xsrc, w_norm, w_qk, w_v, w_mg, ident, qkT, vtok, gtok, mgT):
    P = k.P
    with ExitStack() as es:
        hT = k.sb(es, "phT", [128, 8, S], BF16)
        gain = k.sb(es, "pgain", [128, D], F32)
        xin = [k.sb(es, "pxin", [128, D], F32) for _ in range(2)]
        junk = k.sb(es, "pjunk", [128, D], BF16)
        hn = [k.sb(es, "phn", [128, D], BF16) for _ in range(2)]
        st = [k.sb(es, "pst", [128, 4], F32) for _ in range(2)]
        wvs = k.sb(es, "wvs", [128, NVG], F32)
        wvb = k.sb(es, "wvb", [128, 8, NVG], BF16)
        wst = [k.sb(es, "wst", [128, 8, 128], F32) for _ in range(2)]
        wb = [k.sb(es, "wb", [128, 8, 128], BF16) for _ in range(2)]
        orow = [k.sb(es, "orow", [128, S], BF16) for _ in range(2)]
        vout = [k.sb(es, "vout", [128, NVC], BF16) for _ in range(2)]
        gout = [k.sb(es, "gout", [128, 18], F32) for _ in range(2)]
        ptp = [k.ps(es, "pptp", [128, 1024], BF16) for _ in range(2)]
        pmm = [k.ps(es, "ppmm", [128, 512], F32) for _ in range(4)]
        k.dma("sp", gain[:], dap(w_norm.tensor, w_norm.offset, [[0, 128], [1, D]]), writes=["pgain"])
        wvv = w_v.rearrange("(kc p) f -> p kc f", p=128)
        for kc in range(8):
            k.dma("pool", wvs[:], wvv[:, kc, :], writes=["wvs"])
            k.v("pool", "tensor_copy", reads=["wvs"], writes=["wvb"], out=wvb[:, kc, :], in_=wvs[:])
        for t in range(NT):
            b = t % 2
            r0 = t * 128
            k.dma("sp", xin[b][:], xsrc[r0:r0 + 128, :], writes=[("pxin", b)])
            k.v("dve", "scalar_tensor_tensor", reads=[("pxin", b)], writes=["pjunk", ("pst", b)],
                out=junk[:], in0=xin[b][:], scalar=1.0, in1=xin[b][:], op0=ALU.mult, op1=ALU.mult,
                accum_out=st[b][:, 0:1])
            k.act(st[b][:, 1:2], st[b][:, 0:1], AF.Sqrt, reads=[("pst", b)], writes=[("pst1", b)],
                  scale=1.0 / D, bias=EPS)
            k.v("dve", "reciprocal", reads=[("pst1", b)], writes=[("pst2", b)],
                out=st[b][:, 2:3], in_=st[b][:, 1:2])
            k.v("dve", "scalar_tensor_tensor", reads=[("pxin", b), ("pst2", b), "pgain"], writes=[("phn", b)],
                out=hn[b][:], in0=xin[b][:], scalar=st[b][:, 2:3], in1=gain[:], op0=ALU.mult, op1=ALU.mult)
            for kc in range(8):
                k.tr(ptp[b][:, kc * 128:(kc + 1) * 128], hn[b][:, kc * 128:(kc + 1) * 128], ident[:],
                     reads=[("phn", b), "ident"], writes=[("pptp", b)])
            k.act(hT[:, :, r0:r0 + 128], ptp[b][:].rearrange("p (a c) -> p a c", a=8), AF.Copy,
                  reads=[("pptp", b)], writes=[("phT", t)])
            pa, pb_ = pmm[(t % 2) * 2], pmm[(t % 2) * 2 + 1]
            ta, tb_ = ("ppmm", (t % 2) * 2), ("ppmm", (t % 2) * 2 + 1)
            for kc in range(8):
                k.mm(pa[:, 0:512], hT[:, kc, r0:r0 + 128], wvb[:, kc, 0:512], kc == 0, kc == 7,
                     reads=[("phT", t), "wvb"], writes=[ta])
            for kc in range(8):
                k.mm(pb_[:, 0:NVG - 512], hT[:, kc, r0:r0 + 128], wvb[:, kc, 512:NVG], kc == 0, kc == 7,
                     reads=[("phT", t), "wvb"], writes=[tb_])
            k.act(vout[b][:, 0:512], pa[:, 0:512], AF.Copy, reads=[ta], writes=[("vout", b)])
            k.v("dve", "tensor_copy", reads=[tb_], writes=[("vout", b)], out=vout[b][:, 512:NVC],
                in_=pb_[:, 0:NVC - 512])
            k.act(gout[b][:], pb_[:, NVC - 512:NVG - 512], AF.Sigmoid, reads=[tb_], writes=[("gout", b)])
            k.dma("pool", vtok[r0:r0 + 128, :], vout[b][:], reads=[("vout", b)])
            k.dma("pool", gtok[r0:r0 + 128, :], gout[b][:], reads=[("gout", b)])
        allh = [("phT", t) for t in range(NT)]
        blocks = [("qk", i) for i in range(NQK // 128)] + [("mg", i) for i in range(3 * D // 128)]
        mi = 0
        for bi, (kind, i) in enumerate(blocks):
            b = bi % 2
            wsrc = (w_qk if kind == "qk" else w_mg)[:, i * 128:(i + 1) * 128].rearrange("(kc p) f -> p kc f", p=128)
            k.dma("pool", wst[b][:], wsrc, writes=[("wst", b)])
            k.v("dve", "tensor_copy", reads=[("wst", b)], writes=[("wb", b)], out=wb[b][:], in_=wst[b][:])
            for tb8 in range(8):
                pi = mi % 4
                mi += 1
                for kc in range(8):
                    k.mm(pmm[pi][:], wb[b][:, kc, :], hT[:, kc, tb8 * 512:(tb8 + 1) * 512], kc == 0, kc == 7,
                         reads=allh + [("wb", b)], writes=[("ppmm", pi)])
                if kind == "mg":
                    k.act(orow[b][:, tb8 * 512:(tb8 + 1) * 512], pmm[pi][:], AF.Sigmoid,
                          reads=[("ppmm", pi)], writes=[("orow", b)])
                elif tb8 % 2 == 0:
                    k.act(orow[b][:, tb8 * 512:(tb8 + 1) * 512], pmm[pi][:], AF.Copy,
                          reads=[("ppmm", pi)], writes=[("orow", b)])
                else:
                    k.v("dve", "tensor_copy", reads=[("ppmm", pi)], writes=[("orow", b)],
                        out=orow[b][:, tb8 * 512:(tb8 + 1) * 512], in_=pmm[pi][:])
            dst = (qkT if kind == "qk" else mgT)[i * 128:(i + 1) * 128, :]
            k.dma("sp", dst, orow[b][:], reads=[("orow", b)])
    P.barrier()


def compress_phase(k, qkT, pe_k, pe_v, phi_k1, phi_k2, phi_v1, phi_v2, kcT, vcs, identf):
    P = k.P
    C1 = 1.5957691216057308
    with ExitStack() as es:
        src = k.sb(es, "csrc", [128, S], BF16)
        w1s = k.sb(es, "w1s", [128, 8, 256], F32)
        w1b = k.sb(es, "w1b", [128, 32, 256], BF16)
        w2s = k.sb(es, "w2s", [128, 2, 64], F32)
        w2b = k.sb(es, "w2b", [128, 2, 64], BF16)
        pes = k.sb(es, "pes", [128, 64], F32)
        peb = k.sb(es, "peb", [128, 32], BF16)
        bias = k.sb(es, "cbias", [128, 2], F32)
        xa = k.sb(es, "cxa", [128, 256], F32)
        xb_ = k.sb(es, "cxb", [128, 256], F32)
        xc = k.sb(es, "cxc", [128, 256], F32)
        hid = [k.sb(es, "chid", [128, 256], BF16) for _ in range(2)]
        ko = k.sb(es, "cko", [64, 256], BF16)
        vo = k.sb(es, "cvo", [128, 64], BF16)
        ph = [k.ps(es, "cph", [128, 512], F32) for _ in range(2)]
        pbias = k.ps(es, "cpb", [128, 512], F32)
        po = k.ps(es, "cpo", [128, 512], F32)
        for which, (r0, pe, w1, w2) in enumerate(((R_KCMP, pe_k, phi_k1, phi_k2), (R_VCMP, pe_v, phi_v1, phi_v2))):
            k.dma("sp", src[:], qkT[r0:r0 + 128, :], writes=["csrc"])
            w1v = w1.rearrange("(l d) h -> d l h", d=64)
            for half in range(2):
                for l0 in range(0, 32, 8):
                    k.dma("pool", w1s[half * 64:(half + 1) * 64, :, :], w1v[:, l0:l0 + 8, :], writes=["w1s"])
                    k.v("pool", "tensor_copy", reads=["w1s"], writes=["w1b"],
                        out=w1b[half * 64:(half + 1) * 64, l0:l0 + 8, :], in_=w1s[half * 64:(half + 1) * 64, :, :])
            k.dma("sp", pes[0:32, 0:64], pe, writes=["pes"])
            k.tr(pbias[0:64, 64:96], pes[0:32, 0:64], identf[0:32, 0:32], reads=["pes", "identf"], writes=["cpb"])
            k.v("dve", "tensor_copy", reads=["cpb"], writes=["peb"], out=peb[0:64, :], in_=pbias[0:64, 64:96])
            k.dma("sp", w2s[:], w2.rearrange("(hc p) d -> p hc d", p=128), writes=["w2s"])
            k.v("dve", "tensor_copy", reads=["w2s"], writes=["w2b"], out=w2b[:], in_=w2s[:])
            for hc in range(2):
                for l in range(32):
                    k.mm(pbias[:, hc:hc + 1], w1b[0:64, l, hc * 128:(hc + 1) * 128], peb[0:64, l:l + 1],
                         l == 0, l == 31, reads=["w1b", "peb"], writes=["cpb"])
            k.v("dve", "tensor_copy", reads=["cpb"], writes=["cbias"], out=bias[:], in_=pbias[:, 0:2])
            for g in range(2):
                p0 = g * 64
                for hc in range(2):
                    for l in range(32):
                        k.mm(ph[hc][:, 0:255], w1b[p0:p0 + 64, l, hc * 128:(hc + 1) * 128],
                             src[p0:p0 + 64, l:l + 16 * 254 + 1:16], l == 0, l == 31,
                             reads=["w1b", "csrc"], writes=[("cph", hc)])
                    k.v("dve", "tensor_scalar", reads=[("cph", hc), "cbias"], writes=["cxa"], out=xa[:, 0:255],
                        in0=ph[hc][:, 0:255], scalar1=bias[:, hc:hc + 1], scalar2=None, op0=ALU.add)
                    k.v("dve", "tensor_tensor", reads=["cxa"], writes=["cxb"], out=xb_[:, 0:255], in0=xa[:, 0:255],
                        in1=xa[:, 0:255], op=ALU.mult)
                    k.v("dve", "tensor_scalar", reads=["cxb"], writes=["cxc"], out=xc[:, 0:255], in0=xb_[:, 0:255],
                        scalar1=0.044715, scalar2=1.0, op0=ALU.mult, op1=ALU.add)
                    k.v("dve", "tensor_tensor", reads=["cxc", "cxa"], writes=["cxb"], out=xb_[:, 0:255],
                        in0=xc[:, 0:255], in1=xa[:, 0:255], op=ALU.mult)
                    k.act(xc[:, 0:255], xb_[:, 0:255], AF.Sigmoid, reads=["cxb"], writes=["cxc"], scale=C1)
                    k.v("dve", "tensor_tensor", reads=["cxc", "cxa"], writes=[("chid", hc)], out=hid[hc][:, 0:255],
                        in0=xc[:, 0:255], in1=xa[:, 0:255], op=ALU.mult)
                hh = [("chid", 0), ("chid", 1)]
                if which == 0:
                    for hc in range(2):
                        k.mm(po[0:64, 0:255], w2b[:, hc, :], hid[hc][:, 0:255], hc == 0, hc == 1,
                             reads=hh + ["w2b"], writes=["cpo"])
                    k.v("dve", "tensor_copy", reads=["cpo"], writes=["cko"], out=ko[:, 0:255], in_=po[0:64, 0:255])
                    k.dma("sp", kcT[g, :, 0:255], ko[:, 0:255], reads=["cko"])
                else:
                    for ch in range(2):
                        rows = 128 if ch == 0 else 127
                        for hc in range(2):
                            k.mm(po[0:rows, 0:64], hid[hc][:, ch * 128:ch * 128 + rows], w2b[:, hc, :],
                                 hc == 0, hc == 1, reads=hh + ["w2b"], writes=["cpo"])
                        k.v("dve", "tensor_copy", reads=["cpo"], writes=["cvo"], out=vo[0:rows, :], in_=po[0:rows, 0:64])
                        k.dma("sp", vcs[g, ch * 128:ch * 128 + rows, :], vo[0:rows, :], reads=["cvo"])
    P.barrier()


class AttnCx:
    def __init__(self, k, es, nU):
        self.S = [k.ps(es, "aS", [128, 512], F32) for _ in range(3)]
        self.U = [k.ps(es, "aU", [128, 4, 128], F32) for _ in range(nU)]
        self.L = [k.sb(es, "aL", [128, 512], F32) for _ in range(2)]
        self.E = [k.sb(es, "aE", [128, 512], BF16) for _ in range(3)]
        self.si = self.li = self.ei = 0
        self.zl = k.sb(es, "azl", [1, 128], BF16)
        self.zr = k.sb(es, "azr", [1, 512], BF16)
        k.v("dve", "memset", writes=["azl"], ap=self.zl[:], constant=0.0)
        k.v("dve", "memset", writes=["azr"], ap=self.zr[:], constant=0.0)


def attn_tile(k, cx, rows, kT, q, n, tb, cbias, mask, pv, rd):
    si = cx.si % 3
    cx.si += 1
    Sb = cx.S[si]
    k.mm(Sb[0:rows, 0:n], kT, q, True, mask is None, reads=rd, writes=[("aS", si)])
    if mask is not None:
        k.mm(Sb[0:rows, 0:n], mask[0], mask[1], False, True, reads=rd, writes=[("aS", si)])
    ei = cx.ei % 3
    cx.ei += 1
    Eb = cx.E[ei]
    if tb is not None:
        li = cx.li % 2
        cx.li += 1
        k.v("dve", "scalar_tensor_tensor", reads=[("aS", si)] + rd, writes=[("aL", li)],
            out=cx.L[li][0:rows, 0:n], in0=Sb[0:rows, 0:n], scalar=SCALE, in1=tb, op0=ALU.mult, op1=ALU.add)
        k.act(Eb[0:rows, 0:n], cx.L[li][0:rows, 0:n], AF.Exp, reads=[("aL", li)], writes=[("aE", ei)])
    else:
        k.act(Eb[0:rows, 0:n], Sb[0:rows, 0:n], AF.Exp, reads=[("aS", si)] + rd, writes=[("aE", ei)],
              scale=SCALE, bias=cbias)
    for (U, c0, V, st_, sp_, utok) in pv:
        k.mm(U, Eb[0:rows, c0:c0 + 128], V, st_, sp_, reads=[("aE", ei)] + rd, writes=[utok])


def run_branch(k, cx, tiles, ui, utok):
    first = {}
    last = {}
    for i, t in enumerate(tiles):
        for qt in range(t["qt_lo"], t["qt_hi"]):
            first.setdefault(qt, i)
            last[qt] = i
    k.mm(cx.U[ui][:].rearrange("p a b -> p (a b)"), cx.zl[0:1, :], cx.zr[0:1, :], True, False,
         reads=["azl", "azr"], writes=[utok])
    for i, t in enumerate(tiles):
        lo, hi = t["qt_lo"], t["qt_hi"]
        n = (hi - lo) * 128
        pv = [(cx.U[ui][:, qt, 0:65], (qt - lo) * 128, t["V"], False, last[qt] == i, utok)
              for qt in range(lo, hi)]
        attn_tile(k, cx, t["rows"], t["kT"], t["qfn"](lo * 128, n), n,
                  t["tbfn"](lo * 128, n) if t["tbfn"] is not None else None, t["cbias"],
                  t["maskfn"](lo * 128, n) if t["maskfn"] is not None else None, pv, t["rd"])
    return sorted(first.keys())


def dma_split(k, q, dst, src, tok, step=8):
    for c0 in range(0, NT, step):
        k.dma(q, dst[:, c0:c0 + step, :], src[:, c0:c0 + step, :], writes=[tok])


def load_v_aug(k, q, vt, vtok, col0, tok):
    k.v("pool", "memset", writes=[tok], ap=vt[:, :, 64:65], constant=1.0)
    src = vtok[:, col0:col0 + 64].rearrange("(c p) d -> p c d", p=128)
    for c0 in range(0, NT, 8):
        k.dma(q, vt[:, c0:c0 + 8, 0:64], src[:, c0:c0 + 8, :], writes=[tok])


def cmp_tiles(qb, kc_sb, q_sb, tbc, V_sb, rd):
    qs = qb * 512
    tiles = []
    for ch in range(2):
        Dd = qs - 2048 * ch
        if Dd < 0:
            continue
        rows = 128 if ch == 0 else 127
        tiles.append(dict(
            rows=rows, kT=kc_sb[:, ch * 128:ch * 128 + rows],
            qfn=lambda c0, n, qs=qs: q_sb[:, qs + c0:qs + c0 + n],
            tbfn=lambda c0, n, Dd=Dd, rows=rows: tbc[0:rows, Dd + c0:Dd + c0 + n],
            cbias=None, maskfn=None, V=V_sb[0:rows, ch, :], qt_lo=0, qt_hi=4, rd=rd))
    return tiles


def nsa_phase(k, qkT, vtok, gtok, kcT, vcs, gd, consts, identf, ytok_sb):
    P = k.P
    with ExitStack() as es:
        flipf = k.sb(es, "flipf", [128, 128], F32)
        rev = k.sb(es, "rev", [128, 4096], F32)
        tbc = k.sb(es, "tbc", [128, 4096], F32)
        tbf = k.sb(es, "tbf", [128, 2432], F32)
        tbw = k.sb(es, "tbw", [128, 1408], F32)
        q_sb = k.sb(es, "nq", [64, S], BF16)
        kc_sb = k.sb(es, "nkc", [64, 256], BF16)
        ctsf = k.sb(es, "ctsf", [128, 2, 65], F32)
        cts = k.sb(es, "cts", [128, 2, 65], BF16)
        vc_sb = k.sb(es, "nvc", [128, 2, 65], BF16)
        ksel = k.sb(es, "nksel", [64, S], BF16)
        kwin = k.sb(es, "nkwin", [64, S], BF16)
        vsel = k.sb(es, "nvsel", [128, NT, 65], BF16)
        vwin = k.sb(es, "nvwin", [128, NT, 65], BF16)
        eself = k.sb(es, "eself", [64, 1024], F32)
        esel = k.sb(es, "esel", [64, S], BF16)
        neg = k.sb(es, "nneg", [64, S], BF16)
        imp = k.sb(es, "imp", [128, NT, 64], F32)
        keep = k.sb(es, "keep", [128, NT, 64], F32)
        addc = k.sb(es, "addc", [128, NT, 64], F32)
        gts = k.sb(es, "gts", [128, NT, 18], F32)
        sm = k.sb(es, "nsm", [128, 64], F32)
        wk = [k.sb(es, "nwk", [128, 64], F32) for _ in range(3)]
        m8 = [k.sb(es, "nm8", [128, 8], F32) for _ in range(2)]
        negq = k.sb(es, "negq", [128, 64], F32)
        acc = [k.sb(es, "nacc", [128, 64], F32) for _ in range(2)]
        cx = AttnCx(k, es, 3)
        pfl = [k.ps(es, "pfl", [128, 512], F32) for _ in range(2)]

        k.dma("sp", flipf[:], consts["c_flip"], writes=["flipf"])
        k.dma("sp", ctsf[:], consts["c_cts"].rearrange("(c p) d -> p c d", p=128), writes=["ctsf"])
        k.v("dve", "tensor_copy", reads=["ctsf"], writes=["cts"], out=cts[:], in_=ctsf[:])
        for c0 in range(0, S, 1024):
            k.dma("sp", eself[:], consts["c_esel"][:, c0:c0 + 1024], writes=["eself"])
            k.v("pool", "tensor_copy", reads=["eself"], writes=["esel"], out=esel[:, c0:c0 + 1024], in_=eself[:])
        dma_split(k, "pool", keep, consts["c_keep"].rearrange("(c p) d -> p c d", p=128), "keep")
        dma_split(k, "pool", addc, consts["c_add"].rearrange("(c p) d -> p c d", p=128), "addc")
        dma_split(k, "pool", gts, gtok.rearrange("(c p) d -> p c d", p=128), "gts")

        for g in range(2):
            k.dma("sp", kc_sb[:], kcT[g], writes=["nkc"])
            for r in range(3):
                h = g * 3 + r
                k.dma("sp", q_sb[:], qkT[R_NSAQ + h * 64:R_NSAQ + (h + 1) * 64, :], writes=["nq"])
                make_tb(k, "cmp", gd, h, flipf, rev, tbc, pfl, "tbc")
                for qb in range(8):
                    tiles = cmp_tiles(qb, kc_sb, q_sb, tbc, cts, ["nkc", "nq", "tbc", "cts"])
                    run_branch(k, cx, tiles, 0, ("aU", 0))
                    U = cx.U[0]
                    k.v("dve", "tensor_scalar", reads=[("aU", 0)], writes=["nsm"], out=sm[:, 0:4],
                        in0=U[:, :, 64], scalar1=1e-30, scalar2=None, op0=ALU.max)
                    k.v("dve", "reciprocal", reads=["nsm"], writes=["nsm2"], out=sm[:, 4:8], in_=sm[:, 0:4])
                    for qt in range(4):
                        tl = qb * 4 + qt
                        if r == 0:
                            k.v("dve", "tensor_scalar", reads=[("aU", 0), "nsm2"], writes=[("imp", tl)],
                                out=imp[:, tl, :], in0=U[:, qt, 0:64], scalar1=sm[:, 4 + qt:5 + qt], scalar2=None,
                                op0=ALU.mult)
                        else:
                            k.v("dve", "scalar_tensor_tensor", reads=[("aU", 0), "nsm2", ("imp", tl)],
                                writes=[("imp", tl)], out=imp[:, tl, :], in0=U[:, qt, 0:64],
                                scalar=sm[:, 4 + qt:5 + qt], in1=imp[:, tl, :], op0=ALU.mult, op1=ALU.add)
            for tl in range(NT):
                a, b_, c_ = wk
                k.v("dve", "tensor_tensor", reads=[("imp", tl), "keep"], writes=["nwk0"], out=a[:],
                    in0=imp[:, tl, :], in1=keep[:, tl, :], op=ALU.mult)
                k.v("dve", "tensor_tensor", reads=["nwk0", "addc"], writes=["nwk1"], out=b_[:],
                    in0=a[:], in1=addc[:, tl, :], op=ALU.add)
                k.v("dve", "max", reads=["nwk1"], writes=["nm80"], out=m8[0][:], in_=b_[:])
                k.v("dve", "match_replace", reads=["nwk1", "nm80"], writes=["nwk2"], out=c_[:],
                    in_to_replace=m8[0][:], in_values=b_[:], imm_value=-3.0e38)
                k.v("dve", "max", reads=["nwk2"], writes=["nm81"], out=m8[1][:], in_=c_[:])
                k.v("dve", "tensor_scalar", reads=["nwk1", "nm81"], writes=["nwk0"], out=a[:], in0=b_[:],
                    scalar1=m8[1][:, 7:8], scalar2=None, op0=ALU.is_ge)
                k.v("dve", "tensor_scalar", reads=["nwk0"], writes=["negq"], out=negq[:], in0=a[:],
                    scalar1=-NEGP, scalar2=NEGP, op0=ALU.mult, op1=ALU.add)
                pb = tl % 2
                k.tr(pfl[pb][0:64, 0:128], negq[:, :], identf[:], reads=["negq", "identf"], writes=[("pflip", pb)])
                k.act(neg[:, tl * 128:(tl + 1) * 128], pfl[pb][0:64, 0:128], AF.Copy, reads=[("pflip", pb)],
                      writes=["nneg"])
            k.dma("sp", ksel[:], qkT[R_KSEL + g * 64:R_KSEL + (g + 1) * 64, :], writes=["nksel"])
            k.dma("sp", kwin[:], qkT[R_KWIN + g * 64:R_KWIN + (g + 1) * 64, :], writes=["nkwin"])
            load_v_aug(k, "pool", vsel, vtok, C_VSEL + g * 64, "nvsel")
            load_v_aug(k, "pool", vwin, vtok, C_VWIN + g * 64, "nvwin")
            k.v("pool", "memset", writes=["nvc"], ap=vc_sb[:, :, 64:65], constant=1.0)
            k.dma("pool", vc_sb[:, :, 0:64], vcs[g].rearrange("(c p) d -> p c d", p=128), writes=["nvc"])
            for r in range(3):
                h = g * 3 + r
                k.dma("sp", q_sb[:], qkT[R_NSAQ + h * 64:R_NSAQ + (h + 1) * 64, :], writes=["nq"])
                make_tb(k, "cmp", gd, h, flipf, rev, tbc, pfl, "tbc")
                make_tb(k, "full", gd, h, flipf, rev, tbf, pfl, "tbf")
                make_tb(k, "win", gd, h, flipf, rev, tbw, pfl, "tbw")
                for qb in range(8):
                    qs = qb * 512
                    qfn = lambda c0, n, qs=qs: q_sb[:, qs + c0:qs + c0 + n]
                    tiles = cmp_tiles(qb, kc_sb, q_sb, tbc, vc_sb, ["nkc", "nq", "tbc", "nvc"])
                    run_branch(k, cx, tiles, 0, ("aU", 0))
                    tiles = []
                    for kc in range(0, (qs + 384) // 128 + 1):
                        Dl = qs - 128 * kc
                        lo = max(0, -Dl // 128)
                        far = Dl >= 1664
                        tiles.append(dict(
                            rows=128, kT=ksel[:, kc * 128:(kc + 1) * 128], qfn=qfn,
                            tbfn=None if far else (lambda c0, n, Dl=Dl: tbf[:, Dl + 384 + c0:Dl + 384 + c0 + n]),
                            cbias=tbf[:, 2431:2432] if far else None,
                            maskfn=lambda c0, n, kc=kc, qs=qs: (esel[:, kc * 128:(kc + 1) * 128],
                                                              neg[:, qs + c0:qs + c0 + n]),
                            V=vsel[:, kc, :], qt_lo=lo, qt_hi=4, rd=["nksel", "nq", "tbf", "nvsel", "esel", "nneg"]))
                    run_branch(k, cx, tiles, 1, ("aU", 1))
                    tiles = []
                    for kc in range(max(0, (qs - 512) // 128), (qs + 384) // 128 + 1):
                        Dl = qs - 128 * kc
                        lo = max(0, -Dl // 128)
                        hi = min(4, (639 - Dl) // 128 + 1)
                        tiles.append(dict(
                            rows=128, kT=kwin[:, kc * 128:(kc + 1) * 128], qfn=qfn,
                            tbfn=lambda c0, n, Dl=Dl: tbw[:, Dl + 384 + c0:Dl + 384 + c0 + n],
                            cbias=None, maskfn=None, V=vwin[:, kc, :], qt_lo=lo, qt_hi=hi,
                            rd=["nkwin", "nq", "tbw", "nvwin"]))
                    run_branch(k, cx, tiles, 2, ("aU", 2))
                    for br in range(3):
                        U = cx.U[br]
                        k.v("dve", "tensor_scalar", reads=[("aU", br)], writes=[("nsmc", br)],
                            out=sm[:, 8 + br * 12:12 + br * 12], in0=U[:, :, 64], scalar1=1e-30, scalar2=None,
                            op0=ALU.max)
                        k.v("dve", "reciprocal", reads=[("nsmc", br)], writes=[("nsmr", br)],
                            out=sm[:, 12 + br * 12:16 + br * 12], in_=sm[:, 8 + br * 12:12 + br * 12])
                        k.v("dve", "tensor_tensor", reads=[("nsmr", br), "gts"], writes=[("nsmg", br)],
                            out=sm[:, 16 + br * 12:20 + br * 12], in0=sm[:, 12 + br * 12:16 + br * 12],
                            in1=gts[:, qb * 4:qb * 4 + 4, h * 3 + br], op=ALU.mult)
                    for qt in range(4):
                        tl = qb * 4 + qt
                        a0, a1 = acc
                        k.v("dve", "tensor_scalar", reads=[("aU", 0), ("nsmg", 0)], writes=["nacc0"], out=a0[:],
                            in0=cx.U[0][:, qt, 0:64], scalar1=sm[:, 16 + qt:17 + qt], scalar2=None, op0=ALU.mult)
                        k.v("dve", "scalar_tensor_tensor", reads=[("aU", 1), ("nsmg", 1), "nacc0"], writes=["nacc1"],
                            out=a1[:], in0=cx.U[1][:, qt, 0:64], scalar=sm[:, 28 + qt:29 + qt], in1=a0[:],
                            op0=ALU.mult, op1=ALU.add)
                        k.v("dve", "scalar_tensor_tensor", reads=[("aU", 2), ("nsmg", 2), "nacc1"],
                            writes=[("ytok", tl)], out=ytok_sb[:, tl, h * 64:(h + 1) * 64],
                            in0=cx.U[2][:, qt, 0:64], scalar=sm[:, 40 + qt:41 + qt], in1=a1[:],
                            op0=ALU.mult, op1=ALU.add)
    P.barrier()


def moba_phase(k, qkT, vtok, gd, consts, identf, ytok_sb):
    P = k.P
    with ExitStack() as es:
        flipf = k.sb(es, "mflipf", [128, 128], F32)
        rev = k.sb(es, "mrev", [128, 2432], F32)
        tbf = k.sb(es, "mtbf", [128, 2432], F32)
        q_sb = k.sb(es, "mq", [64, S], BF16)
        k_sb = k.sb(es, "mk", [64, S], BF16)
        v_sb = k.sb(es, "mv", [128, NT, 65], BF16)
        ebf = k.sb(es, "ebf", [16, 2048], F32)
        eb = k.sb(es, "eb", [16, 2048], BF16)
        neg = k.sb(es, "mneg", [16, S], BF16)
        valid = k.sb(es, "mvalid", [128, NT, 16], F32)
        addm = k.sb(es, "maddm", [128, NT, 16], F32)
        ownc = k.sb(es, "mown", [128, NT, 16], F32)
        km = k.sb(es, "mkm", [64, 16], F32)
        kmh = k.sb(es, "mkmh", [64, 16], BF16)
        kmhf = k.sb(es, "mkmhf", [64, 16], F32)
        kml = k.sb(es, "mkml", [64, 16], BF16)
        wk = [k.sb(es, "mwk", [128, 16], F32) for _ in range(3)]
        m8 = k.sb(es, "mm8", [128, 8], F32)
        sm = k.sb(es, "msm", [128, 8], F32)
        cx = AttnCx(k, es, 1)
        pfl = [k.ps(es, "mpfl", [128, 512], F32) for _ in range(2)]
        k.dma("sp", flipf[:], consts["c_flip"], writes=["flipf"])
        k.dma("sp", ebf[:], consts["c_eb"], writes=["ebf"])
        k.v("dve", "tensor_copy", reads=["ebf"], writes=["eb"], out=eb[:], in_=ebf[:])
        dma_split(k, "pool", valid, consts["c_mvalid"].rearrange("(c p) d -> p c d", p=128), "mvalid")
        dma_split(k, "pool", addm, consts["c_maddm"].rearrange("(c p) d -> p c d", p=128), "maddm")
        dma_split(k, "pool", ownc, consts["c_mown"].rearrange("(c p) d -> p c d", p=128), "mown")
        for hb in range(4):
            k.dma("sp", q_sb[:], qkT[R_MQ + hb * 64:R_MQ + (hb + 1) * 64, :], writes=["mq"])
            k.dma("sp", k_sb[:], qkT[R_MK + hb * 64:R_MK + (hb + 1) * 64, :], writes=["mk"])
            load_v_aug(k, "pool", v_sb, vtok, C_MV + hb * 64, "mv")
            make_tb(k, "full", gd, 6 + hb, flipf, rev, tbf, pfl, "tbf")
            k.v("dve", "tensor_reduce", reads=["mk"], writes=["mkm"], out=km[:],
                in_=k_sb[:].rearrange("p (n j) -> p n j", j=256), axis=mybir.AxisListType.X, op=ALU.add)
            k.v("dve", "tensor_scalar", reads=["mkm"], writes=["mkm2"], out=km[:], in0=km[:], scalar1=1.0 / 256,
                scalar2=None, op0=ALU.mult)
            k.v("dve", "tensor_copy", reads=["mkm2"], writes=["mkmh"], out=kmh[:], in_=km[:])
            k.v("dve", "tensor_copy", reads=["mkmh"], writes=["mkmhf"], out=kmhf[:], in_=kmh[:])
            k.v("dve", "tensor_tensor", reads=["mkm2", "mkmhf"], writes=["mkml"], out=kml[:], in0=km[:],
                in1=kmhf[:], op=ALU.subtract)
            for tl in range(NT):
                pb = tl % 2
                G = pfl[pb]
                k.mm(G[:, 0:16], q_sb[:, tl * 128:(tl + 1) * 128], kmh[:, :], True, False,
                     reads=["mq", "mkmh"], writes=[("pflip", pb)])
                k.mm(G[:, 0:16], q_sb[:, tl * 128:(tl + 1) * 128], kml[:, :], False, True,
                     reads=["mq", "mkml"], writes=[("pflip", pb)])
                a, b_, c_ = wk
                k.v("dve", "tensor_tensor", reads=[("pflip", pb), "mvalid"], writes=["mwk0"], out=a[:],
                    in0=G[:, 0:16], in1=valid[:, tl, :], op=ALU.mult)
                k.v("dve", "tensor_tensor", reads=["mwk0", "maddm"], writes=["mwk1"], out=b_[:], in0=a[:],
                    in1=addm[:, tl, :], op=ALU.add)
                k.v("dve", "max", reads=["mwk1"], writes=["mm8"], out=m8[:], in_=b_[:])
                k.v("dve", "tensor_scalar", reads=["mwk1", "mm8"], writes=["mwk2"], out=c_[:], in0=b_[:],
                    scalar1=m8[:, 2:3], scalar2=None, op0=ALU.is_ge)
                k.v("dve", "tensor_tensor", reads=["mwk2", "mvalid"], writes=["mwk0"], out=a[:], in0=c_[:],
                    in1=valid[:, tl, :], op=ALU.mult)
                k.v("dve", "scalar_tensor_tensor", reads=["mwk0", "mown"], writes=["mwk1"], out=b_[:], in0=a[:],
                    scalar=-NEGP, in1=ownc[:, tl, :], op0=ALU.mult, op1=ALU.add)
                k.tr(pfl[pb][0:16, 128:256], b_[:, :], identf[:], reads=["mwk1", "identf"], writes=[("pflip", pb)])
                k.act(neg[:, tl * 128:(tl + 1) * 128], pfl[pb][0:16, 128:256], AF.Copy, reads=[("pflip", pb)],
                      writes=["mneg"])
            for qb in range(8):
                qs = qb * 512
                qfn = lambda c0, n, qs=qs: q_sb[:, qs + c0:qs + c0 + n]
                tiles = []
                for kc in range(0, (qs + 384) // 128 + 1):
                    Dl = qs - 128 * kc
                    lo = max(0, -Dl // 128)
                    far = Dl >= 1664
                    nblk = kc // 2
                    tiles.append(dict(
                        rows=128, kT=k_sb[:, kc * 128:(kc + 1) * 128], qfn=qfn,
                        tbfn=None if far else (lambda c0, n, Dl=Dl: tbf[:, Dl + 384 + c0:Dl + 384 + c0 + n]),
                        cbias=tbf[:, 2431:2432] if far else None,
                        maskfn=lambda c0, n, nblk=nblk, qs=qs: (eb[:, nblk * 128:(nblk + 1) * 128],
                                                              neg[:, qs + c0:qs + c0 + n]),
                        V=v_sb[:, kc, :], qt_lo=lo, qt_hi=4, rd=["mk", "mq", "tbf", "mv", "eb", "mneg"]))
                run_branch(k, cx, tiles, 0, ("aU", 0))
                U = cx.U[0]
                k.v("dve", "tensor_scalar", reads=[("aU", 0)], writes=["msm"], out=sm[:, 0:4], in0=U[:, :, 64],
                    scalar1=1e-30, scalar2=None, op0=ALU.max)
                k.v("dve", "reciprocal", reads=["msm"], writes=["msm2"], out=sm[:, 4:8], in_=sm[:, 0:4])
                for qt in range(4):
                    tl = qb * 4 + qt
                    k.v("dve", "tensor_scalar", reads=[("aU", 0), "msm2"], writes=[("ytok", tl)],
                        out=ytok_sb[:, tl, 384 + hb * 64:384 + (hb + 1) * 64], in0=U[:, qt, 0:64],
                        scalar1=sm[:, 4 + qt:5 + qt], scalar2=None, op0=ALU.mult)
    P.barrier()


def dil_phase(k, qkT, vtok, gd, consts, ytok_sb):
    P = k.P
    with ExitStack() as es:
        flipf = k.sb(es, "dflipf", [128, 128], F32)
        rev = k.sb(es, "drev", [128, 2944], F32)
        tbs = [k.sb(es, "dtb", [128, FAM["dil%d" % g][1]], F32) for g in range(3)]
        q_sb = [k.sb(es, "dq", [64, S], BF16) for _ in range(3)]
        k_sb = [k.sb(es, "dk", [64, S], BF16) for _ in range(3)]
        v_sb = [k.sb(es, "dv", [128, NT, 65], BF16) for _ in range(3)]
        sm = k.sb(es, "dsm", [128, 8], F32)
        cx = AttnCx(k, es, 1)
        pfl = [k.ps(es, "dpfl", [128, 512], F32) for _ in range(2)]
        k.dma("sp", flipf[:], consts["c_flip"], writes=["flipf"])
        for i in range(2):
            for g in range(3):
                hh = g * 2 + i
                k.dma("sp", q_sb[g][:], qkT[R_DQ + hh * 64:R_DQ + (hh + 1) * 64, :], writes=[("dq", g)])
                k.dma("sp", k_sb[g][:], qkT[R_DK + hh * 64:R_DK + (hh + 1) * 64, :], writes=[("dk", g)])
                load_v_aug(k, "pool", v_sb[g], vtok, C_DV + hh * 64, ("dv", g))
                make_tb(k, "dil%d" % g, gd, 10 + hh, flipf, rev, tbs[g], pfl, ("dtb", g))
            for qb in range(8):
                qs = qb * 512
                tiles = []
                for g, (W, dl) in enumerate(DIL):
                    for kc in range(max(0, (qs - W) // 128), (qs + 384) // 128 + 1):
                        Dl = qs - 128 * kc
                        lo = max(0, -Dl // 128)
                        hi = min(4, (W - Dl) // 128 + 1)
                        tiles.append(dict(
                            rows=128, kT=k_sb[g][:, kc * 128:(kc + 1) * 128],
                            qfn=lambda c0, n, g=g, qs=qs: q_sb[g][:, qs + c0:qs + c0 + n],
                            tbfn=lambda c0, n, g=g, Dl=Dl: tbs[g][:, Dl + 384 + c0:Dl + 384 + c0 + n],
                            cbias=None, maskfn=None, V=v_sb[g][:, kc, :], qt_lo=lo, qt_hi=hi,
                            rd=[("dq", g), ("dk", g), ("dv", g), ("dtb", g)]))
                run_branch(k, cx, tiles, 0, ("aU", 0))
                U = cx.U[0]
                k.v("dve", "tensor_scalar", reads=[("aU", 0)], writes=["dsm"], out=sm[:, 0:4], in0=U[:, :, 64],
                    scalar1=1e-30, scalar2=None, op0=ALU.max)
                k.v("dve", "reciprocal", reads=["dsm"], writes=["dsm2"], out=sm[:, 4:8], in_=sm[:, 0:4])
                for qt in range(4):
                    tl = qb * 4 + qt
                    k.v("dve", "tensor_scalar", reads=[("aU", 0), "dsm2"], writes=[("ytok", tl)],
                        out=ytok_sb[:, tl, 640 + i * 64:640 + (i + 1) * 64], in0=U[:, qt, 0:64],
                        scalar1=sm[:, 4 + qt:5 + qt], scalar2=None, op0=ALU.mult)
    P.barrier()


def merge_phase(k, xres, mgT, w_up_a, w_up_b, w_up_c, w_o, ident, ytok_sb):
    P = k.P
    with ExitStack() as es:
        wup = k.sb(es, "wup", [128, 6, D], BF16)
        wo = k.sb(es, "wo", [128, 8, D], BF16)
        stg = [k.sb(es, "gstg", [128, D], F32) for _ in range(2)]
        yT = k.sb(es, "gyT", [128, 6, 512], BF16)
        gt = [k.sb(es, "ggt", [128, 3, 512], BF16) for _ in range(2)]
        m1 = [k.sb(es, "gm1", [128, 512], F32) for _ in range(2)]
        m2 = [k.sb(es, "gm2", [128, 512], F32) for _ in range(2)]
        m3 = [k.sb(es, "gm3", [128, 512], F32) for _ in range(2)]
        m4 = [k.sb(es, "gm4", [128, 512], F32) for _ in range(2)]
        mT = k.sb(es, "gmT", [128, 8, 512], BF16)
        xr = [k.sb(es, "gxr", [128, 512], F32) for _ in range(2)]
        ob = [k.sb(es, "gob", [128, 512], F32) for _ in range(2)]
        ptp = [k.ps(es, "gptp", [128, 1024], BF16) for _ in range(2)]
        pu = [k.ps(es, "gpu", [128, 512], F32) for _ in range(3)]
        po = [k.ps(es, "gpo", [128, 512], F32) for _ in range(2)]
        srcs = [(w_up_a, 0), (w_up_a, 1), (w_up_a, 2), (w_up_b, 0), (w_up_b, 1), (w_up_c, 0)]
        ci = 0
        for fc, (w, j) in enumerate(srcs):
            b = ci % 2
            ci += 1
            k.dma("sp" if b == 0 else "pool", stg[b][:], w[j * 128:(j + 1) * 128, :], writes=[("gstg", b)])
            k.v("dve", "tensor_copy", reads=[("gstg", b)], writes=["wup"], out=wup[:, fc, :], in_=stg[b][:])
        for kc in range(8):
            b = ci % 2
            ci += 1
            k.dma("sp" if b == 0 else "pool", stg[b][:], w_o[kc * 128:(kc + 1) * 128, :], writes=[("gstg", b)])
            k.v("dve", "tensor_copy", reads=[("gstg", b)], writes=["wo"], out=wo[:, kc, :], in_=stg[b][:])
        ui = 0
        oi = 0
        for tb8 in range(8):
            t0 = tb8 * 512
            for fc in range(6):
                pb = fc % 2
                for t in range(4):
                    tl = tb8 * 4 + t
                    k.tr(ptp[pb][:, t * 128:(t + 1) * 128], ytok_sb[:, tl, fc * 128:(fc + 1) * 128], ident[:],
                         reads=[("ytok", tl), "ident"], writes=[("gptp", pb)])
                if fc % 2 == 0:
                    k.act(yT[:, fc, :], ptp[pb][:, 0:512], AF.Copy, reads=[("gptp", pb)], writes=[("gyT", fc)])
                else:
                    k.v("dve", "tensor_copy", reads=[("gptp", pb)], writes=[("gyT", fc)], out=yT[:, fc, :],
                        in_=ptp[pb][:, 0:512])
            for cc in range(8):
                b = ui % 2
                ui += 1
                k.dma("sp", gt[b][:], mgT.rearrange("(b r) t -> r b t", b=3)[cc * 128:(cc + 1) * 128, :, t0:t0 + 512],
                      writes=[("ggt", b)])
                groups = ((0, (0, 1, 2)), (1, (3, 4)), (2, (5,)))
                for br, fcs in groups:
                    for j, fc in enumerate(fcs):
                        k.mm(pu[br][:], wup[:, fc, cc * 128:(cc + 1) * 128], yT[:, fc, :], j == 0, j == len(fcs) - 1,
                             reads=[("gyT", fc), "wup"], writes=[("gpu", br)])
                k.v("dve", "tensor_tensor", reads=[("gpu", 0), ("ggt", b)], writes=[("gm1", b)], out=m1[b][:],
                    in0=pu[0][:], in1=gt[b][:, 0, :], op=ALU.mult)
                k.v("dve", "tensor_tensor", reads=[("gpu", 1), ("ggt", b)], writes=[("gm2", b)], out=m2[b][:],
                    in0=pu[1][:], in1=gt[b][:, 1, :], op=ALU.mult)
                k.v("dve", "tensor_tensor", reads=[("gpu", 2), ("ggt", b)], writes=[("gm3", b)], out=m3[b][:],
                    in0=pu[2][:], in1=gt[b][:, 2, :], op=ALU.mult)
                k.v("pool", "tensor_tensor", reads=[("gm1", b), ("gm2", b)], writes=[("gm4", b)], out=m4[b][:],
                    in0=m1[b][:], in1=m2[b][:], op=ALU.add)
                k.v("pool", "tensor_tensor", reads=[("gm4", b), ("gm3", b)], writes=[("gmT", cc)], out=mT[:, cc, :],
                    in0=m4[b][:], in1=m3[b][:], op=ALU.add)
            allm = [("gmT", cc) for cc in range(8)]
            for t in range(4):
                r0 = t0 + t * 128
                for hh in range(2):
                    b = oi % 2
                    oi += 1
                    k.dma("pool", xr[b][:], xres[r0:r0 + 128, hh * 512:(hh + 1) * 512], writes=[("gxr", b)])
                    for cc in range(8):
                        k.mm(po[b][:], mT[:, cc, t * 128:(t + 1) * 128], wo[:, cc, hh * 512:(hh + 1) * 512],
                             cc == 0, cc == 7, reads=allm + ["wo"], writes=[("gpo", b)])
                    k.v("dve", "tensor_tensor", reads=[("gpo", b), ("gxr", b)], writes=[("gob", b)], out=ob[b][:],
                        in0=po[b][:], in1=xr[b][:], op=ALU.add)
                    k.dma("sp", xres[r0:r0 + 128, hh * 512:(hh + 1) * 512], ob[b][:], reads=[("gob", b)])
    P.barrier()


W_QK_COLS = np.concatenate([np.arange(0, 384), np.arange(384, 512), np.arange(512, 640), np.arange(640, 768),
                            np.arange(896, 1024), np.arange(1170, 1426), np.arange(1426, 1682),
                            np.arange(1938, 2322), np.arange(2322, 2706)])
W_V_COLS = np.concatenate([np.arange(768, 896), np.arange(1024, 1152), np.arange(1682, 1938),
                           np.arange(2706, 3090), np.arange(1152, 1170)])
W_MG_COLS = np.arange(3090, 6162)


def build(stop_after=None, debug=False):
    nc = bass.Bass("TRN2", target_bir_lowering=False)

    def inp(name, shape):
        return nc.dram_tensor(name, list(shape), F32, kind="ExternalInput").ap()

    def scr(name, shape, dt):
        return nc.dram_tensor(name, list(shape), dt, kind="ExternalOutput" if debug else "Internal").ap()

    x = inp("x", [S, D])
    rel_bias = inp("rel_bias", [32, 16])
    ffn_norm = [inp("ffn1_norm", [2, D]), inp("ffn2_norm", [2, D])]
    ffn_wg = [inp("ffn1_w_gate", [2, D, DFF]), inp("ffn2_w_gate", [2, D, DFF])]
    ffn_wu = [inp("ffn1_w_up", [2, D, DFF]), inp("ffn2_w_up", [2, D, DFF])]
    ffn_wd = [inp("ffn1_w_down", [2, DFF, D]), inp("ffn2_w_down", [2, DFF, D])]
    mix_norm = inp("mix_norm", [2, D])
    w_qk = inp("w_qk", [2, D, NQK])
    w_v = inp("w_v", [2, D, NVG])
    w_mg = inp("w_mg", [2, D, 3 * D])
    pe_k = inp("nsa_pe_k", [2, 32, 64])
    pe_v = inp("nsa_pe_v", [2, 32, 64])
    phi_k1 = inp("nsa_phi_k1", [2, 2048, 256])
    phi_k2 = inp("nsa_phi_k2", [2, 256, 64])
    phi_v1 = inp("nsa_phi_v1", [2, 2048, 256])
    phi_v2 = inp("nsa_phi_v2", [2, 256, 64])
    w_up_a = inp("w_up_a", [2, 384, D])
    w_up_b = inp("w_up_b", [2, 256, D])
    w_up_c = inp("w_up_c", [2, 128, D])
    w_o = inp("w_o", [2, D, D])
    final_norm = inp("final_norm", [1, D])
    consts = {nm: inp(nm, shp) for nm, shp in CONST_SHAPES.items()}
    y = nc.dram_tensor("y", [S, D], F32, kind="ExternalOutput").ap()
    xres = scr("xres", [S, D], F32)
    qkT = scr("qkT", [NQK, S], BF16)
    vtok = scr("vtok", [S, NVC], BF16)
    gtok = scr("gtok", [S, 18], F32)
    mgT = scr("mgT", [3 * D, S], BF16)
    kcT = scr("kcT", [2, 64, 256], BF16)
    vcs = scr("vcs", [2, 256, 64], BF16)
    gd = {fam: scr("gd_" + fam, [16, L], F32) for fam, (L, _, _) in FAM.items()}
    ydbg = scr("ydbg", [S, 768], BF16) if debug else None

    P = Prog(nc)
    k = K(nc, P)
    with ExitStack() as es:
        identf = k.sb(es, "identf", [128, 128], F32)
        ident = k.sb(es, "ident", [128, 128], BF16)
        k.dma("sp", identf[:], consts["c_ident"], writes=["identf"])
        k.v("dve", "tensor_copy", reads=["identf"], writes=["ident"], out=ident[:], in_=identf[:])

        def dump_y(ytok_sb):
            if debug:
                yv = ydbg.rearrange("(c p) d -> p c d", p=128)
                for c0 in range(0, NT, 8):
                    k.dma("sp", yv[:, c0:c0 + 8, :], ytok_sb[:, c0:c0 + 8, :], reads=[("ytok", t) for t in range(NT)])
                P.barrier()

        ystack = []

        def dump_y_dummy():
            pass

        def run():
            stages = stop_after
            bias_gen_phase(k, rel_bias, consts, gd)
            for l in range(2):
                ffn_phase(k, x if l == 0 else xres, xres, ffn_norm[0][l:l + 1, :], ffn_wg[0][l], ffn_wu[0][l],
                          ffn_wd[0][l], ident)
                if stages == "ffn1":
                    return
                proj_phase(k, xres, mix_norm[l:l + 1, :], w_qk[l], w_v[l], w_mg[l], ident, qkT, vtok, gtok, mgT)
                if stages == "proj":
                    return
                compress_phase(k, qkT, pe_k[l], pe_v[l], phi_k1[l], phi_k2[l], phi_v1[l], phi_v2[l], kcT, vcs, identf)
                if stages == "cmp":
                    return
                ys = ExitStack()
                ytok_sb = k.sb(ys, "ytok", [128, NT, 768], BF16)
                ystack.append(ys)
                if stages in (None, "nsa", "attn", "merge", "l0"):
                    nsa_phase(k, qkT, vtok, gtok, kcT, vcs, gd, consts, identf, ytok_sb)
                if stages == "nsa":
                    dump_y(ytok_sb)
                    return
                if stages in (None, "moba", "attn", "merge", "l0"):
                    moba_phase(k, qkT, vtok, gd, consts, identf, ytok_sb)
                if stages == "moba":
                    dump_y(ytok_sb)
                    return
                dil_phase(k, qkT, vtok, gd, consts, ytok_sb)
                if stages in ("dil", "attn"):
                    dump_y(ytok_sb)
                    return
                merge_phase(k, xres, mgT, w_up_a[l], w_up_b[l], w_up_c[l], w_o[l], ident, ytok_sb)
                ystack.pop().close()
                if stages == "merge":
                    return
                ffn_phase(k, xres, xres, ffn_norm[1][l:l + 1, :], ffn_wg[1][l], ffn_wu[1][l], ffn_wd[1][l], ident)
                if stages == "l0":
                    return
            final_norm_phase(k, xres, y, final_norm)
        run()
        P.barrier()
        with ExitStack() as es2:
            P.emit(es2)
        while ystack:
            ystack.pop().close()
    return nc


def make_in_maps(inputs, cores):
    consts = host_consts()
    w_in = np.asarray(inputs["w_in"])
    shared = dict(consts)
    shared["w_qk"] = np.ascontiguousarray(w_in[:, :, W_QK_COLS])
    shared["w_v"] = np.ascontiguousarray(w_in[:, :, W_V_COLS])
    shared["w_mg"] = np.ascontiguousarray(w_in[:, :, W_MG_COLS])
    for nm in ("rel_bias", "ffn1_norm", "ffn2_norm", "ffn1_w_gate", "ffn2_w_gate", "ffn1_w_up", "ffn2_w_up",
               "ffn1_w_down", "ffn2_w_down", "mix_norm", "nsa_pe_k", "nsa_pe_v", "nsa_phi_k1", "nsa_phi_k2",
               "nsa_phi_v1", "nsa_phi_v2", "w_up_a", "w_up_b", "w_up_c", "w_o"):
        shared[nm] = np.ascontiguousarray(np.asarray(inputs[nm], dtype=np.float32))
    shared["final_norm"] = np.ascontiguousarray(np.asarray(inputs["final_norm"], dtype=np.float32)).reshape(1, D)
    maps = []
    for b in cores:
        m = dict(shared)
        m["x"] = np.ascontiguousarray(np.asarray(inputs["x"][b], dtype=np.float32))
        maps.append(m)
    return maps


def kernel(**inputs):
    nc = build()
    in_maps = make_in_maps(inputs, range(8))
    res = run_bass_kernel_spmd(nc, in_maps, core_ids=list(range(8)))
    return np.stack([np.asarray(r["y"]) for r in res.results], axis=0).astype(np.float32)
```

```python
import numpy as np
from contextlib import ExitStack
import concourse.bass as bass
import concourse.mybir as mybir
from concourse.bass_utils import run_bass_kernel_spmd

F32 = mybir.dt.float32
BF16 = mybir.dt.bfloat16
AF = mybir.ActivationFunctionType
ALU = mybir.AluOpType

S = 4096
D = 1024
DFF = 2816
NFC = DFF // 128
NT = S // 128
EPS = 1e-6
NEGM = -30000.0
NDS = 16


class _Op:
    __slots__ = ("eng", "fn", "deps", "dma", "signal", "sig", "dsem", "dval", "bar")


class Prog:
    ENG = ("pe", "act", "dve", "pool", "sp")

    def __init__(self, nc):
        self.nc = nc
        self.ops = []
        self.lastw = {}
        self.readers = {}
        self.last_on = {e: None for e in self.ENG}

    def op(self, eng, fn, reads=(), writes=(), dma=False):
        o = _Op()
        o.eng, o.fn, o.dma, o.signal, o.bar = eng, fn, dma, False, None
        deps = set()
        for r in reads:
            w = self.lastw.get(r)
            if w is not None:
                deps.add(w)
        for w_ in writes:
            w = self.lastw.get(w_)
            if w is not None:
                deps.add(w)
            for rd in self.readers.get(w_, ()):
                deps.add(rd)
        idx = len(self.ops)
        for w_ in writes:
            self.lastw[w_] = idx
            self.readers[w_] = []
        for r in reads:
            self.readers.setdefault(r, []).append(idx)
        deps.discard(idx)
        best = {}
        pruned = []
        for d in deps:
            p = self.ops[d]
            if p.dma:
                pruned.append(d)
            elif best.get(p.eng, -1) < d:
                best[p.eng] = d
        pruned.extend(best.values())
        o.deps = sorted(pruned)
        for d in o.deps:
            self.ops[d].signal = True
        self.ops.append(o)
        self.last_on[eng] = idx
        return idx

    def barrier(self):
        o = _Op()
        o.eng, o.fn, o.dma, o.signal, o.deps = None, None, False, False, []
        o.bar = dict(self.last_on)
        for e, i in o.bar.items():
            if i is not None and not self.ops[i].dma:
                self.ops[i].signal = True
        self.ops.append(o)

    def emit(self, es):
        nc = self.nc
        E = {"pe": nc.tensor, "act": nc.scalar, "dve": nc.vector, "pool": nc.gpsimd, "sp": nc.sync}
        sem = {e: es.enter_context(nc.semaphore("s_" + e)) for e in self.ENG}
        dsem = {q: [es.enter_context(nc.semaphore("d_%s%d" % (q, i))) for i in range(NDS)]
                for q in ("sp", "pool")}
        cnt = {e: 0 for e in self.ENG}
        dcnt = {"sp": 0, "pool": 0}
        seen = {e: {} for e in self.ENG}

        def wait(e, s, v):
            key = id(s)
            if seen[e].get(key, 0) < v:
                E[e].wait_ge(s, v)
                seen[e][key] = v

        for o in self.ops:
            if o.bar is not None:
                for q in ("sp", "pool"):
                    k = dcnt[q]
                    if k == 0:
                        continue
                    for i in range(min(k, NDS)):
                        uses = (k - 1 - i) // NDS + 1
                        wait(q, dsem[q][i], 16 * uses)
                    E[q].sem_inc(sem[q], 1)
                    cnt[q] += 1
                for f in self.ENG:
                    for e in self.ENG:
                        if e != f and cnt[e] > 0:
                            wait(f, sem[e], cnt[e])
                continue
            e = o.eng
            for d in o.deps:
                p = self.ops[d]
                if p.dma:
                    wait(e, p.dsem, p.dval)
                else:
                    if p.eng == e and e == "pe":
                        continue
                    wait(e, sem[p.eng], p.sig)
            if o.dma:
                k = dcnt[e]
                s = dsem[e][k % NDS]
                v = 16 * (k // NDS + 1)
                if k >= NDS:
                    wait(e, s, v - 16)
                ins = o.fn()
                ins.then_inc(s, 16)
                o.dsem, o.dval = s, v
                dcnt[e] = k + 1
            else:
                ins = o.fn()
                if o.signal:
                    cnt[e] += 1
                    o.sig = cnt[e]
                    ins.then_inc(sem[e], 1)
        self.ops = []


class K:
    def __init__(self, nc, P):
        self.nc, self.P = nc, P
        self.uid = 0

    def name(self, s):
        self.uid += 1
        return "%s_%d" % (s, self.uid)

    def sb(self, es, nm, shape, dt):
        return es.enter_context(self.nc.sbuf_tensor(self.name(nm), list(shape), dt))

    def ps(self, es, nm, shape, dt):
        return es.enter_context(self.nc.psum_tensor(self.name(nm), list(shape), dt))

    def dma(self, q, out, in_, reads=(), writes=(), slow=False):
        nc = self.nc
        eng = nc.sync if q == "sp" else nc.gpsimd
        if slow:
            return self.P.op(q, lambda: eng.dma_start(out=out, in_=in_, allow_slow_non_contiguous=True),
                             reads, writes, dma=True)
        return self.P.op(q, lambda: eng.dma_start(out=out, in_=in_), reads, writes, dma=True)

    def mm(self, out, lhsT, rhs, start, stop, reads=(), writes=()):
        nc = self.nc
        return self.P.op("pe", lambda: nc.tensor.matmul(out, lhsT=lhsT, rhs=rhs, start=start, stop=stop),
                         reads, writes)

    def tr(self, out, in_, ident, reads=(), writes=()):
        nc = self.nc
        return self.P.op("pe", lambda: nc.tensor.transpose(out, in_, ident), reads, writes)

    def act(self, out, in_, func, reads=(), writes=(), **kw):
        nc = self.nc
        return self.P.op("act", lambda: nc.scalar.activation(out=out, in_=in_, func=func, **kw), reads, writes)

    def v(self, eng, name, reads=(), writes=(), **kw):
        nc = self.nc
        e = nc.vector if eng == "dve" else nc.gpsimd
        return self.P.op(eng, lambda: getattr(e, name)(**kw), reads, writes)


def dap(t, offset, pattern):
    return bass.AP(tensor=t, offset=offset, ap=[list(p) for p in pattern])


def load_cast_weight(k, stg, stg_tok, dst_fn, src_fn, nchunks, width, q_alt=True):
    for c in range(nchunks):
        b = c % 2
        k.dma("sp" if (c % 2 == 0 or not q_alt) else "pool", stg[b][:, 0:width], src_fn(c),
              writes=[stg_tok[b]])
        k.v("pool", "tensor_copy", reads=[stg_tok[b]], writes=[("w", id(dst_fn), c)],
            out=dst_fn(c), in_=stg[b][:, 0:width])


def cast(k, sel, out, in_, reads, writes):
    if sel % 2 == 0:
        k.act(out, in_, AF.Copy, reads=reads, writes=writes)
    else:
        k.v("dve", "tensor_copy", reads=reads, writes=writes, out=out, in_=in_)


def ffn_phase(k, xsrc, xdst, w_norm, w_gate, w_up, w_down, ident):
    nc, P = k.nc, k.P
    G = 512
    with ExitStack() as es:
        wg = k.sb(es, "wg", [128, 8, DFF], BF16)
        wu = k.sb(es, "wu", [128, 8, DFF], BF16)
        wd = k.sb(es, "wd", [128, NFC, D], BF16)
        HW = DFF // 2
        stg = [k.sb(es, "stg", [128, HW], F32) for _ in range(2)]
        gain = k.sb(es, "gain", [128, D], F32)
        xin = [k.sb(es, "xin", [128, D], F32) for _ in range(2)]
        junk = k.sb(es, "junk", [128, D], BF16)
        hn = [k.sb(es, "hn", [128, D], BF16) for _ in range(2)]
        hT = k.sb(es, "hT", [128, 8, G], BF16)
        aT = k.sb(es, "aT", [128, NFC, G], BF16)
        sg = [k.sb(es, "sg", [128, G], F32) for _ in range(2)]
        xr = [k.sb(es, "xr", [128, 512], F32) for _ in range(2)]
        ob = [k.sb(es, "ob", [128, 512], F32) for _ in range(2)]
        st = [k.sb(es, "st", [128, 4], F32) for _ in range(2)]
        pgu = [k.ps(es, "pgu", [128, 512], F32) for _ in range(4)]
        ptp = [k.ps(es, "ptp", [128, 1024], BF16) for _ in range(2)]
        pdn = [k.ps(es, "pdn", [128, 512], F32) for _ in range(2)]

        k.dma("sp", gain[:], dap(w_norm.tensor, w_norm.offset, [[0, 128], [1, D]]), writes=["gain"])
        wgv = w_gate.rearrange("(kc p) f -> p kc f", p=128)
        wuv = w_up.rearrange("(kc p) f -> p kc f", p=128)
        wdv = w_down.rearrange("(fc p) d -> p fc d", p=128)
        ci = 0
        for (dst, srcv, n, width) in ((wg, wgv, 8, DFF), (wu, wuv, 8, DFF)):
            for c in range(n):
                for hh in range(2):
                    b = ci % 2
                    ci += 1
                    k.dma("sp" if b == 0 else "pool", stg[b][:, 0:HW], srcv[:, c, hh * HW:(hh + 1) * HW],
                          writes=[("stg", b)])
                    cast(k, b, dst[:, c, hh * HW:(hh + 1) * HW], stg[b][:, 0:HW], [("stg", b)], [("w", id(dst))])
        for c in range(NFC):
            b = ci % 2
            ci += 1
            k.dma("sp" if b == 0 else "pool", stg[b][:, 0:D], wdv[:, c, :], writes=[("stg", b)])
            cast(k, b, wd[:, c, :], stg[b][:, 0:D], [("stg", b)], [("w", id(wd))])

        gi = 0
        di = 0
        for g in range(S // G):
            for t in range(G // 128):
                r0 = g * G + t * 128
                b = t % 2
                k.dma("sp", xin[b][:], xsrc[r0:r0 + 128, :], writes=[("xin", b)])
                k.v("dve", "scalar_tensor_tensor", reads=[("xin", b)], writes=["junk", ("st", b)],
                    out=junk[:], in0=xin[b][:], scalar=1.0, in1=xin[b][:], op0=ALU.mult, op1=ALU.mult,
                    accum_out=st[b][:, 0:1])
                k.act(st[b][:, 1:2], st[b][:, 0:1], AF.Sqrt, reads=[("st", b)], writes=[("st1", b)],
                      scale=1.0 / D, bias=EPS)
                k.v("dve", "reciprocal", reads=[("st1", b)], writes=[("st2", b)],
                    out=st[b][:, 2:3], in_=st[b][:, 1:2])
                k.v("dve", "scalar_tensor_tensor", reads=[("xin", b), ("st2", b), "gain"], writes=[("hn", b)],
                    out=hn[b][:], in0=xin[b][:], scalar=st[b][:, 2:3], in1=gain[:], op0=ALU.mult, op1=ALU.mult)
                for kc in range(8):
                    k.tr(ptp[b][:, kc * 128:(kc + 1) * 128], hn[b][:, kc * 128:(kc + 1) * 128], ident[:],
                         reads=[("hn", b), "ident"], writes=[("ptp", b)])
                k.act(hT[:, :, t * 128:(t + 1) * 128], ptp[b][:].rearrange("p (a c) -> p a c", a=8), AF.Copy,
                      reads=[("ptp", b)], writes=[("hT", t)])
            hT_all = [("hT", t) for t in range(G // 128)]
            for fc in range(NFC):
                pb = (gi % 2) * 2
                gi += 1
                for kc in range(8):
                    k.mm(pgu[pb][:], wg[:, kc, fc * 128:(fc + 1) * 128], hT[:, kc, :], kc == 0, kc == 7,
                         reads=hT_all + [("w", id(wg))], writes=[("pgu", pb)])
                for kc in range(8):
                    k.mm(pgu[pb + 1][:], wu[:, kc, fc * 128:(fc + 1) * 128], hT[:, kc, :], kc == 0, kc == 7,
                         reads=hT_all + [("w", id(wu))], writes=[("pgu", pb + 1)])
                sb_ = fc % 2
                k.act(sg[sb_][:], pgu[pb][:], AF.Silu, reads=[("pgu", pb)], writes=[("sg", sb_)])
                k.v("dve", "tensor_tensor", reads=[("sg", sb_), ("pgu", pb + 1)], writes=[("aT", fc)],
                    out=aT[:, fc, :], in0=sg[sb_][:], in1=pgu[pb + 1][:], op=ALU.mult)
            aT_all = [("aT", fc) for fc in range(NFC)]
            for t in range(G // 128):
                r0 = g * G + t * 128
                for hh in range(2):
                    b = di % 2
                    di += 1
                    k.dma("pool", xr[b][:], xsrc[r0:r0 + 128, hh * 512:(hh + 1) * 512], writes=[("xr", b)])
                    for fc in range(NFC):
                        k.mm(pdn[b][:], aT[:, fc, t * 128:(t + 1) * 128], wd[:, fc, hh * 512:(hh + 1) * 512],
                             fc == 0, fc == NFC - 1, reads=aT_all + [("w", id(wd))], writes=[("pdn", b)])
                    k.v("dve", "scalar_tensor_tensor", reads=[("pdn", b), ("xr", b)], writes=[("ob", b)],
                        out=ob[b][:], in0=pdn[b][:], scalar=0.5, in1=xr[b][:], op0=ALU.mult, op1=ALU.add)
                    k.dma("sp", xdst[r0:r0 + 128, hh * 512:(hh + 1) * 512], ob[b][:], reads=[("ob", b)])
    P.barrier()


def final_norm_phase(k, xsrc, ydst, w_norm):
    P = k.P
    with ExitStack() as es:
        gain = k.sb(es, "fgain", [128, D], F32)
        xin = [k.sb(es, "fxin", [128, D], F32) for _ in range(2)]
        yo = [k.sb(es, "fyo", [128, D], F32) for _ in range(2)]
        junk = k.sb(es, "fjunk", [128, D], BF16)
        st = [k.sb(es, "fst", [128, 4], F32) for _ in range(2)]
        k.dma("sp", gain[:], dap(w_norm.tensor, w_norm.offset, [[0, 128], [1, D]]), writes=["fgain"])
        for t in range(NT):
            b = t % 2
            r0 = t * 128
            k.dma("sp", xin[b][:], xsrc[r0:r0 + 128, :], writes=[("fxin", b)])
            k.v("dve", "scalar_tensor_tensor", reads=[("fxin", b)], writes=["fjunk", ("fst", b)],
                out=junk[:], in0=xin[b][:], scalar=1.0, in1=xin[b][:], op0=ALU.mult, op1=ALU.mult,
                accum_out=st[b][:, 0:1])
            k.act(st[b][:, 1:2], st[b][:, 0:1], AF.Sqrt, reads=[("fst", b)], writes=[("fst1", b)],
                  scale=1.0 / D, bias=EPS)
            k.v("dve", "reciprocal", reads=[("fst1", b)], writes=[("fst2", b)],
                out=st[b][:, 2:3], in_=st[b][:, 1:2])
            k.v("dve", "scalar_tensor_tensor", reads=[("fxin", b), ("fst2", b), "fgain"], writes=[("fyo", b)],
                out=yo[b][:], in0=xin[b][:], scalar=st[b][:, 2:3], in1=gain[:], op0=ALU.mult, op1=ALU.mult)
            k.dma("pool", ydst[r0:r0 + 128, :], yo[b][:], reads=[("fyo", b)])
    P.barrier()


NQK = 2176
NVC = 896
NVG = NVC + 18
R_NSAQ, R_KCMP, R_VCMP, R_KSEL, R_KWIN, R_MQ, R_MK, R_DQ, R_DK = 0, 384, 512, 640, 768, 896, 1152, 1408, 1792
C_VSEL, C_VWIN, C_MV, C_DV = 0, 128, 256, 512
FAM = {
    "full": (2560, 2432, 1),
    "win": (1536, 1408, 1),
    "cmp": (6144, 4096, 16),
    "dil0": (1152, 1024, 1),
    "dil1": (1536, 1408, 1),
    "dil2": (3072, 2944, 1),
}
DIL = ((128, 1), (512, 4), (2048, 16))
SCALE = 0.125
NEGP = -240000.0


def np_rel_bucket(dist):
    n = np.maximum(dist, 0)
    nf = np.maximum(n, 16).astype(np.float32)
    lg = (np.log(nf / np.float32(16)) / np.float32(np.log(2048 / 16)) * np.float32(16)).astype(np.float32)
    large = 16 + lg.astype(np.int32)
    large = np.minimum(large, 31)
    return np.where(n < 16, n, large)


def host_consts():
    c = {}
    c["c_ident"] = np.eye(128, dtype=np.float32)
    c["c_flip"] = np.ascontiguousarray(np.eye(128, dtype=np.float32)[::-1])
    def fam_oh(length, off, valid_fn):
        w = np.arange(length)
        d = w - off
        ok = valid_fn(d)
        b = np_rel_bucket(d)
        oh = np.zeros((33, length), np.float32)
        oh[b[ok], w[ok]] = 1.0
        oh[32, ~ok] = 1.0
        return oh
    c["oh_full"] = fam_oh(2560, 511, lambda d: d >= 0)
    c["oh_win"] = fam_oh(1536, 511, lambda d: (d >= 0) & (d < 512))
    c["oh_cmp"] = fam_oh(6144, 2063, lambda d: d >= 0)
    for g, (W, dl) in enumerate(DIL):
        c["oh_dil%d" % g] = fam_oh(FAM["dil%d" % g][0], 511, lambda d: (d >= 0) & (d <= W) & (d % dl == 0))
    c["c_negrow"] = np.full((1, 16), NEGM, np.float32)
    n_cmp = 255
    c_start = np.arange(n_cmp) * 16
    s_start = np.arange(64) * 64
    ov = (c_start[:, None] < s_start[None, :] + 64) & (c_start[:, None] + 32 > s_start[None, :])
    cts = np.zeros((256, 65), np.float32)
    cts[:255, :64] = ov
    cts[:255, 64] = 1.0
    c["c_cts"] = cts
    t = np.arange(S)
    blk = np.arange(64)
    cur = t // 64
    keep = np.ones((S, 64), np.float32)
    add = np.zeros((S, 64), np.float32)
    f0 = np.broadcast_to(blk[None, :] == 0, (S, 64))
    f1 = blk[None, :] == cur[:, None]
    f2 = blk[None, :] == cur[:, None] - 1
    fut = blk[None, :] * 64 > t[:, None]
    for f, val in ((f0, 1e4), (f2, 3e4), (f1, 2e4)):
        keep[f] = 0.0
        add[f] = val
    keep[fut] = 0.0
    add[fut] = -1e30
    c["c_keep"] = keep
    c["c_add"] = add
    c["c_esel"] = (np.arange(S)[None, :] // 64 == np.arange(64)[:, None]).astype(np.float32)
    nb = np.arange(16)
    cb = t // 256
    valid = (nb[None, :] < cb[:, None]).astype(np.float32)
    own = (nb[None, :] == cb[:, None]).astype(np.float32)
    c["c_mvalid"] = valid
    c["c_maddm"] = np.where(valid > 0, 0.0, -1e30).astype(np.float32)
    c["c_mown"] = ((own - 1.0) * (-NEGP)).astype(np.float32)
    eb = np.zeros((16, 16, 128), np.float32)
    for n in range(16):
        eb[n, n, :] = 1.0
    c["c_eb"] = eb.reshape(16, 16 * 128)
    return c


CONST_SHAPES = {
    "c_ident": [128, 128], "c_flip": [128, 128], "oh_full": [33, 2560], "oh_win": [33, 1536],
    "oh_cmp": [33, 6144], "oh_dil0": [33, 1152], "oh_dil1": [33, 1536], "oh_dil2": [33, 3072],
    "c_negrow": [1, 16], "c_cts": [256, 65], "c_keep": [S, 64], "c_add": [S, 64], "c_esel": [64, S],
    "c_mvalid": [S, 16], "c_maddm": [S, 16], "c_mown": [S, 16], "c_eb": [16, 2048],
}


def bias_gen_phase(k, rel_bias, consts, gd):
    P = k.P
    with ExitStack() as es:
        tblx = k.sb(es, "tblx", [33, 16], F32)
        oh = k.sb(es, "oh", [33, 6144], F32)
        go = k.sb(es, "go", [16, 6144], F32)
        pb = [k.ps(es, "pbg", [128, 512], F32) for _ in range(2)]
        k.dma("sp", tblx[0:32, :], rel_bias, writes=["tblx"])
        k.dma("sp", tblx[32:33, :], consts["c_negrow"], writes=["tblx"])
        i = 0
        for fam, (L, _, _) in FAM.items():
            k.dma("sp", oh[:, 0:L], consts["oh_" + fam], writes=["oh"])
            for c0 in range(0, L, 512):
                b = i % 2
                i += 1
                k.mm(pb[b][0:16, :], tblx[:, :], oh[:, c0:c0 + 512], True, True,
                     reads=["tblx", "oh"], writes=[("pbg", b)])
                k.v("dve", "tensor_copy", reads=[("pbg", b)], writes=["go"], out=go[:, c0:c0 + 512],
                    in_=pb[b][0:16, :])
            k.dma("sp", gd[fam], go[:, 0:L], reads=["go"])
    P.barrier()


def make_tb(k, fam, gd, h, flipf, rev, tb, pflip, tok):
    L, W, st = FAM[fam]
    g = gd[fam]
    k.dma("sp", rev[:, 0:W], dap(g.tensor, g.offset + h * L, [[st, 128], [1, W]]), writes=["rev"])
    i = 0
    for c0 in range(0, W, 512):
        n = min(512, W - c0)
        b = i % 2
        i += 1
        k.mm(pflip[b][:, 0:n], flipf[:, :], rev[:, c0:c0 + n], True, True, reads=["flipf", "rev"],
             writes=[("pflip", b)])
        k.v("dve", "tensor_copy", reads=[("pflip", b)], writes=[tok], out=tb[:, c0:c0 + n], in_=pflip[b][:, 0:n])


def proj_phase(k, xsrc, w_norm, w_qk, w_v, w_mg, ident, qkT, vtok, gtok, mgT):
    P = k.P
    with ExitStack() as es:
        hT = k.sb(es, "phT", [128, 8, S], BF16)
        gain = k.sb(es, "pgain", [128, D], F32)
        xin = [k.sb(es, "pxin", [128, D], F32) for _ in range(2)]
        junk = k.sb(es, "pjunk", [128, D], BF16)
        hn = [k.sb(es, "phn", [128, D], BF16) for _ in range(2)]
        st = [k.sb(es, "pst", [128, 4], F32) for _ in range(2)]
        wvs = k.sb(es, "wvs", [128, NVG], F32)
        wvb = k.sb(es, "wvb", [128, 8, NVG], BF16)
        wst = [k.sb(es, "wst", [128, 8, 128], F32) for _ in range(2)]
        wb = [k.sb(es, "wb", [128, 8, 128], BF16) for _ in range(2)]
        orow = [k.sb(es, "orow", [128, S], BF16) for _ in range(2)]
        vout = [k.sb(es, "vout", [128, NVC], BF16) for _ in range(2)]
        gout = [k.sb(es, "gout", [128, 18], F32) for _ in range(2)]
        ptp = [k.ps(es, "pptp", [128, 1024], BF16) for _ in range(2)]
        pmm = [k.ps(es, "ppmm", [128, 512], F32) for _ in range(4)]
        k.dma("sp", gain[:], dap(w_norm.tensor, w_norm.offset, [[0, 128], [1, D]]), writes=["pgain"])
        wvv = w_v.rearrange("(kc p) f -> p kc f", p=128)
        for kc in range(8):
            k.dma("pool", wvs[:], wvv[:, kc, :], writes=["wvs"])
            k.v("dve", "tensor_copy", reads=["wvs"], writes=["wvb"], out=wvb[:, kc, :], in_=wvs[:])
        for t in range(NT):
            b = t % 2
            r0 = t * 128
            k.dma("sp", xin[b][:], xsrc[r0:r0 + 128, :], writes=[("pxin", b)])
            k.v("dve", "scalar_tensor_tensor", reads=[("pxin", b)], writes=["pjunk", ("pst", b)],
                out=junk[:], in0=xin[b][:], scalar=1.0, in1=xin[b][:], op0=ALU.mult, op1=ALU.mult,
                accum_out=st[b][:, 0:1])
            k.act(st[b][:, 1:2], st[b][:, 0:1], AF.Sqrt, reads=[("pst", b)], writes=[("pst1", b)],
                  scale=1.0 / D, bias=EPS)
            k.v("dve", "reciprocal", reads=[("pst1", b)], writes=[("pst2", b)],
                out=st[b][:, 2:3], in_=st[b][:, 1:2])
            k.v("dve", "scalar_tensor_tensor", reads=[("pxin", b), ("pst2", b), "pgain"], writes=[("phn", b)],
                out=hn[b][:], in0=xin[b][:], scalar=st[b][:, 2:3], in1=gain[:], op0=ALU.mult, op1=ALU.mult)
            for kc in range(8):
                k.tr(ptp[b][:, kc * 128:(kc + 1) * 128], hn[b][:, kc * 128:(kc + 1) * 128], ident[:],
                     reads=[("phn", b), "ident"], writes=[("pptp", b)])
            k.act(hT[:, :, r0:r0 + 128], ptp[b][:].rearrange("p (a c) -> p a c", a=8), AF.Copy,
                  reads=[("pptp", b)], writes=[("phT", t)])
            pa, pb_ = pmm[(t % 2) * 2], pmm[(t % 2) * 2 + 1]
            ta, tb_ = ("ppmm", (t % 2) * 2), ("ppmm", (t % 2) * 2 + 1)
            for kc in range(8):
                k.mm(pa[:, 0:512], hT[:, kc, r0:r0 + 128], wvb[:, kc, 0:512], kc == 0, kc == 7,
                     reads=[("phT", t), "wvb"], writes=[ta])
            for kc in range(8):
                k.mm(pb_[:, 0:NVG - 512], hT[:, kc, r0:r0 + 128], wvb[:, kc, 512:NVG], kc == 0, kc == 7,
                     reads=[("phT", t), "wvb"], writes=[tb_])
            k.act(vout[b][:, 0:512], pa[:, 0:512], AF.Copy, reads=[ta], writes=[("vout", b)])
            k.v("dve", "tensor_copy", reads=[tb_], writes=[("vout", b)], out=vout[b][:, 512:NVC],
                in_=pb_[:, 0:NVC - 512])
            k.act(gout[b][:], pb_[:, NVC - 512:NVG - 512], AF.Sigmoid, reads=[tb_], writes=[("gout", b)])
            k.dma("pool", vtok[r0:r0 + 128, :], vout[b][:], reads=[("vout", b)])
            k.dma("pool", gtok[r0:r0 + 128, :], gout[b][:], reads=[("gout", b)])
        allh = [("phT", t) for t in range(NT)]
        blocks = [("qk", i) for i in range(NQK // 128)] + [("mg", i) for i in range(3 * D // 128)]
        mi = 0
        for bi, (kind, i) in enumerate(blocks):
            b = bi % 2
            wsrc = (w_qk if kind == "qk" else w_mg)[:, i * 128:(i + 1) * 128].rearrange("(kc p) f -> p kc f", p=128)
            k.dma("pool", wst[b][:], wsrc, writes=[("wst", b)])
            k.v("dve", "tensor_copy", reads=[("wst", b)], writes=[("wb", b)], out=wb[b][:], in_=wst[b][:])
            for tb8 in range(8):
                pi = mi % 4
                mi += 1
                for kc in range(8):
                    k.mm(pmm[pi][:], wb[b][:, kc, :], hT[:, kc, tb8 * 512:(tb8 + 1) * 512], kc == 0, kc == 7,
                         reads=allh + [("wb", b)], writes=[("ppmm", pi)])
                if kind == "mg":
                    k.act(orow[b][:, tb8 * 512:(tb8 + 1) * 512], pmm[pi][:], AF.Sigmoid,
                          reads=[("ppmm", pi)], writes=[("orow", b)])
                elif tb8 % 2 == 0:
                    k.act(orow[b][:, tb8 * 512:(tb8 + 1) * 512], pmm[pi][:], AF.Copy,
                          reads=[("ppmm", pi)], writes=[("orow", b)])
                else:
                    k.v("dve", "tensor_copy", reads=[("ppmm", pi)], writes=[("orow", b)],
                        out=orow[b][:, tb8 * 512:(tb8 + 1) * 512], in_=pmm[pi][:])
            dst = (qkT if kind == "qk" else mgT)[i * 128:(i + 1) * 128, :]
            k.dma("sp", dst, orow[b][:], reads=[("orow", b)])
    P.barrier()


def compress_phase(k, qkT, pe_k, pe_v, phi_k1, phi_k2, phi_v1, phi_v2, kcT, vcs, identf):
    P = k.P
    C1 = 1.5957691216057308
    with ExitStack() as es:
        src = k.sb(es, "csrc", [128, S], BF16)
        w1s = k.sb(es, "w1s", [128, 8, 256], F32)
        w1b = k.sb(es, "w1b", [128, 32, 256], BF16)
        w2s = k.sb(es, "w2s", [128, 2, 64], F32)
        w2b = k.sb(es, "w2b", [128, 2, 64], BF16)
        pes = k.sb(es, "pes", [128, 64], F32)
        peb = k.sb(es, "peb", [128, 32], BF16)
        bias = k.sb(es, "cbias", [128, 2], F32)
        xa = k.sb(es, "cxa", [128, 256], F32)
        xb_ = k.sb(es, "cxb", [128, 256], F32)
        xc = k.sb(es, "cxc", [128, 256], F32)
        hid = [k.sb(es, "chid", [128, 256], BF16) for _ in range(2)]
        ko = k.sb(es, "cko", [64, 256], BF16)
        vo = k.sb(es, "cvo", [128, 64], BF16)
        ph = [k.ps(es, "cph", [128, 512], F32) for _ in range(2)]
        pbias = k.ps(es, "cpb", [128, 512], F32)
        po = k.ps(es, "cpo", [128, 512], F32)
        for which, (r0, pe, w1, w2) in enumerate(((R_KCMP, pe_k, phi_k1, phi_k2), (R_VCMP, pe_v, phi_v1, phi_v2))):
            k.dma("sp", src[:], qkT[r0:r0 + 128, :], writes=["csrc"])
            w1v = w1.rearrange("(l d) h -> d l h", d=64)
            for half in range(2):
                for l0 in range(0, 32, 8):
                    k.dma("pool", w1s[half * 64:(half + 1) * 64, :, :], w1v[:, l0:l0 + 8, :], writes=["w1s"])
                    k.v("dve", "tensor_copy", reads=["w1s"], writes=["w1b"],
                        out=w1b[half * 64:(half + 1) * 64, l0:l0 + 8, :], in_=w1s[half * 64:(half + 1) * 64, :, :])
            k.dma("sp", pes[0:32, 0:64], pe, writes=["pes"])
            k.tr(pbias[0:64, 64:96], pes[0:32, 0:64], identf[0:32, 0:32], reads=["pes", "identf"], writes=["cpb"])
            k.v("dve", "tensor_copy", reads=["cpb"], writes=["peb"], out=peb[0:64, :], in_=pbias[0:64, 64:96])
            k.dma("sp", w2s[:], w2.rearrange("(hc p) d -> p hc d", p=128), writes=["w2s"])
            k.v("dve", "tensor_copy", reads=["w2s"], writes=["w2b"], out=w2b[:], in_=w2s[:])
            for hc in range(2):
                for l in range(32):
                    k.mm(pbias[:, hc:hc + 1], w1b[0:64, l, hc * 128:(hc + 1) * 128], peb[0:64, l:l + 1],
                         l == 0, l == 31, reads=["w1b", "peb"], writes=["cpb"])
            k.v("dve", "tensor_copy", reads=["cpb"], writes=["cbias"], out=bias[:], in_=pbias[:, 0:2])
            for g in range(2):
                p0 = g * 64
                for hc in range(2):
                    for l in range(32):
                        k.mm(ph[hc][:, 0:255], w1b[p0:p0 + 64, l, hc * 128:(hc + 1) * 128],
                             src[p0:p0 + 64, l:l + 16 * 254 + 1:16], l == 0, l == 31,
                             reads=["w1b", "csrc"], writes=[("cph", hc)])
                    k.v("dve", "tensor_scalar", reads=[("cph", hc), "cbias"], writes=["cxa"], out=xa[:, 0:255],
                        in0=ph[hc][:, 0:255], scalar1=bias[:, hc:hc + 1], scalar2=None, op0=ALU.add)
                    k.v("dve", "tensor_tensor", reads=["cxa"], writes=["cxb"], out=xb_[:, 0:255], in0=xa[:, 0:255],
                        in1=xa[:, 0:255], op=ALU.mult)
                    k.v("dve", "tensor_scalar", reads=["cxb"], writes=["cxc"], out=xc[:, 0:255], in0=xb_[:, 0:255],
                        scalar1=0.044715, scalar2=1.0, op0=ALU.mult, op1=ALU.add)
                    k.v("dve", "tensor_tensor", reads=["cxc", "cxa"], writes=["cxb"], out=xb_[:, 0:255],
                        in0=xc[:, 0:255], in1=xa[:, 0:255], op=ALU.mult)
                    k.act(xc[:, 0:255], xb_[:, 0:255], AF.Sigmoid, reads=["cxb"], writes=["cxc"], scale=C1)
                    k.v("dve", "tensor_tensor", reads=["cxc", "cxa"], writes=[("chid", hc)], out=hid[hc][:, 0:255],
                        in0=xc[:, 0:255], in1=xa[:, 0:255], op=ALU.mult)
                hh = [("chid", 0), ("chid", 1)]
                if which == 0:
                    for hc in range(2):
                        k.mm(po[0:64, 0:255], w2b[:, hc, :], hid[hc][:, 0:255], hc == 0, hc == 1,
                             reads=hh + ["w2b"], writes=["cpo"])
                    k.v("dve", "tensor_copy", reads=["cpo"], writes=["cko"], out=ko[:, 0:255], in_=po[0:64, 0:255])
                    k.dma("sp", kcT[g, :, 0:255], ko[:, 0:255], reads=["cko"])
                else:
                    for ch in range(2):
                        rows = 128 if ch == 0 else 127
                        for hc in range(2):
                            k.mm(po[0:rows, 0:64], hid[hc][:, ch * 128:ch * 128 + rows], w2b[:, hc, :],
                                 hc == 0, hc == 1, reads=hh + ["w2b"], writes=["cpo"])
                        k.v("dve", "tensor_copy", reads=["cpo"], writes=["cvo"], out=vo[0:rows, :], in_=po[0:rows, 0:64])
                        k.dma("sp", vcs[g, ch * 128:ch * 128 + rows, :], vo[0:rows, :], reads=["cvo"])
    P.barrier()


class AttnCx:
    def __init__(self, k, es, nU):
        self.S = [k.ps(es, "aS", [128, 512], F32) for _ in range(3)]
        self.U = [k.ps(es, "aU", [128, 4, 128], F32) for _ in range(nU)]
        self.L = [k.sb(es, "aL", [128, 512], F32) for _ in range(3)]
        self.E = [k.sb(es, "aE", [128, 512], BF16) for _ in range(5)]
        self.fifo = []
        self.npv = 0
        self.si = self.li = self.ei = 0
        self.zl = k.sb(es, "azl", [1, 128], BF16)
        self.zr = k.sb(es, "azr", [1, 512], BF16)
        k.v("dve", "memset", writes=["azl"], ap=self.zl[:], constant=0.0)
        k.v("dve", "memset", writes=["azr"], ap=self.zr[:], constant=0.0)


def emit_scores(k, cx, rows, kT, q, n, tb, cbias, mask, rd):
    si = cx.si % 3
    cx.si += 1
    Sb = cx.S[si]
    k.mm(Sb[0:rows, 0:n], kT, q, True, mask is None, reads=rd, writes=[("aS", si)])
    if mask is not None:
        k.mm(Sb[0:rows, 0:n], mask[0], mask[1], False, True, reads=rd, writes=[("aS", si)])
    ei = cx.ei % 5
    cx.ei += 1
    Eb = cx.E[ei]
    if tb is not None:
        li = cx.li % 3
        cx.li += 1
        k.v("dve", "scalar_tensor_tensor", reads=[("aS", si)] + rd, writes=[("aL", li)],
            out=cx.L[li][0:rows, 0:n], in0=Sb[0:rows, 0:n], scalar=SCALE, in1=tb, op0=ALU.mult, op1=ALU.add)
        k.act(Eb[0:rows, 0:n], cx.L[li][0:rows, 0:n], AF.Exp, reads=[("aL", li)], writes=[("aE", ei)])
    else:
        k.act(Eb[0:rows, 0:n], Sb[0:rows, 0:n], AF.Exp, reads=[("aS", si)] + rd, writes=[("aE", ei)],
              scale=SCALE, bias=cbias)
    return ei


LOOKAHEAD = 3


def pipe_drain(k, cx, keep):
    while cx.npv > keep or (keep == 0 and cx.fifo):
        act = cx.fifo.pop(0)
        if act[0] == "pv":
            cx.npv -= 1
        act[1]()


def run_branch(k, cx, tiles, ui, utok):
    last = {}
    for i, t in enumerate(tiles):
        for qt in range(t["qt_lo"], t["qt_hi"]):
            last[qt] = i
    U = cx.U[ui]

    def zero():
        k.mm(U[:].rearrange("p a b -> p (a b)"), cx.zl[0:1, :], cx.zr[0:1, :], True, False,
             reads=["azl", "azr"], writes=[utok])
    cx.fifo.append(("zero", zero))
    for i, t in enumerate(tiles):
        lo, hi = t["qt_lo"], t["qt_hi"]
        n = (hi - lo) * 128
        rows = t["rows"]
        ei = emit_scores(k, cx, rows, t["kT"], t["qfn"](lo * 128, n), n,
                         t["tbfn"](lo * 128, n) if t["tbfn"] is not None else None, t["cbias"],
                         t["maskfn"](lo * 128, n) if t["maskfn"] is not None else None, t["rd"])

        def pv(i=i, t=t, lo=lo, hi=hi, rows=rows, ei=ei):
            Eb = cx.E[ei]
            for qt in range(lo, hi):
                c0 = (qt - lo) * 128
                k.mm(U[:, qt, 0:65], Eb[0:rows, c0:c0 + 128], t["V"], False, last[qt] == i,
                     reads=[("aE", ei)] + t["rd"], writes=[utok])
        cx.fifo.append(("pv", pv))
        cx.npv += 1
        pipe_drain(k, cx, LOOKAHEAD)


def pipe_post(k, cx, fn):
    cx.fifo.append(("post", fn))


def pipe_flush(k, cx):
    pipe_drain(k, cx, 0)


def dma_split(k, q, dst, src, tok, step=8):
    for c0 in range(0, NT, step):
        k.dma(q, dst[:, c0:c0 + step, :], src[:, c0:c0 + step, :], writes=[tok])


def load_v_aug(k, q, vt, vtok, col0, tok):
    k.v("dve", "memset", writes=[tok], ap=vt[:, :, 64:65], constant=1.0)
    src = vtok[:, col0:col0 + 64].rearrange("(c p) d -> p c d", p=128)
    for c0 in range(0, NT, 8):
        k.dma(q, vt[:, c0:c0 + 8, 0:64], src[:, c0:c0 + 8, :], writes=[tok])


def cmp_tiles(qb, kc_sb, q_sb, tbc, V_sb, rd):
    qs = qb * 512
    tiles = []
    for ch in range(2):
        Dd = qs - 2048 * ch
        if Dd < 0:
            continue
        rows = 128 if ch == 0 else 127
        tiles.append(dict(
            rows=rows, kT=kc_sb[:, ch * 128:ch * 128 + rows],
            qfn=lambda c0, n, qs=qs: q_sb[:, qs + c0:qs + c0 + n],
            tbfn=lambda c0, n, Dd=Dd, rows=rows: tbc[0:rows, Dd + c0:Dd + c0 + n],
            cbias=None, maskfn=None, V=V_sb[0:rows, ch, :], qt_lo=0, qt_hi=4, rd=rd))
    return tiles


def nsa_phase(k, qkT, vtok, gtok, kcT, vcs, gd, consts, identf, ytok_sb):
    P = k.P
    with ExitStack() as es:
        flipf = k.sb(es, "flipf", [128, 128], F32)
        rev = k.sb(es, "rev", [128, 4096], F32)
        tbc = k.sb(es, "tbc", [128, 4096], F32)
        tbf = k.sb(es, "tbf", [128, 2432], F32)
        tbw = k.sb(es, "tbw", [128, 1408], F32)
        q_sb = k.sb(es, "nq", [64, S], BF16)
        kc_sb = k.sb(es, "nkc", [64, 256], BF16)
        ctsf = k.sb(es, "ctsf", [128, 2, 65], F32)
        cts = k.sb(es, "cts", [128, 2, 65], BF16)
        vc_sb = k.sb(es, "nvc", [128, 2, 65], BF16)
        ksel = k.sb(es, "nksel", [64, S], BF16)
        kwin = k.sb(es, "nkwin", [64, S], BF16)
        vsel = k.sb(es, "nvsel", [128, NT, 65], BF16)
        vwin = k.sb(es, "nvwin", [128, NT, 65], BF16)
        eself = k.sb(es, "eself", [64, 1024], F32)
        esel = k.sb(es, "esel", [64, S], BF16)
        neg = k.sb(es, "nneg", [64, S], BF16)
        imp = k.sb(es, "imp", [128, NT, 64], F32)
        keep = k.sb(es, "keep", [128, NT, 64], F32)
        addc = k.sb(es, "addc", [128, NT, 64], F32)
        gts = k.sb(es, "gts", [128, NT, 18], F32)
        sm = k.sb(es, "nsm", [128, 64], F32)
        wk = [k.sb(es, "nwk", [128, 64], F32) for _ in range(3)]
        m8 = [k.sb(es, "nm8", [128, 8], F32) for _ in range(2)]
        negq = k.sb(es, "negq", [128, 64], F32)
        acc = [k.sb(es, "nacc", [128, 64], F32) for _ in range(2)]
        cx = AttnCx(k, es, 3)
        pfl = [k.ps(es, "pfl", [128, 512], F32) for _ in range(2)]

        k.dma("sp", flipf[:], consts["c_flip"], writes=["flipf"])
        k.dma("sp", ctsf[:], consts["c_cts"].rearrange("(c p) d -> p c d", p=128), writes=["ctsf"])
        k.v("dve", "tensor_copy", reads=["ctsf"], writes=["cts"], out=cts[:], in_=ctsf[:])
        for c0 in range(0, S, 1024):
            k.dma("sp", eself[:], consts["c_esel"][:, c0:c0 + 1024], writes=["eself"])
            k.v("dve", "tensor_copy", reads=["eself"], writes=["esel"], out=esel[:, c0:c0 + 1024], in_=eself[:])
        dma_split(k, "pool", keep, consts["c_keep"].rearrange("(c p) d -> p c d", p=128), "keep")
        dma_split(k, "pool", addc, consts["c_add"].rearrange("(c p) d -> p c d", p=128), "addc")
        dma_split(k, "pool", gts, gtok.rearrange("(c p) d -> p c d", p=128), "gts")

        for g in range(2):
            k.dma("sp", kc_sb[:], kcT[g], writes=["nkc"])
            for r in range(3):
                h = g * 3 + r
                k.dma("sp", q_sb[:], qkT[R_NSAQ + h * 64:R_NSAQ + (h + 1) * 64, :], writes=["nq"])
                make_tb(k, "cmp", gd, h, flipf, rev, tbc, pfl, "tbc")
                for qb in range(8):
                    tiles = cmp_tiles(qb, kc_sb, q_sb, tbc, cts, ["nkc", "nq", "tbc", "cts"])
                    run_branch(k, cx, tiles, 0, ("aU", 0))

                    def post1(qb=qb, r=r):
                        U = cx.U[0]
                        k.v("dve", "tensor_scalar", reads=[("aU", 0)], writes=["nsm"], out=sm[:, 0:4],
                            in0=U[:, :, 64], scalar1=1e-30, scalar2=None, op0=ALU.max)
                        k.v("dve", "reciprocal", reads=["nsm"], writes=["nsm2"], out=sm[:, 4:8], in_=sm[:, 0:4])
                        for qt in range(4):
                            tl = qb * 4 + qt
                            if r == 0:
                                k.v("dve", "tensor_scalar", reads=[("aU", 0), "nsm2"], writes=[("imp", tl)],
                                    out=imp[:, tl, :], in0=U[:, qt, 0:64], scalar1=sm[:, 4 + qt:5 + qt],
                                    scalar2=None, op0=ALU.mult)
                            else:
                                k.v("dve", "scalar_tensor_tensor", reads=[("aU", 0), "nsm2", ("imp", tl)],
                                    writes=[("imp", tl)], out=imp[:, tl, :], in0=U[:, qt, 0:64],
                                    scalar=sm[:, 4 + qt:5 + qt], in1=imp[:, tl, :], op0=ALU.mult, op1=ALU.add)
                    pipe_post(k, cx, post1)
                pipe_flush(k, cx)
            for tl in range(NT):
                a, b_, c_ = wk
                k.v("dve", "tensor_tensor", reads=[("imp", tl), "keep"], writes=["nwk0"], out=a[:],
                    in0=imp[:, tl, :], in1=keep[:, tl, :], op=ALU.mult)
                k.v("dve", "tensor_tensor", reads=["nwk0", "addc"], writes=["nwk1"], out=b_[:],
                    in0=a[:], in1=addc[:, tl, :], op=ALU.add)
                k.v("dve", "max", reads=["nwk1"], writes=["nm80"], out=m8[0][:], in_=b_[:])
                k.v("dve", "match_replace", reads=["nwk1", "nm80"], writes=["nwk2"], out=c_[:],
                    in_to_replace=m8[0][:], in_values=b_[:], imm_value=-3.0e38)
                k.v("dve", "max", reads=["nwk2"], writes=["nm81"], out=m8[1][:], in_=c_[:])
                k.v("dve", "tensor_scalar", reads=["nwk1", "nm81"], writes=["nwk0"], out=a[:], in0=b_[:],
                    scalar1=m8[1][:, 7:8], scalar2=None, op0=ALU.is_ge)
                k.v("dve", "tensor_scalar", reads=["nwk0"], writes=["negq"], out=negq[:], in0=a[:],
                    scalar1=-NEGP, scalar2=NEGP, op0=ALU.mult, op1=ALU.add)
                pb = tl % 2
                k.tr(pfl[pb][0:64, 0:128], negq[:, :], identf[:], reads=["negq", "identf"], writes=[("pflip", pb)])
                k.act(neg[:, tl * 128:(tl + 1) * 128], pfl[pb][0:64, 0:128], AF.Copy, reads=[("pflip", pb)],
                      writes=["nneg"])
            k.dma("sp", ksel[:], qkT[R_KSEL + g * 64:R_KSEL + (g + 1) * 64, :], writes=["nksel"])
            k.dma("sp", kwin[:], qkT[R_KWIN + g * 64:R_KWIN + (g + 1) * 64, :], writes=["nkwin"])
            load_v_aug(k, "pool", vsel, vtok, C_VSEL + g * 64, "nvsel")
            load_v_aug(k, "pool", vwin, vtok, C_VWIN + g * 64, "nvwin")
            k.v("dve", "memset", writes=["nvc"], ap=vc_sb[:, :, 64:65], constant=1.0)
            k.dma("pool", vc_sb[:, :, 0:64], vcs[g].rearrange("(c p) d -> p c d", p=128), writes=["nvc"])
            for r in range(3):
                h = g * 3 + r
                k.dma("sp", q_sb[:], qkT[R_NSAQ + h * 64:R_NSAQ + (h + 1) * 64, :], writes=["nq"])
                make_tb(k, "cmp", gd, h, flipf, rev, tbc, pfl, "tbc")
                make_tb(k, "full", gd, h, flipf, rev, tbf, pfl, "tbf")
                make_tb(k, "win", gd, h, flipf, rev, tbw, pfl, "tbw")
                for qb in range(8):
                    qs = qb * 512
                    qfn = lambda c0, n, qs=qs: q_sb[:, qs + c0:qs + c0 + n]
                    tiles = cmp_tiles(qb, kc_sb, q_sb, tbc, vc_sb, ["nkc", "nq", "tbc", "nvc"])
                    run_branch(k, cx, tiles, 0, ("aU", 0))
                    tiles = []
                    for kc in range(0, (qs + 384) // 128 + 1):
                        Dl = qs - 128 * kc
                        lo = max(0, -Dl // 128)
                        far = Dl >= 1664
                        tiles.append(dict(
                            rows=128, kT=ksel[:, kc * 128:(kc + 1) * 128], qfn=qfn,
                            tbfn=None if far else (lambda c0, n, Dl=Dl: tbf[:, Dl + 384 + c0:Dl + 384 + c0 + n]),
                            cbias=tbf[:, 2431:2432] if far else None,
                            maskfn=lambda c0, n, kc=kc, qs=qs: (esel[:, kc * 128:(kc + 1) * 128],
                                                              neg[:, qs + c0:qs + c0 + n]),
                            V=vsel[:, kc, :], qt_lo=lo, qt_hi=4, rd=["nksel", "nq", "tbf", "nvsel", "esel", "nneg"]))
                    run_branch(k, cx, tiles, 1, ("aU", 1))
                    tiles = []
                    for kc in range(max(0, (qs - 512) // 128), (qs + 384) // 128 + 1):
                        Dl = qs - 128 * kc
                        lo = max(0, -Dl // 128)
                        hi = min(4, (639 - Dl) // 128 + 1)
                        tiles.append(dict(
                            rows=128, kT=kwin[:, kc * 128:(kc + 1) * 128], qfn=qfn,
                            tbfn=lambda c0, n, Dl=Dl: tbw[:, Dl + 384 + c0:Dl + 384 + c0 + n],
                            cbias=None, maskfn=None, V=vwin[:, kc, :], qt_lo=lo, qt_hi=hi,
                            rd=["nkwin", "nq", "tbw", "nvwin"]))
                    run_branch(k, cx, tiles, 2, ("aU", 2))
                    def post3(qb=qb, h=h):
                        for br in range(3):
                            U = cx.U[br]
                            k.v("dve", "tensor_scalar", reads=[("aU", br)], writes=[("nsmc", br)],
                                out=sm[:, 8 + br * 12:12 + br * 12], in0=U[:, :, 64], scalar1=1e-30, scalar2=None,
                                op0=ALU.max)
                            k.v("dve", "reciprocal", reads=[("nsmc", br)], writes=[("nsmr", br)],
                                out=sm[:, 12 + br * 12:16 + br * 12], in_=sm[:, 8 + br * 12:12 + br * 12])
                            k.v("dve", "tensor_tensor", reads=[("nsmr", br), "gts"], writes=[("nsmg", br)],
                                out=sm[:, 16 + br * 12:20 + br * 12], in0=sm[:, 12 + br * 12:16 + br * 12],
                                in1=gts[:, qb * 4:qb * 4 + 4, h * 3 + br], op=ALU.mult)
                        for qt in range(4):
                            tl = qb * 4 + qt
                            a0, a1 = acc
                            k.v("dve", "tensor_scalar", reads=[("aU", 0), ("nsmg", 0)], writes=["nacc0"], out=a0[:],
                                in0=cx.U[0][:, qt, 0:64], scalar1=sm[:, 16 + qt:17 + qt], scalar2=None, op0=ALU.mult)
                            k.v("dve", "scalar_tensor_tensor", reads=[("aU", 1), ("nsmg", 1), "nacc0"], writes=["nacc1"],
                                out=a1[:], in0=cx.U[1][:, qt, 0:64], scalar=sm[:, 28 + qt:29 + qt], in1=a0[:],
                                op0=ALU.mult, op1=ALU.add)
                            k.v("dve", "scalar_tensor_tensor", reads=[("aU", 2), ("nsmg", 2), "nacc1"],
                                writes=[("ytok", tl)], out=ytok_sb[:, tl, h * 64:(h + 1) * 64],
                                in0=cx.U[2][:, qt, 0:64], scalar=sm[:, 40 + qt:41 + qt], in1=a1[:],
                                op0=ALU.mult, op1=ALU.add)
                    pipe_post(k, cx, post3)
                pipe_flush(k, cx)
    P.barrier()


def moba_phase(k, qkT, vtok, gd, consts, identf, ytok_sb):
    P = k.P
    with ExitStack() as es:
        flipf = k.sb(es, "mflipf", [128, 128], F32)
        rev = k.sb(es, "mrev", [128, 2432], F32)
        tbf = k.sb(es, "mtbf", [128, 2432], F32)
        q_sb = k.sb(es, "mq", [64, S], BF16)
        k_sb = k.sb(es, "mk", [64, S], BF16)
        v_sb = k.sb(es, "mv", [128, NT, 65], BF16)
        ebf = k.sb(es, "ebf", [16, 2048], F32)
        eb = k.sb(es, "eb", [16, 2048], BF16)
        neg = k.sb(es, "mneg", [16, S], BF16)
        valid = k.sb(es, "mvalid", [128, NT, 16], F32)
        addm = k.sb(es, "maddm", [128, NT, 16], F32)
        ownc = k.sb(es, "mown", [128, NT, 16], F32)
        km = k.sb(es, "mkm", [64, 16], F32)
        kmh = k.sb(es, "mkmh", [64, 16], BF16)
        kmhf = k.sb(es, "mkmhf", [64, 16], F32)
        kml = k.sb(es, "mkml", [64, 16], BF16)
        wk = [k.sb(es, "mwk", [128, 16], F32) for _ in range(3)]
        m8 = k.sb(es, "mm8", [128, 8], F32)
        sm = k.sb(es, "msm", [128, 8], F32)
        cx = AttnCx(k, es, 1)
        pfl = [k.ps(es, "mpfl", [128, 512], F32) for _ in range(2)]
        k.dma("sp", flipf[:], consts["c_flip"], writes=["flipf"])
        k.dma("sp", ebf[:], consts["c_eb"], writes=["ebf"])
        k.v("dve", "tensor_copy", reads=["ebf"], writes=["eb"], out=eb[:], in_=ebf[:])
        dma_split(k, "pool", valid, consts["c_mvalid"].rearrange("(c p) d -> p c d", p=128), "mvalid")
        dma_split(k, "pool", addm, consts["c_maddm"].rearrange("(c p) d -> p c d", p=128), "maddm")
        dma_split(k, "pool", ownc, consts["c_mown"].rearrange("(c p) d -> p c d", p=128), "mown")
        for hb in range(4):
            k.dma("sp", q_sb[:], qkT[R_MQ + hb * 64:R_MQ + (hb + 1) * 64, :], writes=["mq"])
            k.dma("sp", k_sb[:], qkT[R_MK + hb * 64:R_MK + (hb + 1) * 64, :], writes=["mk"])
            load_v_aug(k, "pool", v_sb, vtok, C_MV + hb * 64, "mv")
            make_tb(k, "full", gd, 6 + hb, flipf, rev, tbf, pfl, "tbf")
            k.v("dve", "tensor_reduce", reads=["mk"], writes=["mkm"], out=km[:],
                in_=k_sb[:].rearrange("p (n j) -> p n j", j=256), axis=mybir.AxisListType.X, op=ALU.add)
            k.v("dve", "tensor_scalar", reads=["mkm"], writes=["mkm2"], out=km[:], in0=km[:], scalar1=1.0 / 256,
                scalar2=None, op0=ALU.mult)
            k.v("dve", "tensor_copy", reads=["mkm2"], writes=["mkmh"], out=kmh[:], in_=km[:])
            k.v("dve", "tensor_copy", reads=["mkmh"], writes=["mkmhf"], out=kmhf[:], in_=kmh[:])
            k.v("dve", "tensor_tensor", reads=["mkm2", "mkmhf"], writes=["mkml"], out=kml[:], in0=km[:],
                in1=kmhf[:], op=ALU.subtract)
            for tl in range(NT):
                pb = tl % 2
                G = pfl[pb]
                k.mm(G[:, 0:16], q_sb[:, tl * 128:(tl + 1) * 128], kmh[:, :], True, False,
                     reads=["mq", "mkmh"], writes=[("pflip", pb)])
                k.mm(G[:, 0:16], q_sb[:, tl * 128:(tl + 1) * 128], kml[:, :], False, True,
                     reads=["mq", "mkml"], writes=[("pflip", pb)])
                a, b_, c_ = wk
                k.v("dve", "tensor_tensor", reads=[("pflip", pb), "mvalid"], writes=["mwk0"], out=a[:],
                    in0=G[:, 0:16], in1=valid[:, tl, :], op=ALU.mult)
                k.v("dve", "tensor_tensor", reads=["mwk0", "maddm"], writes=["mwk1"], out=b_[:], in0=a[:],
                    in1=addm[:, tl, :], op=ALU.add)
                k.v("dve", "max", reads=["mwk1"], writes=["mm8"], out=m8[:], in_=b_[:])
                k.v("dve", "tensor_scalar", reads=["mwk1", "mm8"], writes=["mwk2"], out=c_[:], in0=b_[:],
                    scalar1=m8[:, 2:3], scalar2=None, op0=ALU.is_ge)
                k.v("dve", "tensor_tensor", reads=["mwk2", "mvalid"], writes=["mwk0"], out=a[:], in0=c_[:],
                    in1=valid[:, tl, :], op=ALU.mult)
                k.v("dve", "scalar_tensor_tensor", reads=["mwk0", "mown"], writes=["mwk1"], out=b_[:], in0=a[:],
                    scalar=-NEGP, in1=ownc[:, tl, :], op0=ALU.mult, op1=ALU.add)
                k.tr(pfl[pb][0:16, 128:256], b_[:, :], identf[:], reads=["mwk1", "identf"], writes=[("pflip", pb)])
                k.act(neg[:, tl * 128:(tl + 1) * 128], pfl[pb][0:16, 128:256], AF.Copy, reads=[("pflip", pb)],
                      writes=["mneg"])
            for qb in range(8):
                qs = qb * 512
                qfn = lambda c0, n, qs=qs: q_sb[:, qs + c0:qs + c0 + n]
                tiles = []
                for kc in range(0, (qs + 384) // 128 + 1):
                    Dl = qs - 128 * kc
                    lo = max(0, -Dl // 128)
                    far = Dl >= 1664
                    nblk = kc // 2
                    tiles.append(dict(
                        rows=128, kT=k_sb[:, kc * 128:(kc + 1) * 128], qfn=qfn,
                        tbfn=None if far else (lambda c0, n, Dl=Dl: tbf[:, Dl + 384 + c0:Dl + 384 + c0 + n]),
                        cbias=tbf[:, 2431:2432] if far else None,
                        maskfn=lambda c0, n, nblk=nblk, qs=qs: (eb[:, nblk * 128:(nblk + 1) * 128],
                                                              neg[:, qs + c0:qs + c0 + n]),
                        V=v_sb[:, kc, :], qt_lo=lo, qt_hi=4, rd=["mk", "mq", "tbf", "mv", "eb", "mneg"]))
                run_branch(k, cx, tiles, 0, ("aU", 0))

                def postm(qb=qb, hb=hb):
                    U = cx.U[0]
                    k.v("dve", "tensor_scalar", reads=[("aU", 0)], writes=["msm"], out=sm[:, 0:4], in0=U[:, :, 64],
                        scalar1=1e-30, scalar2=None, op0=ALU.max)
                    k.v("dve", "reciprocal", reads=["msm"], writes=["msm2"], out=sm[:, 4:8], in_=sm[:, 0:4])
                    for qt in range(4):
                        tl = qb * 4 + qt
                        k.v("dve", "tensor_scalar", reads=[("aU", 0), "msm2"], writes=[("ytok", tl)],
                            out=ytok_sb[:, tl, 384 + hb * 64:384 + (hb + 1) * 64], in0=U[:, qt, 0:64],
                            scalar1=sm[:, 4 + qt:5 + qt], scalar2=None, op0=ALU.mult)
                pipe_post(k, cx, postm)
            pipe_flush(k, cx)
    P.barrier()


def dil_phase(k, qkT, vtok, gd, consts, ytok_sb):
    P = k.P
    with ExitStack() as es:
        flipf = k.sb(es, "dflipf", [128, 128], F32)
        rev = k.sb(es, "drev", [128, 2944], F32)
        tbs = [k.sb(es, "dtb", [128, FAM["dil%d" % g][1]], F32) for g in range(3)]
        q_sb = [k.sb(es, "dq", [64, S], BF16) for _ in range(3)]
        k_sb = [k.sb(es, "dk", [64, S], BF16) for _ in range(3)]
        v_sb = [k.sb(es, "dv", [128, NT, 65], BF16) for _ in range(3)]
        sm = k.sb(es, "dsm", [128, 8], F32)
        cx = AttnCx(k, es, 1)
        pfl = [k.ps(es, "dpfl", [128, 512], F32) for _ in range(2)]
        k.dma("sp", flipf[:], consts["c_flip"], writes=["flipf"])
        for i in range(2):
            for g in range(3):
                hh = g * 2 + i
                k.dma("sp", q_sb[g][:], qkT[R_DQ + hh * 64:R_DQ + (hh + 1) * 64, :], writes=[("dq", g)])
                k.dma("sp", k_sb[g][:], qkT[R_DK + hh * 64:R_DK + (hh + 1) * 64, :], writes=[("dk", g)])
                load_v_aug(k, "pool", v_sb[g], vtok, C_DV + hh * 64, ("dv", g))
                make_tb(k, "dil%d" % g, gd, 10 + hh, flipf, rev, tbs[g], pfl, ("dtb", g))
            for qb in range(8):
                qs = qb * 512
                tiles = []
                for g, (W, dl) in enumerate(DIL):
                    for kc in range(max(0, (qs - W) // 128), (qs + 384) // 128 + 1):
                        Dl = qs - 128 * kc
                        lo = max(0, -Dl // 128)
                        hi = min(4, (W - Dl) // 128 + 1)
                        tiles.append(dict(
                            rows=128, kT=k_sb[g][:, kc * 128:(kc + 1) * 128],
                            qfn=lambda c0, n, g=g, qs=qs: q_sb[g][:, qs + c0:qs + c0 + n],
                            tbfn=lambda c0, n, g=g, Dl=Dl: tbs[g][:, Dl + 384 + c0:Dl + 384 + c0 + n],
                            cbias=None, maskfn=None, V=v_sb[g][:, kc, :], qt_lo=lo, qt_hi=hi,
                            rd=[("dq", g), ("dk", g), ("dv", g), ("dtb", g)]))
                run_branch(k, cx, tiles, 0, ("aU", 0))

                def postd(qb=qb, i=i):
                    U = cx.U[0]
                    k.v("dve", "tensor_scalar", reads=[("aU", 0)], writes=["dsm"], out=sm[:, 0:4], in0=U[:, :, 64],
                        scalar1=1e-30, scalar2=None, op0=ALU.max)
                    k.v("dve", "reciprocal", reads=["dsm"], writes=["dsm2"], out=sm[:, 4:8], in_=sm[:, 0:4])
                    for qt in range(4):
                        tl = qb * 4 + qt
                        k.v("dve", "tensor_scalar", reads=[("aU", 0), "dsm2"], writes=[("ytok", tl)],
                            out=ytok_sb[:, tl, 640 + i * 64:640 + (i + 1) * 64], in0=U[:, qt, 0:64],
                            scalar1=sm[:, 4 + qt:5 + qt], scalar2=None, op0=ALU.mult)
                pipe_post(k, cx, postd)
            pipe_flush(k, cx)
    P.barrier()


def merge_phase(k, xres, mgT, w_up_a, w_up_b, w_up_c, w_o, ident, ytok_sb):
    P = k.P
    with ExitStack() as es:
        wup = k.sb(es, "wup", [128, 6, D], BF16)
        wo = k.sb(es, "wo", [128, 8, D], BF16)
        stg = [k.sb(es, "gstg", [128, D], F32) for _ in range(2)]
        yT = k.sb(es, "gyT", [128, 6, 512], BF16)
        gt = [k.sb(es, "ggt", [128, 3, 512], BF16) for _ in range(2)]
        m1 = [k.sb(es, "gm1", [128, 512], F32) for _ in range(2)]
        m2 = [k.sb(es, "gm2", [128, 512], F32) for _ in range(2)]
        m3 = [k.sb(es, "gm3", [128, 512], F32) for _ in range(2)]
        m4 = [k.sb(es, "gm4", [128, 512], F32) for _ in range(2)]
        mT = k.sb(es, "gmT", [128, 8, 512], BF16)
        xr = [k.sb(es, "gxr", [128, 512], F32) for _ in range(2)]
        ob = [k.sb(es, "gob", [128, 512], F32) for _ in range(2)]
        ptp = [k.ps(es, "gptp", [128, 1024], BF16) for _ in range(2)]
        pu = [k.ps(es, "gpu", [128, 512], F32) for _ in range(3)]
        po = [k.ps(es, "gpo", [128, 512], F32) for _ in range(2)]
        srcs = [(w_up_a, 0), (w_up_a, 1), (w_up_a, 2), (w_up_b, 0), (w_up_b, 1), (w_up_c, 0)]
        ci = 0
        for fc, (w, j) in enumerate(srcs):
            b = ci % 2
            ci += 1
            k.dma("sp" if b == 0 else "pool", stg[b][:], w[j * 128:(j + 1) * 128, :], writes=[("gstg", b)])
            k.v("dve", "tensor_copy", reads=[("gstg", b)], writes=["wup"], out=wup[:, fc, :], in_=stg[b][:])
        for kc in range(8):
            b = ci % 2
            ci += 1
            k.dma("sp" if b == 0 else "pool", stg[b][:], w_o[kc * 128:(kc + 1) * 128, :], writes=[("gstg", b)])
            k.v("dve", "tensor_copy", reads=[("gstg", b)], writes=["wo"], out=wo[:, kc, :], in_=stg[b][:])
        ui = 0
        oi = 0
        for tb8 in range(8):
            t0 = tb8 * 512
            for fc in range(6):
                pb = fc % 2
                for t in range(4):
                    tl = tb8 * 4 + t
                    k.tr(ptp[pb][:, t * 128:(t + 1) * 128], ytok_sb[:, tl, fc * 128:(fc + 1) * 128], ident[:],
                         reads=[("ytok", tl), "ident"], writes=[("gptp", pb)])
                if fc % 2 == 0:
                    k.act(yT[:, fc, :], ptp[pb][:, 0:512], AF.Copy, reads=[("gptp", pb)], writes=[("gyT", fc)])
                else:
                    k.v("dve", "tensor_copy", reads=[("gptp", pb)], writes=[("gyT", fc)], out=yT[:, fc, :],
                        in_=ptp[pb][:, 0:512])
            for cc in range(8):
                b = ui % 2
                ui += 1
                k.dma("sp", gt[b][:], mgT.rearrange("(b r) t -> r b t", b=3)[cc * 128:(cc + 1) * 128, :, t0:t0 + 512],
                      writes=[("ggt", b)])
                groups = ((0, (0, 1, 2)), (1, (3, 4)), (2, (5,)))
                for br, fcs in groups:
                    for j, fc in enumerate(fcs):
                        k.mm(pu[br][:], wup[:, fc, cc * 128:(cc + 1) * 128], yT[:, fc, :], j == 0, j == len(fcs) - 1,
                             reads=[("gyT", fc), "wup"], writes=[("gpu", br)])
                k.v("dve", "tensor_tensor", reads=[("gpu", 0), ("ggt", b)], writes=[("gm1", b)], out=m1[b][:],
                    in0=pu[0][:], in1=gt[b][:, 0, :], op=ALU.mult)
                k.v("dve", "tensor_tensor", reads=[("gpu", 1), ("ggt", b)], writes=[("gm2", b)], out=m2[b][:],
                    in0=pu[1][:], in1=gt[b][:, 1, :], op=ALU.mult)
                k.v("dve", "tensor_tensor", reads=[("gpu", 2), ("ggt", b)], writes=[("gm3", b)], out=m3[b][:],
                    in0=pu[2][:], in1=gt[b][:, 2, :], op=ALU.mult)
                k.v("dve", "tensor_tensor", reads=[("gm1", b), ("gm2", b)], writes=[("gm4", b)], out=m4[b][:],
                    in0=m1[b][:], in1=m2[b][:], op=ALU.add)
                k.v("dve", "tensor_tensor", reads=[("gm4", b), ("gm3", b)], writes=[("gmT", cc)], out=mT[:, cc, :],
                    in0=m4[b][:], in1=m3[b][:], op=ALU.add)
            allm = [("gmT", cc) for cc in range(8)]
            for t in range(4):
                r0 = t0 + t * 128
                for hh in range(2):
                    b = oi % 2
                    oi += 1
                    k.dma("pool", xr[b][:], xres[r0:r0 + 128, hh * 512:(hh + 1) * 512], writes=[("gxr", b)])
                    for cc in range(8):
                        k.mm(po[b][:], mT[:, cc, t * 128:(t + 1) * 128], wo[:, cc, hh * 512:(hh + 1) * 512],
                             cc == 0, cc == 7, reads=allm + ["wo"], writes=[("gpo", b)])
                    k.v("dve", "tensor_tensor", reads=[("gpo", b), ("gxr", b)], writes=[("gob", b)], out=ob[b][:],
                        in0=po[b][:], in1=xr[b][:], op=ALU.add)
                    k.dma("sp", xres[r0:r0 + 128, hh * 512:(hh + 1) * 512], ob[b][:], reads=[("gob", b)])
    P.barrier()


W_QK_COLS = np.concatenate([np.arange(0, 384), np.arange(384, 512), np.arange(512, 640), np.arange(640, 768),
                            np.arange(896, 1024), np.arange(1170, 1426), np.arange(1426, 1682),
                            np.arange(1938, 2322), np.arange(2322, 2706)])
W_V_COLS = np.concatenate([np.arange(768, 896), np.arange(1024, 1152), np.arange(1682, 1938),
                           np.arange(2706, 3090), np.arange(1152, 1170)])
W_MG_COLS = np.arange(3090, 6162)


def build(stop_after=None, debug=False):
    nc = bass.Bass("TRN2", target_bir_lowering=False)

    def inp(name, shape):
        return nc.dram_tensor(name, list(shape), F32, kind="ExternalInput").ap()

    def scr(name, shape, dt):
        return nc.dram_tensor(name, list(shape), dt, kind="ExternalOutput" if debug else "Internal").ap()

    x = inp("x", [S, D])
    rel_bias = inp("rel_bias", [32, 16])
    ffn_norm = [inp("ffn1_norm", [2, D]), inp("ffn2_norm", [2, D])]
    ffn_wg = [inp("ffn1_w_gate", [2, D, DFF]), inp("ffn2_w_gate", [2, D, DFF])]
    ffn_wu = [inp("ffn1_w_up", [2, D, DFF]), inp("ffn2_w_up", [2, D, DFF])]
    ffn_wd = [inp("ffn1_w_down", [2, DFF, D]), inp("ffn2_w_down", [2, DFF, D])]
    mix_norm = inp("mix_norm", [2, D])
    w_qk = inp("w_qk", [2, D, NQK])
    w_v = inp("w_v", [2, D, NVG])
    w_mg = inp("w_mg", [2, D, 3 * D])
    pe_k = inp("nsa_pe_k", [2, 32, 64])
    pe_v = inp("nsa_pe_v", [2, 32, 64])
    phi_k1 = inp("nsa_phi_k1", [2, 2048, 256])
    phi_k2 = inp("nsa_phi_k2", [2, 256, 64])
    phi_v1 = inp("nsa_phi_v1", [2, 2048, 256])
    phi_v2 = inp("nsa_phi_v2", [2, 256, 64])
    w_up_a = inp("w_up_a", [2, 384, D])
    w_up_b = inp("w_up_b", [2, 256, D])
    w_up_c = inp("w_up_c", [2, 128, D])
    w_o = inp("w_o", [2, D, D])
    final_norm = inp("final_norm", [1, D])
    consts = {nm: inp(nm, shp) for nm, shp in CONST_SHAPES.items()}
    y = nc.dram_tensor("y", [S, D], F32, kind="ExternalOutput").ap()
    xres = scr("xres", [S, D], F32)
    qkT = scr("qkT", [NQK, S], BF16)
    vtok = scr("vtok", [S, NVC], BF16)
    gtok = scr("gtok", [S, 18], F32)
    mgT = scr("mgT", [3 * D, S], BF16)
    kcT = scr("kcT", [2, 64, 256], BF16)
    vcs = scr("vcs", [2, 256, 64], BF16)
    gd = {fam: scr("gd_" + fam, [16, L], F32) for fam, (L, _, _) in FAM.items()}
    ydbg = scr("ydbg", [S, 768], BF16) if debug else None

    P = Prog(nc)
    k = K(nc, P)
    with ExitStack() as es:
        identf = k.sb(es, "identf", [128, 128], F32)
        ident = k.sb(es, "ident", [128, 128], BF16)
        k.dma("sp", identf[:], consts["c_ident"], writes=["identf"])
        k.v("dve", "tensor_copy", reads=["identf"], writes=["ident"], out=ident[:], in_=identf[:])

        def dump_y(ytok_sb):
            if debug:
                yv = ydbg.rearrange("(c p) d -> p c d", p=128)
                for c0 in range(0, NT, 8):
                    k.dma("sp", yv[:, c0:c0 + 8, :], ytok_sb[:, c0:c0 + 8, :], reads=[("ytok", t) for t in range(NT)])
                P.barrier()

        ystack = []

        def dump_y_dummy():
            pass

        def run():
            stages = stop_after
            bias_gen_phase(k, rel_bias, consts, gd)
            for l in range(2):
                ffn_phase(k, x if l == 0 else xres, xres, ffn_norm[0][l:l + 1, :], ffn_wg[0][l], ffn_wu[0][l],
                          ffn_wd[0][l], ident)
                if stages == "ffn1":
                    return
                proj_phase(k, xres, mix_norm[l:l + 1, :], w_qk[l], w_v[l], w_mg[l], ident, qkT, vtok, gtok, mgT)
                if stages == "proj":
                    return
                compress_phase(k, qkT, pe_k[l], pe_v[l], phi_k1[l], phi_k2[l], phi_v1[l], phi_v2[l], kcT, vcs, identf)
                if stages == "cmp":
                    return
                ys = ExitStack()
                ytok_sb = k.sb(ys, "ytok", [128, NT, 768], BF16)
                ystack.append(ys)
                if stages in (None, "nsa", "attn", "merge", "l0"):
                    nsa_phase(k, qkT, vtok, gtok, kcT, vcs, gd, consts, identf, ytok_sb)
                if stages == "nsa":
                    dump_y(ytok_sb)
                    return
                if stages in (None, "moba", "attn", "merge", "l0"):
                    moba_phase(k, qkT, vtok, gd, consts, identf, ytok_sb)
                if stages == "moba":
                    dump_y(ytok_sb)
                    return
                dil_phase(k, qkT, vtok, gd, consts, ytok_sb)
                if stages in ("dil", "attn"):
                    dump_y(ytok_sb)
                    return
                merge_phase(k, xres, mgT, w_up_a[l], w_up_b[l], w_up_c[l], w_o[l], ident, ytok_sb)
                ystack.pop().close()
                if stages == "merge":
                    return
                ffn_phase(k, xres, xres, ffn_norm[1][l:l + 1, :], ffn_wg[1][l], ffn_wu[1][l], ffn_wd[1][l], ident)
                if stages == "l0":
                    return
            final_norm_phase(k, xres, y, final_norm)
        run()
        P.barrier()
        with ExitStack() as es2:
            P.emit(es2)
        while ystack:
            ystack.pop().close()
    return nc


def make_in_maps(inputs, cores):
    consts = host_consts()
    w_in = np.asarray(inputs["w_in"])
    shared = dict(consts)
    shared["w_qk"] = np.ascontiguousarray(w_in[:, :, W_QK_COLS])
    shared["w_v"] = np.ascontiguousarray(w_in[:, :, W_V_COLS])
    shared["w_mg"] = np.ascontiguousarray(w_in[:, :, W_MG_COLS])
    for nm in ("rel_bias", "ffn1_norm", "ffn2_norm", "ffn1_w_gate", "ffn2_w_gate", "ffn1_w_up", "ffn2_w_up",
               "ffn1_w_down", "ffn2_w_down", "mix_norm", "nsa_pe_k", "nsa_pe_v", "nsa_phi_k1", "nsa_phi_k2",
               "nsa_phi_v1", "nsa_phi_v2", "w_up_a", "w_up_b", "w_up_c", "w_o"):
        shared[nm] = np.ascontiguousarray(np.asarray(inputs[nm], dtype=np.float32))
    shared["final_norm"] = np.ascontiguousarray(np.asarray(inputs["final_norm"], dtype=np.float32)).reshape(1, D)
    maps = []
    for b in cores:
        m = dict(shared)
        m["x"] = np.ascontiguousarray(np.asarray(inputs["x"][b], dtype=np.float32))
        maps.append(m)
    return maps


def kernel(**inputs):
    nc = build()
    in_maps = make_in_maps(inputs, range(8))
    res = run_bass_kernel_spmd(nc, in_maps, core_ids=list(range(8)))
    return np.stack([np.asarray(r["y"]) for r in res.results], axis=0).astype(np.float32)
```

```python
import numpy as np
from contextlib import ExitStack
import concourse.bass as bass
import concourse.mybir as mybir
from concourse.bass_utils import run_bass_kernel_spmd

F32 = mybir.dt.float32
BF16 = mybir.dt.bfloat16
AF = mybir.ActivationFunctionType
ALU = mybir.AluOpType

S = 4096
D = 1024
DFF = 2816
NFC = DFF // 128
NT = S // 128
EPS = 1e-6
NEGM = -30000.0
NDS = 16


class _Op:
    __slots__ = ("eng", "fn", "deps", "dma", "signal", "sig", "dsem", "dval", "bar")


class Prog:
    ENG = ("pe", "act", "dve", "pool", "sp")

    def __init__(self, nc):
        self.nc = nc
        self.ops = []
        self.lastw = {}
        self.readers = {}
        self.last_on = {e: None for e in self.ENG}

    def op(self, eng, fn, reads=(), writes=(), dma=False):
        o = _Op()
        o.eng, o.fn, o.dma, o.signal, o.bar = eng, fn, dma, False, None
        deps = set()
        for r in reads:
            w = self.lastw.get(r)
            if w is not None:
                deps.add(w)
        for w_ in writes:
            w = self.lastw.get(w_)
            if w is not None:
                deps.add(w)
            for rd in self.readers.get(w_, ()):
                deps.add(rd)
        idx = len(self.ops)
        for w_ in writes:
            self.lastw[w_] = idx
            self.readers[w_] = []
        for r in reads:
            self.readers.setdefault(r, []).append(idx)
        deps.discard(idx)
        best = {}
        pruned = []
        for d in deps:
            p = self.ops[d]
            if p.dma:
                pruned.append(d)
            elif best.get(p.eng, -1) < d:
                best[p.eng] = d
        pruned.extend(best.values())
        o.deps = sorted(pruned)
        for d in o.deps:
            self.ops[d].signal = True
        self.ops.append(o)
        self.last_on[eng] = idx
        return idx

    def barrier(self):
        o = _Op()
        o.eng, o.fn, o.dma, o.signal, o.deps = None, None, False, False, []
        o.bar = dict(self.last_on)
        for e, i in o.bar.items():
            if i is not None and not self.ops[i].dma:
                self.ops[i].signal = True
        self.ops.append(o)

    def emit(self, es):
        nc = self.nc
        E = {"pe": nc.tensor, "act": nc.scalar, "dve": nc.vector, "pool": nc.gpsimd, "sp": nc.sync}
        sem = {e: es.enter_context(nc.semaphore("s_" + e)) for e in self.ENG}
        dsem = {q: [es.enter_context(nc.semaphore("d_%s%d" % (q, i))) for i in range(NDS)]
                for q in ("sp", "pool")}
        cnt = {e: 0 for e in self.ENG}
        dcnt = {"sp": 0, "pool": 0}
        seen = {e: {} for e in self.ENG}

        def wait(e, s, v):
            key = id(s)
            if seen[e].get(key, 0) < v:
                E[e].wait_ge(s, v)
                seen[e][key] = v

        for o in self.ops:
            if o.bar is not None:
                for q in ("sp", "pool"):
                    k = dcnt[q]
                    if k == 0:
                        continue
                    for i in range(min(k, NDS)):
                        uses = (k - 1 - i) // NDS + 1
                        wait(q, dsem[q][i], 16 * uses)
                    E[q].sem_inc(sem[q], 1)
                    cnt[q] += 1
                for f in self.ENG:
                    for e in self.ENG:
                        if e != f and cnt[e] > 0:
                            wait(f, sem[e], cnt[e])
                continue
            e = o.eng
            for d in o.deps:
                p = self.ops[d]
                if p.dma:
                    wait(e, p.dsem, p.dval)
                else:
                    if p.eng == e and e == "pe":
                        continue
                    wait(e, sem[p.eng], p.sig)
            if o.dma:
                k = dcnt[e]
                s = dsem[e][k % NDS]
                v = 16 * (k // NDS + 1)
                if k >= NDS:
                    wait(e, s, v - 16)
                ins = o.fn()
                ins.then_inc(s, 16)
                o.dsem, o.dval = s, v
                dcnt[e] = k + 1
            else:
                ins = o.fn()
                if o.signal:
                    cnt[e] += 1
                    o.sig = cnt[e]
                    ins.then_inc(sem[e], 1)
        self.ops = []


class K:
    def __init__(self, nc, P):
        self.nc, self.P = nc, P
        self.uid = 0

    def name(self, s):
        self.uid += 1
        return "%s_%d" % (s, self.uid)

    def sb(self, es, nm, shape, dt):
        return es.enter_context(self.nc.sbuf_tensor(self.name(nm), list(shape), dt))

    def ps(self, es, nm, shape, dt):
        return es.enter_context(self.nc.psum_tensor(self.name(nm), list(shape), dt))

    def dma(self, q, out, in_, reads=(), writes=(), slow=False):
        nc = self.nc
        eng = nc.sync if q == "sp" else nc.gpsimd
        if slow:
            return self.P.op(q, lambda: eng.dma_start(out=out, in_=in_, allow_slow_non_contiguous=True),
                             reads, writes, dma=True)
        return self.P.op(q, lambda: eng.dma_start(out=out, in_=in_), reads, writes, dma=True)

    def mm(self, out, lhsT, rhs, start, stop, reads=(), writes=()):
        nc = self.nc
        return self.P.op("pe", lambda: nc.tensor.matmul(out, lhsT=lhsT, rhs=rhs, start=start, stop=stop),
                         reads, writes)

    def tr(self, out, in_, ident, reads=(), writes=()):
        nc = self.nc
        return self.P.op("pe", lambda: nc.tensor.transpose(out, in_, ident), reads, writes)

    def act(self, out, in_, func, reads=(), writes=(), **kw):
        nc = self.nc
        return self.P.op("act", lambda: nc.scalar.activation(out=out, in_=in_, func=func, **kw), reads, writes)

    def v(self, eng, name, reads=(), writes=(), **kw):
        nc = self.nc
        e = nc.vector if eng == "dve" else nc.gpsimd
        return self.P.op(eng, lambda: getattr(e, name)(**kw), reads, writes)


def dap(t, offset, pattern):
    return bass.AP(tensor=t, offset=offset, ap=[list(p) for p in pattern])


def load_cast_weight(k, stg, stg_tok, dst_fn, src_fn, nchunks, width, q_alt=True):
    for c in range(nchunks):
        b = c % 2
        k.dma("sp" if (c % 2 == 0 or not q_alt) else "pool", stg[b][:, 0:width], src_fn(c),
              writes=[stg_tok[b]])
        k.v("pool", "tensor_copy", reads=[stg_tok[b]], writes=[("w", id(dst_fn), c)],
            out=dst_fn(c), in_=stg[b][:, 0:width])


def cast(k, sel, out, in_, reads, writes):
    if sel % 2 == 0:
        k.act(out, in_, AF.Copy, reads=reads, writes=writes)
    else:
        k.v("dve", "tensor_copy", reads=reads, writes=writes, out=out, in_=in_)


def ffn_phase(k, xsrc, xdst, w_norm, w_gate, w_up, w_down, ident):
    nc, P = k.nc, k.P
    G = 512
    with ExitStack() as es:
        wg = k.sb(es, "wg", [128, 8, DFF], BF16)
        wu = k.sb(es, "wu", [128, 8, DFF], BF16)
        wd = k.sb(es, "wd", [128, NFC, D], BF16)
        HW = DFF // 2
        stg = [k.sb(es, "stg", [128, HW], F32) for _ in range(2)]
        gain = k.sb(es, "gain", [128, D], F32)
        xin = [k.sb(es, "xin", [128, D], F32) for _ in range(2)]
        junk = k.sb(es, "junk", [128, D], BF16)
        hn = [k.sb(es, "hn", [128, D], BF16) for _ in range(2)]
        hT = k.sb(es, "hT", [128, 8, G], BF16)
        aT = k.sb(es, "aT", [128, NFC, G], BF16)
        sg = [k.sb(es, "sg", [128, G], F32) for _ in range(2)]
        xr = [k.sb(es, "xr", [128, 512], F32) for _ in range(2)]
        ob = [k.sb(es, "ob", [128, 512], F32) for _ in range(2)]
        st = [k.sb(es, "st", [128, 4], F32) for _ in range(2)]
        pgu = [k.ps(es, "pgu", [128, 512], F32) for _ in range(4)]
        ptp = [k.ps(es, "ptp", [128, 1024], BF16) for _ in range(2)]
        pdn = [k.ps(es, "pdn", [128, 512], F32) for _ in range(2)]

        k.dma("sp", gain[:], dap(w_norm.tensor, w_norm.offset, [[0, 128], [1, D]]), writes=["gain"])
        wgv = w_gate.rearrange("(kc p) f -> p kc f", p=128)
        wuv = w_up.rearrange("(kc p) f -> p kc f", p=128)
        wdv = w_down.rearrange("(fc p) d -> p fc d", p=128)
        ci = 0
        for (dst, srcv, n, width) in ((wg, wgv, 8, DFF), (wu, wuv, 8, DFF)):
            for c in range(n):
                for hh in range(2):
                    b = ci % 2
                    ci += 1
                    k.dma("sp" if b == 0 else "pool", stg[b][:, 0:HW], srcv[:, c, hh * HW:(hh + 1) * HW],
                          writes=[("stg", b)])
                    cast(k, b, dst[:, c, hh * HW:(hh + 1) * HW], stg[b][:, 0:HW], [("stg", b)], [("w", id(dst))])
        for c in range(NFC):
            b = ci % 2
            ci += 1
            k.dma("sp" if b == 0 else "pool", stg[b][:, 0:D], wdv[:, c, :], writes=[("stg", b)])
            cast(k, b, wd[:, c, :], stg[b][:, 0:D], [("stg", b)], [("w", id(wd))])

        gi = 0
        di = 0
        for g in range(S // G):
            for t in range(G // 128):
                r0 = g * G + t * 128
                b = t % 2
                k.dma("sp", xin[b][:], xsrc[r0:r0 + 128, :], writes=[("xin", b)])
                k.v("dve", "scalar_tensor_tensor", reads=[("xin", b)], writes=["junk", ("st", b)],
                    out=junk[:], in0=xin[b][:], scalar=1.0, in1=xin[b][:], op0=ALU.mult, op1=ALU.mult,
                    accum_out=st[b][:, 0:1])
                k.act(st[b][:, 1:2], st[b][:, 0:1], AF.Sqrt, reads=[("st", b)], writes=[("st1", b)],
                      scale=1.0 / D, bias=EPS)
                k.v("dve", "reciprocal", reads=[("st1", b)], writes=[("st2", b)],
                    out=st[b][:, 2:3], in_=st[b][:, 1:2])
                k.v("dve", "scalar_tensor_tensor", reads=[("xin", b), ("st2", b), "gain"], writes=[("hn", b)],
                    out=hn[b][:], in0=xin[b][:], scalar=st[b][:, 2:3], in1=gain[:], op0=ALU.mult, op1=ALU.mult)
                for kc in range(8):
                    k.tr(ptp[b][:, kc * 128:(kc + 1) * 128], hn[b][:, kc * 128:(kc + 1) * 128], ident[:],
                         reads=[("hn", b), "ident"], writes=[("ptp", b)])
                k.act(hT[:, :, t * 128:(t + 1) * 128], ptp[b][:].rearrange("p (a c) -> p a c", a=8), AF.Copy,
                      reads=[("ptp", b)], writes=[("hT", t)])
            hT_all = [("hT", t) for t in range(G // 128)]
            for fc in range(NFC):
                pb = (gi % 2) * 2
                gi += 1
                for kc in range(8):
                    k.mm(pgu[pb][:], wg[:, kc, fc * 128:(fc + 1) * 128], hT[:, kc, :], kc == 0, kc == 7,
                         reads=hT_all + [("w", id(wg))], writes=[("pgu", pb)])
                for kc in range(8):
                    k.mm(pgu[pb + 1][:], wu[:, kc, fc * 128:(fc + 1) * 128], hT[:, kc, :], kc == 0, kc == 7,
                         reads=hT_all + [("w", id(wu))], writes=[("pgu", pb + 1)])
                sb_ = fc % 2
                k.act(sg[sb_][:], pgu[pb][:], AF.Silu, reads=[("pgu", pb)], writes=[("sg", sb_)])
                k.v("dve", "tensor_tensor", reads=[("sg", sb_), ("pgu", pb + 1)], writes=[("aT", fc)],
                    out=aT[:, fc, :], in0=sg[sb_][:], in1=pgu[pb + 1][:], op=ALU.mult)
            aT_all = [("aT", fc) for fc in range(NFC)]
            for t in range(G // 128):
                r0 = g * G + t * 128
                for hh in range(2):
                    b = di % 2
                    di += 1
                    k.dma("pool", xr[b][:], xsrc[r0:r0 + 128, hh * 512:(hh + 1) * 512], writes=[("xr", b)])
                    for fc in range(NFC):
                        k.mm(pdn[b][:], aT[:, fc, t * 128:(t + 1) * 128], wd[:, fc, hh * 512:(hh + 1) * 512],
                             fc == 0, fc == NFC - 1, reads=aT_all + [("w", id(wd))], writes=[("pdn", b)])
                    k.v("dve", "scalar_tensor_tensor", reads=[("pdn", b), ("xr", b)], writes=[("ob", b)],
                        out=ob[b][:], in0=pdn[b][:], scalar=0.5, in1=xr[b][:], op0=ALU.mult, op1=ALU.add)
                    k.dma("sp", xdst[r0:r0 + 128, hh * 512:(hh + 1) * 512], ob[b][:], reads=[("ob", b)])
    P.barrier()


def final_norm_phase(k, xsrc, ydst, w_norm):
    P = k.P
    with ExitStack() as es:
        gain = k.sb(es, "fgain", [128, D], F32)
        xin = [k.sb(es, "fxin", [128, D], F32) for _ in range(2)]
        yo = [k.sb(es, "fyo", [128, D], F32) for _ in range(2)]
        junk = k.sb(es, "fjunk", [128, D], BF16)
        st = [k.sb(es, "fst", [128, 4], F32) for _ in range(2)]
        k.dma("sp", gain[:], dap(w_norm.tensor, w_norm.offset, [[0, 128], [1, D]]), writes=["fgain"])
        for t in range(NT):
            b = t % 2
            r0 = t * 128
            k.dma("sp", xin[b][:], xsrc[r0:r0 + 128, :], writes=[("fxin", b)])
            k.v("dve", "scalar_tensor_tensor", reads=[("fxin", b)], writes=["fjunk", ("fst", b)],
                out=junk[:], in0=xin[b][:], scalar=1.0, in1=xin[b][:], op0=ALU.mult, op1=ALU.mult,
                accum_out=st[b][:, 0:1])
            k.act(st[b][:, 1:2], st[b][:, 0:1], AF.Sqrt, reads=[("fst", b)], writes=[("fst1", b)],
                  scale=1.0 / D, bias=EPS)
            k.v("dve", "reciprocal", reads=[("fst1", b)], writes=[("fst2", b)],
                out=st[b][:, 2:3], in_=st[b][:, 1:2])
            k.v("dve", "scalar_tensor_tensor", reads=[("fxin", b), ("fst2", b), "fgain"], writes=[("fyo", b)],
                out=yo[b][:], in0=xin[b][:], scalar=st[b][:, 2:3], in1=gain[:], op0=ALU.mult, op1=ALU.mult)
            k.dma("pool", ydst[r0:r0 + 128, :], yo[b][:], reads=[("fyo", b)])
    P.barrier()


NQK = 2176
NVC = 896
NVG = NVC + 18
R_NSAQ, R_KCMP, R_VCMP, R_KSEL, R_KWIN, R_MQ, R_MK, R_DQ, R_DK = 0, 384, 512, 640, 768, 896, 1152, 1408, 1792
C_VSEL, C_VWIN, C_MV, C_DV = 0, 128, 256, 512
FAM = {
    "full": (2560, 2432, 1),
    "win": (1536, 1408, 1),
    "cmp": (6144, 4096, 16),
    "dil0": (1152, 1024, 1),
    "dil1": (1536, 1408, 1),
    "dil2": (3072, 2944, 1),
}
DIL = ((128, 1), (512, 4), (2048, 16))
SCALE = 0.125
NEGP = -240000.0


def np_rel_bucket(dist):
    n = np.maximum(dist, 0)
    nf = np.maximum(n, 16).astype(np.float32)
    lg = (np.log(nf / np.float32(16)) / np.float32(np.log(2048 / 16)) * np.float32(16)).astype(np.float32)
    large = 16 + lg.astype(np.int32)
    large = np.minimum(large, 31)
    return np.where(n < 16, n, large)


def host_consts():
    c = {}
    c["c_ident"] = np.eye(128, dtype=np.float32)
    c["c_flip"] = np.ascontiguousarray(np.eye(128, dtype=np.float32)[::-1])
    def fam_oh(length, off, valid_fn):
        w = np.arange(length)
        d = w - off
        ok = valid_fn(d)
        b = np_rel_bucket(d)
        oh = np.zeros((33, length), np.float32)
        oh[b[ok], w[ok]] = 1.0
        oh[32, ~ok] = 1.0
        return oh
    c["oh_full"] = fam_oh(2560, 511, lambda d: d >= 0)
    c["oh_win"] = fam_oh(1536, 511, lambda d: (d >= 0) & (d < 512))
    c["oh_cmp"] = fam_oh(6144, 2063, lambda d: d >= 0)
    for g, (W, dl) in enumerate(DIL):
        c["oh_dil%d" % g] = fam_oh(FAM["dil%d" % g][0], 511, lambda d: (d >= 0) & (d <= W) & (d % dl == 0))
    c["c_negrow"] = np.full((1, 16), NEGM, np.float32)
    n_cmp = 255
    c_start = np.arange(n_cmp) * 16
    s_start = np.arange(64) * 64
    ov = (c_start[:, None] < s_start[None, :] + 64) & (c_start[:, None] + 32 > s_start[None, :])
    cts = np.zeros((256, 65), np.float32)
    cts[:255, :64] = ov
    cts[:255, 64] = 1.0
    c["c_cts"] = cts
    t = np.arange(S)
    blk = np.arange(64)
    cur = t // 64
    keep = np.ones((S, 64), np.float32)
    add = np.zeros((S, 64), np.float32)
    f0 = np.broadcast_to(blk[None, :] == 0, (S, 64))
    f1 = blk[None, :] == cur[:, None]
    f2 = blk[None, :] == cur[:, None] - 1
    fut = blk[None, :] * 64 > t[:, None]
    for f, val in ((f0, 1e4), (f2, 3e4), (f1, 2e4)):
        keep[f] = 0.0
        add[f] = val
    keep[fut] = 0.0
    add[fut] = -1e30
    c["c_keep"] = keep
    c["c_add"] = add
    c["c_esel"] = (np.arange(S)[None, :] // 64 == np.arange(64)[:, None]).astype(np.float32)
    nb = np.arange(16)
    cb = t // 256
    valid = (nb[None, :] < cb[:, None]).astype(np.float32)
    own = (nb[None, :] == cb[:, None]).astype(np.float32)
    c["c_mvalid"] = valid
    c["c_maddm"] = np.where(valid > 0, 0.0, -1e30).astype(np.float32)
    c["c_mown"] = ((own - 1.0) * (-NEGP)).astype(np.float32)
    eb = np.zeros((16, 16, 128), np.float32)
    for n in range(16):
        eb[n, n, :] = 1.0
    c["c_eb"] = eb.reshape(16, 16 * 128)
    c["c_ebig"] = (np.arange(S)[None, :] // 256 == np.arange(16)[:, None]).astype(np.float32)
    return c


CONST_SHAPES = {
    "c_ident": [128, 128], "c_flip": [128, 128], "oh_full": [33, 2560], "oh_win": [33, 1536],
    "oh_cmp": [33, 6144], "oh_dil0": [33, 1152], "oh_dil1": [33, 1536], "oh_dil2": [33, 3072],
    "c_negrow": [1, 16], "c_cts": [256, 65], "c_keep": [S, 64], "c_add": [S, 64], "c_esel": [64, S],
    "c_mvalid": [S, 16], "c_maddm": [S, 16], "c_mown": [S, 16], "c_eb": [16, 2048], "c_ebig": [16, S],
}


def bias_gen_phase(k, rel_bias, consts, gd):
    P = k.P
    with ExitStack() as es:
        tblx = k.sb(es, "tblx", [33, 16], F32)
        oh = k.sb(es, "oh", [33, 6144], F32)
        go = k.sb(es, "go", [16, 6144], F32)
        pb = [k.ps(es, "pbg", [128, 512], F32) for _ in range(2)]
        k.dma("sp", tblx[0:32, :], rel_bias, writes=["tblx"])
        k.dma("sp", tblx[32:33, :], consts["c_negrow"], writes=["tblx"])
        i = 0
        for fam, (L, _, _) in FAM.items():
            k.dma("sp", oh[:, 0:L], consts["oh_" + fam], writes=["oh"])
            for c0 in range(0, L, 512):
                b = i % 2
                i += 1
                k.mm(pb[b][0:16, :], tblx[:, :], oh[:, c0:c0 + 512], True, True,
                     reads=["tblx", "oh"], writes=[("pbg", b)])
                k.v("dve", "tensor_copy", reads=[("pbg", b)], writes=["go"], out=go[:, c0:c0 + 512],
                    in_=pb[b][0:16, :])
            k.dma("sp", gd[fam], go[:, 0:L], reads=["go"])
    P.barrier()


def make_tb(k, fam, gd, h, flipf, rev, tb, pflip, tok):
    L, W, st = FAM[fam]
    g = gd[fam]
    k.dma("sp", rev[:, 0:W], dap(g.tensor, g.offset + h * L, [[st, 128], [1, W]]), writes=["rev"])
    i = 0
    for c0 in range(0, W, 512):
        n = min(512, W - c0)
        b = i % len(pflip)
        i += 1
        k.mm(pflip[b][:, 0:n], flipf[:, :], rev[:, c0:c0 + n], True, True, reads=["flipf", "rev"],
             writes=[("pflip", b)])
        k.v("dve", "tensor_copy", reads=[("pflip", b)], writes=[tok], out=tb[:, c0:c0 + n], in_=pflip[b][:, 0:n])


def proj_phase(k, xsrc, w_norm, w_qk, w_v, w_mg, ident, qkT, vtok, gtok, mgT):
    P = k.P
    with ExitStack() as es:
        hT = k.sb(es, "phT", [128, 8, S], BF16)
        gain = k.sb(es, "pgain", [128, D], F32)
        xin = [k.sb(es, "pxin", [128, D], F32) for _ in range(2)]
        junk = k.sb(es, "pjunk", [128, D], BF16)
        hn = [k.sb(es, "phn", [128, D], BF16) for _ in range(2)]
        st = [k.sb(es, "pst", [128, 4], F32) for _ in range(2)]
        wvs = k.sb(es, "wvs", [128, NVG], F32)
        wvb = k.sb(es, "wvb", [128, 8, NVG], BF16)
        wst = [k.sb(es, "wst", [128, 8, 128], F32) for _ in range(2)]
        wb = [k.sb(es, "wb", [128, 8, 128], BF16) for _ in range(2)]
        orow = [k.sb(es, "orow", [128, S], BF16) for _ in range(2)]
        vout = [k.sb(es, "vout", [128, NVC], BF16) for _ in range(2)]
        gout = [k.sb(es, "gout", [128, 18], F32) for _ in range(2)]
        ptp = [k.ps(es, "pptp", [128, 1024], BF16) for _ in range(2)]
        pmm = [k.ps(es, "ppmm", [128, 512], F32) for _ in range(4)]
        k.dma("sp", gain[:], dap(w_norm.tensor, w_norm.offset, [[0, 128], [1, D]]), writes=["pgain"])
        wvv = w_v.rearrange("(kc p) f -> p kc f", p=128)
        for kc in range(8):
            k.dma("pool", wvs[:], wvv[:, kc, :], writes=["wvs"])
            k.v("dve", "tensor_copy", reads=["wvs"], writes=["wvb"], out=wvb[:, kc, :], in_=wvs[:])
        for t in range(NT):
            b = t % 2
            r0 = t * 128
            k.dma("sp", xin[b][:], xsrc[r0:r0 + 128, :], writes=[("pxin", b)])
            k.v("dve", "scalar_tensor_tensor", reads=[("pxin", b)], writes=["pjunk", ("pst", b)],
                out=junk[:], in0=xin[b][:], scalar=1.0, in1=xin[b][:], op0=ALU.mult, op1=ALU.mult,
                accum_out=st[b][:, 0:1])
            k.act(st[b][:, 1:2], st[b][:, 0:1], AF.Sqrt, reads=[("pst", b)], writes=[("pst1", b)],
                  scale=1.0 / D, bias=EPS)
            k.v("dve", "reciprocal", reads=[("pst1", b)], writes=[("pst2", b)],
                out=st[b][:, 2:3], in_=st[b][:, 1:2])
            k.v("dve", "scalar_tensor_tensor", reads=[("pxin", b), ("pst2", b), "pgain"], writes=[("phn", b)],
                out=hn[b][:], in0=xin[b][:], scalar=st[b][:, 2:3], in1=gain[:], op0=ALU.mult, op1=ALU.mult)
            for kc in range(8):
                k.tr(ptp[b][:, kc * 128:(kc + 1) * 128], hn[b][:, kc * 128:(kc + 1) * 128], ident[:],
                     reads=[("phn", b), "ident"], writes=[("pptp", b)])
            k.act(hT[:, :, r0:r0 + 128], ptp[b][:].rearrange("p (a c) -> p a c", a=8), AF.Copy,
                  reads=[("pptp", b)], writes=[("phT", t)])
            pa, pb_ = pmm[(t % 2) * 2], pmm[(t % 2) * 2 + 1]
            ta, tb_ = ("ppmm", (t % 2) * 2), ("ppmm", (t % 2) * 2 + 1)
            for kc in range(8):
                k.mm(pa[:, 0:512], hT[:, kc, r0:r0 + 128], wvb[:, kc, 0:512], kc == 0, kc == 7,
                     reads=[("phT", t), "wvb"], writes=[ta])
            for kc in range(8):
                k.mm(pb_[:, 0:NVG - 512], hT[:, kc, r0:r0 + 128], wvb[:, kc, 512:NVG], kc == 0, kc == 7,
                     reads=[("phT", t), "wvb"], writes=[tb_])
            k.act(vout[b][:, 0:512], pa[:, 0:512], AF.Copy, reads=[ta], writes=[("vout", b)])
            k.v("dve", "tensor_copy", reads=[tb_], writes=[("vout", b)], out=vout[b][:, 512:NVC],
                in_=pb_[:, 0:NVC - 512])
            k.act(gout[b][:], pb_[:, NVC - 512:NVG - 512], AF.Sigmoid, reads=[tb_], writes=[("gout", b)])
            k.dma("pool", vtok[r0:r0 + 128, :], vout[b][:], reads=[("vout", b)])
            k.dma("pool", gtok[r0:r0 + 128, :], gout[b][:], reads=[("gout", b)])
        allh = [("phT", t) for t in range(NT)]
        blocks = [("qk", i) for i in range(NQK // 128)] + [("mg", i) for i in range(3 * D // 128)]
        mi = 0
        for bi, (kind, i) in enumerate(blocks):
            b = bi % 2
            wsrc = (w_qk if kind == "qk" else w_mg)[:, i * 128:(i + 1) * 128].rearrange("(kc p) f -> p kc f", p=128)
            k.dma("pool", wst[b][:], wsrc, writes=[("wst", b)])
            k.v("dve", "tensor_copy", reads=[("wst", b)], writes=[("wb", b)], out=wb[b][:], in_=wst[b][:])
            for tb8 in range(8):
                pi = mi % 4
                mi += 1
                for kc in range(8):
                    k.mm(pmm[pi][:], wb[b][:, kc, :], hT[:, kc, tb8 * 512:(tb8 + 1) * 512], kc == 0, kc == 7,
                         reads=allh + [("wb", b)], writes=[("ppmm", pi)])
                if kind == "mg":
                    k.act(orow[b][:, tb8 * 512:(tb8 + 1) * 512], pmm[pi][:], AF.Sigmoid,
                          reads=[("ppmm", pi)], writes=[("orow", b)])
                elif tb8 % 2 == 0:
                    k.act(orow[b][:, tb8 * 512:(tb8 + 1) * 512], pmm[pi][:], AF.Copy,
                          reads=[("ppmm", pi)], writes=[("orow", b)])
                else:
                    k.v("dve", "tensor_copy", reads=[("ppmm", pi)], writes=[("orow", b)],
                        out=orow[b][:, tb8 * 512:(tb8 + 1) * 512], in_=pmm[pi][:])
            dst = (qkT if kind == "qk" else mgT)[i * 128:(i + 1) * 128, :]
            k.dma("sp", dst, orow[b][:], reads=[("orow", b)])
    P.barrier()


def compress_phase(k, qkT, pe_k, pe_v, phi_k1, phi_k2, phi_v1, phi_v2, kcT, vcs, identf):
    P = k.P
    C1 = 1.5957691216057308
    with ExitStack() as es:
        src = k.sb(es, "csrc", [128, S], BF16)
        w1s = k.sb(es, "w1s", [128, 8, 256], F32)
        w1b = k.sb(es, "w1b", [128, 32, 256], BF16)
        w2s = k.sb(es, "w2s", [128, 2, 64], F32)
        w2b = k.sb(es, "w2b", [128, 2, 64], BF16)
        pes = k.sb(es, "pes", [128, 64], F32)
        peb = k.sb(es, "peb", [128, 32], BF16)
        bias = k.sb(es, "cbias", [128, 2], F32)
        xa = k.sb(es, "cxa", [128, 256], F32)
        xb_ = k.sb(es, "cxb", [128, 256], F32)
        xc = k.sb(es, "cxc", [128, 256], F32)
        hid = [k.sb(es, "chid", [128, 256], BF16) for _ in range(2)]
        ko = k.sb(es, "cko", [64, 256], BF16)
        vo = k.sb(es, "cvo", [128, 64], BF16)
        ph = [k.ps(es, "cph", [128, 512], F32) for _ in range(2)]
        pbias = k.ps(es, "cpb", [128, 512], F32)
        po = k.ps(es, "cpo", [128, 512], F32)
        for which, (r0, pe, w1, w2) in enumerate(((R_KCMP, pe_k, phi_k1, phi_k2), (R_VCMP, pe_v, phi_v1, phi_v2))):
            k.dma("sp", src[:], qkT[r0:r0 + 128, :], writes=["csrc"])
            w1v = w1.rearrange("(l d) h -> d l h", d=64)
            for half in range(2):
                for l0 in range(0, 32, 8):
                    k.dma("pool", w1s[half * 64:(half + 1) * 64, :, :], w1v[:, l0:l0 + 8, :], writes=["w1s"])
                    k.v("dve", "tensor_copy", reads=["w1s"], writes=["w1b"],
                        out=w1b[half * 64:(half + 1) * 64, l0:l0 + 8, :], in_=w1s[half * 64:(half + 1) * 64, :, :])
            k.dma("sp", pes[0:32, 0:64], pe, writes=["pes"])
            k.tr(pbias[0:64, 64:96], pes[0:32, 0:64], identf[0:32, 0:32], reads=["pes", "identf"], writes=["cpb"])
            k.v("dve", "tensor_copy", reads=["cpb"], writes=["peb"], out=peb[0:64, :], in_=pbias[0:64, 64:96])
            k.dma("sp", w2s[:], w2.rearrange("(hc p) d -> p hc d", p=128), writes=["w2s"])
            k.v("dve", "tensor_copy", reads=["w2s"], writes=["w2b"], out=w2b[:], in_=w2s[:])
            for hc in range(2):
                for l in range(32):
                    k.mm(pbias[:, hc:hc + 1], w1b[0:64, l, hc * 128:(hc + 1) * 128], peb[0:64, l:l + 1],
                         l == 0, l == 31, reads=["w1b", "peb"], writes=["cpb"])
            k.v("dve", "tensor_copy", reads=["cpb"], writes=["cbias"], out=bias[:], in_=pbias[:, 0:2])
            for g in range(2):
                p0 = g * 64
                for hc in range(2):
                    for l in range(32):
                        k.mm(ph[hc][:, 0:255], w1b[p0:p0 + 64, l, hc * 128:(hc + 1) * 128],
                             src[p0:p0 + 64, l:l + 16 * 254 + 1:16], l == 0, l == 31,
                             reads=["w1b", "csrc"], writes=[("cph", hc)])
                    k.v("dve", "tensor_scalar", reads=[("cph", hc), "cbias"], writes=["cxa"], out=xa[:, 0:255],
                        in0=ph[hc][:, 0:255], scalar1=bias[:, hc:hc + 1], scalar2=None, op0=ALU.add)
                    k.v("dve", "tensor_tensor", reads=["cxa"], writes=["cxb"], out=xb_[:, 0:255], in0=xa[:, 0:255],
                        in1=xa[:, 0:255], op=ALU.mult)
                    k.v("dve", "tensor_scalar", reads=["cxb"], writes=["cxc"], out=xc[:, 0:255], in0=xb_[:, 0:255],
                        scalar1=0.044715, scalar2=1.0, op0=ALU.mult, op1=ALU.add)
                    k.v("dve", "tensor_tensor", reads=["cxc", "cxa"], writes=["cxb"], out=xb_[:, 0:255],
                        in0=xc[:, 0:255], in1=xa[:, 0:255], op=ALU.mult)
                    k.act(xc[:, 0:255], xb_[:, 0:255], AF.Sigmoid, reads=["cxb"], writes=["cxc"], scale=C1)
                    k.v("dve", "tensor_tensor", reads=["cxc", "cxa"], writes=[("chid", hc)], out=hid[hc][:, 0:255],
                        in0=xc[:, 0:255], in1=xa[:, 0:255], op=ALU.mult)
                hh = [("chid", 0), ("chid", 1)]
                if which == 0:
                    for hc in range(2):
                        k.mm(po[0:64, 0:255], w2b[:, hc, :], hid[hc][:, 0:255], hc == 0, hc == 1,
                             reads=hh + ["w2b"], writes=["cpo"])
                    k.v("dve", "tensor_copy", reads=["cpo"], writes=["cko"], out=ko[:, 0:255], in_=po[0:64, 0:255])
                    k.dma("sp", kcT[g, :, 0:255], ko[:, 0:255], reads=["cko"])
                else:
                    for ch in range(2):
                        rows = 128 if ch == 0 else 127
                        for hc in range(2):
                            k.mm(po[0:rows, 0:64], hid[hc][:, ch * 128:ch * 128 + rows], w2b[:, hc, :],
                                 hc == 0, hc == 1, reads=hh + ["w2b"], writes=["cpo"])
                        k.v("dve", "tensor_copy", reads=["cpo"], writes=["cvo"], out=vo[0:rows, :], in_=po[0:rows, 0:64])
                        k.dma("sp", vcs[g, ch * 128:ch * 128 + rows, :], vo[0:rows, :], reads=["cvo"])
    P.barrier()


FILL_CNT = 0
FILL_N = 512


class AttnCx:
    def __init__(self, k, es, nU):
        if FILL_CNT > 0:
            self.fill = k.ps(es, "afill", [128, 512], F32)
            self.fsrc = k.sb(es, "afsrc", [128, 512], BF16)
            k.v("dve", "memset", writes=["afsrc"], ap=self.fsrc[:], constant=0.0)
        self.S = [k.ps(es, "aS", [128, 512], F32) for _ in range(3)]
        self.U = [k.ps(es, "aU", [128, 4, 128], F32) for _ in range(nU)]
        self.L = [k.sb(es, "aL", [128, 512], F32) for _ in range(3)]
        self.E = [k.sb(es, "aE", [128, 512], BF16) for _ in range(5)]
        self.fifo = []
        self.npv = 0
        self.si = self.li = self.ei = 0
        self.zl = k.sb(es, "azl", [1, 128], BF16)
        self.zr = k.sb(es, "azr", [1, 512], BF16)
        k.v("dve", "memset", writes=["azl"], ap=self.zl[:], constant=0.0)
        k.v("dve", "memset", writes=["azr"], ap=self.zr[:], constant=0.0)


def emit_scores(k, cx, rows, kT, q, n, tb, cbias, mask, rd):
    si = cx.si % 3
    cx.si += 1
    Sb = cx.S[si]
    k.mm(Sb[0:rows, 0:n], kT, q, True, mask is None, reads=rd, writes=[("aS", si)])
    if mask is not None:
        k.mm(Sb[0:rows, 0:n], mask[0], mask[1], False, True, reads=rd, writes=[("aS", si)])
    for _ in range(FILL_CNT):
        k.mm(cx.fill[:, 0:FILL_N], cx.fsrc[:, 0:128], cx.fsrc[:, 0:FILL_N], True, True, reads=["afsrc"])
    ei = cx.ei % 5
    cx.ei += 1
    Eb = cx.E[ei]
    if tb is not None:
        li = cx.li % 3
        cx.li += 1
        k.v("dve", "scalar_tensor_tensor", reads=[("aS", si)] + rd, writes=[("aL", li)],
            out=cx.L[li][0:rows, 0:n], in0=Sb[0:rows, 0:n], scalar=SCALE, in1=tb, op0=ALU.mult, op1=ALU.add)
        k.act(Eb[0:rows, 0:n], cx.L[li][0:rows, 0:n], AF.Exp, reads=[("aL", li)], writes=[("aE", ei)])
    else:
        k.act(Eb[0:rows, 0:n], Sb[0:rows, 0:n], AF.Exp, reads=[("aS", si)] + rd, writes=[("aE", ei)],
              scale=SCALE, bias=cbias)
    return ei


LOOKAHEAD = 3


def pipe_drain(k, cx, keep):
    while cx.npv > keep or (keep == 0 and cx.fifo):
        act = cx.fifo.pop(0)
        if act[0] == "pv":
            cx.npv -= 1
        act[1]()


def run_branch(k, cx, tiles, ui, utok):
    last = {}
    for i, t in enumerate(tiles):
        for qt in range(t["qt_lo"], t["qt_hi"]):
            last[qt] = i
    U = cx.U[ui]

    def zero():
        k.mm(U[:].rearrange("p a b -> p (a b)"), cx.zl[0:1, :], cx.zr[0:1, :], True, False,
             reads=["azl", "azr"], writes=[utok])
    cx.fifo.append(("zero", zero))
    for i, t in enumerate(tiles):
        lo, hi = t["qt_lo"], t["qt_hi"]
        n = (hi - lo) * 128
        rows = t["rows"]
        ei = emit_scores(k, cx, rows, t["kT"], t["qfn"](lo * 128, n), n,
                         t["tbfn"](lo * 128, n) if t["tbfn"] is not None else None, t["cbias"],
                         t["maskfn"](lo * 128, n) if t["maskfn"] is not None else None, t["rd"])

        def pv(i=i, t=t, lo=lo, hi=hi, rows=rows, ei=ei):
            Eb = cx.E[ei]
            for qt in range(lo, hi):
                c0 = (qt - lo) * 128
                k.mm(U[:, qt, 0:65], Eb[0:rows, c0:c0 + 128], t["V"], False, last[qt] == i,
                     reads=[("aE", ei)] + t["rd"], writes=[utok])
        cx.fifo.append(("pv", pv))
        cx.npv += 1
        pipe_drain(k, cx, LOOKAHEAD)


def pipe_post(k, cx, fn):
    cx.fifo.append(("post", fn))


def pipe_flush(k, cx):
    pipe_drain(k, cx, 0)


def dma_split(k, q, dst, src, tok, step=8):
    for c0 in range(0, NT, step):
        k.dma(q, dst[:, c0:c0 + step, :], src[:, c0:c0 + step, :], writes=[tok])


def load_v_aug(k, q, vt, vtok, col0, tok):
    k.v("dve", "memset", writes=[tok], ap=vt[:, :, 64:65], constant=1.0)
    src = vtok[:, col0:col0 + 64].rearrange("(c p) d -> p c d", p=128)
    for c0 in range(0, NT, 8):
        k.dma(q, vt[:, c0:c0 + 8, 0:64], src[:, c0:c0 + 8, :], writes=[tok])


def cmp_tiles(qb, kc_sb, q_sb, tbc, V_sb, rd):
    qs = qb * 512
    tiles = []
    for ch in range(2):
        Dd = qs - 2048 * ch
        if Dd < 0:
            continue
        rows = 128 if ch == 0 else 127
        tiles.append(dict(
            rows=rows, kT=kc_sb[:, ch * 128:ch * 128 + rows],
            qfn=lambda c0, n, qs=qs: q_sb[:, qs + c0:qs + c0 + n],
            tbfn=lambda c0, n, Dd=Dd, rows=rows: tbc[0:rows, Dd + c0:Dd + c0 + n],
            cbias=None, maskfn=None, V=V_sb[0:rows, ch, :], qt_lo=0, qt_hi=4, rd=rd))
    return tiles


def nsa_phase(k, qkT, vtok, gtok, kcT, vcs, gd, consts, identf, ytok_sb):
    P = k.P
    with ExitStack() as es:
        flipf = k.sb(es, "flipf", [128, 128], F32)
        rev = k.sb(es, "rev", [128, 4096], F32)
        tbc = k.sb(es, "tbc", [128, 4096], F32)
        tbf = k.sb(es, "tbf", [128, 2432], F32)
        tbw = k.sb(es, "tbw", [128, 1408], F32)
        q_sb = k.sb(es, "nq", [128, S], BF16)
        kc_sb = k.sb(es, "nkc", [128, 256], BF16)
        ctsf = k.sb(es, "ctsf", [128, 2, 65], F32)
        cts = k.sb(es, "cts", [128, 2, 65], BF16)
        vc_sb = k.sb(es, "nvc", [128, 2, 65], BF16)
        ksel = k.sb(es, "nksel", [128, S], BF16)
        kwin = k.sb(es, "nkwin", [128, S], BF16)
        vsel = k.sb(es, "nvsel", [128, NT, 65], BF16)
        vwin = k.sb(es, "nvwin", [128, NT, 65], BF16)
        eself = k.sb(es, "eself", [128, 1024], F32)
        imp = k.sb(es, "imp", [128, NT, 64], F32)
        keep = k.sb(es, "keep", [128, NT, 64], F32)
        addc = k.sb(es, "addc", [128, NT, 64], F32)
        gts = k.sb(es, "gts", [128, NT, 18], F32)
        sm = k.sb(es, "nsm", [128, 64], F32)
        wk = [k.sb(es, "nwk", [128, 64], F32) for _ in range(3)]
        m8 = [k.sb(es, "nm8", [128, 8], F32) for _ in range(2)]
        negq = k.sb(es, "negq", [128, 64], F32)
        acc = [k.sb(es, "nacc", [128, 64], F32) for _ in range(2)]
        cx = AttnCx(k, es, 3)
        pfl = [k.ps(es, "pfl", [128, 512], F32) for _ in range(2)]

        k.dma("sp", flipf[:], consts["c_flip"], writes=["flipf"])
        k.dma("sp", ctsf[:], consts["c_cts"].rearrange("(c p) d -> p c d", p=128), writes=["ctsf"])
        k.v("dve", "tensor_copy", reads=["ctsf"], writes=["cts"], out=cts[:], in_=ctsf[:])
        for c0 in range(0, S, 1024):
            k.dma("sp", eself[64:128, :], consts["c_esel"][:, c0:c0 + 1024], writes=["eself"])
            k.v("dve", "tensor_copy", reads=["eself"], writes=["esel"], out=ksel[64:128, c0:c0 + 1024], in_=eself[64:128, :])
        k.v("dve", "memset", writes=["nneg"], ap=q_sb[64:128, :], constant=0.0)
        k.v("dve", "memset", writes=["nkc0"], ap=kc_sb[64:128, :], constant=0.0)
        k.v("dve", "memset", writes=["nkwin0"], ap=kwin[64:128, :], constant=0.0)
        dma_split(k, "pool", keep, consts["c_keep"].rearrange("(c p) d -> p c d", p=128), "keep")
        dma_split(k, "pool", addc, consts["c_add"].rearrange("(c p) d -> p c d", p=128), "addc")
        dma_split(k, "pool", gts, gtok.rearrange("(c p) d -> p c d", p=128), "gts")

        for g in range(2):
            k.dma("sp", kc_sb[0:64, :], kcT[g], writes=["nkc"])
            for r in range(3):
                h = g * 3 + r
                k.dma("sp", q_sb[0:64, :], qkT[R_NSAQ + h * 64:R_NSAQ + (h + 1) * 64, :], writes=["nq"])
                make_tb(k, "cmp", gd, h, flipf, rev, tbc, pfl, "tbc")
                for qb in range(8):
                    tiles = cmp_tiles(qb, kc_sb, q_sb, tbc, cts, ["nkc", "nkc0", "nneg", "nq", "tbc", "cts"])
                    run_branch(k, cx, tiles, 0, ("aU", 0))

                    def post1(qb=qb, r=r):
                        U = cx.U[0]
                        k.v("dve", "tensor_scalar", reads=[("aU", 0)], writes=["nsm"], out=sm[:, 0:4],
                            in0=U[:, :, 64], scalar1=1e-30, scalar2=None, op0=ALU.max)
                        k.v("dve", "reciprocal", reads=["nsm"], writes=["nsm2"], out=sm[:, 4:8], in_=sm[:, 0:4])
                        for qt in range(4):
                            tl = qb * 4 + qt
                            if r == 0:
                                k.v("dve", "tensor_scalar", reads=[("aU", 0), "nsm2"], writes=[("imp", tl)],
                                    out=imp[:, tl, :], in0=U[:, qt, 0:64], scalar1=sm[:, 4 + qt:5 + qt],
                                    scalar2=None, op0=ALU.mult)
                            else:
                                k.v("dve", "scalar_tensor_tensor", reads=[("aU", 0), "nsm2", ("imp", tl)],
                                    writes=[("imp", tl)], out=imp[:, tl, :], in0=U[:, qt, 0:64],
                                    scalar=sm[:, 4 + qt:5 + qt], in1=imp[:, tl, :], op0=ALU.mult, op1=ALU.add)
                    pipe_post(k, cx, post1)
                pipe_flush(k, cx)
            for tl in range(NT):
                a, b_, c_ = wk
                k.v("dve", "tensor_tensor", reads=[("imp", tl), "keep"], writes=["nwk0"], out=a[:],
                    in0=imp[:, tl, :], in1=keep[:, tl, :], op=ALU.mult)
                k.v("dve", "tensor_tensor", reads=["nwk0", "addc"], writes=["nwk1"], out=b_[:],
                    in0=a[:], in1=addc[:, tl, :], op=ALU.add)
                k.v("dve", "max", reads=["nwk1"], writes=["nm80"], out=m8[0][:], in_=b_[:])
                k.v("dve", "match_replace", reads=["nwk1", "nm80"], writes=["nwk2"], out=c_[:],
                    in_to_replace=m8[0][:], in_values=b_[:], imm_value=-3.0e38)
                k.v("dve", "max", reads=["nwk2"], writes=["nm81"], out=m8[1][:], in_=c_[:])
                k.v("dve", "tensor_scalar", reads=["nwk1", "nm81"], writes=["nwk0"], out=a[:], in0=b_[:],
                    scalar1=m8[1][:, 7:8], scalar2=None, op0=ALU.is_ge)
                k.v("dve", "tensor_scalar", reads=["nwk0"], writes=["negq"], out=negq[:], in0=a[:],
                    scalar1=-NEGP, scalar2=NEGP, op0=ALU.mult, op1=ALU.add)
                pb = tl % 2
                k.tr(pfl[pb][0:64, 0:128], negq[:, :], identf[:], reads=["negq", "identf"], writes=[("pflip", pb)])
                k.v("dve", "tensor_copy", reads=[("pflip", pb)], writes=["nneg"],
                    out=q_sb[64:128, tl * 128:(tl + 1) * 128], in_=pfl[pb][0:64, 0:128])
            k.dma("sp", ksel[0:64, :], qkT[R_KSEL + g * 64:R_KSEL + (g + 1) * 64, :], writes=["nksel"])
            k.dma("sp", kwin[0:64, :], qkT[R_KWIN + g * 64:R_KWIN + (g + 1) * 64, :], writes=["nkwin"])
            load_v_aug(k, "pool", vsel, vtok, C_VSEL + g * 64, "nvsel")
            load_v_aug(k, "pool", vwin, vtok, C_VWIN + g * 64, "nvwin")
            k.v("dve", "memset", writes=["nvc"], ap=vc_sb[:, :, 64:65], constant=1.0)
            k.dma("pool", vc_sb[:, :, 0:64], vcs[g].rearrange("(c p) d -> p c d", p=128), writes=["nvc"])
            for r in range(3):
                h = g * 3 + r
                k.dma("sp", q_sb[0:64, :], qkT[R_NSAQ + h * 64:R_NSAQ + (h + 1) * 64, :], writes=["nq"])
                make_tb(k, "cmp", gd, h, flipf, rev, tbc, pfl, "tbc")
                make_tb(k, "full", gd, h, flipf, rev, tbf, pfl, "tbf")
                make_tb(k, "win", gd, h, flipf, rev, tbw, pfl, "tbw")
                for qb in range(8):
                    qs = qb * 512
                    qfn = lambda c0, n, qs=qs: q_sb[:, qs + c0:qs + c0 + n]
                    tiles = cmp_tiles(qb, kc_sb, q_sb, tbc, vc_sb, ["nkc", "nkc0", "nneg", "nq", "tbc", "nvc"])
                    run_branch(k, cx, tiles, 0, ("aU", 0))
                    tiles = []
                    for kc in range(0, (qs + 384) // 128 + 1):
                        Dl = qs - 128 * kc
                        lo = max(0, -Dl // 128)
                        far = Dl >= 1664
                        tiles.append(dict(
                            rows=128, kT=ksel[:, kc * 128:(kc + 1) * 128], qfn=qfn,
                            tbfn=None if far else (lambda c0, n, Dl=Dl: tbf[:, Dl + 384 + c0:Dl + 384 + c0 + n]),
                            cbias=tbf[:, 2431:2432] if far else None,
                            maskfn=None,
                            V=vsel[:, kc, :], qt_lo=lo, qt_hi=4, rd=["nksel", "nq", "tbf", "nvsel", "esel", "nneg"]))
                    run_branch(k, cx, tiles, 1, ("aU", 1))
                    tiles = []
                    for kc in range(max(0, (qs - 512) // 128), (qs + 384) // 128 + 1):
                        Dl = qs - 128 * kc
                        lo = max(0, -Dl // 128)
                        hi = min(4, (639 - Dl) // 128 + 1)
                        tiles.append(dict(
                            rows=128, kT=kwin[:, kc * 128:(kc + 1) * 128], qfn=qfn,
                            tbfn=lambda c0, n, Dl=Dl: tbw[:, Dl + 384 + c0:Dl + 384 + c0 + n],
                            cbias=None, maskfn=None, V=vwin[:, kc, :], qt_lo=lo, qt_hi=hi,
                            rd=["nkwin", "nkwin0", "nneg", "nq", "tbw", "nvwin"]))
                    run_branch(k, cx, tiles, 2, ("aU", 2))
                    def post3(qb=qb, h=h):
                        for br in range(3):
                            U = cx.U[br]
                            k.v("dve", "tensor_scalar", reads=[("aU", br)], writes=[("nsmc", br)],
                                out=sm[:, 8 + br * 12:12 + br * 12], in0=U[:, :, 64], scalar1=1e-30, scalar2=None,
                                op0=ALU.max)
                            k.v("dve", "reciprocal", reads=[("nsmc", br)], writes=[("nsmr", br)],
                                out=sm[:, 12 + br * 12:16 + br * 12], in_=sm[:, 8 + br * 12:12 + br * 12])
                            k.v("dve", "tensor_tensor", reads=[("nsmr", br), "gts"], writes=[("nsmg", br)],
                                out=sm[:, 16 + br * 12:20 + br * 12], in0=sm[:, 12 + br * 12:16 + br * 12],
                                in1=gts[:, qb * 4:qb * 4 + 4, h * 3 + br], op=ALU.mult)
                        for qt in range(4):
                            tl = qb * 4 + qt
                            a0, a1 = acc
                            k.v("dve", "tensor_scalar", reads=[("aU", 0), ("nsmg", 0)], writes=["nacc0"], out=a0[:],
                                in0=cx.U[0][:, qt, 0:64], scalar1=sm[:, 16 + qt:17 + qt], scalar2=None, op0=ALU.mult)
                            k.v("dve", "scalar_tensor_tensor", reads=[("aU", 1), ("nsmg", 1), "nacc0"], writes=["nacc1"],
                                out=a1[:], in0=cx.U[1][:, qt, 0:64], scalar=sm[:, 28 + qt:29 + qt], in1=a0[:],
                                op0=ALU.mult, op1=ALU.add)
                            k.v("dve", "scalar_tensor_tensor", reads=[("aU", 2), ("nsmg", 2), "nacc1"],
                                writes=[("ytok", tl)], out=ytok_sb[:, tl, h * 64:(h + 1) * 64],
                                in0=cx.U[2][:, qt, 0:64], scalar=sm[:, 40 + qt:41 + qt], in1=a1[:],
                                op0=ALU.mult, op1=ALU.add)
                    pipe_post(k, cx, post3)
                pipe_flush(k, cx)
    P.barrier()


def moba_phase(k, qkT, vtok, gd, consts, identf, ytok_sb):
    P = k.P
    with ExitStack() as es:
        flipf = k.sb(es, "mflipf", [128, 128], F32)
        rev = k.sb(es, "mrev", [128, 2432], F32)
        tbf = k.sb(es, "mtbf", [128, 2432], F32)
        q_sb = k.sb(es, "mq", [128, S], BF16)
        k_sb = k.sb(es, "mk", [128, S], BF16)
        v_sb = k.sb(es, "mv", [128, NT, 65], BF16)
        ebf = k.sb(es, "ebf", [128, 1024], F32)
        valid = k.sb(es, "mvalid", [128, NT, 16], F32)
        addm = k.sb(es, "maddm", [128, NT, 16], F32)
        ownc = k.sb(es, "mown", [128, NT, 16], F32)
        km = k.sb(es, "mkm", [64, 16], F32)
        kmh = k.sb(es, "mkmh", [64, 16], BF16)
        kmhf = k.sb(es, "mkmhf", [64, 16], F32)
        kml = k.sb(es, "mkml", [64, 16], BF16)
        wk = [k.sb(es, "mwk", [128, 16], F32) for _ in range(3)]
        m8 = k.sb(es, "mm8", [128, 8], F32)
        sm = k.sb(es, "msm", [128, 8], F32)
        cx = AttnCx(k, es, 1)
        pfl = [k.ps(es, "mpfl", [128, 512], F32) for _ in range(2)]
        k.dma("sp", flipf[:], consts["c_flip"], writes=["flipf"])
        k.v("dve", "memset", writes=["mneg"], ap=q_sb[64:128, :], constant=0.0)
        k.v("dve", "memset", writes=["eb"], ap=k_sb[64:128, :], constant=0.0)
        for c0 in range(0, S, 1024):
            k.dma("sp", ebf[64:80, :], consts["c_ebig"][:, c0:c0 + 1024], writes=["ebf"])
            k.v("dve", "tensor_copy", reads=["ebf"], writes=["eb"], out=k_sb[64:80, c0:c0 + 1024], in_=ebf[64:80, :])
        dma_split(k, "pool", valid, consts["c_mvalid"].rearrange("(c p) d -> p c d", p=128), "mvalid")
        dma_split(k, "pool", addm, consts["c_maddm"].rearrange("(c p) d -> p c d", p=128), "maddm")
        dma_split(k, "pool", ownc, consts["c_mown"].rearrange("(c p) d -> p c d", p=128), "mown")
        for hb in range(4):
            k.dma("sp", q_sb[0:64, :], qkT[R_MQ + hb * 64:R_MQ + (hb + 1) * 64, :], writes=["mq"])
            k.dma("sp", k_sb[0:64, :], qkT[R_MK + hb * 64:R_MK + (hb + 1) * 64, :], writes=["mk"])
            load_v_aug(k, "pool", v_sb, vtok, C_MV + hb * 64, "mv")
            make_tb(k, "full", gd, 6 + hb, flipf, rev, tbf, pfl, "tbf")
            k.v("dve", "tensor_reduce", reads=["mk"], writes=["mkm"], out=km[:],
                in_=k_sb[0:64, :].rearrange("p (n j) -> p n j", j=256), axis=mybir.AxisListType.X, op=ALU.add)
            k.v("dve", "tensor_scalar", reads=["mkm"], writes=["mkm2"], out=km[:], in0=km[:], scalar1=1.0 / 256,
                scalar2=None, op0=ALU.mult)
            k.v("dve", "tensor_copy", reads=["mkm2"], writes=["mkmh"], out=kmh[:], in_=km[:])
            k.v("dve", "tensor_copy", reads=["mkmh"], writes=["mkmhf"], out=kmhf[:], in_=kmh[:])
            k.v("dve", "tensor_tensor", reads=["mkm2", "mkmhf"], writes=["mkml"], out=kml[:], in0=km[:],
                in1=kmhf[:], op=ALU.subtract)
            for tl in range(NT):
                pb = tl % 2
                G = pfl[pb]
                k.mm(G[:, 0:16], q_sb[0:64, tl * 128:(tl + 1) * 128], kmh[:, :], True, False,
                     reads=["mq", "mkmh"], writes=[("pflip", pb)])
                k.mm(G[:, 0:16], q_sb[0:64, tl * 128:(tl + 1) * 128], kml[:, :], False, True,
                     reads=["mq", "mkml"], writes=[("pflip", pb)])
                a, b_, c_ = wk
                k.v("dve", "tensor_tensor", reads=[("pflip", pb), "mvalid"], writes=["mwk0"], out=a[:],
                    in0=G[:, 0:16], in1=valid[:, tl, :], op=ALU.mult)
                k.v("dve", "tensor_tensor", reads=["mwk0", "maddm"], writes=["mwk1"], out=b_[:], in0=a[:],
                    in1=addm[:, tl, :], op=ALU.add)
                k.v("dve", "max", reads=["mwk1"], writes=["mm8"], out=m8[:], in_=b_[:])
                k.v("dve", "tensor_scalar", reads=["mwk1", "mm8"], writes=["mwk2"], out=c_[:], in0=b_[:],
                    scalar1=m8[:, 2:3], scalar2=None, op0=ALU.is_ge)
                k.v("dve", "tensor_tensor", reads=["mwk2", "mvalid"], writes=["mwk0"], out=a[:], in0=c_[:],
                    in1=valid[:, tl, :], op=ALU.mult)
                k.v("dve", "scalar_tensor_tensor", reads=["mwk0", "mown"], writes=["mwk1"], out=b_[:], in0=a[:],
                    scalar=-NEGP, in1=ownc[:, tl, :], op0=ALU.mult, op1=ALU.add)
                k.tr(pfl[pb][0:16, 128:256], b_[:, :], identf[:], reads=["mwk1", "identf"], writes=[("pflip", pb)])
                k.v("dve", "tensor_copy", reads=[("pflip", pb)], writes=["mneg"],
                    out=q_sb[64:80, tl * 128:(tl + 1) * 128], in_=pfl[pb][0:16, 128:256])
            for qb in range(8):
                qs = qb * 512
                qfn = lambda c0, n, qs=qs: q_sb[:, qs + c0:qs + c0 + n]
                tiles = []
                for kc in range(0, (qs + 384) // 128 + 1):
                    Dl = qs - 128 * kc
                    lo = max(0, -Dl // 128)
                    far = Dl >= 1664
                    nblk = kc // 2
                    tiles.append(dict(
                        rows=128, kT=k_sb[:, kc * 128:(kc + 1) * 128], qfn=qfn,
                        tbfn=None if far else (lambda c0, n, Dl=Dl: tbf[:, Dl + 384 + c0:Dl + 384 + c0 + n]),
                        cbias=tbf[:, 2431:2432] if far else None,
                        maskfn=None,
                        V=v_sb[:, kc, :], qt_lo=lo, qt_hi=4, rd=["mk", "mq", "tbf", "mv", "eb", "mneg"]))
                run_branch(k, cx, tiles, 0, ("aU", 0))

                def postm(qb=qb, hb=hb):
                    U = cx.U[0]
                    k.v("dve", "tensor_scalar", reads=[("aU", 0)], writes=["msm"], out=sm[:, 0:4], in0=U[:, :, 64],
                        scalar1=1e-30, scalar2=None, op0=ALU.max)
                    k.v("dve", "reciprocal", reads=["msm"], writes=["msm2"], out=sm[:, 4:8], in_=sm[:, 0:4])
                    for qt in range(4):
                        tl = qb * 4 + qt
                        k.v("dve", "tensor_scalar", reads=[("aU", 0), "msm2"], writes=[("ytok", tl)],
                            out=ytok_sb[:, tl, 384 + hb * 64:384 + (hb + 1) * 64], in0=U[:, qt, 0:64],
                            scalar1=sm[:, 4 + qt:5 + qt], scalar2=None, op0=ALU.mult)
                pipe_post(k, cx, postm)
            pipe_flush(k, cx)
    P.barrier()


def dil_phase(k, qkT, vtok, gd, consts, ytok_sb):
    P = k.P
    with ExitStack() as es:
        flipf = k.sb(es, "dflipf", [128, 128], F32)
        rev = k.sb(es, "drev", [128, 2944], F32)
        tbs = [k.sb(es, "dtb", [128, FAM["dil%d" % g][1]], F32) for g in range(3)]
        q_sb = [k.sb(es, "dq", [128, S], BF16) for _ in range(3)]
        k_sb = [k.sb(es, "dk", [128, S], BF16) for _ in range(3)]
        v_sb = [k.sb(es, "dv", [128, NT, 65], BF16) for _ in range(3)]
        sm = k.sb(es, "dsm", [128, 8], F32)
        cx = AttnCx(k, es, 1)
        pfl = [k.ps(es, "dpfl", [128, 512], F32) for _ in range(2)]
        k.dma("sp", flipf[:], consts["c_flip"], writes=["flipf"])
        for g in range(3):
            k.v("dve", "memset", writes=[("dq0", g)], ap=q_sb[g][64:128, :], constant=0.0)
            k.v("dve", "memset", writes=[("dk0", g)], ap=k_sb[g][64:128, :], constant=0.0)
        for i in range(2):
            for g in range(3):
                hh = g * 2 + i
                k.dma("sp", q_sb[g][0:64, :], qkT[R_DQ + hh * 64:R_DQ + (hh + 1) * 64, :], writes=[("dq", g)])
                k.dma("sp", k_sb[g][0:64, :], qkT[R_DK + hh * 64:R_DK + (hh + 1) * 64, :], writes=[("dk", g)])
                load_v_aug(k, "pool", v_sb[g], vtok, C_DV + hh * 64, ("dv", g))
                make_tb(k, "dil%d" % g, gd, 10 + hh, flipf, rev, tbs[g], pfl, ("dtb", g))
            for qb in range(8):
                qs = qb * 512
                tiles = []
                for g, (W, dl) in enumerate(DIL):
                    for kc in range(max(0, (qs - W) // 128), (qs + 384) // 128 + 1):
                        Dl = qs - 128 * kc
                        lo = max(0, -Dl // 128)
                        hi = min(4, (W - Dl) // 128 + 1)
                        tiles.append(dict(
                            rows=128, kT=k_sb[g][:, kc * 128:(kc + 1) * 128],
                            qfn=lambda c0, n, g=g, qs=qs: q_sb[g][:, qs + c0:qs + c0 + n],
                            tbfn=lambda c0, n, g=g, Dl=Dl: tbs[g][:, Dl + 384 + c0:Dl + 384 + c0 + n],
                            cbias=None, maskfn=None, V=v_sb[g][:, kc, :], qt_lo=lo, qt_hi=hi,
                            rd=[("dq", g), ("dk", g), ("dq0", g), ("dk0", g), ("dv", g), ("dtb", g)]))
                run_branch(k, cx, tiles, 0, ("aU", 0))

                def postd(qb=qb, i=i):
                    U = cx.U[0]
                    k.v("dve", "tensor_scalar", reads=[("aU", 0)], writes=["dsm"], out=sm[:, 0:4], in0=U[:, :, 64],
                        scalar1=1e-30, scalar2=None, op0=ALU.max)
                    k.v("dve", "reciprocal", reads=["dsm"], writes=["dsm2"], out=sm[:, 4:8], in_=sm[:, 0:4])
                    for qt in range(4):
                        tl = qb * 4 + qt
                        k.v("dve", "tensor_scalar", reads=[("aU", 0), "dsm2"], writes=[("ytok", tl)],
                            out=ytok_sb[:, tl, 640 + i * 64:640 + (i + 1) * 64], in0=U[:, qt, 0:64],
                            scalar1=sm[:, 4 + qt:5 + qt], scalar2=None, op0=ALU.mult)
                pipe_post(k, cx, postd)
            pipe_flush(k, cx)
    P.barrier()


def merge_phase(k, xres, mgT, w_up_a, w_up_b, w_up_c, w_o, ident, ytok_sb):
    P = k.P
    with ExitStack() as es:
        wup = k.sb(es, "wup", [128, 6, D], BF16)
        wo = k.sb(es, "wo", [128, 8, D], BF16)
        stg = [k.sb(es, "gstg", [128, D], F32) for _ in range(2)]
        yT = k.sb(es, "gyT", [128, 6, 512], BF16)
        gt = [k.sb(es, "ggt", [128, 3, 512], BF16) for _ in range(2)]
        m1 = [k.sb(es, "gm1", [128, 512], F32) for _ in range(2)]
        m2 = [k.sb(es, "gm2", [128, 512], F32) for _ in range(2)]
        m3 = [k.sb(es, "gm3", [128, 512], F32) for _ in range(2)]
        m4 = [k.sb(es, "gm4", [128, 512], F32) for _ in range(2)]
        mT = k.sb(es, "gmT", [128, 8, 512], BF16)
        xr = [k.sb(es, "gxr", [128, 512], F32) for _ in range(2)]
        ob = [k.sb(es, "gob", [128, 512], F32) for _ in range(2)]
        ptp = [k.ps(es, "gptp", [128, 1024], BF16) for _ in range(2)]
        pu = [k.ps(es, "gpu", [128, 512], F32) for _ in range(3)]
        po = [k.ps(es, "gpo", [128, 512], F32) for _ in range(2)]
        srcs = [(w_up_a, 0), (w_up_a, 1), (w_up_a, 2), (w_up_b, 0), (w_up_b, 1), (w_up_c, 0)]
        ci = 0
        for fc, (w, j) in enumerate(srcs):
            b = ci % 2
            ci += 1
            k.dma("sp" if b == 0 else "pool", stg[b][:], w[j * 128:(j + 1) * 128, :], writes=[("gstg", b)])
            k.v("dve", "tensor_copy", reads=[("gstg", b)], writes=["wup"], out=wup[:, fc, :], in_=stg[b][:])
        for kc in range(8):
            b = ci % 2
            ci += 1
            k.dma("sp" if b == 0 else "pool", stg[b][:], w_o[kc * 128:(kc + 1) * 128, :], writes=[("gstg", b)])
            k.v("dve", "tensor_copy", reads=[("gstg", b)], writes=["wo"], out=wo[:, kc, :], in_=stg[b][:])
        ui = 0
        oi = 0
        for tb8 in range(8):
            t0 = tb8 * 512
            for fc in range(6):
                pb = fc % 2
                for t in range(4):
                    tl = tb8 * 4 + t
                    k.tr(ptp[pb][:, t * 128:(t + 1) * 128], ytok_sb[:, tl, fc * 128:(fc + 1) * 128], ident[:],
                         reads=[("ytok", tl), "ident"], writes=[("gptp", pb)])
                if fc % 2 == 0:
                    k.act(yT[:, fc, :], ptp[pb][:, 0:512], AF.Copy, reads=[("gptp", pb)], writes=[("gyT", fc)])
                else:
                    k.v("dve", "tensor_copy", reads=[("gptp", pb)], writes=[("gyT", fc)], out=yT[:, fc, :],
                        in_=ptp[pb][:, 0:512])
            for cc in range(8):
                b = ui % 2
                ui += 1
                k.dma("sp", gt[b][:], mgT.rearrange("(b r) t -> r b t", b=3)[cc * 128:(cc + 1) * 128, :, t0:t0 + 512],
                      writes=[("ggt", b)])
                groups = ((0, (0, 1, 2)), (1, (3, 4)), (2, (5,)))
                for br, fcs in groups:
                    for j, fc in enumerate(fcs):
                        k.mm(pu[br][:], wup[:, fc, cc * 128:(cc + 1) * 128], yT[:, fc, :], j == 0, j == len(fcs) - 1,
                             reads=[("gyT", fc), "wup"], writes=[("gpu", br)])
                k.v("dve", "tensor_tensor", reads=[("gpu", 0), ("ggt", b)], writes=[("gm1", b)], out=m1[b][:],
                    in0=pu[0][:], in1=gt[b][:, 0, :], op=ALU.mult)
                k.v("dve", "tensor_tensor", reads=[("gpu", 1), ("ggt", b)], writes=[("gm2", b)], out=m2[b][:],
                    in0=pu[1][:], in1=gt[b][:, 1, :], op=ALU.mult)
                k.v("dve", "tensor_tensor", reads=[("gpu", 2), ("ggt", b)], writes=[("gm3", b)], out=m3[b][:],
                    in0=pu[2][:], in1=gt[b][:, 2, :], op=ALU.mult)
                k.v("dve", "tensor_tensor", reads=[("gm1", b), ("gm2", b)], writes=[("gm4", b)], out=m4[b][:],
                    in0=m1[b][:], in1=m2[b][:], op=ALU.add)
                k.v("dve", "tensor_tensor", reads=[("gm4", b), ("gm3", b)], writes=[("gmT", cc)], out=mT[:, cc, :],
                    in0=m4[b][:], in1=m3[b][:], op=ALU.add)
            allm = [("gmT", cc) for cc in range(8)]
            for t in range(4):
                r0 = t0 + t * 128
                for hh in range(2):
                    b = oi % 2
                    oi += 1
                    k.dma("pool", xr[b][:], xres[r0:r0 + 128, hh * 512:(hh + 1) * 512], writes=[("gxr", b)])
                    for cc in range(8):
                        k.mm(po[b][:], mT[:, cc, t * 128:(t + 1) * 128], wo[:, cc, hh * 512:(hh + 1) * 512],
                             cc == 0, cc == 7, reads=allm + ["wo"], writes=[("gpo", b)])
                    k.v("dve", "tensor_tensor", reads=[("gpo", b), ("gxr", b)], writes=[("gob", b)], out=ob[b][:],
                        in0=po[b][:], in1=xr[b][:], op=ALU.add)
                    k.dma("sp", xres[r0:r0 + 128, hh * 512:(hh + 1) * 512], ob[b][:], reads=[("gob", b)])
    P.barrier()


W_QK_COLS = np.concatenate([np.arange(0, 384), np.arange(384, 512), np.arange(512, 640), np.arange(640, 768),
                            np.arange(896, 1024), np.arange(1170, 1426), np.arange(1426, 1682),
                            np.arange(1938, 2322), np.arange(2322, 2706)])
W_V_COLS = np.concatenate([np.arange(768, 896), np.arange(1024, 1152), np.arange(1682, 1938),
                           np.arange(2706, 3090), np.arange(1152, 1170)])
W_MG_COLS = np.arange(3090, 6162)


def build(stop_after=None, debug=False):
    nc = bass.Bass("TRN2", target_bir_lowering=False)

    def inp(name, shape):
        return nc.dram_tensor(name, list(shape), F32, kind="ExternalInput").ap()

    def scr(name, shape, dt):
        return nc.dram_tensor(name, list(shape), dt, kind="ExternalOutput" if debug else "Internal").ap()

    x = inp("x", [S, D])
    rel_bias = inp("rel_bias", [32, 16])
    ffn_norm = [inp("ffn1_norm", [2, D]), inp("ffn2_norm", [2, D])]
    ffn_wg = [inp("ffn1_w_gate", [2, D, DFF]), inp("ffn2_w_gate", [2, D, DFF])]
    ffn_wu = [inp("ffn1_w_up", [2, D, DFF]), inp("ffn2_w_up", [2, D, DFF])]
    ffn_wd = [inp("ffn1_w_down", [2, DFF, D]), inp("ffn2_w_down", [2, DFF, D])]
    mix_norm = inp("mix_norm", [2, D])
    w_qk = inp("w_qk", [2, D, NQK])
    w_v = inp("w_v", [2, D, NVG])
    w_mg = inp("w_mg", [2, D, 3 * D])
    pe_k = inp("nsa_pe_k", [2, 32, 64])
    pe_v = inp("nsa_pe_v", [2, 32, 64])
    phi_k1 = inp("nsa_phi_k1", [2, 2048, 256])
    phi_k2 = inp("nsa_phi_k2", [2, 256, 64])
    phi_v1 = inp("nsa_phi_v1", [2, 2048, 256])
    phi_v2 = inp("nsa_phi_v2", [2, 256, 64])
    w_up_a = inp("w_up_a", [2, 384, D])
    w_up_b = inp("w_up_b", [2, 256, D])
    w_up_c = inp("w_up_c", [2, 128, D])
    w_o = inp("w_o", [2, D, D])
    final_norm = inp("final_norm", [1, D])
    consts = {nm: inp(nm, shp) for nm, shp in CONST_SHAPES.items()}
    y = nc.dram_tensor("y", [S, D], F32, kind="ExternalOutput").ap()
    xres = scr("xres", [S, D], F32)
    qkT = scr("qkT", [NQK, S], BF16)
    vtok = scr("vtok", [S, NVC], BF16)
    gtok = scr("gtok", [S, 18], F32)
    mgT = scr("mgT", [3 * D, S], BF16)
    kcT = scr("kcT", [2, 64, 256], BF16)
    vcs = scr("vcs", [2, 256, 64], BF16)
    gd = {fam: scr("gd_" + fam, [16, L], F32) for fam, (L, _, _) in FAM.items()}
    ydbg = scr("ydbg", [S, 768], BF16) if debug else None

    P = Prog(nc)
    k = K(nc, P)
    with ExitStack() as es:
        identf = k.sb(es, "identf", [128, 128], F32)
        ident = k.sb(es, "ident", [128, 128], BF16)
        k.dma("sp", identf[:], consts["c_ident"], writes=["identf"])
        k.v("dve", "tensor_copy", reads=["identf"], writes=["ident"], out=ident[:], in_=identf[:])

        def dump_y(ytok_sb):
            if debug:
                yv = ydbg.rearrange("(c p) d -> p c d", p=128)
                for c0 in range(0, NT, 8):
                    k.dma("sp", yv[:, c0:c0 + 8, :], ytok_sb[:, c0:c0 + 8, :], reads=[("ytok", t) for t in range(NT)])
                P.barrier()

        ystack = []

        def dump_y_dummy():
            pass

        def run():
            stages = stop_after
            bias_gen_phase(k, rel_bias, consts, gd)
            for l in range(2):
                ffn_phase(k, x if l == 0 else xres, xres, ffn_norm[0][l:l + 1, :], ffn_wg[0][l], ffn_wu[0][l],
                          ffn_wd[0][l], ident)
                if stages == "ffn1":
                    return
                proj_phase(k, xres, mix_norm[l:l + 1, :], w_qk[l], w_v[l], w_mg[l], ident, qkT, vtok, gtok, mgT)
                if stages == "proj":
                    return
                compress_phase(k, qkT, pe_k[l], pe_v[l], phi_k1[l], phi_k2[l], phi_v1[l], phi_v2[l], kcT, vcs, identf)
                if stages == "cmp":
                    return
                ys = ExitStack()
                ytok_sb = k.sb(ys, "ytok", [128, NT, 768], BF16)
                ystack.append(ys)
                if stages in (None, "nsa", "attn", "merge", "l0"):
                    nsa_phase(k, qkT, vtok, gtok, kcT, vcs, gd, consts, identf, ytok_sb)
                if stages == "nsa":
                    dump_y(ytok_sb)
                    return
                if stages in (None, "moba", "attn", "merge", "l0"):
                    moba_phase(k, qkT, vtok, gd, consts, identf, ytok_sb)
                if stages == "moba":
                    dump_y(ytok_sb)
                    return
                dil_phase(k, qkT, vtok, gd, consts, ytok_sb)
                if stages in ("dil", "attn"):
                    dump_y(ytok_sb)
                    return
                merge_phase(k, xres, mgT, w_up_a[l], w_up_b[l], w_up_c[l], w_o[l], ident, ytok_sb)
                ystack.pop().close()
                if stages == "merge":
                    return
                ffn_phase(k, xres, xres, ffn_norm[1][l:l + 1, :], ffn_wg[1][l], ffn_wu[1][l], ffn_wd[1][l], ident)
                if stages == "l0":
                    return
            final_norm_phase(k, xres, y, final_norm)
        run()
        P.barrier()
        with ExitStack() as es2:
            P.emit(es2)
        while ystack:
            ystack.pop().close()
    return nc


def make_in_maps(inputs, cores):
    consts = host_consts()
    w_in = np.asarray(inputs["w_in"])
    shared = dict(consts)
    shared["w_qk"] = np.ascontiguousarray(w_in[:, :, W_QK_COLS])
    shared["w_v"] = np.ascontiguousarray(w_in[:, :, W_V_COLS])
    shared["w_mg"] = np.ascontiguousarray(w_in[:, :, W_MG_COLS])
    for nm in ("rel_bias", "ffn1_norm", "ffn2_norm", "ffn1_w_gate", "ffn2_w_gate", "ffn1_w_up", "ffn2_w_up",
               "ffn1_w_down", "ffn2_w_down", "mix_norm", "nsa_pe_k", "nsa_pe_v", "nsa_phi_k1", "nsa_phi_k2",
               "nsa_phi_v1", "nsa_phi_v2", "w_up_a", "w_up_b", "w_up_c", "w_o"):
        shared[nm] = np.ascontiguousarray(np.asarray(inputs[nm], dtype=np.float32))
    shared["final_norm"] = np.ascontiguousarray(np.asarray(inputs["final_norm"], dtype=np.float32)).reshape(1, D)
    maps = []
    for b in cores:
        m = dict(shared)
        m["x"] = np.ascontiguousarray(np.asarray(inputs["x"][b], dtype=np.float32))
        maps.append(m)
    return maps


def kernel(**inputs):
    nc = build()
    in_maps = make_in_maps(inputs, range(8))
    res = run_bass_kernel_spmd(nc, in_maps, core_ids=list(range(8)))
    return np.stack([np.asarray(r["y"]) for r in res.results], axis=0).astype(np.float32)
```

```python
import numpy as np
from contextlib import ExitStack
import concourse.bass as bass
import concourse.mybir as mybir
from concourse.bass_utils import run_bass_kernel_spmd

F32 = mybir.dt.float32
BF16 = mybir.dt.bfloat16
AF = mybir.ActivationFunctionType
ALU = mybir.AluOpType

S = 4096
D = 1024
DFF = 2816
NFC = DFF // 128
NT = S // 128
EPS = 1e-6
NEGM = -30000.0
NDS = 16


class _Op:
    __slots__ = ("eng", "fn", "deps", "dma", "signal", "sig", "dsem", "dval", "bar")


class Prog:
    ENG = ("pe", "act", "dve", "pool", "sp")

    def __init__(self, nc):
        self.nc = nc
        self.ops = []
        self.lastw = {}
        self.readers = {}
        self.last_on = {e: None for e in self.ENG}

    def op(self, eng, fn, reads=(), writes=(), dma=False):
        o = _Op()
        o.eng, o.fn, o.dma, o.signal, o.bar = eng, fn, dma, False, None
        deps = set()
        for r in reads:
            w = self.lastw.get(r)
            if w is not None:
                deps.add(w)
        for w_ in writes:
            w = self.lastw.get(w_)
            if w is not None:
                deps.add(w)
            for rd in self.readers.get(w_, ()):
                deps.add(rd)
        idx = len(self.ops)
        for w_ in writes:
            self.lastw[w_] = idx
            self.readers[w_] = []
        for r in reads:
            self.readers.setdefault(r, []).append(idx)
        deps.discard(idx)
        best = {}
        pruned = []
        for d in deps:
            p = self.ops[d]
            if p.dma:
                pruned.append(d)
            elif best.get(p.eng, -1) < d:
                best[p.eng] = d
        pruned.extend(best.values())
        o.deps = sorted(pruned)
        for d in o.deps:
            self.ops[d].signal = True
        self.ops.append(o)
        self.last_on[eng] = idx
        return idx

    def barrier(self):
        o = _Op()
        o.eng, o.fn, o.dma, o.signal, o.deps = None, None, False, False, []
        o.bar = dict(self.last_on)
        for e, i in o.bar.items():
            if i is not None and not self.ops[i].dma:
                self.ops[i].signal = True
        self.ops.append(o)

    def emit(self, es):
        nc = self.nc
        E = {"pe": nc.tensor, "act": nc.scalar, "dve": nc.vector, "pool": nc.gpsimd, "sp": nc.sync}
        sem = {e: es.enter_context(nc.semaphore("s_" + e)) for e in self.ENG}
        dsem = {q: [es.enter_context(nc.semaphore("d_%s%d" % (q, i))) for i in range(NDS)]
                for q in ("sp", "pool")}
        cnt = {e: 0 for e in self.ENG}
        dcnt = {"sp": 0, "pool": 0}
        seen = {e: {} for e in self.ENG}

        def wait(e, s, v):
            key = id(s)
            if seen[e].get(key, 0) < v:
                E[e].wait_ge(s, v)
                seen[e][key] = v

        for o in self.ops:
            if o.bar is not None:
                for q in ("sp", "pool"):
                    k = dcnt[q]
                    if k == 0:
                        continue
                    for i in range(min(k, NDS)):
                        uses = (k - 1 - i) // NDS + 1
                        wait(q, dsem[q][i], 16 * uses)
                    E[q].sem_inc(sem[q], 1)
                    cnt[q] += 1
                for f in self.ENG:
                    for e in self.ENG:
                        if e != f and cnt[e] > 0:
                            wait(f, sem[e], cnt[e])
                continue
            e = o.eng
            for d in o.deps:
                p = self.ops[d]
                if p.dma:
                    wait(e, p.dsem, p.dval)
                else:
                    if p.eng == e and e == "pe":
                        continue
                    wait(e, sem[p.eng], p.sig)
            if o.dma:
                k = dcnt[e]
                s = dsem[e][k % NDS]
                v = 16 * (k // NDS + 1)
                if k >= NDS:
                    wait(e, s, v - 16)
                ins = o.fn()
                ins.then_inc(s, 16)
                o.dsem, o.dval = s, v
                dcnt[e] = k + 1
            else:
                ins = o.fn()
                if o.signal:
                    cnt[e] += 1
                    o.sig = cnt[e]
                    ins.then_inc(sem[e], 1)
        self.ops = []


class K:
    def __init__(self, nc, P):
        self.nc, self.P = nc, P
        self.uid = 0

    def name(self, s):
        self.uid += 1
        return "%s_%d" % (s, self.uid)

    def sb(self, es, nm, shape, dt):
        return es.enter_context(self.nc.sbuf_tensor(self.name(nm), list(shape), dt))

    def ps(self, es, nm, shape, dt):
        return es.enter_context(self.nc.psum_tensor(self.name(nm), list(shape), dt))

    def dma(self, q, out, in_, reads=(), writes=(), slow=False):
        nc = self.nc
        eng = nc.sync if q == "sp" else nc.gpsimd
        if slow:
            return self.P.op(q, lambda: eng.dma_start(out=out, in_=in_, allow_slow_non_contiguous=True),
                             reads, writes, dma=True)
        return self.P.op(q, lambda: eng.dma_start(out=out, in_=in_), reads, writes, dma=True)

    def mm(self, out, lhsT, rhs, start, stop, reads=(), writes=()):
        nc = self.nc
        return self.P.op("pe", lambda: nc.tensor.matmul(out, lhsT=lhsT, rhs=rhs, start=start, stop=stop),
                         reads, writes)

    def tr(self, out, in_, ident, reads=(), writes=()):
        nc = self.nc
        return self.P.op("pe", lambda: nc.tensor.transpose(out, in_, ident), reads, writes)

    def act(self, out, in_, func, reads=(), writes=(), **kw):
        nc = self.nc
        return self.P.op("act", lambda: nc.scalar.activation(out=out, in_=in_, func=func, **kw), reads, writes)

    def v(self, eng, name, reads=(), writes=(), **kw):
        nc = self.nc
        e = nc.vector if eng == "dve" else nc.gpsimd
        return self.P.op(eng, lambda: getattr(e, name)(**kw), reads, writes)


def dap(t, offset, pattern):
    return bass.AP(tensor=t, offset=offset, ap=[list(p) for p in pattern])


def load_cast_weight(k, stg, stg_tok, dst_fn, src_fn, nchunks, width, q_alt=True):
    for c in range(nchunks):
        b = c % 2
        k.dma("sp" if (c % 2 == 0 or not q_alt) else "pool", stg[b][:, 0:width], src_fn(c),
              writes=[stg_tok[b]])
        k.v("pool", "tensor_copy", reads=[stg_tok[b]], writes=[("w", id(dst_fn), c)],
            out=dst_fn(c), in_=stg[b][:, 0:width])


def cast(k, sel, out, in_, reads, writes):
    if sel % 2 == 0:
        k.act(out, in_, AF.Copy, reads=reads, writes=writes)
    else:
        k.v("dve", "tensor_copy", reads=reads, writes=writes, out=out, in_=in_)


def ffn_phase(k, xsrc, xdst, w_norm, w_gate, w_up, w_down, ident):
    nc, P = k.nc, k.P
    G = 512
    with ExitStack() as es:
        wg = k.sb(es, "wg", [128, 8, DFF], BF16)
        wu = k.sb(es, "wu", [128, 8, DFF], BF16)
        wd = k.sb(es, "wd", [128, NFC, D], BF16)
        HW = DFF // 2
        stg = [k.sb(es, "stg", [128, HW], F32) for _ in range(2)]
        gain = k.sb(es, "gain", [128, D], F32)
        xin = [k.sb(es, "xin", [128, D], F32) for _ in range(2)]
        junk = k.sb(es, "junk", [128, D], BF16)
        hn = [k.sb(es, "hn", [128, D], BF16) for _ in range(2)]
        hT = k.sb(es, "hT", [128, 8, G], BF16)
        aT = k.sb(es, "aT", [128, NFC, G], BF16)
        sg = [k.sb(es, "sg", [128, G], F32) for _ in range(2)]
        xr = [k.sb(es, "xr", [128, 512], F32) for _ in range(2)]
        ob = [k.sb(es, "ob", [128, 512], F32) for _ in range(2)]
        st = [k.sb(es, "st", [128, 4], F32) for _ in range(2)]
        pgu = [k.ps(es, "pgu", [128, 512], F32) for _ in range(4)]
        ptp = [k.ps(es, "ptp", [128, 1024], BF16) for _ in range(2)]
        pdn = [k.ps(es, "pdn", [128, 512], F32) for _ in range(2)]

        k.dma("sp", gain[:], dap(w_norm.tensor, w_norm.offset, [[0, 128], [1, D]]), writes=["gain"])
        wgv = w_gate.rearrange("(kc p) f -> p kc f", p=128)
        wuv = w_up.rearrange("(kc p) f -> p kc f", p=128)
        wdv = w_down.rearrange("(fc p) d -> p fc d", p=128)
        ci = 0
        for (dst, srcv, n, width) in ((wg, wgv, 8, DFF), (wu, wuv, 8, DFF)):
            for c in range(n):
                for hh in range(2):
                    b = ci % 2
                    ci += 1
                    k.dma("sp" if b == 0 else "pool", stg[b][:, 0:HW], srcv[:, c, hh * HW:(hh + 1) * HW],
                          writes=[("stg", b)])
                    cast(k, b, dst[:, c, hh * HW:(hh + 1) * HW], stg[b][:, 0:HW], [("stg", b)], [("w", id(dst))])
        for c in range(NFC):
            b = ci % 2
            ci += 1
            k.dma("sp" if b == 0 else "pool", stg[b][:, 0:D], wdv[:, c, :], writes=[("stg", b)])
            cast(k, b, wd[:, c, :], stg[b][:, 0:D], [("stg", b)], [("w", id(wd))])

        gi = 0
        di = 0
        for g in range(S // G):
            for t in range(G // 128):
                r0 = g * G + t * 128
                b = t % 2
                k.dma("sp", xin[b][:], xsrc[r0:r0 + 128, :], writes=[("xin", b)])
                k.v("dve", "scalar_tensor_tensor", reads=[("xin", b)], writes=["junk", ("st", b)],
                    out=junk[:], in0=xin[b][:], scalar=1.0, in1=xin[b][:], op0=ALU.mult, op1=ALU.mult,
                    accum_out=st[b][:, 0:1])
                k.act(st[b][:, 1:2], st[b][:, 0:1], AF.Sqrt, reads=[("st", b)], writes=[("st1", b)],
                      scale=1.0 / D, bias=EPS)
                k.v("dve", "reciprocal", reads=[("st1", b)], writes=[("st2", b)],
                    out=st[b][:, 2:3], in_=st[b][:, 1:2])
                k.v("dve", "scalar_tensor_tensor", reads=[("xin", b), ("st2", b), "gain"], writes=[("hn", b)],
                    out=hn[b][:], in0=xin[b][:], scalar=st[b][:, 2:3], in1=gain[:], op0=ALU.mult, op1=ALU.mult)
                for kc in range(8):
                    k.tr(ptp[b][:, kc * 128:(kc + 1) * 128], hn[b][:, kc * 128:(kc + 1) * 128], ident[:],
                         reads=[("hn", b), "ident"], writes=[("ptp", b)])
                k.act(hT[:, :, t * 128:(t + 1) * 128], ptp[b][:].rearrange("p (a c) -> p a c", a=8), AF.Copy,
                      reads=[("ptp", b)], writes=[("hT", t)])
            hT_all = [("hT", t) for t in range(G // 128)]
            for fc in range(NFC):
                pb = (gi % 2) * 2
                gi += 1
                for kc in range(8):
                    k.mm(pgu[pb][:], wg[:, kc, fc * 128:(fc + 1) * 128], hT[:, kc, :], kc == 0, kc == 7,
                         reads=hT_all + [("w", id(wg))], writes=[("pgu", pb)])
                for kc in range(8):
                    k.mm(pgu[pb + 1][:], wu[:, kc, fc * 128:(fc + 1) * 128], hT[:, kc, :], kc == 0, kc == 7,
                         reads=hT_all + [("w", id(wu))], writes=[("pgu", pb + 1)])
                sb_ = fc % 2
                k.act(sg[sb_][:], pgu[pb][:], AF.Silu, reads=[("pgu", pb)], writes=[("sg", sb_)])
                k.v("dve", "tensor_tensor", reads=[("sg", sb_), ("pgu", pb + 1)], writes=[("aT", fc)],
                    out=aT[:, fc, :], in0=sg[sb_][:], in1=pgu[pb + 1][:], op=ALU.mult)
            aT_all = [("aT", fc) for fc in range(NFC)]
            for t in range(G // 128):
                r0 = g * G + t * 128
                for hh in range(2):
                    b = di % 2
                    di += 1
                    k.dma("pool", xr[b][:], xsrc[r0:r0 + 128, hh * 512:(hh + 1) * 512], writes=[("xr", b)])
                    for fc in range(NFC):
                        k.mm(pdn[b][:], aT[:, fc, t * 128:(t + 1) * 128], wd[:, fc, hh * 512:(hh + 1) * 512],
                             fc == 0, fc == NFC - 1, reads=aT_all + [("w", id(wd))], writes=[("pdn", b)])
                    k.v("dve", "scalar_tensor_tensor", reads=[("pdn", b), ("xr", b)], writes=[("ob", b)],
                        out=ob[b][:], in0=pdn[b][:], scalar=0.5, in1=xr[b][:], op0=ALU.mult, op1=ALU.add)
                    k.dma("sp", xdst[r0:r0 + 128, hh * 512:(hh + 1) * 512], ob[b][:], reads=[("ob", b)])
    P.barrier()


def final_norm_phase(k, xsrc, ydst, w_norm):
    P = k.P
    with ExitStack() as es:
        gain = k.sb(es, "fgain", [128, D], F32)
        xin = [k.sb(es, "fxin", [128, D], F32) for _ in range(2)]
        yo = [k.sb(es, "fyo", [128, D], F32) for _ in range(2)]
        junk = k.sb(es, "fjunk", [128, D], BF16)
        st = [k.sb(es, "fst", [128, 4], F32) for _ in range(2)]
        k.dma("sp", gain[:], dap(w_norm.tensor, w_norm.offset, [[0, 128], [1, D]]), writes=["fgain"])
        for t in range(NT):
            b = t % 2
            r0 = t * 128
            k.dma("sp", xin[b][:], xsrc[r0:r0 + 128, :], writes=[("fxin", b)])
            k.v("dve", "scalar_tensor_tensor", reads=[("fxin", b)], writes=["fjunk", ("fst", b)],
                out=junk[:], in0=xin[b][:], scalar=1.0, in1=xin[b][:], op0=ALU.mult, op1=ALU.mult,
                accum_out=st[b][:, 0:1])
            k.act(st[b][:, 1:2], st[b][:, 0:1], AF.Sqrt, reads=[("fst", b)], writes=[("fst1", b)],
                  scale=1.0 / D, bias=EPS)
            k.v("dve", "reciprocal", reads=[("fst1", b)], writes=[("fst2", b)],
                out=st[b][:, 2:3], in_=st[b][:, 1:2])
            k.v("dve", "scalar_tensor_tensor", reads=[("fxin", b), ("fst2", b), "fgain"], writes=[("fyo", b)],
                out=yo[b][:], in0=xin[b][:], scalar=st[b][:, 2:3], in1=gain[:], op0=ALU.mult, op1=ALU.mult)
            k.dma("pool", ydst[r0:r0 + 128, :], yo[b][:], reads=[("fyo", b)])
    P.barrier()


NQK = 2176
NVC = 896
NVG = NVC + 18
R_NSAQ, R_KCMP, R_VCMP, R_KSEL, R_KWIN, R_MQ, R_MK, R_DQ, R_DK = 0, 384, 512, 640, 768, 896, 1152, 1408, 1792
C_VSEL, C_VWIN, C_MV, C_DV = 0, 128, 256, 512
FAM = {
    "full": (2560, 2432, 1),
    "win": (1536, 1408, 1),
    "cmp": (6144, 4096, 16),
    "dil0": (1152, 1024, 1),
    "dil1": (1536, 1408, 1),
    "dil2": (3072, 2944, 1),
}
DIL = ((128, 1), (512, 4), (2048, 16))
SCALE = 0.125
NEGP = -240000.0


def np_rel_bucket(dist):
    n = np.maximum(dist, 0)
    nf = np.maximum(n, 16).astype(np.float32)
    lg = (np.log(nf / np.float32(16)) / np.float32(np.log(2048 / 16)) * np.float32(16)).astype(np.float32)
    large = 16 + lg.astype(np.int32)
    large = np.minimum(large, 31)
    return np.where(n < 16, n, large)


def host_consts():
    c = {}
    c["c_ident"] = np.eye(128, dtype=np.float32)
    c["c_flip"] = np.ascontiguousarray(np.eye(128, dtype=np.float32)[::-1])
    def fam_oh(length, off, valid_fn):
        w = np.arange(length)
        d = w - off
        ok = valid_fn(d)
        b = np_rel_bucket(d)
        oh = np.zeros((33, length), np.float32)
        oh[b[ok], w[ok]] = 1.0
        oh[32, ~ok] = 1.0
        return oh
    c["oh_full"] = fam_oh(2560, 511, lambda d: d >= 0)
    c["oh_win"] = fam_oh(1536, 511, lambda d: (d >= 0) & (d < 512))
    c["oh_cmp"] = fam_oh(6144, 2063, lambda d: d >= 0)
    for g, (W, dl) in enumerate(DIL):
        c["oh_dil%d" % g] = fam_oh(FAM["dil%d" % g][0], 511, lambda d: (d >= 0) & (d <= W) & (d % dl == 0))
    c["c_negrow"] = np.full((1, 16), NEGM, np.float32)
    n_cmp = 255
    c_start = np.arange(n_cmp) * 16
    s_start = np.arange(64) * 64
    ov = (c_start[:, None] < s_start[None, :] + 64) & (c_start[:, None] + 32 > s_start[None, :])
    cts = np.zeros((256, 65), np.float32)
    cts[:255, :64] = ov
    cts[:255, 64] = 1.0
    c["c_cts"] = cts
    t = np.arange(S)
    blk = np.arange(64)
    cur = t // 64
    keep = np.ones((S, 64), np.float32)
    add = np.zeros((S, 64), np.float32)
    f0 = np.broadcast_to(blk[None, :] == 0, (S, 64))
    f1 = blk[None, :] == cur[:, None]
    f2 = blk[None, :] == cur[:, None] - 1
    fut = blk[None, :] * 64 > t[:, None]
    for f, val in ((f0, 1e4), (f2, 3e4), (f1, 2e4)):
        keep[f] = 0.0
        add[f] = val
    keep[fut] = 0.0
    add[fut] = -1e30
    c["c_keep"] = keep
    c["c_add"] = add
    c["c_esel"] = (np.arange(S)[None, :] // 64 == np.arange(64)[:, None]).astype(np.float32)
    nb = np.arange(16)
    cb = t // 256
    valid = (nb[None, :] < cb[:, None]).astype(np.float32)
    own = (nb[None, :] == cb[:, None]).astype(np.float32)
    c["c_mvalid"] = valid
    c["c_maddm"] = np.where(valid > 0, 0.0, -1e30).astype(np.float32)
    c["c_mown"] = ((own - 1.0) * (-NEGP)).astype(np.float32)
    eb = np.zeros((16, 16, 128), np.float32)
    for n in range(16):
        eb[n, n, :] = 1.0
    c["c_eb"] = eb.reshape(16, 16 * 128)
    c["c_ebig"] = (np.arange(S)[None, :] // 256 == np.arange(16)[:, None]).astype(np.float32)
    return c


CONST_SHAPES = {
    "c_ident": [128, 128], "c_flip": [128, 128], "oh_full": [33, 2560], "oh_win": [33, 1536],
    "oh_cmp": [33, 6144], "oh_dil0": [33, 1152], "oh_dil1": [33, 1536], "oh_dil2": [33, 3072],
    "c_negrow": [1, 16], "c_cts": [256, 65], "c_keep": [S, 64], "c_add": [S, 64], "c_esel": [64, S],
    "c_mvalid": [S, 16], "c_maddm": [S, 16], "c_mown": [S, 16], "c_eb": [16, 2048], "c_ebig": [16, S],
}


def bias_gen_phase(k, rel_bias, consts, gd):
    P = k.P
    with ExitStack() as es:
        tblx = k.sb(es, "tblx", [33, 16], F32)
        oh = k.sb(es, "oh", [33, 6144], F32)
        go = k.sb(es, "go", [16, 6144], F32)
        pb = [k.ps(es, "pbg", [128, 512], F32) for _ in range(2)]
        k.dma("sp", tblx[0:32, :], rel_bias, writes=["tblx"])
        k.dma("sp", tblx[32:33, :], consts["c_negrow"], writes=["tblx"])
        i = 0
        for fam, (L, _, _) in FAM.items():
            k.dma("sp", oh[:, 0:L], consts["oh_" + fam], writes=["oh"])
            for c0 in range(0, L, 512):
                b = i % 2
                i += 1
                k.mm(pb[b][0:16, :], tblx[:, :], oh[:, c0:c0 + 512], True, True,
                     reads=["tblx", "oh"], writes=[("pbg", b)])
                k.v("dve", "tensor_copy", reads=[("pbg", b)], writes=["go"], out=go[:, c0:c0 + 512],
                    in_=pb[b][0:16, :])
            k.dma("sp", gd[fam], go[:, 0:L], reads=["go"])
    P.barrier()


def make_tb(k, fam, gd, h, flipf, rev, tb, pflip, tok):
    L, W, st = FAM[fam]
    g = gd[fam]
    k.dma("sp", rev[:, 0:W], dap(g.tensor, g.offset + h * L, [[st, 128], [1, W]]), writes=["rev"])
    i = 0
    for c0 in range(0, W, 512):
        n = min(512, W - c0)
        b = i % len(pflip)
        i += 1
        k.mm(pflip[b][:, 0:n], flipf[:, :], rev[:, c0:c0 + n], True, True, reads=["flipf", "rev"],
             writes=[("pflip", b)])
        if i % 2 == 0:
            k.act(tb[:, c0:c0 + n], pflip[b][:, 0:n], AF.Copy, reads=[("pflip", b)], writes=[tok], scale=1.0 / SCALE)
        else:
            k.v("dve", "tensor_scalar", reads=[("pflip", b)], writes=[tok], out=tb[:, c0:c0 + n],
                in0=pflip[b][:, 0:n], scalar1=1.0 / SCALE, scalar2=None, op0=ALU.mult)


def proj_phase(k, xsrc, w_norm, w_qk, w_v, w_mg, ident, qkT, vtok, gtok, mgT):
    P = k.P
    with ExitStack() as es:
        hT = k.sb(es, "phT", [128, 8, S], BF16)
        gain = k.sb(es, "pgain", [128, D], F32)
        xin = [k.sb(es, "pxin", [128, D], F32) for _ in range(2)]
        junk = k.sb(es, "pjunk", [128, D], BF16)
        hn = [k.sb(es, "phn", [128, D], BF16) for _ in range(2)]
        st = [k.sb(es, "pst", [128, 4], F32) for _ in range(2)]
        wvs = k.sb(es, "wvs", [128, NVG], F32)
        wvb = k.sb(es, "wvb", [128, 8, NVG], BF16)
        wst = [k.sb(es, "wst", [128, 8, 128], F32) for _ in range(2)]
        wb = [k.sb(es, "wb", [128, 8, 128], BF16) for _ in range(2)]
        orow = [k.sb(es, "orow", [128, S], BF16) for _ in range(2)]
        vout = [k.sb(es, "vout", [128, NVC], BF16) for _ in range(2)]
        gout = [k.sb(es, "gout", [128, 18], F32) for _ in range(2)]
        ptp = [k.ps(es, "pptp", [128, 1024], BF16) for _ in range(2)]
        pmm = [k.ps(es, "ppmm", [128, 512], F32) for _ in range(4)]
        k.dma("sp", gain[:], dap(w_norm.tensor, w_norm.offset, [[0, 128], [1, D]]), writes=["pgain"])
        wvv = w_v.rearrange("(kc p) f -> p kc f", p=128)
        for kc in range(8):
            k.dma("pool", wvs[:], wvv[:, kc, :], writes=["wvs"])
            k.v("dve", "tensor_copy", reads=["wvs"], writes=["wvb"], out=wvb[:, kc, :], in_=wvs[:])
        for t in range(NT):
            b = t % 2
            r0 = t * 128
            k.dma("sp", xin[b][:], xsrc[r0:r0 + 128, :], writes=[("pxin", b)])
            k.v("dve", "scalar_tensor_tensor", reads=[("pxin", b)], writes=["pjunk", ("pst", b)],
                out=junk[:], in0=xin[b][:], scalar=1.0, in1=xin[b][:], op0=ALU.mult, op1=ALU.mult,
                accum_out=st[b][:, 0:1])
            k.act(st[b][:, 1:2], st[b][:, 0:1], AF.Sqrt, reads=[("pst", b)], writes=[("pst1", b)],
                  scale=1.0 / D, bias=EPS)
            k.v("dve", "reciprocal", reads=[("pst1", b)], writes=[("pst2", b)],
                out=st[b][:, 2:3], in_=st[b][:, 1:2])
            k.v("dve", "scalar_tensor_tensor", reads=[("pxin", b), ("pst2", b), "pgain"], writes=[("phn", b)],
                out=hn[b][:], in0=xin[b][:], scalar=st[b][:, 2:3], in1=gain[:], op0=ALU.mult, op1=ALU.mult)
            for kc in range(8):
                k.tr(ptp[b][:, kc * 128:(kc + 1) * 128], hn[b][:, kc * 128:(kc + 1) * 128], ident[:],
                     reads=[("phn", b), "ident"], writes=[("pptp", b)])
            k.act(hT[:, :, r0:r0 + 128], ptp[b][:].rearrange("p (a c) -> p a c", a=8), AF.Copy,
                  reads=[("pptp", b)], writes=[("phT", t)])
            pa, pb_ = pmm[(t % 2) * 2], pmm[(t % 2) * 2 + 1]
            ta, tb_ = ("ppmm", (t % 2) * 2), ("ppmm", (t % 2) * 2 + 1)
            for kc in range(8):
                k.mm(pa[:, 0:512], hT[:, kc, r0:r0 + 128], wvb[:, kc, 0:512], kc == 0, kc == 7,
                     reads=[("phT", t), "wvb"], writes=[ta])
            for kc in range(8):
                k.mm(pb_[:, 0:NVG - 512], hT[:, kc, r0:r0 + 128], wvb[:, kc, 512:NVG], kc == 0, kc == 7,
                     reads=[("phT", t), "wvb"], writes=[tb_])
            k.act(vout[b][:, 0:512], pa[:, 0:512], AF.Copy, reads=[ta], writes=[("vout", b)])
            k.v("dve", "tensor_copy", reads=[tb_], writes=[("vout", b)], out=vout[b][:, 512:NVC],
                in_=pb_[:, 0:NVC - 512])
            k.act(gout[b][:], pb_[:, NVC - 512:NVG - 512], AF.Sigmoid, reads=[tb_], writes=[("gout", b)])
            k.dma("pool", vtok[r0:r0 + 128, :], vout[b][:], reads=[("vout", b)])
            k.dma("pool", gtok[r0:r0 + 128, :], gout[b][:], reads=[("gout", b)])
        allh = [("phT", t) for t in range(NT)]
        blocks = [("qk", i) for i in range(NQK // 128)] + [("mg", i) for i in range(3 * D // 128)]
        mi = 0
        for bi, (kind, i) in enumerate(blocks):
            b = bi % 2
            wsrc = (w_qk if kind == "qk" else w_mg)[:, i * 128:(i + 1) * 128].rearrange("(kc p) f -> p kc f", p=128)
            k.dma("pool", wst[b][:], wsrc, writes=[("wst", b)])
            k.v("dve", "tensor_copy", reads=[("wst", b)], writes=[("wb", b)], out=wb[b][:], in_=wst[b][:])
            for tb8 in range(8):
                pi = mi % 4
                mi += 1
                for kc in range(8):
                    k.mm(pmm[pi][:], wb[b][:, kc, :], hT[:, kc, tb8 * 512:(tb8 + 1) * 512], kc == 0, kc == 7,
                         reads=allh + [("wb", b)], writes=[("ppmm", pi)])
                if kind == "mg":
                    k.act(orow[b][:, tb8 * 512:(tb8 + 1) * 512], pmm[pi][:], AF.Sigmoid,
                          reads=[("ppmm", pi)], writes=[("orow", b)])
                elif tb8 % 2 == 0:
                    k.act(orow[b][:, tb8 * 512:(tb8 + 1) * 512], pmm[pi][:], AF.Copy,
                          reads=[("ppmm", pi)], writes=[("orow", b)])
                else:
                    k.v("dve", "tensor_copy", reads=[("ppmm", pi)], writes=[("orow", b)],
                        out=orow[b][:, tb8 * 512:(tb8 + 1) * 512], in_=pmm[pi][:])
            dst = (qkT if kind == "qk" else mgT)[i * 128:(i + 1) * 128, :]
            k.dma("sp", dst, orow[b][:], reads=[("orow", b)])
    P.barrier()


def compress_phase(k, qkT, pe_k, pe_v, phi_k1, phi_k2, phi_v1, phi_v2, kcT, vcs, identf):
    P = k.P
    C1 = 1.5957691216057308
    with ExitStack() as es:
        src = k.sb(es, "csrc", [128, S], BF16)
        w1s = k.sb(es, "w1s", [128, 8, 256], F32)
        w1b = k.sb(es, "w1b", [128, 32, 256], BF16)
        w2s = k.sb(es, "w2s", [128, 2, 64], F32)
        w2b = k.sb(es, "w2b", [128, 2, 64], BF16)
        pes = k.sb(es, "pes", [128, 64], F32)
        peb = k.sb(es, "peb", [128, 32], BF16)
        bias = k.sb(es, "cbias", [128, 2], F32)
        xa = k.sb(es, "cxa", [128, 256], F32)
        xb_ = k.sb(es, "cxb", [128, 256], F32)
        xc = k.sb(es, "cxc", [128, 256], F32)
        hid = [k.sb(es, "chid", [128, 256], BF16) for _ in range(2)]
        ko = k.sb(es, "cko", [64, 256], BF16)
        vo = k.sb(es, "cvo", [128, 64], BF16)
        ph = [k.ps(es, "cph", [128, 512], F32) for _ in range(2)]
        pbias = k.ps(es, "cpb", [128, 512], F32)
        po = k.ps(es, "cpo", [128, 512], F32)
        for which, (r0, pe, w1, w2) in enumerate(((R_KCMP, pe_k, phi_k1, phi_k2), (R_VCMP, pe_v, phi_v1, phi_v2))):
            k.dma("sp", src[:], qkT[r0:r0 + 128, :], writes=["csrc"])
            w1v = w1.rearrange("(l d) h -> d l h", d=64)
            for half in range(2):
                for l0 in range(0, 32, 8):
                    k.dma("pool", w1s[half * 64:(half + 1) * 64, :, :], w1v[:, l0:l0 + 8, :], writes=["w1s"])
                    k.v("dve", "tensor_copy", reads=["w1s"], writes=["w1b"],
                        out=w1b[half * 64:(half + 1) * 64, l0:l0 + 8, :], in_=w1s[half * 64:(half + 1) * 64, :, :])
            k.dma("sp", pes[0:32, 0:64], pe, writes=["pes"])
            k.tr(pbias[0:64, 64:96], pes[0:32, 0:64], identf[0:32, 0:32], reads=["pes", "identf"], writes=["cpb"])
            k.v("dve", "tensor_copy", reads=["cpb"], writes=["peb"], out=peb[0:64, :], in_=pbias[0:64, 64:96])
            k.dma("sp", w2s[:], w2.rearrange("(hc p) d -> p hc d", p=128), writes=["w2s"])
            k.v("dve", "tensor_copy", reads=["w2s"], writes=["w2b"], out=w2b[:], in_=w2s[:])
            for hc in range(2):
                for l in range(32):
                    k.mm(pbias[:, hc:hc + 1], w1b[0:64, l, hc * 128:(hc + 1) * 128], peb[0:64, l:l + 1],
                         l == 0, l == 31, reads=["w1b", "peb"], writes=["cpb"])
            k.v("dve", "tensor_copy", reads=["cpb"], writes=["cbias"], out=bias[:], in_=pbias[:, 0:2])
            for g in range(2):
                p0 = g * 64
                for hc in range(2):
                    for l in range(32):
                        k.mm(ph[hc][:, 0:255], w1b[p0:p0 + 64, l, hc * 128:(hc + 1) * 128],
                             src[p0:p0 + 64, l:l + 16 * 254 + 1:16], l == 0, l == 31,
                             reads=["w1b", "csrc"], writes=[("cph", hc)])
                    k.v("dve", "tensor_scalar", reads=[("cph", hc), "cbias"], writes=["cxa"], out=xa[:, 0:255],
                        in0=ph[hc][:, 0:255], scalar1=bias[:, hc:hc + 1], scalar2=None, op0=ALU.add)
                    k.v("dve", "tensor_tensor", reads=["cxa"], writes=["cxb"], out=xb_[:, 0:255], in0=xa[:, 0:255],
                        in1=xa[:, 0:255], op=ALU.mult)
                    k.v("dve", "tensor_scalar", reads=["cxb"], writes=["cxc"], out=xc[:, 0:255], in0=xb_[:, 0:255],
                        scalar1=0.044715, scalar2=1.0, op0=ALU.mult, op1=ALU.add)
                    k.v("dve", "tensor_tensor", reads=["cxc", "cxa"], writes=["cxb"], out=xb_[:, 0:255],
                        in0=xc[:, 0:255], in1=xa[:, 0:255], op=ALU.mult)
                    k.act(xc[:, 0:255], xb_[:, 0:255], AF.Sigmoid, reads=["cxb"], writes=["cxc"], scale=C1)
                    k.v("dve", "tensor_tensor", reads=["cxc", "cxa"], writes=[("chid", hc)], out=hid[hc][:, 0:255],
                        in0=xc[:, 0:255], in1=xa[:, 0:255], op=ALU.mult)
                hh = [("chid", 0), ("chid", 1)]
                if which == 0:
                    for hc in range(2):
                        k.mm(po[0:64, 0:255], w2b[:, hc, :], hid[hc][:, 0:255], hc == 0, hc == 1,
                             reads=hh + ["w2b"], writes=["cpo"])
                    k.v("dve", "tensor_copy", reads=["cpo"], writes=["cko"], out=ko[:, 0:255], in_=po[0:64, 0:255])
                    k.dma("sp", kcT[g, :, 0:255], ko[:, 0:255], reads=["cko"])
                else:
                    for ch in range(2):
                        rows = 128 if ch == 0 else 127
                        for hc in range(2):
                            k.mm(po[0:rows, 0:64], hid[hc][:, ch * 128:ch * 128 + rows], w2b[:, hc, :],
                                 hc == 0, hc == 1, reads=hh + ["w2b"], writes=["cpo"])
                        k.v("dve", "tensor_copy", reads=["cpo"], writes=["cvo"], out=vo[0:rows, :], in_=po[0:rows, 0:64])
                        k.dma("sp", vcs[g, ch * 128:ch * 128 + rows, :], vo[0:rows, :], reads=["cvo"])
    P.barrier()


FILL_CNT = 0
FILL_N = 512


class AttnCx:
    def __init__(self, k, es, nU, ident):
        self.ident = ident
        if FILL_CNT > 0:
            self.fill = k.ps(es, "afill", [128, 512], F32)
            self.fsrc = k.sb(es, "afsrc", [128, 512], BF16)
            k.v("dve", "memset", writes=["afsrc"], ap=self.fsrc[:], constant=0.0)
        self.S = [k.ps(es, "aS", [128, 512], F32) for _ in range(3)]
        self.U = [k.ps(es, "aU", [128, 4, 128], F32) for _ in range(nU)]
        self.E = [k.sb(es, "aE", [128, 512], BF16) for _ in range(5)]
        self.fifo = []
        self.npv = 0
        self.si = self.li = self.ei = 0
        self.zl = k.sb(es, "azl", [1, 128], BF16)
        self.zr = k.sb(es, "azr", [1, 512], BF16)
        k.v("dve", "memset", writes=["azl"], ap=self.zl[:], constant=0.0)
        k.v("dve", "memset", writes=["azr"], ap=self.zr[:], constant=0.0)


def emit_scores(k, cx, rows, kT, q, n, tb, cbias, mask, rd):
    si = cx.si % 3
    cx.si += 1
    Sb = cx.S[si]
    k.mm(Sb[0:rows, 0:n], kT, q, True, tb is None, reads=rd, writes=[("aS", si)])
    if tb is not None:
        k.mm(Sb[0:rows, 0:n], cx.ident[0:rows, 0:rows], tb, False, True, reads=rd + ["ident"], writes=[("aS", si)])
    ei = cx.ei % 5
    cx.ei += 1
    Eb = cx.E[ei]
    if tb is not None:
        k.act(Eb[0:rows, 0:n], Sb[0:rows, 0:n], AF.Exp, reads=[("aS", si)], writes=[("aE", ei)], scale=SCALE)
    else:
        k.act(Eb[0:rows, 0:n], Sb[0:rows, 0:n], AF.Exp, reads=[("aS", si)] + rd, writes=[("aE", ei)],
              scale=SCALE, bias=cbias)
    return ei


LOOKAHEAD = 3


def pipe_drain(k, cx, keep):
    while cx.npv > keep or (keep == 0 and cx.fifo):
        act = cx.fifo.pop(0)
        if act[0] == "pv":
            cx.npv -= 1
        act[1]()


def run_branch(k, cx, tiles, ui, utok):
    last = {}
    for i, t in enumerate(tiles):
        for qt in range(t["qt_lo"], t["qt_hi"]):
            last[qt] = i
    U = cx.U[ui]

    def zero():
        k.mm(U[:].rearrange("p a b -> p (a b)"), cx.zl[0:1, :], cx.zr[0:1, :], True, False,
             reads=["azl", "azr"], writes=[utok])
    cx.fifo.append(("zero", zero))
    for i, t in enumerate(tiles):
        lo, hi = t["qt_lo"], t["qt_hi"]
        n = (hi - lo) * 128
        rows = t["rows"]
        ei = emit_scores(k, cx, rows, t["kT"], t["qfn"](lo * 128, n), n,
                         t["tbfn"](lo * 128, n) if t["tbfn"] is not None else None, t["cbias"],
                         t["maskfn"](lo * 128, n) if t["maskfn"] is not None else None, t["rd"])

        def pv(i=i, t=t, lo=lo, hi=hi, rows=rows, ei=ei):
            Eb = cx.E[ei]
            for qt in range(lo, hi):
                c0 = (qt - lo) * 128
                k.mm(U[:, qt, 0:65], Eb[0:rows, c0:c0 + 128], t["V"], False, last[qt] == i,
                     reads=[("aE", ei)] + t["rd"], writes=[utok])
        cx.fifo.append(("pv", pv))
        cx.npv += 1
        pipe_drain(k, cx, LOOKAHEAD)


def pipe_post(k, cx, fn):
    cx.fifo.append(("post", fn))


def pipe_flush(k, cx):
    pipe_drain(k, cx, 0)


def dma_split(k, q, dst, src, tok, step=8):
    for c0 in range(0, NT, step):
        k.dma(q, dst[:, c0:c0 + step, :], src[:, c0:c0 + step, :], writes=[tok])


def load_v_aug(k, q, vt, vtok, col0, tok):
    k.v("dve", "memset", writes=[tok], ap=vt[:, :, 64:65], constant=1.0)
    src = vtok[:, col0:col0 + 64].rearrange("(c p) d -> p c d", p=128)
    for c0 in range(0, NT, 8):
        k.dma(q, vt[:, c0:c0 + 8, 0:64], src[:, c0:c0 + 8, :], writes=[tok])


def cmp_tiles(qb, kc_sb, q_sb, tbc, V_sb, rd):
    qs = qb * 512
    tiles = []
    for ch in range(2):
        Dd = qs - 2048 * ch
        if Dd < 0:
            continue
        rows = 128 if ch == 0 else 127
        tiles.append(dict(
            rows=rows, kT=kc_sb[:, ch * 128:ch * 128 + rows],
            qfn=lambda c0, n, qs=qs: q_sb[:, qs + c0:qs + c0 + n],
            tbfn=lambda c0, n, Dd=Dd, rows=rows: tbc[0:rows, Dd + c0:Dd + c0 + n],
            cbias=None, maskfn=None, V=V_sb[0:rows, ch, :], qt_lo=0, qt_hi=4, rd=rd))
    return tiles


def nsa_phase(k, qkT, vtok, gtok, kcT, vcs, gd, consts, identf, ident, ytok_sb):
    P = k.P
    with ExitStack() as es:
        flipf = k.sb(es, "flipf", [128, 128], F32)
        rev = k.sb(es, "rev", [128, 4096], F32)
        tbc = k.sb(es, "tbc", [128, 4096], BF16)
        tbf = k.sb(es, "tbf", [128, 2432], BF16)
        cbf = k.sb(es, "cbf", [128, 1], F32)
        tbw = k.sb(es, "tbw", [128, 1408], BF16)
        q_sb = k.sb(es, "nq", [128, S], BF16)
        kc_sb = k.sb(es, "nkc", [128, 256], BF16)
        ctsf = k.sb(es, "ctsf", [128, 2, 65], F32)
        cts = k.sb(es, "cts", [128, 2, 65], BF16)
        vc_sb = k.sb(es, "nvc", [128, 2, 65], BF16)
        ksel = k.sb(es, "nksel", [128, S], BF16)
        kwin = k.sb(es, "nkwin", [128, S], BF16)
        vsel = k.sb(es, "nvsel", [128, NT, 65], BF16)
        vwin = k.sb(es, "nvwin", [128, NT, 65], BF16)
        eself = k.sb(es, "eself", [128, 1024], F32)
        imp = k.sb(es, "imp", [128, NT, 64], F32)
        keep = k.sb(es, "keep", [128, NT, 64], F32)
        addc = k.sb(es, "addc", [128, NT, 64], F32)
        gts = k.sb(es, "gts", [128, NT, 18], F32)
        sm = k.sb(es, "nsm", [128, 64], F32)
        wk = [k.sb(es, "nwk", [128, 64], F32) for _ in range(3)]
        m8 = [k.sb(es, "nm8", [128, 8], F32) for _ in range(2)]
        negq = k.sb(es, "negq", [128, 64], F32)
        acc = [k.sb(es, "nacc", [128, 64], F32) for _ in range(2)]
        cx = AttnCx(k, es, 3, ident)
        pfl = [k.ps(es, "pfl", [128, 512], F32) for _ in range(2)]

        k.dma("sp", flipf[:], consts["c_flip"], writes=["flipf"])
        k.dma("sp", ctsf[:], consts["c_cts"].rearrange("(c p) d -> p c d", p=128), writes=["ctsf"])
        k.v("dve", "tensor_copy", reads=["ctsf"], writes=["cts"], out=cts[:], in_=ctsf[:])
        for c0 in range(0, S, 1024):
            k.dma("sp", eself[64:128, :], consts["c_esel"][:, c0:c0 + 1024], writes=["eself"])
            k.v("dve", "tensor_copy", reads=["eself"], writes=["esel"], out=ksel[64:128, c0:c0 + 1024], in_=eself[64:128, :])
        k.v("dve", "memset", writes=["nneg"], ap=q_sb[64:128, :], constant=0.0)
        k.v("dve", "memset", writes=["nkc0"], ap=kc_sb[64:128, :], constant=0.0)
        k.v("dve", "memset", writes=["nkwin0"], ap=kwin[64:128, :], constant=0.0)
        dma_split(k, "pool", keep, consts["c_keep"].rearrange("(c p) d -> p c d", p=128), "keep")
        dma_split(k, "pool", addc, consts["c_add"].rearrange("(c p) d -> p c d", p=128), "addc")
        dma_split(k, "pool", gts, gtok.rearrange("(c p) d -> p c d", p=128), "gts")

        for g in range(2):
            k.dma("sp", kc_sb[0:64, :], kcT[g], writes=["nkc"])
            for r in range(3):
                h = g * 3 + r
                k.dma("sp", q_sb[0:64, :], qkT[R_NSAQ + h * 64:R_NSAQ + (h + 1) * 64, :], writes=["nq"])
                make_tb(k, "cmp", gd, h, flipf, rev, tbc, pfl, "tbc")
                for qb in range(8):
                    tiles = cmp_tiles(qb, kc_sb, q_sb, tbc, cts, ["nkc", "nkc0", "nneg", "nq", "tbc", "cts"])
                    run_branch(k, cx, tiles, 0, ("aU", 0))

                    def post1(qb=qb, r=r):
                        U = cx.U[0]
                        k.v("dve", "tensor_scalar", reads=[("aU", 0)], writes=["nsm"], out=sm[:, 0:4],
                            in0=U[:, :, 64], scalar1=1e-30, scalar2=None, op0=ALU.max)
                        k.v("dve", "reciprocal", reads=["nsm"], writes=["nsm2"], out=sm[:, 4:8], in_=sm[:, 0:4])
                        for qt in range(4):
                            tl = qb * 4 + qt
                            if r == 0:
                                k.v("dve", "tensor_scalar", reads=[("aU", 0), "nsm2"], writes=[("imp", tl)],
                                    out=imp[:, tl, :], in0=U[:, qt, 0:64], scalar1=sm[:, 4 + qt:5 + qt],
                                    scalar2=None, op0=ALU.mult)
                            else:
                                k.v("dve", "scalar_tensor_tensor", reads=[("aU", 0), "nsm2", ("imp", tl)],
                                    writes=[("imp", tl)], out=imp[:, tl, :], in0=U[:, qt, 0:64],
                                    scalar=sm[:, 4 + qt:5 + qt], in1=imp[:, tl, :], op0=ALU.mult, op1=ALU.add)
                    pipe_post(k, cx, post1)
                pipe_flush(k, cx)
            for tl in range(NT):
                a, b_, c_ = wk
                k.v("dve", "tensor_tensor", reads=[("imp", tl), "keep"], writes=["nwk0"], out=a[:],
                    in0=imp[:, tl, :], in1=keep[:, tl, :], op=ALU.mult)
                k.v("dve", "tensor_tensor", reads=["nwk0", "addc"], writes=["nwk1"], out=b_[:],
                    in0=a[:], in1=addc[:, tl, :], op=ALU.add)
                k.v("dve", "max", reads=["nwk1"], writes=["nm80"], out=m8[0][:], in_=b_[:])
                k.v("dve", "match_replace", reads=["nwk1", "nm80"], writes=["nwk2"], out=c_[:],
                    in_to_replace=m8[0][:], in_values=b_[:], imm_value=-3.0e38)
                k.v("dve", "max", reads=["nwk2"], writes=["nm81"], out=m8[1][:], in_=c_[:])
                k.v("dve", "tensor_scalar", reads=["nwk1", "nm81"], writes=["nwk0"], out=a[:], in0=b_[:],
                    scalar1=m8[1][:, 7:8], scalar2=None, op0=ALU.is_ge)
                k.v("dve", "tensor_scalar", reads=["nwk0"], writes=["negq"], out=negq[:], in0=a[:],
                    scalar1=-NEGP, scalar2=NEGP, op0=ALU.mult, op1=ALU.add)
                pb = tl % 2
                k.tr(pfl[pb][0:64, 0:128], negq[:, :], identf[:], reads=["negq", "identf"], writes=[("pflip", pb)])
                k.v("dve", "tensor_copy", reads=[("pflip", pb)], writes=["nneg"],
                    out=q_sb[64:128, tl * 128:(tl + 1) * 128], in_=pfl[pb][0:64, 0:128])
            k.dma("sp", ksel[0:64, :], qkT[R_KSEL + g * 64:R_KSEL + (g + 1) * 64, :], writes=["nksel"])
            k.dma("sp", kwin[0:64, :], qkT[R_KWIN + g * 64:R_KWIN + (g + 1) * 64, :], writes=["nkwin"])
            load_v_aug(k, "pool", vsel, vtok, C_VSEL + g * 64, "nvsel")
            load_v_aug(k, "pool", vwin, vtok, C_VWIN + g * 64, "nvwin")
            k.v("dve", "memset", writes=["nvc"], ap=vc_sb[:, :, 64:65], constant=1.0)
            k.dma("pool", vc_sb[:, :, 0:64], vcs[g].rearrange("(c p) d -> p c d", p=128), writes=["nvc"])
            for r in range(3):
                h = g * 3 + r
                k.dma("sp", q_sb[0:64, :], qkT[R_NSAQ + h * 64:R_NSAQ + (h + 1) * 64, :], writes=["nq"])
                make_tb(k, "cmp", gd, h, flipf, rev, tbc, pfl, "tbc")
                make_tb(k, "full", gd, h, flipf, rev, tbf, pfl, "tbf")
                k.v("dve", "tensor_scalar", reads=["tbf"], writes=["cbf"], out=cbf[:, 0:1], in0=tbf[:, 2431:2432],
                    scalar1=SCALE, scalar2=None, op0=ALU.mult)
                make_tb(k, "win", gd, h, flipf, rev, tbw, pfl, "tbw")
                for qb in range(8):
                    qs = qb * 512
                    qfn = lambda c0, n, qs=qs: q_sb[:, qs + c0:qs + c0 + n]
                    tiles = cmp_tiles(qb, kc_sb, q_sb, tbc, vc_sb, ["nkc", "nkc0", "nneg", "nq", "tbc", "nvc"])
                    run_branch(k, cx, tiles, 0, ("aU", 0))
                    tiles = []
                    for kc in range(0, (qs + 384) // 128 + 1):
                        Dl = qs - 128 * kc
                        lo = max(0, -Dl // 128)
                        far = Dl >= 1664
                        tiles.append(dict(
                            rows=128, kT=ksel[:, kc * 128:(kc + 1) * 128], qfn=qfn,
                            tbfn=None if far else (lambda c0, n, Dl=Dl: tbf[:, Dl + 384 + c0:Dl + 384 + c0 + n]),
                            cbias=cbf[:, 0:1] if far else None,
                            maskfn=None,
                            V=vsel[:, kc, :], qt_lo=lo, qt_hi=4, rd=["nksel", "nq", "tbf", "cbf", "nvsel", "esel", "nneg"]))
                    run_branch(k, cx, tiles, 1, ("aU", 1))
                    tiles = []
                    for kc in range(max(0, (qs - 512) // 128), (qs + 384) // 128 + 1):
                        Dl = qs - 128 * kc
                        lo = max(0, -Dl // 128)
                        hi = min(4, (639 - Dl) // 128 + 1)
                        tiles.append(dict(
                            rows=128, kT=kwin[:, kc * 128:(kc + 1) * 128], qfn=qfn,
                            tbfn=lambda c0, n, Dl=Dl: tbw[:, Dl + 384 + c0:Dl + 384 + c0 + n],
                            cbias=None, maskfn=None, V=vwin[:, kc, :], qt_lo=lo, qt_hi=hi,
                            rd=["nkwin", "nkwin0", "nneg", "nq", "tbw", "nvwin"]))
                    run_branch(k, cx, tiles, 2, ("aU", 2))
                    def post3(qb=qb, h=h):
                        for br in range(3):
                            U = cx.U[br]
                            k.v("dve", "tensor_scalar", reads=[("aU", br)], writes=[("nsmc", br)],
                                out=sm[:, 8 + br * 12:12 + br * 12], in0=U[:, :, 64], scalar1=1e-30, scalar2=None,
                                op0=ALU.max)
                            k.v("dve", "reciprocal", reads=[("nsmc", br)], writes=[("nsmr", br)],
                                out=sm[:, 12 + br * 12:16 + br * 12], in_=sm[:, 8 + br * 12:12 + br * 12])
                            k.v("dve", "tensor_tensor", reads=[("nsmr", br), "gts"], writes=[("nsmg", br)],
                                out=sm[:, 16 + br * 12:20 + br * 12], in0=sm[:, 12 + br * 12:16 + br * 12],
                                in1=gts[:, qb * 4:qb * 4 + 4, h * 3 + br], op=ALU.mult)
                        for qt in range(4):
                            tl = qb * 4 + qt
                            a0, a1 = acc
                            k.v("dve", "tensor_scalar", reads=[("aU", 0), ("nsmg", 0)], writes=["nacc0"], out=a0[:],
                                in0=cx.U[0][:, qt, 0:64], scalar1=sm[:, 16 + qt:17 + qt], scalar2=None, op0=ALU.mult)
                            k.v("dve", "scalar_tensor_tensor", reads=[("aU", 1), ("nsmg", 1), "nacc0"], writes=["nacc1"],
                                out=a1[:], in0=cx.U[1][:, qt, 0:64], scalar=sm[:, 28 + qt:29 + qt], in1=a0[:],
                                op0=ALU.mult, op1=ALU.add)
                            k.v("dve", "scalar_tensor_tensor", reads=[("aU", 2), ("nsmg", 2), "nacc1"],
                                writes=[("ytok", tl)], out=ytok_sb[:, tl, h * 64:(h + 1) * 64],
                                in0=cx.U[2][:, qt, 0:64], scalar=sm[:, 40 + qt:41 + qt], in1=a1[:],
                                op0=ALU.mult, op1=ALU.add)
                    pipe_post(k, cx, post3)
                pipe_flush(k, cx)
    P.barrier()


def moba_phase(k, qkT, vtok, gd, consts, identf, ident, ytok_sb):
    P = k.P
    with ExitStack() as es:
        flipf = k.sb(es, "mflipf", [128, 128], F32)
        rev = k.sb(es, "mrev", [128, 2432], F32)
        tbf = k.sb(es, "mtbf", [128, 2432], BF16)
        cbf = k.sb(es, "mcbf", [128, 1], F32)
        q_sb = k.sb(es, "mq", [128, S], BF16)
        k_sb = k.sb(es, "mk", [128, S], BF16)
        v_sb = k.sb(es, "mv", [128, NT, 65], BF16)
        ebf = k.sb(es, "ebf", [128, 1024], F32)
        valid = k.sb(es, "mvalid", [128, NT, 16], F32)
        addm = k.sb(es, "maddm", [128, NT, 16], F32)
        ownc = k.sb(es, "mown", [128, NT, 16], F32)
        km = k.sb(es, "mkm", [64, 16], F32)
        kmh = k.sb(es, "mkmh", [64, 16], BF16)
        kmhf = k.sb(es, "mkmhf", [64, 16], F32)
        kml = k.sb(es, "mkml", [64, 16], BF16)
        wk = [k.sb(es, "mwk", [128, 16], F32) for _ in range(3)]
        m8 = k.sb(es, "mm8", [128, 8], F32)
        sm = k.sb(es, "msm", [128, 8], F32)
        cx = AttnCx(k, es, 1, ident)
        pfl = [k.ps(es, "mpfl", [128, 512], F32) for _ in range(2)]
        k.dma("sp", flipf[:], consts["c_flip"], writes=["flipf"])
        k.v("dve", "memset", writes=["mneg"], ap=q_sb[64:128, :], constant=0.0)
        k.v("dve", "memset", writes=["eb"], ap=k_sb[64:128, :], constant=0.0)
        for c0 in range(0, S, 1024):
            k.dma("sp", ebf[64:80, :], consts["c_ebig"][:, c0:c0 + 1024], writes=["ebf"])
            k.v("dve", "tensor_copy", reads=["ebf"], writes=["eb"], out=k_sb[64:80, c0:c0 + 1024], in_=ebf[64:80, :])
        dma_split(k, "pool", valid, consts["c_mvalid"].rearrange("(c p) d -> p c d", p=128), "mvalid")
        dma_split(k, "pool", addm, consts["c_maddm"].rearrange("(c p) d -> p c d", p=128), "maddm")
        dma_split(k, "pool", ownc, consts["c_mown"].rearrange("(c p) d -> p c d", p=128), "mown")
        for hb in range(4):
            k.dma("sp", q_sb[0:64, :], qkT[R_MQ + hb * 64:R_MQ + (hb + 1) * 64, :], writes=["mq"])
            k.dma("sp", k_sb[0:64, :], qkT[R_MK + hb * 64:R_MK + (hb + 1) * 64, :], writes=["mk"])
            load_v_aug(k, "pool", v_sb, vtok, C_MV + hb * 64, "mv")
            make_tb(k, "full", gd, 6 + hb, flipf, rev, tbf, pfl, "tbf")
            k.v("dve", "tensor_scalar", reads=["tbf"], writes=["cbf"], out=cbf[:, 0:1], in0=tbf[:, 2431:2432],
                scalar1=SCALE, scalar2=None, op0=ALU.mult)
            k.v("dve", "tensor_reduce", reads=["mk"], writes=["mkm"], out=km[:],
                in_=k_sb[0:64, :].rearrange("p (n j) -> p n j", j=256), axis=mybir.AxisListType.X, op=ALU.add)
            k.v("dve", "tensor_scalar", reads=["mkm"], writes=["mkm2"], out=km[:], in0=km[:], scalar1=1.0 / 256,
                scalar2=None, op0=ALU.mult)
            k.v("dve", "tensor_copy", reads=["mkm2"], writes=["mkmh"], out=kmh[:], in_=km[:])
            k.v("dve", "tensor_copy", reads=["mkmh"], writes=["mkmhf"], out=kmhf[:], in_=kmh[:])
            k.v("dve", "tensor_tensor", reads=["mkm2", "mkmhf"], writes=["mkml"], out=kml[:], in0=km[:],
                in1=kmhf[:], op=ALU.subtract)
            for tl in range(NT):
                pb = tl % 2
                G = pfl[pb]
                k.mm(G[:, 0:16], q_sb[0:64, tl * 128:(tl + 1) * 128], kmh[:, :], True, False,
                     reads=["mq", "mkmh"], writes=[("pflip", pb)])
                k.mm(G[:, 0:16], q_sb[0:64, tl * 128:(tl + 1) * 128], kml[:, :], False, True,
                     reads=["mq", "mkml"], writes=[("pflip", pb)])
                a, b_, c_ = wk
                k.v("dve", "tensor_tensor", reads=[("pflip", pb), "mvalid"], writes=["mwk0"], out=a[:],
                    in0=G[:, 0:16], in1=valid[:, tl, :], op=ALU.mult)
                k.v("dve", "tensor_tensor", reads=["mwk0", "maddm"], writes=["mwk1"], out=b_[:], in0=a[:],
                    in1=addm[:, tl, :], op=ALU.add)
                k.v("dve", "max", reads=["mwk1"], writes=["mm8"], out=m8[:], in_=b_[:])
                k.v("dve", "tensor_scalar", reads=["mwk1", "mm8"], writes=["mwk2"], out=c_[:], in0=b_[:],
                    scalar1=m8[:, 2:3], scalar2=None, op0=ALU.is_ge)
                k.v("dve", "tensor_tensor", reads=["mwk2", "mvalid"], writes=["mwk0"], out=a[:], in0=c_[:],
                    in1=valid[:, tl, :], op=ALU.mult)
                k.v("dve", "scalar_tensor_tensor", reads=["mwk0", "mown"], writes=["mwk1"], out=b_[:], in0=a[:],
                    scalar=-NEGP, in1=ownc[:, tl, :], op0=ALU.mult, op1=ALU.add)
                k.tr(pfl[pb][0:16, 128:256], b_[:, :], identf[:], reads=["mwk1", "identf"], writes=[("pflip", pb)])
                k.v("dve", "tensor_copy", reads=[("pflip", pb)], writes=["mneg"],
                    out=q_sb[64:80, tl * 128:(tl + 1) * 128], in_=pfl[pb][0:16, 128:256])
            for qb in range(8):
                qs = qb * 512
                qfn = lambda c0, n, qs=qs: q_sb[:, qs + c0:qs + c0 + n]
                tiles = []
                for kc in range(0, (qs + 384) // 128 + 1):
                    Dl = qs - 128 * kc
                    lo = max(0, -Dl // 128)
                    far = Dl >= 1664
                    nblk = kc // 2
                    tiles.append(dict(
                        rows=128, kT=k_sb[:, kc * 128:(kc + 1) * 128], qfn=qfn,
                        tbfn=None if far else (lambda c0, n, Dl=Dl: tbf[:, Dl + 384 + c0:Dl + 384 + c0 + n]),
                        cbias=cbf[:, 0:1] if far else None,
                        maskfn=None,
                        V=v_sb[:, kc, :], qt_lo=lo, qt_hi=4, rd=["mk", "mq", "tbf", "cbf", "mv", "eb", "mneg"]))
                run_branch(k, cx, tiles, 0, ("aU", 0))

                def postm(qb=qb, hb=hb):
                    U = cx.U[0]
                    k.v("dve", "tensor_scalar", reads=[("aU", 0)], writes=["msm"], out=sm[:, 0:4], in0=U[:, :, 64],
                        scalar1=1e-30, scalar2=None, op0=ALU.max)
                    k.v("dve", "reciprocal", reads=["msm"], writes=["msm2"], out=sm[:, 4:8], in_=sm[:, 0:4])
                    for qt in range(4):
                        tl = qb * 4 + qt
                        k.v("dve", "tensor_scalar", reads=[("aU", 0), "msm2"], writes=[("ytok", tl)],
                            out=ytok_sb[:, tl, 384 + hb * 64:384 + (hb + 1) * 64], in0=U[:, qt, 0:64],
                            scalar1=sm[:, 4 + qt:5 + qt], scalar2=None, op0=ALU.mult)
                pipe_post(k, cx, postm)
            pipe_flush(k, cx)
    P.barrier()


def dil_phase(k, qkT, vtok, gd, consts, ident, ytok_sb):
    P = k.P
    with ExitStack() as es:
        flipf = k.sb(es, "dflipf", [128, 128], F32)
        rev = k.sb(es, "drev", [128, 2944], F32)
        tbs = [k.sb(es, "dtb", [128, FAM["dil%d" % g][1]], BF16) for g in range(3)]
        q_sb = [k.sb(es, "dq", [128, S], BF16) for _ in range(3)]
        k_sb = [k.sb(es, "dk", [128, S], BF16) for _ in range(3)]
        v_sb = [k.sb(es, "dv", [128, NT, 65], BF16) for _ in range(3)]
        sm = k.sb(es, "dsm", [128, 8], F32)
        cx = AttnCx(k, es, 1, ident)
        pfl = [k.ps(es, "dpfl", [128, 512], F32) for _ in range(2)]
        k.dma("sp", flipf[:], consts["c_flip"], writes=["flipf"])
        for g in range(3):
            k.v("dve", "memset", writes=[("dq0", g)], ap=q_sb[g][64:128, :], constant=0.0)
            k.v("dve", "memset", writes=[("dk0", g)], ap=k_sb[g][64:128, :], constant=0.0)
        for i in range(2):
            for g in range(3):
                hh = g * 2 + i
                k.dma("sp", q_sb[g][0:64, :], qkT[R_DQ + hh * 64:R_DQ + (hh + 1) * 64, :], writes=[("dq", g)])
                k.dma("sp", k_sb[g][0:64, :], qkT[R_DK + hh * 64:R_DK + (hh + 1) * 64, :], writes=[("dk", g)])
                load_v_aug(k, "pool", v_sb[g], vtok, C_DV + hh * 64, ("dv", g))
                make_tb(k, "dil%d" % g, gd, 10 + hh, flipf, rev, tbs[g], pfl, ("dtb", g))
            for qb in range(8):
                qs = qb * 512
                tiles = []
                for g, (W, dl) in enumerate(DIL):
                    for kc in range(max(0, (qs - W) // 128), (qs + 384) // 128 + 1):
                        Dl = qs - 128 * kc
                        lo = max(0, -Dl // 128)
                        hi = min(4, (W - Dl) // 128 + 1)
                        tiles.append(dict(
                            rows=128, kT=k_sb[g][:, kc * 128:(kc + 1) * 128],
                            qfn=lambda c0, n, g=g, qs=qs: q_sb[g][:, qs + c0:qs + c0 + n],
                            tbfn=lambda c0, n, g=g, Dl=Dl: tbs[g][:, Dl + 384 + c0:Dl + 384 + c0 + n],
                            cbias=None, maskfn=None, V=v_sb[g][:, kc, :], qt_lo=lo, qt_hi=hi,
                            rd=[("dq", g), ("dk", g), ("dq0", g), ("dk0", g), ("dv", g), ("dtb", g)]))
                run_branch(k, cx, tiles, 0, ("aU", 0))

                def postd(qb=qb, i=i):
                    U = cx.U[0]
                    k.v("dve", "tensor_scalar", reads=[("aU", 0)], writes=["dsm"], out=sm[:, 0:4], in0=U[:, :, 64],
                        scalar1=1e-30, scalar2=None, op0=ALU.max)
                    k.v("dve", "reciprocal", reads=["dsm"], writes=["dsm2"], out=sm[:, 4:8], in_=sm[:, 0:4])
                    for qt in range(4):
                        tl = qb * 4 + qt
                        k.v("dve", "tensor_scalar", reads=[("aU", 0), "dsm2"], writes=[("ytok", tl)],
                            out=ytok_sb[:, tl, 640 + i * 64:640 + (i + 1) * 64], in0=U[:, qt, 0:64],
                            scalar1=sm[:, 4 + qt:5 + qt], scalar2=None, op0=ALU.mult)
                pipe_post(k, cx, postd)
            pipe_flush(k, cx)
    P.barrier()


def merge_phase(k, xres, mgT, w_up_a, w_up_b, w_up_c, w_o, ident, ytok_sb):
    P = k.P
    with ExitStack() as es:
        wup = k.sb(es, "wup", [128, 6, D], BF16)
        wo = k.sb(es, "wo", [128, 8, D], BF16)
        stg = [k.sb(es, "gstg", [128, D], F32) for _ in range(2)]
        yT = k.sb(es, "gyT", [128, 6, 512], BF16)
        gt = [k.sb(es, "ggt", [128, 3, 512], BF16) for _ in range(2)]
        m1 = [k.sb(es, "gm1", [128, 512], F32) for _ in range(2)]
        m2 = [k.sb(es, "gm2", [128, 512], F32) for _ in range(2)]
        m3 = [k.sb(es, "gm3", [128, 512], F32) for _ in range(2)]
        m4 = [k.sb(es, "gm4", [128, 512], F32) for _ in range(2)]
        mT = k.sb(es, "gmT", [128, 8, 512], BF16)
        xr = [k.sb(es, "gxr", [128, 512], F32) for _ in range(2)]
        ob = [k.sb(es, "gob", [128, 512], F32) for _ in range(2)]
        ptp = [k.ps(es, "gptp", [128, 1024], BF16) for _ in range(2)]
        pu = [k.ps(es, "gpu", [128, 512], F32) for _ in range(3)]
        po = [k.ps(es, "gpo", [128, 512], F32) for _ in range(2)]
        srcs = [(w_up_a, 0), (w_up_a, 1), (w_up_a, 2), (w_up_b, 0), (w_up_b, 1), (w_up_c, 0)]
        ci = 0
        for fc, (w, j) in enumerate(srcs):
            b = ci % 2
            ci += 1
            k.dma("sp" if b == 0 else "pool", stg[b][:], w[j * 128:(j + 1) * 128, :], writes=[("gstg", b)])
            k.v("dve", "tensor_copy", reads=[("gstg", b)], writes=["wup"], out=wup[:, fc, :], in_=stg[b][:])
        for kc in range(8):
            b = ci % 2
            ci += 1
            k.dma("sp" if b == 0 else "pool", stg[b][:], w_o[kc * 128:(kc + 1) * 128, :], writes=[("gstg", b)])
            k.v("dve", "tensor_copy", reads=[("gstg", b)], writes=["wo"], out=wo[:, kc, :], in_=stg[b][:])
        ui = 0
        oi = 0
        for tb8 in range(8):
            t0 = tb8 * 512
            for fc in range(6):
                pb = fc % 2
                for t in range(4):
                    tl = tb8 * 4 + t
                    k.tr(ptp[pb][:, t * 128:(t + 1) * 128], ytok_sb[:, tl, fc * 128:(fc + 1) * 128], ident[:],
                         reads=[("ytok", tl), "ident"], writes=[("gptp", pb)])
                if fc % 2 == 0:
                    k.act(yT[:, fc, :], ptp[pb][:, 0:512], AF.Copy, reads=[("gptp", pb)], writes=[("gyT", fc)])
                else:
                    k.v("dve", "tensor_copy", reads=[("gptp", pb)], writes=[("gyT", fc)], out=yT[:, fc, :],
                        in_=ptp[pb][:, 0:512])
            for cc in range(8):
                b = ui % 2
                ui += 1
                k.dma("sp", gt[b][:], mgT.rearrange("(b r) t -> r b t", b=3)[cc * 128:(cc + 1) * 128, :, t0:t0 + 512],
                      writes=[("ggt", b)])
                groups = ((0, (0, 1, 2)), (1, (3, 4)), (2, (5,)))
                for br, fcs in groups:
                    for j, fc in enumerate(fcs):
                        k.mm(pu[br][:], wup[:, fc, cc * 128:(cc + 1) * 128], yT[:, fc, :], j == 0, j == len(fcs) - 1,
                             reads=[("gyT", fc), "wup"], writes=[("gpu", br)])
                k.v("dve", "tensor_tensor", reads=[("gpu", 0), ("ggt", b)], writes=[("gm1", b)], out=m1[b][:],
                    in0=pu[0][:], in1=gt[b][:, 0, :], op=ALU.mult)
                k.v("dve", "tensor_tensor", reads=[("gpu", 1), ("ggt", b)], writes=[("gm2", b)], out=m2[b][:],
                    in0=pu[1][:], in1=gt[b][:, 1, :], op=ALU.mult)
                k.v("dve", "tensor_tensor", reads=[("gpu", 2), ("ggt", b)], writes=[("gm3", b)], out=m3[b][:],
                    in0=pu[2][:], in1=gt[b][:, 2, :], op=ALU.mult)
                k.v("dve", "tensor_tensor", reads=[("gm1", b), ("gm2", b)], writes=[("gm4", b)], out=m4[b][:],
                    in0=m1[b][:], in1=m2[b][:], op=ALU.add)
                k.v("dve", "tensor_tensor", reads=[("gm4", b), ("gm3", b)], writes=[("gmT", cc)], out=mT[:, cc, :],
                    in0=m4[b][:], in1=m3[b][:], op=ALU.add)
            allm = [("gmT", cc) for cc in range(8)]
            for t in range(4):
                r0 = t0 + t * 128
                for hh in range(2):
                    b = oi % 2
                    oi += 1
                    k.dma("pool", xr[b][:], xres[r0:r0 + 128, hh * 512:(hh + 1) * 512], writes=[("gxr", b)])
                    for cc in range(8):
                        k.mm(po[b][:], mT[:, cc, t * 128:(t + 1) * 128], wo[:, cc, hh * 512:(hh + 1) * 512],
                             cc == 0, cc == 7, reads=allm + ["wo"], writes=[("gpo", b)])
                    k.v("dve", "tensor_tensor", reads=[("gpo", b), ("gxr", b)], writes=[("gob", b)], out=ob[b][:],
                        in0=po[b][:], in1=xr[b][:], op=ALU.add)
                    k.dma("sp", xres[r0:r0 + 128, hh * 512:(hh + 1) * 512], ob[b][:], reads=[("gob", b)])
    P.barrier()


W_QK_COLS = np.concatenate([np.arange(0, 384), np.arange(384, 512), np.arange(512, 640), np.arange(640, 768),
                            np.arange(896, 1024), np.arange(1170, 1426), np.arange(1426, 1682),
                            np.arange(1938, 2322), np.arange(2322, 2706)])
W_V_COLS = np.concatenate([np.arange(768, 896), np.arange(1024, 1152), np.arange(1682, 1938),
                           np.arange(2706, 3090), np.arange(1152, 1170)])
W_MG_COLS = np.arange(3090, 6162)


def build(stop_after=None, debug=False):
    nc = bass.Bass("TRN2", target_bir_lowering=False)

    def inp(name, shape):
        return nc.dram_tensor(name, list(shape), F32, kind="ExternalInput").ap()

    def scr(name, shape, dt):
        return nc.dram_tensor(name, list(shape), dt, kind="ExternalOutput" if debug else "Internal").ap()

    x = inp("x", [S, D])
    rel_bias = inp("rel_bias", [32, 16])
    ffn_norm = [inp("ffn1_norm", [2, D]), inp("ffn2_norm", [2, D])]
    ffn_wg = [inp("ffn1_w_gate", [2, D, DFF]), inp("ffn2_w_gate", [2, D, DFF])]
    ffn_wu = [inp("ffn1_w_up", [2, D, DFF]), inp("ffn2_w_up", [2, D, DFF])]
    ffn_wd = [inp("ffn1_w_down", [2, DFF, D]), inp("ffn2_w_down", [2, DFF, D])]
    mix_norm = inp("mix_norm", [2, D])
    w_qk = inp("w_qk", [2, D, NQK])
    w_v = inp("w_v", [2, D, NVG])
    w_mg = inp("w_mg", [2, D, 3 * D])
    pe_k = inp("nsa_pe_k", [2, 32, 64])
    pe_v = inp("nsa_pe_v", [2, 32, 64])
    phi_k1 = inp("nsa_phi_k1", [2, 2048, 256])
    phi_k2 = inp("nsa_phi_k2", [2, 256, 64])
    phi_v1 = inp("nsa_phi_v1", [2, 2048, 256])
    phi_v2 = inp("nsa_phi_v2", [2, 256, 64])
    w_up_a = inp("w_up_a", [2, 384, D])
    w_up_b = inp("w_up_b", [2, 256, D])
    w_up_c = inp("w_up_c", [2, 128, D])
    w_o = inp("w_o", [2, D, D])
    final_norm = inp("final_norm", [1, D])
    consts = {nm: inp(nm, shp) for nm, shp in CONST_SHAPES.items()}
    y = nc.dram_tensor("y", [S, D], F32, kind="ExternalOutput").ap()
    xres = scr("xres", [S, D], F32)
    qkT = scr("qkT", [NQK, S], BF16)
    vtok = scr("vtok", [S, NVC], BF16)
    gtok = scr("gtok", [S, 18], F32)
    mgT = scr("mgT", [3 * D, S], BF16)
    kcT = scr("kcT", [2, 64, 256], BF16)
    vcs = scr("vcs", [2, 256, 64], BF16)
    gd = {fam: scr("gd_" + fam, [16, L], F32) for fam, (L, _, _) in FAM.items()}
    ydbg = scr("ydbg", [S, 768], BF16) if debug else None

    P = Prog(nc)
    k = K(nc, P)
    with ExitStack() as es:
        identf = k.sb(es, "identf", [128, 128], F32)
        ident = k.sb(es, "ident", [128, 128], BF16)
        k.dma("sp", identf[:], consts["c_ident"], writes=["identf"])
        k.v("dve", "tensor_copy", reads=["identf"], writes=["ident"], out=ident[:], in_=identf[:])

        def dump_y(ytok_sb):
            if debug:
                yv = ydbg.rearrange("(c p) d -> p c d", p=128)
                for c0 in range(0, NT, 8):
                    k.dma("sp", yv[:, c0:c0 + 8, :], ytok_sb[:, c0:c0 + 8, :], reads=[("ytok", t) for t in range(NT)])
                P.barrier()

        ystack = []

        def dump_y_dummy():
            pass

        def run():
            stages = stop_after
            bias_gen_phase(k, rel_bias, consts, gd)
            for l in range(2):
                ffn_phase(k, x if l == 0 else xres, xres, ffn_norm[0][l:l + 1, :], ffn_wg[0][l], ffn_wu[0][l],
                          ffn_wd[0][l], ident)
                if stages == "ffn1":
                    return
                proj_phase(k, xres, mix_norm[l:l + 1, :], w_qk[l], w_v[l], w_mg[l], ident, qkT, vtok, gtok, mgT)
                if stages == "proj":
                    return
                compress_phase(k, qkT, pe_k[l], pe_v[l], phi_k1[l], phi_k2[l], phi_v1[l], phi_v2[l], kcT, vcs, identf)
                if stages == "cmp":
                    return
                ys = ExitStack()
                ytok_sb = k.sb(ys, "ytok", [128, NT, 768], BF16)
                ystack.append(ys)
                if stages in (None, "nsa", "attn", "merge", "l0"):
                    nsa_phase(k, qkT, vtok, gtok, kcT, vcs, gd, consts, identf, ident, ytok_sb)
                if stages == "nsa":
                    dump_y(ytok_sb)
                    return
                if stages in (None, "moba", "attn", "merge", "l0"):
                    moba_phase(k, qkT, vtok, gd, consts, identf, ident, ytok_sb)
                if stages == "moba":
                    dump_y(ytok_sb)
                    return
                dil_phase(k, qkT, vtok, gd, consts, ident, ytok_sb)
                if stages in ("dil", "attn"):
                    dump_y(ytok_sb)
                    return
                merge_phase(k, xres, mgT, w_up_a[l], w_up_b[l], w_up_c[l], w_o[l], ident, ytok_sb)
                ystack.pop().close()
                if stages == "merge":
                    return
                ffn_phase(k, xres, xres, ffn_norm[1][l:l + 1, :], ffn_wg[1][l], ffn_wu[1][l], ffn_wd[1][l], ident)
                if stages == "l0":
                    return
            final_norm_phase(k, xres, y, final_norm)
        run()
        P.barrier()
        with ExitStack() as es2:
            P.emit(es2)
        while ystack:
            ystack.pop().close()
    return nc


def make_in_maps(inputs, cores):
    consts = host_consts()
    w_in = np.asarray(inputs["w_in"])
    shared = dict(consts)
    shared["w_qk"] = np.ascontiguousarray(w_in[:, :, W_QK_COLS])
    shared["w_v"] = np.ascontiguousarray(w_in[:, :, W_V_COLS])
    shared["w_mg"] = np.ascontiguousarray(w_in[:, :, W_MG_COLS])
    for nm in ("rel_bias", "ffn1_norm", "ffn2_norm", "ffn1_w_gate", "ffn2_w_gate", "ffn1_w_up", "ffn2_w_up",
               "ffn1_w_down", "ffn2_w_down", "mix_norm", "nsa_pe_k", "nsa_pe_v", "nsa_phi_k1", "nsa_phi_k2",
               "nsa_phi_v1", "nsa_phi_v2", "w_up_a", "w_up_b", "w_up_c", "w_o"):
        shared[nm] = np.ascontiguousarray(np.asarray(inputs[nm], dtype=np.float32))
    shared["final_norm"] = np.ascontiguousarray(np.asarray(inputs["final_norm"], dtype=np.float32)).reshape(1, D)
    maps = []
    for b in cores:
        m = dict(shared)
        m["x"] = np.ascontiguousarray(np.asarray(inputs["x"][b], dtype=np.float32))
        maps.append(m)
    return maps


def kernel(**inputs):
    nc = build()
    in_maps = make_in_maps(inputs, range(8))
    res = run_bass_kernel_spmd(nc, in_maps, core_ids=list(range(8)))
    return np.stack([np.asarray(r["y"]) for r in res.results], axis=0).astype(np.float32)
```

```python
import numpy as np
from contextlib import ExitStack
import concourse.bass as bass
import concourse.mybir as mybir
from concourse.bass_utils import run_bass_kernel_spmd

F32 = mybir.dt.float32
BF16 = mybir.dt.bfloat16
AF = mybir.ActivationFunctionType
ALU = mybir.AluOpType

S = 4096
D = 1024
DFF = 2816
NFC = DFF // 128
NT = S // 128
EPS = 1e-6
NEGM = -30000.0
NDS = 16


class _Op:
    __slots__ = ("eng", "fn", "deps", "dma", "signal", "sig", "dsem", "dval", "bar")


class Prog:
    ENG = ("pe", "act", "dve", "pool", "sp")

    def __init__(self, nc):
        self.nc = nc
        self.ops = []
        self.lastw = {}
        self.readers = {}
        self.last_on = {e: None for e in self.ENG}

    def op(self, eng, fn, reads=(), writes=(), dma=False):
        o = _Op()
        o.eng, o.fn, o.dma, o.signal, o.bar = eng, fn, dma, False, None
        deps = set()
        for r in reads:
            w = self.lastw.get(r)
            if w is not None:
                deps.add(w)
        for w_ in writes:
            w = self.lastw.get(w_)
            if w is not None:
                deps.add(w)
            for rd in self.readers.get(w_, ()):
                deps.add(rd)
        idx = len(self.ops)
        for w_ in writes:
            self.lastw[w_] = idx
            self.readers[w_] = []
        for r in reads:
            self.readers.setdefault(r, []).append(idx)
        deps.discard(idx)
        best = {}
        pruned = []
        for d in deps:
            p = self.ops[d]
            if p.dma:
                pruned.append(d)
            elif best.get(p.eng, -1) < d:
                best[p.eng] = d
        pruned.extend(best.values())
        o.deps = sorted(pruned)
        for d in o.deps:
            self.ops[d].signal = True
        self.ops.append(o)
        self.last_on[eng] = idx
        return idx

    def barrier(self):
        o = _Op()
        o.eng, o.fn, o.dma, o.signal, o.deps = None, None, False, False, []
        o.bar = dict(self.last_on)
        for e, i in o.bar.items():
            if i is not None and not self.ops[i].dma:
                self.ops[i].signal = True
        self.ops.append(o)

    def emit(self, es):
        nc = self.nc
        E = {"pe": nc.tensor, "act": nc.scalar, "dve": nc.vector, "pool": nc.gpsimd, "sp": nc.sync}
        sem = {e: es.enter_context(nc.semaphore("s_" + e)) for e in self.ENG}
        dsem = {q: [es.enter_context(nc.semaphore("d_%s%d" % (q, i))) for i in range(NDS)]
                for q in ("sp", "pool")}
        cnt = {e: 0 for e in self.ENG}
        dcnt = {"sp": 0, "pool": 0}
        seen = {e: {} for e in self.ENG}

        def wait(e, s, v):
            key = id(s)
            if seen[e].get(key, 0) < v:
                E[e].wait_ge(s, v)
                seen[e][key] = v

        for o in self.ops:
            if o.bar is not None:
                for q in ("sp", "pool"):
                    k = dcnt[q]
                    if k == 0:
                        continue
                    for i in range(min(k, NDS)):
                        uses = (k - 1 - i) // NDS + 1
                        wait(q, dsem[q][i], 16 * uses)
                    E[q].sem_inc(sem[q], 1)
                    cnt[q] += 1
                for f in self.ENG:
                    for e in self.ENG:
                        if e != f and cnt[e] > 0:
                            wait(f, sem[e], cnt[e])
                continue
            e = o.eng
            for d in o.deps:
                p = self.ops[d]
                if p.dma:
                    wait(e, p.dsem, p.dval)
                else:
                    if p.eng == e and e == "pe":
                        continue
                    wait(e, sem[p.eng], p.sig)
            if o.dma:
                k = dcnt[e]
                s = dsem[e][k % NDS]
                v = 16 * (k // NDS + 1)
                if k >= NDS:
                    wait(e, s, v - 16)
                ins = o.fn()
                ins.then_inc(s, 16)
                o.dsem, o.dval = s, v
                dcnt[e] = k + 1
            else:
                ins = o.fn()
                if o.signal:
                    cnt[e] += 1
                    o.sig = cnt[e]
                    ins.then_inc(sem[e], 1)
        self.ops = []


class K:
    def __init__(self, nc, P):
        self.nc, self.P = nc, P
        self.uid = 0

    def name(self, s):
        self.uid += 1
        return "%s_%d" % (s, self.uid)

    def sb(self, es, nm, shape, dt):
        return es.enter_context(self.nc.sbuf_tensor(self.name(nm), list(shape), dt))

    def ps(self, es, nm, shape, dt):
        return es.enter_context(self.nc.psum_tensor(self.name(nm), list(shape), dt))

    def dma(self, q, out, in_, reads=(), writes=(), slow=False):
        nc = self.nc
        eng = nc.sync if q == "sp" else nc.gpsimd
        if slow:
            return self.P.op(q, lambda: eng.dma_start(out=out, in_=in_, allow_slow_non_contiguous=True),
                             reads, writes, dma=True)
        return self.P.op(q, lambda: eng.dma_start(out=out, in_=in_), reads, writes, dma=True)

    def mm(self, out, lhsT, rhs, start, stop, reads=(), writes=()):
        nc = self.nc
        return self.P.op("pe", lambda: nc.tensor.matmul(out, lhsT=lhsT, rhs=rhs, start=start, stop=stop),
                         reads, writes)

    def tr(self, out, in_, ident, reads=(), writes=()):
        nc = self.nc
        return self.P.op("pe", lambda: nc.tensor.transpose(out, in_, ident), reads, writes)

    def act(self, out, in_, func, reads=(), writes=(), **kw):
        nc = self.nc
        return self.P.op("act", lambda: nc.scalar.activation(out=out, in_=in_, func=func, **kw), reads, writes)

    def v(self, eng, name, reads=(), writes=(), **kw):
        nc = self.nc
        e = nc.vector if eng == "dve" else nc.gpsimd
        return self.P.op(eng, lambda: getattr(e, name)(**kw), reads, writes)


def dap(t, offset, pattern):
    return bass.AP(tensor=t, offset=offset, ap=[list(p) for p in pattern])


def cast(k, sel, out, in_, reads, writes):
    if sel % 2 == 0:
        k.act(out, in_, AF.Copy, reads=reads, writes=writes)
    else:
        k.v("dve", "tensor_copy", reads=reads, writes=writes, out=out, in_=in_)


def ffn_phase(k, xsrc, xdst, w_norm, w_gate, w_up, w_down, ident):
    nc, P = k.nc, k.P
    G = 512
    with ExitStack() as es:
        wg = k.sb(es, "wg", [128, 8, DFF], BF16)
        wu = k.sb(es, "wu", [128, 8, DFF], BF16)
        wd = k.sb(es, "wd", [128, NFC, D], BF16)
        HW = DFF // 2
        NSTG = 3
        stg = [k.sb(es, "stg", [128, HW], F32) for _ in range(NSTG)]
        gain = k.sb(es, "gain", [128, D], F32)
        xin = [k.sb(es, "xin", [128, D], F32) for _ in range(2)]
        hn = [k.sb(es, "hn", [128, D], BF16) for _ in range(2)]
        hT = k.sb(es, "hT", [128, 8, G], BF16)
        aT = k.sb(es, "aT", [128, NFC, G], BF16)
        sg = [k.sb(es, "sg", [128, G], F32) for _ in range(2)]
        xr = [k.sb(es, "xr", [128, 512], F32) for _ in range(2)]
        ob = [k.sb(es, "ob", [128, 512], F32) for _ in range(2)]
        st = [k.sb(es, "st", [128, 4], F32) for _ in range(2)]
        pgu = [k.ps(es, "pgu", [128, 512], F32) for _ in range(4)]
        ptp = [k.ps(es, "ptp", [128, 1024], BF16) for _ in range(2)]
        pdn = [k.ps(es, "pdn", [128, 512], F32) for _ in range(2)]

        k.dma("sp", gain[:], dap(w_norm.tensor, w_norm.offset, [[0, 128], [1, D]]), writes=["gain"])
        wgv = w_gate.rearrange("(kc p) f -> p kc f", p=128)
        wuv = w_up.rearrange("(kc p) f -> p kc f", p=128)
        wdv = w_down.rearrange("(fc p) d -> p fc d", p=128)
        wq = []
        cnt = [0]

        def chunk(dst_ap, src_ap, width, tok):
            def go():
                ci = cnt[0]
                cnt[0] += 1
                b = ci % NSTG
                k.dma("sp" if ci % 2 == 0 else "pool", stg[b][:, 0:width], src_ap, writes=[("stg", b)])
                cast(k, ci, dst_ap, stg[b][:, 0:width], [("stg", b)], [tok])
            return go
        for hh in range(2):
            for c in range(8):
                for (dst, srcv, nm) in ((wg, wgv, "wg"), (wu, wuv, "wu")):
                    wq.append(chunk(dst[:, c, hh * HW:(hh + 1) * HW], srcv[:, c, hh * HW:(hh + 1) * HW], HW,
                                    (nm, c, hh)))
        for c in range(NFC):
            wq.append(chunk(wd[:, c, :], wdv[:, c, :], D, ("wd", c)))

        def emit_w(n):
            for _ in range(n):
                if wq:
                    wq.pop(0)()

        gi = 0
        di = 0
        for g in range(S // G):
            for t in range(G // 128):
                r0 = g * G + t * 128
                b = t % 2
                k.dma("sp", xin[b][:], xsrc[r0:r0 + 128, :], writes=[("xin", b)])
                k.v("dve", "scalar_tensor_tensor", reads=[("xin", b)], writes=[("hn", b), ("st", b)],
                    out=hn[b][:], in0=xin[b][:], scalar=1.0, in1=xin[b][:], op0=ALU.mult, op1=ALU.mult,
                    accum_out=st[b][:, 0:1])
                k.act(st[b][:, 1:2], st[b][:, 0:1], AF.Sqrt, reads=[("st", b)], writes=[("st1", b)],
                      scale=1.0 / D, bias=EPS)
                k.v("dve", "reciprocal", reads=[("st1", b)], writes=[("st2", b)],
                    out=st[b][:, 2:3], in_=st[b][:, 1:2])
                k.v("dve", "scalar_tensor_tensor", reads=[("xin", b), ("st2", b), "gain"], writes=[("hn", b)],
                    out=hn[b][:], in0=xin[b][:], scalar=st[b][:, 2:3], in1=gain[:], op0=ALU.mult, op1=ALU.mult)
                for kc in range(8):
                    k.tr(ptp[b][:, kc * 128:(kc + 1) * 128], hn[b][:, kc * 128:(kc + 1) * 128], ident[:],
                         reads=[("hn", b), "ident"], writes=[("ptp", b)])
                k.act(hT[:, :, t * 128:(t + 1) * 128], ptp[b][:].rearrange("p (a c) -> p a c", a=8), AF.Copy,
                      reads=[("ptp", b)], writes=[("hT", t)])
            hT_all = [("hT", t) for t in range(G // 128)]
            emit_w(16 if g == 0 else 0)
            for fc in range(NFC):
                emit_w(2)
                pb = (gi % 2) * 2
                gi += 1
                for kc in range(8):
                    k.mm(pgu[pb][:], wg[:, kc, fc * 128:(fc + 1) * 128], hT[:, kc, :], kc == 0, kc == 7,
                         reads=hT_all + [("wg", kc, fc // (NFC // 2))], writes=[("pgu", pb)])
                for kc in range(8):
                    k.mm(pgu[pb + 1][:], wu[:, kc, fc * 128:(fc + 1) * 128], hT[:, kc, :], kc == 0, kc == 7,
                         reads=hT_all + [("wu", kc, fc // (NFC // 2))], writes=[("pgu", pb + 1)])
                sb_ = fc % 2
                k.act(sg[sb_][:], pgu[pb][:], AF.Silu, reads=[("pgu", pb)], writes=[("sg", sb_)])
                k.v("dve", "tensor_tensor", reads=[("sg", sb_), ("pgu", pb + 1)], writes=[("aT", fc)],
                    out=aT[:, fc, :], in0=sg[sb_][:], in1=pgu[pb + 1][:], op=ALU.mult)
            aT_all = [("aT", fc) for fc in range(NFC)]
            for t in range(G // 128):
                r0 = g * G + t * 128
                for hh in range(2):
                    b = di % 2
                    di += 1
                    k.dma("pool", xr[b][:], xsrc[r0:r0 + 128, hh * 512:(hh + 1) * 512], writes=[("xr", b)])
                    for fc in range(NFC):
                        k.mm(pdn[b][:], aT[:, fc, t * 128:(t + 1) * 128], wd[:, fc, hh * 512:(hh + 1) * 512],
                             fc == 0, fc == NFC - 1, reads=aT_all + [("wd", fc)], writes=[("pdn", b)])
                    k.v("dve", "scalar_tensor_tensor", reads=[("pdn", b), ("xr", b)], writes=[("ob", b)],
                        out=ob[b][:], in0=pdn[b][:], scalar=0.5, in1=xr[b][:], op0=ALU.mult, op1=ALU.add)
                    k.dma("sp", xdst[r0:r0 + 128, hh * 512:(hh + 1) * 512], ob[b][:], reads=[("ob", b)])
    P.barrier()


def final_norm_phase(k, xsrc, ydst, w_norm):
    P = k.P
    with ExitStack() as es:
        gain = k.sb(es, "fgain", [128, D], F32)
        xin = [k.sb(es, "fxin", [128, D], F32) for _ in range(2)]
        yo = [k.sb(es, "fyo", [128, D], F32) for _ in range(2)]
        junk = k.sb(es, "fjunk", [128, D], BF16)
        st = [k.sb(es, "fst", [128, 4], F32) for _ in range(2)]
        k.dma("sp", gain[:], dap(w_norm.tensor, w_norm.offset, [[0, 128], [1, D]]), writes=["fgain"])
        for t in range(NT):
            b = t % 2
            r0 = t * 128
            k.dma("sp", xin[b][:], xsrc[r0:r0 + 128, :], writes=[("fxin", b)])
            k.v("dve", "scalar_tensor_tensor", reads=[("fxin", b)], writes=["fjunk", ("fst", b)],
                out=junk[:], in0=xin[b][:], scalar=1.0, in1=xin[b][:], op0=ALU.mult, op1=ALU.mult,
                accum_out=st[b][:, 0:1])
            k.act(st[b][:, 1:2], st[b][:, 0:1], AF.Sqrt, reads=[("fst", b)], writes=[("fst1", b)],
                  scale=1.0 / D, bias=EPS)
            k.v("dve", "reciprocal", reads=[("fst1", b)], writes=[("fst2", b)],
                out=st[b][:, 2:3], in_=st[b][:, 1:2])
            k.v("dve", "scalar_tensor_tensor", reads=[("fxin", b), ("fst2", b), "fgain"], writes=[("fyo", b)],
                out=yo[b][:], in0=xin[b][:], scalar=st[b][:, 2:3], in1=gain[:], op0=ALU.mult, op1=ALU.mult)
            k.dma("pool", ydst[r0:r0 + 128, :], yo[b][:], reads=[("fyo", b)])
    P.barrier()


NQK = 2176
NVC = 896
NVG = NVC + 18
R_NSAQ, R_KCMP, R_VCMP, R_KSEL, R_KWIN, R_MQ, R_MK, R_DQ, R_DK = 0, 384, 512, 640, 768, 896, 1152, 1408, 1792
C_VSEL, C_VWIN, C_MV, C_DV = 0, 128, 256, 512
FAM = {
    "full": (2560, 2432, 1),
    "win": (1536, 1408, 1),
    "cmp": (6144, 4096, 16),
    "dil0": (1152, 1024, 1),
    "dil1": (1536, 1408, 1),
    "dil2": (3072, 2944, 1),
}
DIL = ((128, 1), (512, 4), (2048, 16))
SCALE = 0.125
NEGP = -240000.0


def np_rel_bucket(dist):
    n = np.maximum(dist, 0)
    nf = np.maximum(n, 16).astype(np.float32)
    lg = (np.log(nf / np.float32(16)) / np.float32(np.log(2048 / 16)) * np.float32(16)).astype(np.float32)
    large = 16 + lg.astype(np.int32)
    large = np.minimum(large, 31)
    return np.where(n < 16, n, large)


def host_consts():
    c = {}
    c["c_ident"] = np.eye(128, dtype=np.float32)
    c["c_flip"] = np.ascontiguousarray(np.eye(128, dtype=np.float32)[::-1])
    def fam_oh(length, off, valid_fn):
        w = np.arange(length)
        d = w - off
        ok = valid_fn(d)
        b = np_rel_bucket(d)
        oh = np.zeros((33, length), np.float32)
        oh[b[ok], w[ok]] = 1.0
        oh[32, ~ok] = 1.0
        return oh
    c["oh_full"] = fam_oh(2560, 511, lambda d: d >= 0)
    c["oh_win"] = fam_oh(1536, 511, lambda d: (d >= 0) & (d < 512))
    c["oh_cmp"] = fam_oh(6144, 2063, lambda d: d >= 0)
    for g, (W, dl) in enumerate(DIL):
        c["oh_dil%d" % g] = fam_oh(FAM["dil%d" % g][0], 511, lambda d: (d >= 0) & (d <= W) & (d % dl == 0))
    c["c_negrow"] = np.full((1, 16), NEGM, np.float32)
    n_cmp = 255
    c_start = np.arange(n_cmp) * 16
    s_start = np.arange(64) * 64
    ov = (c_start[:, None] < s_start[None, :] + 64) & (c_start[:, None] + 32 > s_start[None, :])
    cts = np.zeros((256, 65), np.float32)
    cts[:255, :64] = ov
    cts[:255, 64] = 1.0
    c["c_cts"] = cts
    t = np.arange(S)
    blk = np.arange(64)
    cur = t // 64
    keep = np.ones((S, 64), np.float32)
    add = np.zeros((S, 64), np.float32)
    f0 = np.broadcast_to(blk[None, :] == 0, (S, 64))
    f1 = blk[None, :] == cur[:, None]
    f2 = blk[None, :] == cur[:, None] - 1
    fut = blk[None, :] * 64 > t[:, None]
    for f, val in ((f0, 1e4), (f2, 3e4), (f1, 2e4)):
        keep[f] = 0.0
        add[f] = val
    keep[fut] = 0.0
    add[fut] = -1e30
    c["c_keep"] = keep
    c["c_add"] = add
    c["c_esel"] = (np.arange(S)[None, :] // 64 == np.arange(64)[:, None]).astype(np.float32)
    nb = np.arange(16)
    cb = t // 256
    valid = (nb[None, :] < cb[:, None]).astype(np.float32)
    own = (nb[None, :] == cb[:, None]).astype(np.float32)
    c["c_mvalid"] = valid
    c["c_maddm"] = np.where(valid > 0, 0.0, -1e30).astype(np.float32)
    c["c_mown"] = ((own - 1.0) * (-NEGP)).astype(np.float32)
    eb = np.zeros((16, 16, 128), np.float32)
    for n in range(16):
        eb[n, n, :] = 1.0
    c["c_eb"] = eb.reshape(16, 16 * 128)
    c["c_ebig"] = (np.arange(S)[None, :] // 256 == np.arange(16)[:, None]).astype(np.float32)
    return c


CONST_SHAPES = {
    "c_ident": [128, 128], "c_flip": [128, 128], "oh_full": [33, 2560], "oh_win": [33, 1536],
    "oh_cmp": [33, 6144], "oh_dil0": [33, 1152], "oh_dil1": [33, 1536], "oh_dil2": [33, 3072],
    "c_negrow": [1, 16], "c_cts": [256, 65], "c_keep": [S, 64], "c_add": [S, 64], "c_esel": [64, S],
    "c_mvalid": [S, 16], "c_maddm": [S, 16], "c_mown": [S, 16], "c_eb": [16, 2048], "c_ebig": [16, S],
}


def bias_gen_phase(k, rel_bias, consts, gd):
    P = k.P
    with ExitStack() as es:
        tblx = k.sb(es, "tblx", [33, 16], F32)
        oh = k.sb(es, "oh", [33, 6144], F32)
        go = k.sb(es, "go", [16, 6144], F32)
        pb = [k.ps(es, "pbg", [128, 512], F32) for _ in range(2)]
        k.dma("sp", tblx[0:32, :], rel_bias, writes=["tblx"])
        k.dma("sp", tblx[32:33, :], consts["c_negrow"], writes=["tblx"])
        i = 0
        for fam, (L, _, _) in FAM.items():
            k.dma("sp", oh[:, 0:L], consts["oh_" + fam], writes=["oh"])
            for c0 in range(0, L, 512):
                b = i % 2
                i += 1
                k.mm(pb[b][0:16, :], tblx[:, :], oh[:, c0:c0 + 512], True, True,
                     reads=["tblx", "oh"], writes=[("pbg", b)])
                k.v("dve", "tensor_copy", reads=[("pbg", b)], writes=["go"], out=go[:, c0:c0 + 512],
                    in_=pb[b][0:16, :])
            k.dma("sp", gd[fam], go[:, 0:L], reads=["go"])
    P.barrier()


def make_tb(k, fam, gd, h, flipf, rev, tb, pflip, tok):
    L, W, st = FAM[fam]
    g = gd[fam]
    k.dma("sp", rev[:, 0:W], dap(g.tensor, g.offset + h * L, [[st, 128], [1, W]]), writes=["rev"])
    i = 0
    for c0 in range(0, W, 512):
        n = min(512, W - c0)
        b = i % len(pflip)
        i += 1
        k.mm(pflip[b][:, 0:n], flipf[:, :], rev[:, c0:c0 + n], True, True, reads=["flipf", "rev"],
             writes=[("pflip", b)])
        if i % 2 == 0:
            k.act(tb[:, c0:c0 + n], pflip[b][:, 0:n], AF.Copy, reads=[("pflip", b)], writes=[tok], scale=1.0 / SCALE)
        else:
            k.v("dve", "tensor_scalar", reads=[("pflip", b)], writes=[tok], out=tb[:, c0:c0 + n],
                in0=pflip[b][:, 0:n], scalar1=1.0 / SCALE, scalar2=None, op0=ALU.mult)


def proj_phase(k, xsrc, w_norm, w_qk, w_v, w_mg, ident, qkT, vtok, gtok, mgT):
    P = k.P
    with ExitStack() as es:
        hT = k.sb(es, "phT", [128, 8, S], BF16)
        gain = k.sb(es, "pgain", [128, D], F32)
        xin = [k.sb(es, "pxin", [128, D], F32) for _ in range(3)]
        hn = [k.sb(es, "phn", [128, D], BF16) for _ in range(3)]
        st = [k.sb(es, "pst", [128, 4], F32) for _ in range(3)]
        wvs = k.sb(es, "wvs", [128, NVG], F32)
        wvb = k.sb(es, "wvb", [128, 8, NVG], BF16)
        wst = [k.sb(es, "wst", [128, 8, 128], F32) for _ in range(3)]
        wb = [k.sb(es, "wb", [128, 8, 128], BF16) for _ in range(3)]
        orow = [k.sb(es, "orow", [128, S], BF16) for _ in range(2)]
        vout = [k.sb(es, "vout", [128, NVC], BF16) for _ in range(3)]
        gout = [k.sb(es, "gout", [128, 18], F32) for _ in range(3)]
        ptp = [k.ps(es, "pptp", [128, 1024], BF16) for _ in range(3)]
        pmm = [k.ps(es, "ppmm", [128, 512], F32) for _ in range(4)]
        k.dma("sp", gain[:], dap(w_norm.tensor, w_norm.offset, [[0, 128], [1, D]]), writes=["pgain"])
        wvv = w_v.rearrange("(kc p) f -> p kc f", p=128)
        for kc in range(8):
            k.dma("pool", wvs[:], wvv[:, kc, :], writes=["wvs"])
            k.v("dve", "tensor_copy", reads=["wvs"], writes=["wvb"], out=wvb[:, kc, :], in_=wvs[:])
        for t in range(NT):
            b = t % 3
            r0 = t * 128
            k.dma("sp", xin[b][:], xsrc[r0:r0 + 128, :], writes=[("pxin", b)])
            k.v("dve", "scalar_tensor_tensor", reads=[("pxin", b)], writes=[("phn", b), ("pst", b)],
                out=hn[b][:], in0=xin[b][:], scalar=1.0, in1=xin[b][:], op0=ALU.mult, op1=ALU.mult,
                accum_out=st[b][:, 0:1])
            k.act(st[b][:, 1:2], st[b][:, 0:1], AF.Sqrt, reads=[("pst", b)], writes=[("pst1", b)],
                  scale=1.0 / D, bias=EPS)
            k.v("dve", "reciprocal", reads=[("pst1", b)], writes=[("pst2", b)],
                out=st[b][:, 2:3], in_=st[b][:, 1:2])
            k.v("dve", "scalar_tensor_tensor", reads=[("pxin", b), ("pst2", b), "pgain"], writes=[("phn", b)],
                out=hn[b][:], in0=xin[b][:], scalar=st[b][:, 2:3], in1=gain[:], op0=ALU.mult, op1=ALU.mult)
            for kc in range(8):
                k.tr(ptp[b][:, kc * 128:(kc + 1) * 128], hn[b][:, kc * 128:(kc + 1) * 128], ident[:],
                     reads=[("phn", b), "ident"], writes=[("pptp", b)])
            k.act(hT[:, :, r0:r0 + 128], ptp[b][:].rearrange("p (a c) -> p a c", a=8), AF.Copy,
                  reads=[("pptp", b)], writes=[("phT", t)])
            pa, pb_ = pmm[(t % 2) * 2], pmm[(t % 2) * 2 + 1]
            ta, tb_ = ("ppmm", (t % 2) * 2), ("ppmm", (t % 2) * 2 + 1)
            for kc in range(8):
                k.mm(pa[:, 0:512], hT[:, kc, r0:r0 + 128], wvb[:, kc, 0:512], kc == 0, kc == 7,
                     reads=[("phT", t), "wvb"], writes=[ta])
            for kc in range(8):
                k.mm(pb_[:, 0:NVG - 512], hT[:, kc, r0:r0 + 128], wvb[:, kc, 512:NVG], kc == 0, kc == 7,
                     reads=[("phT", t), "wvb"], writes=[tb_])
            k.act(vout[b][:, 0:512], pa[:, 0:512], AF.Copy, reads=[ta], writes=[("vout", b)])
            k.v("dve", "tensor_copy", reads=[tb_], writes=[("vout", b)], out=vout[b][:, 512:NVC],
                in_=pb_[:, 0:NVC - 512])
            k.act(gout[b][:], pb_[:, NVC - 512:NVG - 512], AF.Sigmoid, reads=[tb_], writes=[("gout", b)])
            k.dma("pool", vtok[r0:r0 + 128, :], vout[b][:], reads=[("vout", b)])
            k.dma("pool", gtok[r0:r0 + 128, :], gout[b][:], reads=[("gout", b)])
        allh = [("phT", t) for t in range(NT)]
        blocks = [("qk", i) for i in range(NQK // 128)] + [("mg", i) for i in range(3 * D // 128)]
        mi = 0
        def prefetch(bi):
            kind, i = blocks[bi]
            b = bi % 3
            wsrc = (w_qk if kind == "qk" else w_mg)[:, i * 128:(i + 1) * 128].rearrange("(kc p) f -> p kc f", p=128)
            k.dma("pool", wst[b][:], wsrc, writes=[("wst", b)])
            if bi % 2 == 0:
                k.v("dve", "tensor_copy", reads=[("wst", b)], writes=[("wb", b)], out=wb[b][:], in_=wst[b][:])
            else:
                k.act(wb[b][:], wst[b][:], AF.Copy, reads=[("wst", b)], writes=[("wb", b)])
        prefetch(0)
        prefetch(1)
        for bi, (kind, i) in enumerate(blocks):
            b = bi % 3
            if bi + 2 < len(blocks):
                prefetch(bi + 2)
            for tb8 in range(8):
                pi = mi % 4
                mi += 1
                for kc in range(8):
                    k.mm(pmm[pi][:], wb[b][:, kc, :], hT[:, kc, tb8 * 512:(tb8 + 1) * 512], kc == 0, kc == 7,
                         reads=allh + [("wb", b)], writes=[("ppmm", pi)])
                ob_ = bi % 2
                if kind == "mg":
                    k.act(orow[ob_][:, tb8 * 512:(tb8 + 1) * 512], pmm[pi][:], AF.Sigmoid,
                          reads=[("ppmm", pi)], writes=[("orow", ob_)])
                elif tb8 % 2 == 0:
                    k.act(orow[ob_][:, tb8 * 512:(tb8 + 1) * 512], pmm[pi][:], AF.Copy,
                          reads=[("ppmm", pi)], writes=[("orow", ob_)])
                else:
                    k.v("dve", "tensor_copy", reads=[("ppmm", pi)], writes=[("orow", ob_)],
                        out=orow[ob_][:, tb8 * 512:(tb8 + 1) * 512], in_=pmm[pi][:])
            dst = (qkT if kind == "qk" else mgT)[i * 128:(i + 1) * 128, :]
            k.dma("sp", dst, orow[bi % 2][:], reads=[("orow", bi % 2)])
    P.barrier()


def compress_phase(k, qkT, pe_k, pe_v, phi_k1, phi_k2, phi_v1, phi_v2, kcT, vcs, identf):
    P = k.P
    C1 = 1.5957691216057308
    with ExitStack() as es:
        src = k.sb(es, "csrc", [128, S], BF16)
        w1s = k.sb(es, "w1s", [128, 8, 256], F32)
        w1b = k.sb(es, "w1b", [128, 32, 256], BF16)
        w2s = k.sb(es, "w2s", [128, 2, 64], F32)
        w2b = k.sb(es, "w2b", [128, 2, 64], BF16)
        pes = k.sb(es, "pes", [128, 64], F32)
        peb = k.sb(es, "peb", [128, 32], BF16)
        bias = k.sb(es, "cbias", [128, 2], F32)
        xa = k.sb(es, "cxa", [128, 256], F32)
        xb_ = k.sb(es, "cxb", [128, 256], F32)
        xc = k.sb(es, "cxc", [128, 256], F32)
        hid = [k.sb(es, "chid", [128, 256], BF16) for _ in range(2)]
        ko = k.sb(es, "cko", [64, 256], BF16)
        vo = k.sb(es, "cvo", [128, 64], BF16)
        ph = [k.ps(es, "cph", [128, 512], F32) for _ in range(2)]
        pbias = k.ps(es, "cpb", [128, 512], F32)
        po = k.ps(es, "cpo", [128, 512], F32)
        for which, (r0, pe, w1, w2) in enumerate(((R_KCMP, pe_k, phi_k1, phi_k2), (R_VCMP, pe_v, phi_v1, phi_v2))):
            k.dma("sp", src[:], qkT[r0:r0 + 128, :], writes=["csrc"])
            w1v = w1.rearrange("(l d) h -> d l h", d=64)
            for half in range(2):
                for l0 in range(0, 32, 8):
                    k.dma("pool", w1s[half * 64:(half + 1) * 64, :, :], w1v[:, l0:l0 + 8, :], writes=["w1s"])
                    k.v("dve", "tensor_copy", reads=["w1s"], writes=["w1b"],
                        out=w1b[half * 64:(half + 1) * 64, l0:l0 + 8, :], in_=w1s[half * 64:(half + 1) * 64, :, :])
            k.dma("sp", pes[0:32, 0:64], pe, writes=["pes"])
            k.tr(pbias[0:64, 64:96], pes[0:32, 0:64], identf[0:32, 0:32], reads=["pes", "identf"], writes=["cpb"])
            k.v("dve", "tensor_copy", reads=["cpb"], writes=["peb"], out=peb[0:64, :], in_=pbias[0:64, 64:96])
            k.dma("sp", w2s[:], w2.rearrange("(hc p) d -> p hc d", p=128), writes=["w2s"])
            k.v("dve", "tensor_copy", reads=["w2s"], writes=["w2b"], out=w2b[:], in_=w2s[:])
            for hc in range(2):
                for l in range(32):
                    k.mm(pbias[:, hc:hc + 1], w1b[0:64, l, hc * 128:(hc + 1) * 128], peb[0:64, l:l + 1],
                         l == 0, l == 31, reads=["w1b", "peb"], writes=["cpb"])
            k.v("dve", "tensor_copy", reads=["cpb"], writes=["cbias"], out=bias[:], in_=pbias[:, 0:2])
            for g in range(2):
                p0 = g * 64
                for hc in range(2):
                    for l in range(32):
                        k.mm(ph[hc][:, 0:255], w1b[p0:p0 + 64, l, hc * 128:(hc + 1) * 128],
                             src[p0:p0 + 64, l:l + 16 * 254 + 1:16], l == 0, l == 31,
                             reads=["w1b", "csrc"], writes=[("cph", hc)])
                    k.v("dve", "tensor_scalar", reads=[("cph", hc), "cbias"], writes=["cxa"], out=xa[:, 0:255],
                        in0=ph[hc][:, 0:255], scalar1=bias[:, hc:hc + 1], scalar2=None, op0=ALU.add)
                    k.v("dve", "tensor_tensor", reads=["cxa"], writes=["cxb"], out=xb_[:, 0:255], in0=xa[:, 0:255],
                        in1=xa[:, 0:255], op=ALU.mult)
                    k.v("dve", "tensor_scalar", reads=["cxb"], writes=["cxc"], out=xc[:, 0:255], in0=xb_[:, 0:255],
                        scalar1=0.044715, scalar2=1.0, op0=ALU.mult, op1=ALU.add)
                    k.v("dve", "tensor_tensor", reads=["cxc", "cxa"], writes=["cxb"], out=xb_[:, 0:255],
                        in0=xc[:, 0:255], in1=xa[:, 0:255], op=ALU.mult)
                    k.act(xc[:, 0:255], xb_[:, 0:255], AF.Sigmoid, reads=["cxb"], writes=["cxc"], scale=C1)
                    k.v("dve", "tensor_tensor", reads=["cxc", "cxa"], writes=[("chid", hc)], out=hid[hc][:, 0:255],
                        in0=xc[:, 0:255], in1=xa[:, 0:255], op=ALU.mult)
                hh = [("chid", 0), ("chid", 1)]
                if which == 0:
                    for hc in range(2):
                        k.mm(po[0:64, 0:255], w2b[:, hc, :], hid[hc][:, 0:255], hc == 0, hc == 1,
                             reads=hh + ["w2b"], writes=["cpo"])
                    k.v("dve", "tensor_copy", reads=["cpo"], writes=["cko"], out=ko[:, 0:255], in_=po[0:64, 0:255])
                    k.dma("sp", kcT[g, :, 0:255], ko[:, 0:255], reads=["cko"])
                else:
                    for ch in range(2):
                        rows = 128 if ch == 0 else 127
                        for hc in range(2):
                            k.mm(po[0:rows, 0:64], hid[hc][:, ch * 128:ch * 128 + rows], w2b[:, hc, :],
                                 hc == 0, hc == 1, reads=hh + ["w2b"], writes=["cpo"])
                        k.v("dve", "tensor_copy", reads=["cpo"], writes=["cvo"], out=vo[0:rows, :], in_=po[0:rows, 0:64])
                        k.dma("sp", vcs[g, ch * 128:ch * 128 + rows, :], vo[0:rows, :], reads=["cvo"])
    P.barrier()


FILL_CNT = 0
FILL_N = 512


class AttnCx:
    def __init__(self, k, es, nU, ident):
        self.ident = ident
        if FILL_CNT > 0:
            self.fill = k.ps(es, "afill", [128, 512], F32)
            self.fsrc = k.sb(es, "afsrc", [128, 512], BF16)
            k.v("dve", "memset", writes=["afsrc"], ap=self.fsrc[:], constant=0.0)
        self.S = [k.ps(es, "aS", [128, 512], F32) for _ in range(3)]
        self.U = [k.ps(es, "aU", [128, 4, 128], F32) for _ in range(nU)]
        self.E = [k.sb(es, "aE", [128, 512], BF16) for _ in range(5)]
        self.fifo = []
        self.npv = 0
        self.si = self.li = self.ei = 0
        self.zl = k.sb(es, "azl", [1, 128], BF16)
        self.zr = k.sb(es, "azr", [1, 512], BF16)
        k.v("dve", "memset", writes=["azl"], ap=self.zl[:], constant=0.0)
        k.v("dve", "memset", writes=["azr"], ap=self.zr[:], constant=0.0)


def emit_scores(k, cx, rows, kT, q, n, tb, cbias, mask, rd):
    si = cx.si % 3
    cx.si += 1
    Sb = cx.S[si]
    k.mm(Sb[0:rows, 0:n], kT, q, True, tb is None, reads=rd, writes=[("aS", si)])
    if tb is not None:
        k.mm(Sb[0:rows, 0:n], cx.ident[0:rows, 0:rows], tb, False, True, reads=rd + ["ident"], writes=[("aS", si)])
    ei = cx.ei % 5
    cx.ei += 1
    Eb = cx.E[ei]
    if tb is not None:
        k.act(Eb[0:rows, 0:n], Sb[0:rows, 0:n], AF.Exp, reads=[("aS", si)], writes=[("aE", ei)], scale=SCALE)
    else:
        k.act(Eb[0:rows, 0:n], Sb[0:rows, 0:n], AF.Exp, reads=[("aS", si)] + rd, writes=[("aE", ei)],
              scale=SCALE, bias=cbias)
    return ei


LOOKAHEAD = 3


def pipe_drain(k, cx, keep):
    while cx.npv > keep or (keep == 0 and cx.fifo):
        act = cx.fifo.pop(0)
        if act[0] == "pv":
            cx.npv -= 1
        act[1]()


def run_branch(k, cx, tiles, ui, utok):
    last = {}
    for i, t in enumerate(tiles):
        for qt in range(t["qt_lo"], t["qt_hi"]):
            last[qt] = i
    U = cx.U[ui]

    def zero():
        k.mm(U[:].rearrange("p a b -> p (a b)"), cx.zl[0:1, :], cx.zr[0:1, :], True, False,
             reads=["azl", "azr"], writes=[utok])
    cx.fifo.append(("zero", zero))
    for i, t in enumerate(tiles):
        lo, hi = t["qt_lo"], t["qt_hi"]
        n = (hi - lo) * 128
        rows = t["rows"]
        ei = emit_scores(k, cx, rows, t["kT"], t["qfn"](lo * 128, n), n,
                         t["tbfn"](lo * 128, n) if t["tbfn"] is not None else None, t["cbias"],
                         t["maskfn"](lo * 128, n) if t["maskfn"] is not None else None, t["rd"])

        def pv(i=i, t=t, lo=lo, hi=hi, rows=rows, ei=ei):
            Eb = cx.E[ei]
            for qt in range(lo, hi):
                c0 = (qt - lo) * 128
                k.mm(U[:, qt, 0:65], Eb[0:rows, c0:c0 + 128], t["V"], False, last[qt] == i,
                     reads=[("aE", ei)] + t["rd"], writes=[utok])
        cx.fifo.append(("pv", pv))
        cx.npv += 1
        pipe_drain(k, cx, LOOKAHEAD)


def pipe_post(k, cx, fn):
    cx.fifo.append(("post", fn))


def pipe_flush(k, cx):
    pipe_drain(k, cx, 0)


def dma_split(k, q, dst, src, tok, step=8):
    for c0 in range(0, NT, step):
        k.dma(q, dst[:, c0:c0 + step, :], src[:, c0:c0 + step, :], writes=[tok])


def load_v_aug(k, q, vt, vtok, col0, tok):
    k.v("dve", "memset", writes=[tok], ap=vt[:, :, 64:65], constant=1.0)
    src = vtok[:, col0:col0 + 64].rearrange("(c p) d -> p c d", p=128)
    for c0 in range(0, NT, 8):
        k.dma(q, vt[:, c0:c0 + 8, 0:64], src[:, c0:c0 + 8, :], writes=[tok])


def cmp_tiles(qb, kc_sb, q_sb, tbc, V_sb, rd):
    qs = qb * 512
    tiles = []
    for ch in range(2):
        Dd = qs - 2048 * ch
        if Dd < 0:
            continue
        rows = 128 if ch == 0 else 127
        tiles.append(dict(
            rows=rows, kT=kc_sb[:, ch * 128:ch * 128 + rows],
            qfn=lambda c0, n, qs=qs: q_sb[:, qs + c0:qs + c0 + n],
            tbfn=lambda c0, n, Dd=Dd, rows=rows: tbc[0:rows, Dd + c0:Dd + c0 + n],
            cbias=None, maskfn=None, V=V_sb[0:rows, ch, :], qt_lo=0, qt_hi=4, rd=rd))
    return tiles


def nsa_phase(k, qkT, vtok, gtok, kcT, vcs, gd, consts, identf, ident, ytok_sb):
    P = k.P
    with ExitStack() as es:
        flipf = k.sb(es, "flipf", [128, 128], F32)
        rev = k.sb(es, "rev", [128, 4096], F32)
        tbc = k.sb(es, "tbc", [128, 4096], BF16)
        tbf = k.sb(es, "tbf", [128, 2432], BF16)
        cbf = k.sb(es, "cbf", [128, 1], F32)
        tbw = k.sb(es, "tbw", [128, 1408], BF16)
        q_sb = k.sb(es, "nq", [128, S], BF16)
        kc_sb = k.sb(es, "nkc", [128, 256], BF16)
        ctsf = k.sb(es, "ctsf", [128, 2, 65], F32)
        cts = k.sb(es, "cts", [128, 2, 65], BF16)
        vc_sb = k.sb(es, "nvc", [128, 2, 65], BF16)
        ksel = k.sb(es, "nksel", [128, S], BF16)
        kwin = k.sb(es, "nkwin", [128, S], BF16)
        vsel = k.sb(es, "nvsel", [128, NT, 65], BF16)
        vwin = k.sb(es, "nvwin", [128, NT, 65], BF16)
        eself = k.sb(es, "eself", [128, 1024], F32)
        imp = k.sb(es, "imp", [128, NT, 64], F32)
        keep = k.sb(es, "keep", [128, NT, 64], F32)
        addc = k.sb(es, "addc", [128, NT, 64], F32)
        gts = k.sb(es, "gts", [128, NT, 18], F32)
        sm = k.sb(es, "nsm", [128, 64], F32)
        wk = [k.sb(es, "nwk", [128, 64], F32) for _ in range(3)]
        m8 = [k.sb(es, "nm8", [128, 8], F32) for _ in range(2)]
        negq = k.sb(es, "negq", [128, 64], F32)
        acc = [k.sb(es, "nacc", [128, 64], F32) for _ in range(2)]
        cx = AttnCx(k, es, 3, ident)
        pfl = [k.ps(es, "pfl", [128, 512], F32) for _ in range(2)]

        k.dma("sp", flipf[:], consts["c_flip"], writes=["flipf"])
        k.dma("sp", ctsf[:], consts["c_cts"].rearrange("(c p) d -> p c d", p=128), writes=["ctsf"])
        k.v("dve", "tensor_copy", reads=["ctsf"], writes=["cts"], out=cts[:], in_=ctsf[:])
        for c0 in range(0, S, 1024):
            k.dma("sp", eself[64:128, :], consts["c_esel"][:, c0:c0 + 1024], writes=["eself"])
            k.v("dve", "tensor_copy", reads=["eself"], writes=["esel"], out=ksel[64:128, c0:c0 + 1024], in_=eself[64:128, :])
        k.v("dve", "memset", writes=["nneg"], ap=q_sb[64:128, :], constant=0.0)
        k.v("dve", "memset", writes=["nkc0"], ap=kc_sb[64:128, :], constant=0.0)
        k.v("dve", "memset", writes=["nkwin0"], ap=kwin[64:128, :], constant=0.0)
        dma_split(k, "pool", keep, consts["c_keep"].rearrange("(c p) d -> p c d", p=128), "keep")
        dma_split(k, "pool", addc, consts["c_add"].rearrange("(c p) d -> p c d", p=128), "addc")
        dma_split(k, "pool", gts, gtok.rearrange("(c p) d -> p c d", p=128), "gts")

        for g in range(2):
            k.dma("sp", kc_sb[0:64, :], kcT[g], writes=["nkc"])
            for r in range(3):
                h = g * 3 + r
                k.dma("sp", q_sb[0:64, :], qkT[R_NSAQ + h * 64:R_NSAQ + (h + 1) * 64, :], writes=["nq"])
                make_tb(k, "cmp", gd, h, flipf, rev, tbc, pfl, "tbc")
                for qb in range(8):
                    tiles = cmp_tiles(qb, kc_sb, q_sb, tbc, cts, ["nkc", "nkc0", "nneg", "nq", "tbc", "cts"])
                    run_branch(k, cx, tiles, 0, ("aU", 0))

                    def post1(qb=qb, r=r):
                        U = cx.U[0]
                        k.v("dve", "tensor_scalar", reads=[("aU", 0)], writes=["nsm"], out=sm[:, 0:4],
                            in0=U[:, :, 64], scalar1=1e-30, scalar2=None, op0=ALU.max)
                        k.v("dve", "reciprocal", reads=["nsm"], writes=["nsm2"], out=sm[:, 4:8], in_=sm[:, 0:4])
                        for qt in range(4):
                            tl = qb * 4 + qt
                            if r == 0:
                                k.v("dve", "tensor_scalar", reads=[("aU", 0), "nsm2"], writes=[("imp", tl)],
                                    out=imp[:, tl, :], in0=U[:, qt, 0:64], scalar1=sm[:, 4 + qt:5 + qt],
                                    scalar2=None, op0=ALU.mult)
                            else:
                                k.v("dve", "scalar_tensor_tensor", reads=[("aU", 0), "nsm2", ("imp", tl)],
                                    writes=[("imp", tl)], out=imp[:, tl, :], in0=U[:, qt, 0:64],
                                    scalar=sm[:, 4 + qt:5 + qt], in1=imp[:, tl, :], op0=ALU.mult, op1=ALU.add)
                    pipe_post(k, cx, post1)
                pipe_flush(k, cx)
            for tl in range(NT):
                a, b_, c_ = wk
                k.v("dve", "tensor_tensor", reads=[("imp", tl), "keep"], writes=["nwk0"], out=a[:],
                    in0=imp[:, tl, :], in1=keep[:, tl, :], op=ALU.mult)
                k.v("dve", "tensor_tensor", reads=["nwk0", "addc"], writes=["nwk1"], out=b_[:],
                    in0=a[:], in1=addc[:, tl, :], op=ALU.add)
                k.v("dve", "max", reads=["nwk1"], writes=["nm80"], out=m8[0][:], in_=b_[:])
                k.v("dve", "match_replace", reads=["nwk1", "nm80"], writes=["nwk2"], out=c_[:],
                    in_to_replace=m8[0][:], in_values=b_[:], imm_value=-3.0e38)
                k.v("dve", "max", reads=["nwk2"], writes=["nm81"], out=m8[1][:], in_=c_[:])
                k.v("dve", "tensor_scalar", reads=["nwk1", "nm81"], writes=["nwk0"], out=a[:], in0=b_[:],
                    scalar1=m8[1][:, 7:8], scalar2=None, op0=ALU.is_ge)
                k.v("dve", "tensor_scalar", reads=["nwk0"], writes=["negq"], out=negq[:], in0=a[:],
                    scalar1=-NEGP, scalar2=NEGP, op0=ALU.mult, op1=ALU.add)
                pb = tl % 2
                k.tr(pfl[pb][0:64, 0:128], negq[:, :], identf[:], reads=["negq", "identf"], writes=[("pflip", pb)])
                k.v("dve", "tensor_copy", reads=[("pflip", pb)], writes=["nneg"],
                    out=q_sb[64:128, tl * 128:(tl + 1) * 128], in_=pfl[pb][0:64, 0:128])
            k.dma("sp", ksel[0:64, :], qkT[R_KSEL + g * 64:R_KSEL + (g + 1) * 64, :], writes=["nksel"])
            k.dma("sp", kwin[0:64, :], qkT[R_KWIN + g * 64:R_KWIN + (g + 1) * 64, :], writes=["nkwin"])
            load_v_aug(k, "pool", vsel, vtok, C_VSEL + g * 64, "nvsel")
            load_v_aug(k, "pool", vwin, vtok, C_VWIN + g * 64, "nvwin")
            k.v("dve", "memset", writes=["nvc"], ap=vc_sb[:, :, 64:65], constant=1.0)
            k.dma("pool", vc_sb[:, :, 0:64], vcs[g].rearrange("(c p) d -> p c d", p=128), writes=["nvc"])
            for r in range(3):
                h = g * 3 + r
                k.dma("sp", q_sb[0:64, :], qkT[R_NSAQ + h * 64:R_NSAQ + (h + 1) * 64, :], writes=["nq"])
                make_tb(k, "cmp", gd, h, flipf, rev, tbc, pfl, "tbc")
                make_tb(k, "full", gd, h, flipf, rev, tbf, pfl, "tbf")
                k.v("dve", "tensor_scalar", reads=["tbf"], writes=["cbf"], out=cbf[:, 0:1], in0=tbf[:, 2431:2432],
                    scalar1=SCALE, scalar2=None, op0=ALU.mult)
                make_tb(k, "win", gd, h, flipf, rev, tbw, pfl, "tbw")
                for qb in range(8):
                    qs = qb * 512
                    qfn = lambda c0, n, qs=qs: q_sb[:, qs + c0:qs + c0 + n]
                    tiles = cmp_tiles(qb, kc_sb, q_sb, tbc, vc_sb, ["nkc", "nkc0", "nneg", "nq", "tbc", "nvc"])
                    run_branch(k, cx, tiles, 0, ("aU", 0))
                    tiles = []
                    for kc in range(0, (qs + 384) // 128 + 1):
                        Dl = qs - 128 * kc
                        lo = max(0, -Dl // 128)
                        far = Dl >= 1664
                        tiles.append(dict(
                            rows=128, kT=ksel[:, kc * 128:(kc + 1) * 128], qfn=qfn,
                            tbfn=None if far else (lambda c0, n, Dl=Dl: tbf[:, Dl + 384 + c0:Dl + 384 + c0 + n]),
                            cbias=cbf[:, 0:1] if far else None,
                            maskfn=None,
                            V=vsel[:, kc, :], qt_lo=lo, qt_hi=4, rd=["nksel", "nq", "tbf", "cbf", "nvsel", "esel", "nneg"]))
                    run_branch(k, cx, tiles, 1, ("aU", 1))
                    tiles = []
                    for kc in range(max(0, (qs - 512) // 128), (qs + 384) // 128 + 1):
                        Dl = qs - 128 * kc
                        lo = max(0, -Dl // 128)
                        hi = min(4, (639 - Dl) // 128 + 1)
                        tiles.append(dict(
                            rows=128, kT=kwin[:, kc * 128:(kc + 1) * 128], qfn=qfn,
                            tbfn=lambda c0, n, Dl=Dl: tbw[:, Dl + 384 + c0:Dl + 384 + c0 + n],
                            cbias=None, maskfn=None, V=vwin[:, kc, :], qt_lo=lo, qt_hi=hi,
                            rd=["nkwin", "nkwin0", "nneg", "nq", "tbw", "nvwin"]))
                    run_branch(k, cx, tiles, 2, ("aU", 2))
                    def post3(qb=qb, h=h):
                        for br in range(3):
                            U = cx.U[br]
                            k.v("dve", "tensor_scalar", reads=[("aU", br)], writes=[("nsmc", br)],
                                out=sm[:, 8 + br * 12:12 + br * 12], in0=U[:, :, 64], scalar1=1e-30, scalar2=None,
                                op0=ALU.max)
                            k.v("dve", "reciprocal", reads=[("nsmc", br)], writes=[("nsmr", br)],
                                out=sm[:, 12 + br * 12:16 + br * 12], in_=sm[:, 8 + br * 12:12 + br * 12])
                            k.v("dve", "tensor_tensor", reads=[("nsmr", br), "gts"], writes=[("nsmg", br)],
                                out=sm[:, 16 + br * 12:20 + br * 12], in0=sm[:, 12 + br * 12:16 + br * 12],
                                in1=gts[:, qb * 4:qb * 4 + 4, h * 3 + br], op=ALU.mult)
                        for qt in range(4):
                            tl = qb * 4 + qt
                            a0, a1 = acc
                            k.v("dve", "tensor_scalar", reads=[("aU", 0), ("nsmg", 0)], writes=["nacc0"], out=a0[:],
                                in0=cx.U[0][:, qt, 0:64], scalar1=sm[:, 16 + qt:17 + qt], scalar2=None, op0=ALU.mult)
                            k.v("dve", "scalar_tensor_tensor", reads=[("aU", 1), ("nsmg", 1), "nacc0"], writes=["nacc1"],
                                out=a1[:], in0=cx.U[1][:, qt, 0:64], scalar=sm[:, 28 + qt:29 + qt], in1=a0[:],
                                op0=ALU.mult, op1=ALU.add)
                            k.v("dve", "scalar_tensor_tensor", reads=[("aU", 2), ("nsmg", 2), "nacc1"],
                                writes=[("ytok", tl)], out=ytok_sb[:, tl, h * 64:(h + 1) * 64],
                                in0=cx.U[2][:, qt, 0:64], scalar=sm[:, 40 + qt:41 + qt], in1=a1[:],
                                op0=ALU.mult, op1=ALU.add)
                    pipe_post(k, cx, post3)
                pipe_flush(k, cx)
    P.barrier()


def moba_phase(k, qkT, vtok, gd, consts, identf, ident, ytok_sb):
    P = k.P
    with ExitStack() as es:
        flipf = k.sb(es, "mflipf", [128, 128], F32)
        rev = k.sb(es, "mrev", [128, 2432], F32)
        tbf = k.sb(es, "mtbf", [128, 2432], BF16)
        cbf = k.sb(es, "mcbf", [128, 1], F32)
        q_sb = k.sb(es, "mq", [128, S], BF16)
        k_sb = k.sb(es, "mk", [128, S], BF16)
        v_sb = k.sb(es, "mv", [128, NT, 65], BF16)
        ebf = k.sb(es, "ebf", [128, 1024], F32)
        valid = k.sb(es, "mvalid", [128, NT, 16], F32)
        addm = k.sb(es, "maddm", [128, NT, 16], F32)
        ownc = k.sb(es, "mown", [128, NT, 16], F32)
        km = k.sb(es, "mkm", [64, 16], F32)
        kmh = k.sb(es, "mkmh", [64, 16], BF16)
        kmhf = k.sb(es, "mkmhf", [64, 16], F32)
        kml = k.sb(es, "mkml", [64, 16], BF16)
        wk = [k.sb(es, "mwk", [128, 16], F32) for _ in range(3)]
        m8 = k.sb(es, "mm8", [128, 8], F32)
        sm = k.sb(es, "msm", [128, 8], F32)
        cx = AttnCx(k, es, 1, ident)
        pfl = [k.ps(es, "mpfl", [128, 512], F32) for _ in range(2)]
        k.dma("sp", flipf[:], consts["c_flip"], writes=["flipf"])
        k.v("dve", "memset", writes=["mneg"], ap=q_sb[64:128, :], constant=0.0)
        k.v("dve", "memset", writes=["eb"], ap=k_sb[64:128, :], constant=0.0)
        for c0 in range(0, S, 1024):
            k.dma("sp", ebf[64:80, :], consts["c_ebig"][:, c0:c0 + 1024], writes=["ebf"])
            k.v("dve", "tensor_copy", reads=["ebf"], writes=["eb"], out=k_sb[64:80, c0:c0 + 1024], in_=ebf[64:80, :])
        dma_split(k, "pool", valid, consts["c_mvalid"].rearrange("(c p) d -> p c d", p=128), "mvalid")
        dma_split(k, "pool", addm, consts["c_maddm"].rearrange("(c p) d -> p c d", p=128), "maddm")
        dma_split(k, "pool", ownc, consts["c_mown"].rearrange("(c p) d -> p c d", p=128), "mown")
        for hb in range(4):
            k.dma("sp", q_sb[0:64, :], qkT[R_MQ + hb * 64:R_MQ + (hb + 1) * 64, :], writes=["mq"])
            k.dma("sp", k_sb[0:64, :], qkT[R_MK + hb * 64:R_MK + (hb + 1) * 64, :], writes=["mk"])
            load_v_aug(k, "pool", v_sb, vtok, C_MV + hb * 64, "mv")
            make_tb(k, "full", gd, 6 + hb, flipf, rev, tbf, pfl, "tbf")
            k.v("dve", "tensor_scalar", reads=["tbf"], writes=["cbf"], out=cbf[:, 0:1], in0=tbf[:, 2431:2432],
                scalar1=SCALE, scalar2=None, op0=ALU.mult)
            k.v("dve", "tensor_reduce", reads=["mk"], writes=["mkm"], out=km[:],
                in_=k_sb[0:64, :].rearrange("p (n j) -> p n j", j=256), axis=mybir.AxisListType.X, op=ALU.add)
            k.v("dve", "tensor_scalar", reads=["mkm"], writes=["mkm2"], out=km[:], in0=km[:], scalar1=1.0 / 256,
                scalar2=None, op0=ALU.mult)
            k.v("dve", "tensor_copy", reads=["mkm2"], writes=["mkmh"], out=kmh[:], in_=km[:])
            k.v("dve", "tensor_copy", reads=["mkmh"], writes=["mkmhf"], out=kmhf[:], in_=kmh[:])
            k.v("dve", "tensor_tensor", reads=["mkm2", "mkmhf"], writes=["mkml"], out=kml[:], in0=km[:],
                in1=kmhf[:], op=ALU.subtract)
            for tl in range(NT):
                pb = tl % 2
                G = pfl[pb]
                k.mm(G[:, 0:16], q_sb[0:64, tl * 128:(tl + 1) * 128], kmh[:, :], True, False,
                     reads=["mq", "mkmh"], writes=[("pflip", pb)])
                k.mm(G[:, 0:16], q_sb[0:64, tl * 128:(tl + 1) * 128], kml[:, :], False, True,
                     reads=["mq", "mkml"], writes=[("pflip", pb)])
                a, b_, c_ = wk
                k.v("dve", "tensor_tensor", reads=[("pflip", pb), "mvalid"], writes=["mwk0"], out=a[:],
                    in0=G[:, 0:16], in1=valid[:, tl, :], op=ALU.mult)
                k.v("dve", "tensor_tensor", reads=["mwk0", "maddm"], writes=["mwk1"], out=b_[:], in0=a[:],
                    in1=addm[:, tl, :], op=ALU.add)
                k.v("dve", "max", reads=["mwk1"], writes=["mm8"], out=m8[:], in_=b_[:])
                k.v("dve", "tensor_scalar", reads=["mwk1", "mm8"], writes=["mwk2"], out=c_[:], in0=b_[:],
                    scalar1=m8[:, 2:3], scalar2=None, op0=ALU.is_ge)
                k.v("dve", "tensor_tensor", reads=["mwk2", "mvalid"], writes=["mwk0"], out=a[:], in0=c_[:],
                    in1=valid[:, tl, :], op=ALU.mult)
                k.v("dve", "scalar_tensor_tensor", reads=["mwk0", "mown"], writes=["mwk1"], out=b_[:], in0=a[:],
                    scalar=-NEGP, in1=ownc[:, tl, :], op0=ALU.mult, op1=ALU.add)
                k.tr(pfl[pb][0:16, 128:256], b_[:, :], identf[:], reads=["mwk1", "identf"], writes=[("pflip", pb)])
                k.v("dve", "tensor_copy", reads=[("pflip", pb)], writes=["mneg"],
                    out=q_sb[64:80, tl * 128:(tl + 1) * 128], in_=pfl[pb][0:16, 128:256])
            for qb in range(8):
                qs = qb * 512
                qfn = lambda c0, n, qs=qs: q_sb[:, qs + c0:qs + c0 + n]
                tiles = []
                for kc in range(0, (qs + 384) // 128 + 1):
                    Dl = qs - 128 * kc
                    lo = max(0, -Dl // 128)
                    far = Dl >= 1664
                    nblk = kc // 2
                    tiles.append(dict(
                        rows=128, kT=k_sb[:, kc * 128:(kc + 1) * 128], qfn=qfn,
                        tbfn=None if far else (lambda c0, n, Dl=Dl: tbf[:, Dl + 384 + c0:Dl + 384 + c0 + n]),
                        cbias=cbf[:, 0:1] if far else None,
                        maskfn=None,
                        V=v_sb[:, kc, :], qt_lo=lo, qt_hi=4, rd=["mk", "mq", "tbf", "cbf", "mv", "eb", "mneg"]))
                run_branch(k, cx, tiles, 0, ("aU", 0))

                def postm(qb=qb, hb=hb):
                    U = cx.U[0]
                    k.v("dve", "tensor_scalar", reads=[("aU", 0)], writes=["msm"], out=sm[:, 0:4], in0=U[:, :, 64],
                        scalar1=1e-30, scalar2=None, op0=ALU.max)
                    k.v("dve", "reciprocal", reads=["msm"], writes=["msm2"], out=sm[:, 4:8], in_=sm[:, 0:4])
                    for qt in range(4):
                        tl = qb * 4 + qt
                        k.v("dve", "tensor_scalar", reads=[("aU", 0), "msm2"], writes=[("ytok", tl)],
                            out=ytok_sb[:, tl, 384 + hb * 64:384 + (hb + 1) * 64], in0=U[:, qt, 0:64],
                            scalar1=sm[:, 4 + qt:5 + qt], scalar2=None, op0=ALU.mult)
                pipe_post(k, cx, postm)
            pipe_flush(k, cx)
    P.barrier()


def dil_phase(k, qkT, vtok, gd, consts, ident, ytok_sb):
    P = k.P
    with ExitStack() as es:
        flipf = k.sb(es, "dflipf", [128, 128], F32)
        rev = k.sb(es, "drev", [128, 2944], F32)
        tbs = [k.sb(es, "dtb", [128, FAM["dil%d" % g][1]], BF16) for g in range(3)]
        q_sb = [k.sb(es, "dq", [128, S], BF16) for _ in range(3)]
        k_sb = [k.sb(es, "dk", [128, S], BF16) for _ in range(3)]
        v_sb = [k.sb(es, "dv", [128, NT, 65], BF16) for _ in range(3)]
        sm = k.sb(es, "dsm", [128, 8], F32)
        cx = AttnCx(k, es, 1, ident)
        pfl = [k.ps(es, "dpfl", [128, 512], F32) for _ in range(2)]
        k.dma("sp", flipf[:], consts["c_flip"], writes=["flipf"])
        for g in range(3):
            k.v("dve", "memset", writes=[("dq0", g)], ap=q_sb[g][64:128, :], constant=0.0)
            k.v("dve", "memset", writes=[("dk0", g)], ap=k_sb[g][64:128, :], constant=0.0)
        for i in range(2):
            for g in range(3):
                hh = g * 2 + i
                k.dma("sp", q_sb[g][0:64, :], qkT[R_DQ + hh * 64:R_DQ + (hh + 1) * 64, :], writes=[("dq", g)])
                k.dma("sp", k_sb[g][0:64, :], qkT[R_DK + hh * 64:R_DK + (hh + 1) * 64, :], writes=[("dk", g)])
                load_v_aug(k, "pool", v_sb[g], vtok, C_DV + hh * 64, ("dv", g))
                make_tb(k, "dil%d" % g, gd, 10 + hh, flipf, rev, tbs[g], pfl, ("dtb", g))
            for qb in range(8):
                qs = qb * 512
                tiles = []
                for g, (W, dl) in enumerate(DIL):
                    for kc in range(max(0, (qs - W) // 128), (qs + 384) // 128 + 1):
                        Dl = qs - 128 * kc
                        lo = max(0, -Dl // 128)
                        hi = min(4, (W - Dl) // 128 + 1)
                        tiles.append(dict(
                            rows=128, kT=k_sb[g][:, kc * 128:(kc + 1) * 128],
                            qfn=lambda c0, n, g=g, qs=qs: q_sb[g][:, qs + c0:qs + c0 + n],
                            tbfn=lambda c0, n, g=g, Dl=Dl: tbs[g][:, Dl + 384 + c0:Dl + 384 + c0 + n],
                            cbias=None, maskfn=None, V=v_sb[g][:, kc, :], qt_lo=lo, qt_hi=hi,
                            rd=[("dq", g), ("dk", g), ("dq0", g), ("dk0", g), ("dv", g), ("dtb", g)]))
                run_branch(k, cx, tiles, 0, ("aU", 0))

                def postd(qb=qb, i=i):
                    U = cx.U[0]
                    k.v("dve", "tensor_scalar", reads=[("aU", 0)], writes=["dsm"], out=sm[:, 0:4], in0=U[:, :, 64],
                        scalar1=1e-30, scalar2=None, op0=ALU.max)
                    k.v("dve", "reciprocal", reads=["dsm"], writes=["dsm2"], out=sm[:, 4:8], in_=sm[:, 0:4])
                    for qt in range(4):
                        tl = qb * 4 + qt
                        k.v("dve", "tensor_scalar", reads=[("aU", 0), "dsm2"], writes=[("ytok", tl)],
                            out=ytok_sb[:, tl, 640 + i * 64:640 + (i + 1) * 64], in0=U[:, qt, 0:64],
                            scalar1=sm[:, 4 + qt:5 + qt], scalar2=None, op0=ALU.mult)
                pipe_post(k, cx, postd)
            pipe_flush(k, cx)
    P.barrier()


def merge_phase(k, xres, mgT, w_up_a, w_up_b, w_up_c, w_o, ident, ytok_sb):
    P = k.P
    with ExitStack() as es:
        wup = k.sb(es, "wup", [128, 6, D], BF16)
        wo = k.sb(es, "wo", [128, 8, D], BF16)
        stg = [k.sb(es, "gstg", [128, D], F32) for _ in range(2)]
        yT = k.sb(es, "gyT", [128, 6, 512], BF16)
        gt = [k.sb(es, "ggt", [128, 3, 512], BF16) for _ in range(2)]
        m1 = [k.sb(es, "gm1", [128, 512], F32) for _ in range(2)]
        m2 = [k.sb(es, "gm2", [128, 512], F32) for _ in range(2)]
        m3 = [k.sb(es, "gm3", [128, 512], F32) for _ in range(2)]
        m4 = [k.sb(es, "gm4", [128, 512], F32) for _ in range(2)]
        mT = k.sb(es, "gmT", [128, 8, 512], BF16)
        xr = [k.sb(es, "gxr", [128, 512], F32) for _ in range(2)]
        ob = [k.sb(es, "gob", [128, 512], F32) for _ in range(2)]
        ptp = [k.ps(es, "gptp", [128, 1024], BF16) for _ in range(2)]
        pu = [k.ps(es, "gpu", [128, 512], F32) for _ in range(3)]
        po = [k.ps(es, "gpo", [128, 512], F32) for _ in range(2)]
        srcs = [(w_up_a, 0), (w_up_a, 1), (w_up_a, 2), (w_up_b, 0), (w_up_b, 1), (w_up_c, 0)]
        ci = 0
        for fc, (w, j) in enumerate(srcs):
            b = ci % 2
            ci += 1
            k.dma("sp" if b == 0 else "pool", stg[b][:], w[j * 128:(j + 1) * 128, :], writes=[("gstg", b)])
            k.v("dve", "tensor_copy", reads=[("gstg", b)], writes=["wup"], out=wup[:, fc, :], in_=stg[b][:])
        for kc in range(8):
            b = ci % 2
            ci += 1
            k.dma("sp" if b == 0 else "pool", stg[b][:], w_o[kc * 128:(kc + 1) * 128, :], writes=[("gstg", b)])
            k.v("dve", "tensor_copy", reads=[("gstg", b)], writes=["wo"], out=wo[:, kc, :], in_=stg[b][:])
        ui = 0
        oi = 0
        for tb8 in range(8):
            t0 = tb8 * 512
            for fc in range(6):
                pb = fc % 2
                for t in range(4):
                    tl = tb8 * 4 + t
                    k.tr(ptp[pb][:, t * 128:(t + 1) * 128], ytok_sb[:, tl, fc * 128:(fc + 1) * 128], ident[:],
                         reads=[("ytok", tl), "ident"], writes=[("gptp", pb)])
                if fc % 2 == 0:
                    k.act(yT[:, fc, :], ptp[pb][:, 0:512], AF.Copy, reads=[("gptp", pb)], writes=[("gyT", fc)])
                else:
                    k.v("dve", "tensor_copy", reads=[("gptp", pb)], writes=[("gyT", fc)], out=yT[:, fc, :],
                        in_=ptp[pb][:, 0:512])
            for cc in range(8):
                b = ui % 2
                ui += 1
                k.dma("sp", gt[b][:], mgT.rearrange("(b r) t -> r b t", b=3)[cc * 128:(cc + 1) * 128, :, t0:t0 + 512],
                      writes=[("ggt", b)])
                groups = ((0, (0, 1, 2)), (1, (3, 4)), (2, (5,)))
                for br, fcs in groups:
                    for j, fc in enumerate(fcs):
                        k.mm(pu[br][:], wup[:, fc, cc * 128:(cc + 1) * 128], yT[:, fc, :], j == 0, j == len(fcs) - 1,
                             reads=[("gyT", fc), "wup"], writes=[("gpu", br)])
                k.v("dve", "tensor_tensor", reads=[("gpu", 0), ("ggt", b)], writes=[("gm1", b)], out=m1[b][:],
                    in0=pu[0][:], in1=gt[b][:, 0, :], op=ALU.mult)
                k.v("dve", "tensor_tensor", reads=[("gpu", 1), ("ggt", b)], writes=[("gm2", b)], out=m2[b][:],
                    in0=pu[1][:], in1=gt[b][:, 1, :], op=ALU.mult)
                k.v("dve", "tensor_tensor", reads=[("gpu", 2), ("ggt", b)], writes=[("gm3", b)], out=m3[b][:],
                    in0=pu[2][:], in1=gt[b][:, 2, :], op=ALU.mult)
                k.v("dve", "tensor_tensor", reads=[("gm1", b), ("gm2", b)], writes=[("gm4", b)], out=m4[b][:],
                    in0=m1[b][:], in1=m2[b][:], op=ALU.add)
                k.v("dve", "tensor_tensor", reads=[("gm4", b), ("gm3", b)], writes=[("gmT", cc)], out=mT[:, cc, :],
                    in0=m4[b][:], in1=m3[b][:], op=ALU.add)
            allm = [("gmT", cc) for cc in range(8)]
            for t in range(4):
                r0 = t0 + t * 128
                for hh in range(2):
                    b = oi % 2
                    oi += 1
                    k.dma("pool", xr[b][:], xres[r0:r0 + 128, hh * 512:(hh + 1) * 512], writes=[("gxr", b)])
                    for cc in range(8):
                        k.mm(po[b][:], mT[:, cc, t * 128:(t + 1) * 128], wo[:, cc, hh * 512:(hh + 1) * 512],
                             cc == 0, cc == 7, reads=allm + ["wo"], writes=[("gpo", b)])
                    k.v("dve", "tensor_tensor", reads=[("gpo", b), ("gxr", b)], writes=[("gob", b)], out=ob[b][:],
                        in0=po[b][:], in1=xr[b][:], op=ALU.add)
                    k.dma("sp", xres[r0:r0 + 128, hh * 512:(hh + 1) * 512], ob[b][:], reads=[("gob", b)])
    P.barrier()


W_QK_COLS = np.concatenate([np.arange(0, 384), np.arange(384, 512), np.arange(512, 640), np.arange(640, 768),
                            np.arange(896, 1024), np.arange(1170, 1426), np.arange(1426, 1682),
                            np.arange(1938, 2322), np.arange(2322, 2706)])
W_V_COLS = np.concatenate([np.arange(768, 896), np.arange(1024, 1152), np.arange(1682, 1938),
                           np.arange(2706, 3090), np.arange(1152, 1170)])
W_MG_COLS = np.arange(3090, 6162)


def build(stop_after=None, debug=False):
    nc = bass.Bass("TRN2", target_bir_lowering=False)

    def inp(name, shape):
        return nc.dram_tensor(name, list(shape), F32, kind="ExternalInput").ap()

    def scr(name, shape, dt):
        return nc.dram_tensor(name, list(shape), dt, kind="ExternalOutput" if debug else "Internal").ap()

    x = inp("x", [S, D])
    rel_bias = inp("rel_bias", [32, 16])
    ffn_norm = [inp("ffn1_norm", [2, D]), inp("ffn2_norm", [2, D])]
    ffn_wg = [inp("ffn1_w_gate", [2, D, DFF]), inp("ffn2_w_gate", [2, D, DFF])]
    ffn_wu = [inp("ffn1_w_up", [2, D, DFF]), inp("ffn2_w_up", [2, D, DFF])]
    ffn_wd = [inp("ffn1_w_down", [2, DFF, D]), inp("ffn2_w_down", [2, DFF, D])]
    mix_norm = inp("mix_norm", [2, D])
    w_qk = inp("w_qk", [2, D, NQK])
    w_v = inp("w_v", [2, D, NVG])
    w_mg = inp("w_mg", [2, D, 3 * D])
    pe_k = inp("nsa_pe_k", [2, 32, 64])
    pe_v = inp("nsa_pe_v", [2, 32, 64])
    phi_k1 = inp("nsa_phi_k1", [2, 2048, 256])
    phi_k2 = inp("nsa_phi_k2", [2, 256, 64])
    phi_v1 = inp("nsa_phi_v1", [2, 2048, 256])
    phi_v2 = inp("nsa_phi_v2", [2, 256, 64])
    w_up_a = inp("w_up_a", [2, 384, D])
    w_up_b = inp("w_up_b", [2, 256, D])
    w_up_c = inp("w_up_c", [2, 128, D])
    w_o = inp("w_o", [2, D, D])
    final_norm = inp("final_norm", [1, D])
    consts = {nm: inp(nm, shp) for nm, shp in CONST_SHAPES.items()}
    y = nc.dram_tensor("y", [S, D], F32, kind="ExternalOutput").ap()
    xres = scr("xres", [S, D], F32)
    qkT = scr("qkT", [NQK, S], BF16)
    vtok = scr("vtok", [S, NVC], BF16)
    gtok = scr("gtok", [S, 18], F32)
    mgT = scr("mgT", [3 * D, S], BF16)
    kcT = scr("kcT", [2, 64, 256], BF16)
    vcs = scr("vcs", [2, 256, 64], BF16)
    gd = {fam: scr("gd_" + fam, [16, L], F32) for fam, (L, _, _) in FAM.items()}
    ydbg = scr("ydbg", [S, 768], BF16) if debug else None

    P = Prog(nc)
    k = K(nc, P)
    with ExitStack() as es:
        identf = k.sb(es, "identf", [128, 128], F32)
        ident = k.sb(es, "ident", [128, 128], BF16)
        k.dma("sp", identf[:], consts["c_ident"], writes=["identf"])
        k.v("dve", "tensor_copy", reads=["identf"], writes=["ident"], out=ident[:], in_=identf[:])

        def dump_y(ytok_sb):
            if debug:
                yv = ydbg.rearrange("(c p) d -> p c d", p=128)
                for c0 in range(0, NT, 8):
                    k.dma("sp", yv[:, c0:c0 + 8, :], ytok_sb[:, c0:c0 + 8, :], reads=[("ytok", t) for t in range(NT)])
                P.barrier()

        ystack = []

        def dump_y_dummy():
            pass

        def run():
            stages = stop_after
            bias_gen_phase(k, rel_bias, consts, gd)
            for l in range(2):
                ffn_phase(k, x if l == 0 else xres, xres, ffn_norm[0][l:l + 1, :], ffn_wg[0][l], ffn_wu[0][l],
                          ffn_wd[0][l], ident)
                if stages == "ffn1":
                    return
                proj_phase(k, xres, mix_norm[l:l + 1, :], w_qk[l], w_v[l], w_mg[l], ident, qkT, vtok, gtok, mgT)
                if stages == "proj":
                    return
                compress_phase(k, qkT, pe_k[l], pe_v[l], phi_k1[l], phi_k2[l], phi_v1[l], phi_v2[l], kcT, vcs, identf)
                if stages == "cmp":
                    return
                ys = ExitStack()
                ytok_sb = k.sb(ys, "ytok", [128, NT, 768], BF16)
                ystack.append(ys)
                if stages in (None, "nsa", "attn", "merge", "l0"):
                    nsa_phase(k, qkT, vtok, gtok, kcT, vcs, gd, consts, identf, ident, ytok_sb)
                if stages == "nsa":
                    dump_y(ytok_sb)
                    return
                if stages in (None, "moba", "attn", "merge", "l0"):
                    moba_phase(k, qkT, vtok, gd, consts, identf, ident, ytok_sb)
                if stages == "moba":
                    dump_y(ytok_sb)
                    return
                dil_phase(k, qkT, vtok, gd, consts, ident, ytok_sb)
                if stages in ("dil", "attn"):
                    dump_y(ytok_sb)
                    return
                merge_phase(k, xres, mgT, w_up_a[l], w_up_b[l], w_up_c[l], w_o[l], ident, ytok_sb)
                ystack.pop().close()
                if stages == "merge":
                    return
                ffn_phase(k, xres, xres, ffn_norm[1][l:l + 1, :], ffn_wg[1][l], ffn_wu[1][l], ffn_wd[1][l], ident)
                if stages == "l0":
                    return
            final_norm_phase(k, xres, y, final_norm)
        run()
        P.barrier()
        with ExitStack() as es2:
            P.emit(es2)
        while ystack:
            ystack.pop().close()
    return nc


def make_in_maps(inputs, cores):
    consts = host_consts()
    w_in = np.asarray(inputs["w_in"])
    shared = dict(consts)
    shared["w_qk"] = np.ascontiguousarray(w_in[:, :, W_QK_COLS])
    shared["w_v"] = np.ascontiguousarray(w_in[:, :, W_V_COLS])
    shared["w_mg"] = np.ascontiguousarray(w_in[:, :, W_MG_COLS])
    for nm in ("rel_bias", "ffn1_norm", "ffn2_norm", "ffn1_w_gate", "ffn2_w_gate", "ffn1_w_up", "ffn2_w_up",
               "ffn1_w_down", "ffn2_w_down", "mix_norm", "nsa_pe_k", "nsa_pe_v", "nsa_phi_k1", "nsa_phi_k2",
               "nsa_phi_v1", "nsa_phi_v2", "w_up_a", "w_up_b", "w_up_c", "w_o"):
        shared[nm] = np.ascontiguousarray(np.asarray(inputs[nm], dtype=np.float32))
    shared["final_norm"] = np.ascontiguousarray(np.asarray(inputs["final_norm"], dtype=np.float32)).reshape(1, D)
    maps = []
    for b in cores:
        m = dict(shared)
        m["x"] = np.ascontiguousarray(np.asarray(inputs["x"][b], dtype=np.float32))
        maps.append(m)
    return maps


def kernel(**inputs):
    nc = build()
    in_maps = make_in_maps(inputs, range(8))
    res = run_bass_kernel_spmd(nc, in_maps, core_ids=list(range(8)))
    return np.stack([np.asarray(r["y"]) for r in res.results], axis=0).astype(np.float32)
```

```python
import numpy as np
from contextlib import ExitStack
import concourse.bass as bass
import concourse.mybir as mybir
from concourse.bass_utils import run_bass_kernel_spmd

F32 = mybir.dt.float32
BF16 = mybir.dt.bfloat16
AF = mybir.ActivationFunctionType
ALU = mybir.AluOpType

S = 4096
D = 1024
DFF = 2816
NFC = DFF // 128
NT = S // 128
EPS = 1e-6
NEGM = -30000.0
NDS = 16


class _Op:
    __slots__ = ("eng", "fn", "deps", "dma", "signal", "sig", "dsem", "dval", "bar")


class Prog:
    ENG = ("pe", "act", "dve", "pool", "sp")

    def __init__(self, nc):
        self.nc = nc
        self.ops = []
        self.lastw = {}
        self.readers = {}
        self.last_on = {e: None for e in self.ENG}

    def op(self, eng, fn, reads=(), writes=(), dma=False):
        o = _Op()
        o.eng, o.fn, o.dma, o.signal, o.bar = eng, fn, dma, False, None
        deps = set()
        for r in reads:
            w = self.lastw.get(r)
            if w is not None:
                deps.add(w)
        for w_ in writes:
            w = self.lastw.get(w_)
            if w is not None:
                deps.add(w)
            for rd in self.readers.get(w_, ()):
                deps.add(rd)
        idx = len(self.ops)
        for w_ in writes:
            self.lastw[w_] = idx
            self.readers[w_] = []
        for r in reads:
            self.readers.setdefault(r, []).append(idx)
        deps.discard(idx)
        best = {}
        pruned = []
        for d in deps:
            p = self.ops[d]
            if p.dma:
                pruned.append(d)
            elif best.get(p.eng, -1) < d:
                best[p.eng] = d
        pruned.extend(best.values())
        o.deps = sorted(pruned)
        for d in o.deps:
            self.ops[d].signal = True
        self.ops.append(o)
        self.last_on[eng] = idx
        return idx

    def barrier(self):
        o = _Op()
        o.eng, o.fn, o.dma, o.signal, o.deps = None, None, False, False, []
        o.bar = dict(self.last_on)
        for e, i in o.bar.items():
            if i is not None and not self.ops[i].dma:
                self.ops[i].signal = True
        self.ops.append(o)

    def emit(self, es):
        nc = self.nc
        E = {"pe": nc.tensor, "act": nc.scalar, "dve": nc.vector, "pool": nc.gpsimd, "sp": nc.sync}
        sem = {e: es.enter_context(nc.semaphore("s_" + e)) for e in self.ENG}
        dsem = {q: [es.enter_context(nc.semaphore("d_%s%d" % (q, i))) for i in range(NDS)]
                for q in ("sp", "pool")}
        cnt = {e: 0 for e in self.ENG}
        dcnt = {"sp": 0, "pool": 0}
        seen = {e: {} for e in self.ENG}

        def wait(e, s, v):
            key = id(s)
            if seen[e].get(key, 0) < v:
                E[e].wait_ge(s, v)
                seen[e][key] = v

        for o in self.ops:
            if o.bar is not None:
                for q in ("sp", "pool"):
                    k = dcnt[q]
                    if k == 0:
                        continue
                    for i in range(min(k, NDS)):
                        uses = (k - 1 - i) // NDS + 1
                        wait(q, dsem[q][i], 16 * uses)
                    E[q].sem_inc(sem[q], 1)
                    cnt[q] += 1
                for f in self.ENG:
                    for e in self.ENG:
                        if e != f and cnt[e] > 0:
                            wait(f, sem[e], cnt[e])
                continue
            e = o.eng
            for d in o.deps:
                p = self.ops[d]
                if p.dma:
                    wait(e, p.dsem, p.dval)
                else:
                    if p.eng == e and e == "pe":
                        continue
                    wait(e, sem[p.eng], p.sig)
            if o.dma:
                k = dcnt[e]
                s = dsem[e][k % NDS]
                v = 16 * (k // NDS + 1)
                if k >= NDS:
                    wait(e, s, v - 16)
                ins = o.fn()
                ins.then_inc(s, 16)
                o.dsem, o.dval = s, v
                dcnt[e] = k + 1
            else:
                ins = o.fn()
                if o.signal:
                    cnt[e] += 1
                    o.sig = cnt[e]
                    ins.then_inc(sem[e], 1)
        self.ops = []


class K:
    def __init__(self, nc, P):
        self.nc, self.P = nc, P
        self.uid = 0

    def name(self, s):
        self.uid += 1
        return "%s_%d" % (s, self.uid)

    def sb(self, es, nm, shape, dt):
        return es.enter_context(self.nc.sbuf_tensor(self.name(nm), list(shape), dt))

    def ps(self, es, nm, shape, dt):
        return es.enter_context(self.nc.psum_tensor(self.name(nm), list(shape), dt))

    def dma(self, q, out, in_, reads=(), writes=(), slow=False):
        nc = self.nc
        eng = nc.sync if q == "sp" else nc.gpsimd
        if slow:
            return self.P.op(q, lambda: eng.dma_start(out=out, in_=in_, allow_slow_non_contiguous=True),
                             reads, writes, dma=True)
        return self.P.op(q, lambda: eng.dma_start(out=out, in_=in_), reads, writes, dma=True)

    def mm(self, out, lhsT, rhs, start, stop, reads=(), writes=()):
        nc = self.nc
        return self.P.op("pe", lambda: nc.tensor.matmul(out, lhsT=lhsT, rhs=rhs, start=start, stop=stop),
                         reads, writes)

    def tr(self, out, in_, ident, reads=(), writes=()):
        nc = self.nc
        return self.P.op("pe", lambda: nc.tensor.transpose(out, in_, ident), reads, writes)

    def act(self, out, in_, func, reads=(), writes=(), **kw):
        nc = self.nc
        return self.P.op("act", lambda: nc.scalar.activation(out=out, in_=in_, func=func, **kw), reads, writes)

    def v(self, eng, name, reads=(), writes=(), **kw):
        nc = self.nc
        e = nc.vector if eng == "dve" else nc.gpsimd
        return self.P.op(eng, lambda: getattr(e, name)(**kw), reads, writes)


def dap(t, offset, pattern):
    return bass.AP(tensor=t, offset=offset, ap=[list(p) for p in pattern])


def cast(k, sel, out, in_, reads, writes):
    if sel % 2 == 0:
        k.act(out, in_, AF.Copy, reads=reads, writes=writes)
    else:
        k.v("dve", "tensor_copy", reads=reads, writes=writes, out=out, in_=in_)


def ffn_phase(k, xsrc, xdst, w_norm, w_gate, w_up, w_down, ident):
    nc, P = k.nc, k.P
    G = 512
    with ExitStack() as es:
        wg = k.sb(es, "wg", [128, 8, DFF], BF16)
        wu = k.sb(es, "wu", [128, 8, DFF], BF16)
        wd = k.sb(es, "wd", [128, NFC, D], BF16)
        HW = DFF // 2
        NSTG = 3
        stg = [k.sb(es, "stg", [128, HW], F32) for _ in range(NSTG)]
        gain = k.sb(es, "gain", [128, D], F32)
        xin = [k.sb(es, "xin", [128, D], F32) for _ in range(2)]
        hn = [k.sb(es, "hn", [128, D], BF16) for _ in range(2)]
        hT = k.sb(es, "hT", [128, 8, G], BF16)
        aT = k.sb(es, "aT", [128, NFC, G], BF16)
        sg = [k.sb(es, "sg", [128, G], F32) for _ in range(2)]
        xr = [k.sb(es, "xr", [128, 512], F32) for _ in range(2)]
        ob = [k.sb(es, "ob", [128, 512], F32) for _ in range(2)]
        st = [k.sb(es, "st", [128, 4], F32) for _ in range(2)]
        pgu = [k.ps(es, "pgu", [128, 512], F32) for _ in range(4)]
        ptp = [k.ps(es, "ptp", [128, 1024], BF16) for _ in range(2)]
        pdn = [k.ps(es, "pdn", [128, 512], F32) for _ in range(2)]

        k.dma("sp", gain[:], dap(w_norm.tensor, w_norm.offset, [[0, 128], [1, D]]), writes=["gain"])
        wgv = w_gate.rearrange("(kc p) f -> p kc f", p=128)
        wuv = w_up.rearrange("(kc p) f -> p kc f", p=128)
        wdv = w_down.rearrange("(fc p) d -> p fc d", p=128)
        wq = []
        cnt = [0]

        def chunk(dst_ap, src_ap, width, tok):
            def go():
                ci = cnt[0]
                cnt[0] += 1
                b = ci % NSTG
                k.dma("sp" if ci % 2 == 0 else "pool", stg[b][:, 0:width], src_ap, writes=[("stg", b)])
                cast(k, ci, dst_ap, stg[b][:, 0:width], [("stg", b)], [tok])
            return go
        for hh in range(2):
            for c in range(8):
                for (dst, srcv, nm) in ((wg, wgv, "wg"), (wu, wuv, "wu")):
                    wq.append(chunk(dst[:, c, hh * HW:(hh + 1) * HW], srcv[:, c, hh * HW:(hh + 1) * HW], HW,
                                    (nm, c, hh)))
        for c in range(NFC):
            wq.append(chunk(wd[:, c, :], wdv[:, c, :], D, ("wd", c)))

        def emit_w(n):
            for _ in range(n):
                if wq:
                    wq.pop(0)()

        gi = 0
        di = 0
        for g in range(S // G):
            for t in range(G // 128):
                r0 = g * G + t * 128
                b = t % 2
                k.dma("sp", xin[b][:], xsrc[r0:r0 + 128, :], writes=[("xin", b)])
                k.v("dve", "scalar_tensor_tensor", reads=[("xin", b)], writes=[("hn", b), ("st", b)],
                    out=hn[b][:], in0=xin[b][:], scalar=1.0, in1=xin[b][:], op0=ALU.mult, op1=ALU.mult,
                    accum_out=st[b][:, 0:1])
                k.act(st[b][:, 1:2], st[b][:, 0:1], AF.Sqrt, reads=[("st", b)], writes=[("st1", b)],
                      scale=1.0 / D, bias=EPS)
                k.v("dve", "reciprocal", reads=[("st1", b)], writes=[("st2", b)],
                    out=st[b][:, 2:3], in_=st[b][:, 1:2])
                k.v("dve", "scalar_tensor_tensor", reads=[("xin", b), ("st2", b), "gain"], writes=[("hn", b)],
                    out=hn[b][:], in0=xin[b][:], scalar=st[b][:, 2:3], in1=gain[:], op0=ALU.mult, op1=ALU.mult)
                for kc in range(8):
                    k.tr(ptp[b][:, kc * 128:(kc + 1) * 128], hn[b][:, kc * 128:(kc + 1) * 128], ident[:],
                         reads=[("hn", b), "ident"], writes=[("ptp", b)])
                k.act(hT[:, :, t * 128:(t + 1) * 128], ptp[b][:].rearrange("p (a c) -> p a c", a=8), AF.Copy,
                      reads=[("ptp", b)], writes=[("hT", t)])
            hT_all = [("hT", t) for t in range(G // 128)]
            emit_w(16 if g == 0 else 0)
            for fc in range(NFC):
                emit_w(2)
                pb = (gi % 2) * 2
                gi += 1
                for kc in range(8):
                    k.mm(pgu[pb][:], wg[:, kc, fc * 128:(fc + 1) * 128], hT[:, kc, :], kc == 0, kc == 7,
                         reads=hT_all + [("wg", kc, fc // (NFC // 2))], writes=[("pgu", pb)])
                for kc in range(8):
                    k.mm(pgu[pb + 1][:], wu[:, kc, fc * 128:(fc + 1) * 128], hT[:, kc, :], kc == 0, kc == 7,
                         reads=hT_all + [("wu", kc, fc // (NFC // 2))], writes=[("pgu", pb + 1)])
                sb_ = fc % 2
                k.act(sg[sb_][:], pgu[pb][:], AF.Silu, reads=[("pgu", pb)], writes=[("sg", sb_)])
                k.v("dve", "tensor_tensor", reads=[("sg", sb_), ("pgu", pb + 1)], writes=[("aT", fc)],
                    out=aT[:, fc, :], in0=sg[sb_][:], in1=pgu[pb + 1][:], op=ALU.mult)
            aT_all = [("aT", fc) for fc in range(NFC)]
            for t in range(G // 128):
                r0 = g * G + t * 128
                for hh in range(2):
                    b = di % 2
                    di += 1
                    k.dma("pool", xr[b][:], xsrc[r0:r0 + 128, hh * 512:(hh + 1) * 512], writes=[("xr", b)])
                    for fc in range(NFC):
                        k.mm(pdn[b][:], aT[:, fc, t * 128:(t + 1) * 128], wd[:, fc, hh * 512:(hh + 1) * 512],
                             fc == 0, fc == NFC - 1, reads=aT_all + [("wd", fc)], writes=[("pdn", b)])
                    k.v("dve", "scalar_tensor_tensor", reads=[("pdn", b), ("xr", b)], writes=[("ob", b)],
                        out=ob[b][:], in0=pdn[b][:], scalar=0.5, in1=xr[b][:], op0=ALU.mult, op1=ALU.add)
                    k.dma("sp", xdst[r0:r0 + 128, hh * 512:(hh + 1) * 512], ob[b][:], reads=[("ob", b)])
    P.barrier()


def final_norm_phase(k, xsrc, ydst, w_norm):
    P = k.P
    with ExitStack() as es:
        gain = k.sb(es, "fgain", [128, D], F32)
        xin = [k.sb(es, "fxin", [128, D], F32) for _ in range(2)]
        yo = [k.sb(es, "fyo", [128, D], F32) for _ in range(2)]
        junk = k.sb(es, "fjunk", [128, D], BF16)
        st = [k.sb(es, "fst", [128, 4], F32) for _ in range(2)]
        k.dma("sp", gain[:], dap(w_norm.tensor, w_norm.offset, [[0, 128], [1, D]]), writes=["fgain"])
        for t in range(NT):
            b = t % 2
            r0 = t * 128
            k.dma("sp", xin[b][:], xsrc[r0:r0 + 128, :], writes=[("fxin", b)])
            k.v("dve", "scalar_tensor_tensor", reads=[("fxin", b)], writes=["fjunk", ("fst", b)],
                out=junk[:], in0=xin[b][:], scalar=1.0, in1=xin[b][:], op0=ALU.mult, op1=ALU.mult,
                accum_out=st[b][:, 0:1])
            k.act(st[b][:, 1:2], st[b][:, 0:1], AF.Sqrt, reads=[("fst", b)], writes=[("fst1", b)],
                  scale=1.0 / D, bias=EPS)
            k.v("dve", "reciprocal", reads=[("fst1", b)], writes=[("fst2", b)],
                out=st[b][:, 2:3], in_=st[b][:, 1:2])
            k.v("dve", "scalar_tensor_tensor", reads=[("fxin", b), ("fst2", b), "fgain"], writes=[("fyo", b)],
                out=yo[b][:], in0=xin[b][:], scalar=st[b][:, 2:3], in1=gain[:], op0=ALU.mult, op1=ALU.mult)
            k.dma("pool", ydst[r0:r0 + 128, :], yo[b][:], reads=[("fyo", b)])
    P.barrier()


NQK = 2176
NVC = 896
NVG = NVC + 18
R_NSAQ, R_KCMP, R_VCMP, R_KSEL, R_KWIN, R_MQ, R_MK, R_DQ, R_DK = 0, 384, 512, 640, 768, 896, 1152, 1408, 1792
C_VSEL, C_VWIN, C_MV, C_DV = 0, 128, 256, 512
FAM = {
    "full": (2560, 2432, 1),
    "win": (1536, 1408, 1),
    "cmp": (6144, 4096, 16),
    "dil0": (1152, 1024, 1),
    "dil1": (1536, 1408, 1),
    "dil2": (3072, 2944, 1),
}
DIL = ((128, 1), (512, 4), (2048, 16))
SCALE = 0.125
NEGP = -240000.0


def np_rel_bucket(dist):
    n = np.maximum(dist, 0)
    nf = np.maximum(n, 16).astype(np.float32)
    lg = (np.log(nf / np.float32(16)) / np.float32(np.log(2048 / 16)) * np.float32(16)).astype(np.float32)
    large = 16 + lg.astype(np.int32)
    large = np.minimum(large, 31)
    return np.where(n < 16, n, large)


def host_consts():
    c = {}
    c["c_ident"] = np.eye(128, dtype=np.float32)
    c["c_flip"] = np.ascontiguousarray(np.eye(128, dtype=np.float32)[::-1])
    def fam_oh(length, off, valid_fn):
        w = np.arange(length)
        d = w - off
        ok = valid_fn(d)
        b = np_rel_bucket(d)
        oh = np.zeros((33, length), np.float32)
        oh[b[ok], w[ok]] = 1.0
        oh[32, ~ok] = 1.0
        return oh
    c["oh_full"] = fam_oh(2560, 511, lambda d: d >= 0)
    c["oh_win"] = fam_oh(1536, 511, lambda d: (d >= 0) & (d < 512))
    c["oh_cmp"] = fam_oh(6144, 2063, lambda d: d >= 0)
    for g, (W, dl) in enumerate(DIL):
        c["oh_dil%d" % g] = fam_oh(FAM["dil%d" % g][0], 511, lambda d: (d >= 0) & (d <= W) & (d % dl == 0))
    c["c_negrow"] = np.full((1, 16), NEGM, np.float32)
    n_cmp = 255
    c_start = np.arange(n_cmp) * 16
    s_start = np.arange(64) * 64
    ov = (c_start[:, None] < s_start[None, :] + 64) & (c_start[:, None] + 32 > s_start[None, :])
    cts = np.zeros((256, 65), np.float32)
    cts[:255, :64] = ov
    cts[:255, 64] = 1.0
    c["c_cts"] = cts
    t = np.arange(S)
    blk = np.arange(64)
    cur = t // 64
    keep = np.ones((S, 64), np.float32)
    add = np.zeros((S, 64), np.float32)
    f0 = np.broadcast_to(blk[None, :] == 0, (S, 64))
    f1 = blk[None, :] == cur[:, None]
    f2 = blk[None, :] == cur[:, None] - 1
    fut = blk[None, :] * 64 > t[:, None]
    for f, val in ((f0, 1e4), (f2, 3e4), (f1, 2e4)):
        keep[f] = 0.0
        add[f] = val
    keep[fut] = 0.0
    add[fut] = -1e30
    c["c_keep"] = keep
    c["c_add"] = add
    c["c_esel"] = (np.arange(S)[None, :] // 64 == np.arange(64)[:, None]).astype(np.float32)
    nb = np.arange(16)
    cb = t // 256
    valid = (nb[None, :] < cb[:, None]).astype(np.float32)
    own = (nb[None, :] == cb[:, None]).astype(np.float32)
    c["c_mvalid"] = valid
    c["c_maddm"] = np.where(valid > 0, 0.0, -1e30).astype(np.float32)
    c["c_mown"] = ((own - 1.0) * (-NEGP)).astype(np.float32)
    eb = np.zeros((16, 16, 128), np.float32)
    for n in range(16):
        eb[n, n, :] = 1.0
    c["c_eb"] = eb.reshape(16, 16 * 128)
    c["c_ebig"] = (np.arange(S)[None, :] // 256 == np.arange(16)[:, None]).astype(np.float32)
    return c


CONST_SHAPES = {
    "c_ident": [128, 128], "c_flip": [128, 128], "oh_full": [33, 2560], "oh_win": [33, 1536],
    "oh_cmp": [33, 6144], "oh_dil0": [33, 1152], "oh_dil1": [33, 1536], "oh_dil2": [33, 3072],
    "c_negrow": [1, 16], "c_cts": [256, 65], "c_keep": [S, 64], "c_add": [S, 64], "c_esel": [64, S],
    "c_mvalid": [S, 16], "c_maddm": [S, 16], "c_mown": [S, 16], "c_eb": [16, 2048], "c_ebig": [16, S],
}


def bias_gen_phase(k, rel_bias, consts, gd):
    P = k.P
    with ExitStack() as es:
        tblx = k.sb(es, "tblx", [33, 16], F32)
        oh = k.sb(es, "oh", [33, 6144], F32)
        go = k.sb(es, "go", [16, 6144], F32)
        pb = [k.ps(es, "pbg", [128, 512], F32) for _ in range(2)]
        k.dma("sp", tblx[0:32, :], rel_bias, writes=["tblx"])
        k.dma("sp", tblx[32:33, :], consts["c_negrow"], writes=["tblx"])
        i = 0
        for fam, (L, _, _) in FAM.items():
            k.dma("sp", oh[:, 0:L], consts["oh_" + fam], writes=["oh"])
            for c0 in range(0, L, 512):
                b = i % 2
                i += 1
                k.mm(pb[b][0:16, :], tblx[:, :], oh[:, c0:c0 + 512], True, True,
                     reads=["tblx", "oh"], writes=[("pbg", b)])
                k.v("dve", "tensor_copy", reads=[("pbg", b)], writes=["go"], out=go[:, c0:c0 + 512],
                    in_=pb[b][0:16, :])
            k.dma("sp", gd[fam], go[:, 0:L], reads=["go"])
    P.barrier()


def make_tb(k, fam, gd, h, flipf, rev, tb, pflip, tok):
    L, W, st = FAM[fam]
    g = gd[fam]
    k.dma("sp", rev[:, 0:W], dap(g.tensor, g.offset + h * L, [[st, 128], [1, W]]), writes=["rev"])
    i = 0
    for c0 in range(0, W, 512):
        n = min(512, W - c0)
        b = i % len(pflip)
        i += 1
        k.mm(pflip[b][:, 0:n], flipf[:, :], rev[:, c0:c0 + n], True, True, reads=["flipf", "rev"],
             writes=[("pflip", b)])
        if i % 2 == 0:
            k.act(tb[:, c0:c0 + n], pflip[b][:, 0:n], AF.Copy, reads=[("pflip", b)], writes=[tok], scale=1.0 / SCALE)
        else:
            k.v("dve", "tensor_scalar", reads=[("pflip", b)], writes=[tok], out=tb[:, c0:c0 + n],
                in0=pflip[b][:, 0:n], scalar1=1.0 / SCALE, scalar2=None, op0=ALU.mult)


def proj_phase(k, xsrc, w_norm, w_qk, w_v, w_mg, ident, qkT, vtok, gtok, mgT):
    P = k.P
    with ExitStack() as es:
        hT = k.sb(es, "phT", [128, 8, S], BF16)
        gain = k.sb(es, "pgain", [128, D], F32)
        xin = [k.sb(es, "pxin", [128, D], F32) for _ in range(3)]
        hn = [k.sb(es, "phn", [128, D], BF16) for _ in range(3)]
        st = [k.sb(es, "pst", [128, 4], F32) for _ in range(3)]
        wvs = k.sb(es, "wvs", [128, NVG], F32)
        wvb = k.sb(es, "wvb", [128, 8, NVG], BF16)
        wst = [k.sb(es, "wst", [128, 8, 128], F32) for _ in range(3)]
        wb = [k.sb(es, "wb", [128, 8, 128], BF16) for _ in range(3)]
        orow = [k.sb(es, "orow", [128, S], BF16) for _ in range(2)]
        vout = [k.sb(es, "vout", [128, NVC], BF16) for _ in range(3)]
        gout = [k.sb(es, "gout", [128, 18], F32) for _ in range(3)]
        ptp = [k.ps(es, "pptp", [128, 1024], BF16) for _ in range(3)]
        pmm = [k.ps(es, "ppmm", [128, 512], F32) for _ in range(4)]
        k.dma("sp", gain[:], dap(w_norm.tensor, w_norm.offset, [[0, 128], [1, D]]), writes=["pgain"])
        wvv = w_v.rearrange("(kc p) f -> p kc f", p=128)
        for kc in range(8):
            k.dma("pool", wvs[:], wvv[:, kc, :], writes=["wvs"])
            k.v("dve", "tensor_copy", reads=["wvs"], writes=["wvb"], out=wvb[:, kc, :], in_=wvs[:])
        for t in range(NT):
            b = t % 3
            r0 = t * 128
            k.dma("sp", xin[b][:], xsrc[r0:r0 + 128, :], writes=[("pxin", b)])
            k.v("dve", "scalar_tensor_tensor", reads=[("pxin", b)], writes=[("phn", b), ("pst", b)],
                out=hn[b][:], in0=xin[b][:], scalar=1.0, in1=xin[b][:], op0=ALU.mult, op1=ALU.mult,
                accum_out=st[b][:, 0:1])
            k.act(st[b][:, 1:2], st[b][:, 0:1], AF.Sqrt, reads=[("pst", b)], writes=[("pst1", b)],
                  scale=1.0 / D, bias=EPS)
            k.v("dve", "reciprocal", reads=[("pst1", b)], writes=[("pst2", b)],
                out=st[b][:, 2:3], in_=st[b][:, 1:2])
            k.v("dve", "scalar_tensor_tensor", reads=[("pxin", b), ("pst2", b), "pgain"], writes=[("phn", b)],
                out=hn[b][:], in0=xin[b][:], scalar=st[b][:, 2:3], in1=gain[:], op0=ALU.mult, op1=ALU.mult)
            for kc in range(8):
                k.tr(ptp[b][:, kc * 128:(kc + 1) * 128], hn[b][:, kc * 128:(kc + 1) * 128], ident[:],
                     reads=[("phn", b), "ident"], writes=[("pptp", b)])
            k.act(hT[:, :, r0:r0 + 128], ptp[b][:].rearrange("p (a c) -> p a c", a=8), AF.Copy,
                  reads=[("pptp", b)], writes=[("phT", t)])
            pa, pb_ = pmm[(t % 2) * 2], pmm[(t % 2) * 2 + 1]
            ta, tb_ = ("ppmm", (t % 2) * 2), ("ppmm", (t % 2) * 2 + 1)
            for kc in range(8):
                k.mm(pa[:, 0:512], hT[:, kc, r0:r0 + 128], wvb[:, kc, 0:512], kc == 0, kc == 7,
                     reads=[("phT", t), "wvb"], writes=[ta])
            for kc in range(8):
                k.mm(pb_[:, 0:NVG - 512], hT[:, kc, r0:r0 + 128], wvb[:, kc, 512:NVG], kc == 0, kc == 7,
                     reads=[("phT", t), "wvb"], writes=[tb_])
            k.act(vout[b][:, 0:512], pa[:, 0:512], AF.Copy, reads=[ta], writes=[("vout", b)])
            k.v("dve", "tensor_copy", reads=[tb_], writes=[("vout", b)], out=vout[b][:, 512:NVC],
                in_=pb_[:, 0:NVC - 512])
            k.act(gout[b][:], pb_[:, NVC - 512:NVG - 512], AF.Sigmoid, reads=[tb_], writes=[("gout", b)])
            k.dma("pool", vtok[r0:r0 + 128, :], vout[b][:], reads=[("vout", b)])
            k.dma("pool", gtok[r0:r0 + 128, :], gout[b][:], reads=[("gout", b)])
        allh = [("phT", t) for t in range(NT)]
        blocks = [("qk", i) for i in range(NQK // 128)] + [("mg", i) for i in range(3 * D // 128)]
        mi = 0
        def prefetch(bi):
            kind, i = blocks[bi]
            b = bi % 3
            wsrc = (w_qk if kind == "qk" else w_mg)[:, i * 128:(i + 1) * 128].rearrange("(kc p) f -> p kc f", p=128)
            k.dma("pool", wst[b][:], wsrc, writes=[("wst", b)])
            if bi % 2 == 0:
                k.v("dve", "tensor_copy", reads=[("wst", b)], writes=[("wb", b)], out=wb[b][:], in_=wst[b][:])
            else:
                k.act(wb[b][:], wst[b][:], AF.Copy, reads=[("wst", b)], writes=[("wb", b)])
        prefetch(0)
        prefetch(1)
        for bi, (kind, i) in enumerate(blocks):
            b = bi % 3
            if bi + 2 < len(blocks):
                prefetch(bi + 2)
            for tb8 in range(8):
                pi = mi % 4
                mi += 1
                for kc in range(8):
                    k.mm(pmm[pi][:], wb[b][:, kc, :], hT[:, kc, tb8 * 512:(tb8 + 1) * 512], kc == 0, kc == 7,
                         reads=allh + [("wb", b)], writes=[("ppmm", pi)])
                ob_ = bi % 2
                if kind == "mg":
                    k.act(orow[ob_][:, tb8 * 512:(tb8 + 1) * 512], pmm[pi][:], AF.Sigmoid,
                          reads=[("ppmm", pi)], writes=[("orow", ob_)])
                elif tb8 % 2 == 0:
                    k.act(orow[ob_][:, tb8 * 512:(tb8 + 1) * 512], pmm[pi][:], AF.Copy,
                          reads=[("ppmm", pi)], writes=[("orow", ob_)])
                else:
                    k.v("dve", "tensor_copy", reads=[("ppmm", pi)], writes=[("orow", ob_)],
                        out=orow[ob_][:, tb8 * 512:(tb8 + 1) * 512], in_=pmm[pi][:])
            dst = (qkT if kind == "qk" else mgT)[i * 128:(i + 1) * 128, :]
            k.dma("sp", dst, orow[bi % 2][:], reads=[("orow", bi % 2)])
    P.barrier()


def compress_phase(k, qkT, pe_k, pe_v, phi_k1, phi_k2, phi_v1, phi_v2, kcT, vcs, identf):
    P = k.P
    C1 = 1.5957691216057308
    with ExitStack() as es:
        src = k.sb(es, "csrc", [128, S], BF16)
        w1s = k.sb(es, "w1s", [128, 8, 256], F32)
        w1b = k.sb(es, "w1b", [128, 32, 256], BF16)
        w2s = k.sb(es, "w2s", [128, 2, 64], F32)
        w2b = k.sb(es, "w2b", [128, 2, 64], BF16)
        pes = k.sb(es, "pes", [128, 64], F32)
        peb = k.sb(es, "peb", [128, 32], BF16)
        bias = k.sb(es, "cbias", [128, 2], F32)
        xa = k.sb(es, "cxa", [128, 256], F32)
        xb_ = k.sb(es, "cxb", [128, 256], F32)
        xc = k.sb(es, "cxc", [128, 256], F32)
        hid = [k.sb(es, "chid", [128, 256], BF16) for _ in range(2)]
        ko = k.sb(es, "cko", [64, 256], BF16)
        vo = k.sb(es, "cvo", [128, 64], BF16)
        ph = [k.ps(es, "cph", [128, 512], F32) for _ in range(2)]
        pbias = k.ps(es, "cpb", [128, 512], F32)
        po = k.ps(es, "cpo", [128, 512], F32)
        for which, (r0, pe, w1, w2) in enumerate(((R_KCMP, pe_k, phi_k1, phi_k2), (R_VCMP, pe_v, phi_v1, phi_v2))):
            k.dma("sp", src[:], qkT[r0:r0 + 128, :], writes=["csrc"])
            w1v = w1.rearrange("(l d) h -> d l h", d=64)
            for half in range(2):
                for l0 in range(0, 32, 8):
                    k.dma("pool", w1s[half * 64:(half + 1) * 64, :, :], w1v[:, l0:l0 + 8, :], writes=["w1s"])
                    k.v("dve", "tensor_copy", reads=["w1s"], writes=["w1b"],
                        out=w1b[half * 64:(half + 1) * 64, l0:l0 + 8, :], in_=w1s[half * 64:(half + 1) * 64, :, :])
            k.dma("sp", pes[0:32, 0:64], pe, writes=["pes"])
            k.tr(pbias[0:64, 64:96], pes[0:32, 0:64], identf[0:32, 0:32], reads=["pes", "identf"], writes=["cpb"])
            k.v("dve", "tensor_copy", reads=["cpb"], writes=["peb"], out=peb[0:64, :], in_=pbias[0:64, 64:96])
            k.dma("sp", w2s[:], w2.rearrange("(hc p) d -> p hc d", p=128), writes=["w2s"])
            k.v("dve", "tensor_copy", reads=["w2s"], writes=["w2b"], out=w2b[:], in_=w2s[:])
            for hc in range(2):
                for l in range(32):
                    k.mm(pbias[:, hc:hc + 1], w1b[0:64, l, hc * 128:(hc + 1) * 128], peb[0:64, l:l + 1],
                         l == 0, l == 31, reads=["w1b", "peb"], writes=["cpb"])
            k.v("dve", "tensor_copy", reads=["cpb"], writes=["cbias"], out=bias[:], in_=pbias[:, 0:2])
            for g in range(2):
                p0 = g * 64
                for hc in range(2):
                    for l in range(32):
                        k.mm(ph[hc][:, 0:255], w1b[p0:p0 + 64, l, hc * 128:(hc + 1) * 128],
                             src[p0:p0 + 64, l:l + 16 * 254 + 1:16], l == 0, l == 31,
                             reads=["w1b", "csrc"], writes=[("cph", hc)])
                    k.v("dve", "tensor_scalar", reads=[("cph", hc), "cbias"], writes=["cxa"], out=xa[:, 0:255],
                        in0=ph[hc][:, 0:255], scalar1=bias[:, hc:hc + 1], scalar2=None, op0=ALU.add)
                    k.v("dve", "tensor_tensor", reads=["cxa"], writes=["cxb"], out=xb_[:, 0:255], in0=xa[:, 0:255],
                        in1=xa[:, 0:255], op=ALU.mult)
                    k.v("dve", "tensor_scalar", reads=["cxb"], writes=["cxc"], out=xc[:, 0:255], in0=xb_[:, 0:255],
                        scalar1=0.044715, scalar2=1.0, op0=ALU.mult, op1=ALU.add)
                    k.v("dve", "tensor_tensor", reads=["cxc", "cxa"], writes=["cxb"], out=xb_[:, 0:255],
                        in0=xc[:, 0:255], in1=xa[:, 0:255], op=ALU.mult)
                    k.act(xc[:, 0:255], xb_[:, 0:255], AF.Sigmoid, reads=["cxb"], writes=["cxc"], scale=C1)
                    k.v("dve", "tensor_tensor", reads=["cxc", "cxa"], writes=[("chid", hc)], out=hid[hc][:, 0:255],
                        in0=xc[:, 0:255], in1=xa[:, 0:255], op=ALU.mult)
                hh = [("chid", 0), ("chid", 1)]
                if which == 0:
                    for hc in range(2):
                        k.mm(po[0:64, 0:255], w2b[:, hc, :], hid[hc][:, 0:255], hc == 0, hc == 1,
                             reads=hh + ["w2b"], writes=["cpo"])
                    k.v("dve", "tensor_copy", reads=["cpo"], writes=["cko"], out=ko[:, 0:255], in_=po[0:64, 0:255])
                    k.dma("sp", kcT[g, :, 0:255], ko[:, 0:255], reads=["cko"])
                else:
                    for ch in range(2):
                        rows = 128 if ch == 0 else 127
                        for hc in range(2):
                            k.mm(po[0:rows, 0:64], hid[hc][:, ch * 128:ch * 128 + rows], w2b[:, hc, :],
                                 hc == 0, hc == 1, reads=hh + ["w2b"], writes=["cpo"])
                        k.v("dve", "tensor_copy", reads=["cpo"], writes=["cvo"], out=vo[0:rows, :], in_=po[0:rows, 0:64])
                        k.dma("sp", vcs[g, ch * 128:ch * 128 + rows, :], vo[0:rows, :], reads=["cvo"])
    P.barrier()


FILL_CNT = 0
FILL_N = 512


class AttnCx:
    def __init__(self, k, es, nU, ident):
        self.ident = ident
        if FILL_CNT > 0:
            self.fill = k.ps(es, "afill", [128, 512], F32)
            self.fsrc = k.sb(es, "afsrc", [128, 512], BF16)
            k.v("dve", "memset", writes=["afsrc"], ap=self.fsrc[:], constant=0.0)
        self.S = [k.ps(es, "aS", [128, 512], F32) for _ in range(3)]
        self.U = [k.ps(es, "aU", [128, 4, 128], F32) for _ in range(nU)]
        self.E = [k.sb(es, "aE", [128, 512], BF16) for _ in range(5)]
        self.fifo = []
        self.npv = 0
        self.si = self.li = self.ei = 0
        self.zl = k.sb(es, "azl", [1, 128], BF16)
        self.zr = k.sb(es, "azr", [1, 512], BF16)
        k.v("dve", "memset", writes=["azl"], ap=self.zl[:], constant=0.0)
        k.v("dve", "memset", writes=["azr"], ap=self.zr[:], constant=0.0)


def emit_scores(k, cx, rows, kT, q, n, tb, cbias, mask, rd):
    si = cx.si % 3
    cx.si += 1
    Sb = cx.S[si]
    k.mm(Sb[0:rows, 0:n], kT, q, True, tb is None, reads=rd, writes=[("aS", si)])
    if tb is not None:
        k.mm(Sb[0:rows, 0:n], cx.ident[0:rows, 0:rows], tb, False, True, reads=rd + ["ident"], writes=[("aS", si)])
    ei = cx.ei % 5
    cx.ei += 1
    Eb = cx.E[ei]
    if tb is not None:
        k.act(Eb[0:rows, 0:n], Sb[0:rows, 0:n], AF.Exp, reads=[("aS", si)], writes=[("aE", ei)], scale=SCALE)
    else:
        k.act(Eb[0:rows, 0:n], Sb[0:rows, 0:n], AF.Exp, reads=[("aS", si)] + rd, writes=[("aE", ei)],
              scale=SCALE, bias=cbias)
    return ei


LOOKAHEAD = 3


def pipe_drain(k, cx, keep):
    while cx.npv > keep or (keep == 0 and cx.fifo):
        act = cx.fifo.pop(0)
        if act[0] == "pv":
            cx.npv -= 1
        act[1]()


def run_branch(k, cx, tiles, ui, utok, mid=None):
    last = {}
    for i, t in enumerate(tiles):
        for qt in range(t["qt_lo"], t["qt_hi"]):
            last[qt] = i
    U = cx.U[ui]

    def zero():
        k.mm(U[:].rearrange("p a b -> p (a b)"), cx.zl[0:1, :], cx.zr[0:1, :], True, False,
             reads=["azl", "azr"], writes=[utok])
    cx.fifo.append(("zero", zero))
    for i, t in enumerate(tiles):
        lo, hi = t["qt_lo"], t["qt_hi"]
        n = (hi - lo) * 128
        rows = t["rows"]
        ei = emit_scores(k, cx, rows, t["kT"], t["qfn"](lo * 128, n), n,
                         t["tbfn"](lo * 128, n) if t["tbfn"] is not None else None, t["cbias"],
                         t["maskfn"](lo * 128, n) if t["maskfn"] is not None else None, t["rd"])

        def pv(i=i, t=t, lo=lo, hi=hi, rows=rows, ei=ei):
            Eb = cx.E[ei]
            for qt in range(lo, hi):
                c0 = (qt - lo) * 128
                k.mm(U[:, qt, 0:65], Eb[0:rows, c0:c0 + 128], t["V"], False, last[qt] == i,
                     reads=[("aE", ei)] + t["rd"], writes=[utok])
        cx.fifo.append(("pv", pv))
        cx.npv += 1
        pipe_drain(k, cx, LOOKAHEAD)
        if mid is not None and i == (len(tiles) - 1) // 2:
            mid()


def pipe_post(k, cx, fn):
    cx.fifo.append(("post", fn))


def pipe_flush(k, cx):
    pipe_drain(k, cx, 0)


def dma_split(k, q, dst, src, tok, step=8):
    for c0 in range(0, NT, step):
        k.dma(q, dst[:, c0:c0 + step, :], src[:, c0:c0 + step, :], writes=[tok])


def load_v_aug(k, q, vt, vtok, col0, tok):
    k.v("dve", "memset", writes=[tok], ap=vt[:, :, 64:65], constant=1.0)
    src = vtok[:, col0:col0 + 64].rearrange("(c p) d -> p c d", p=128)
    for c0 in range(0, NT, 8):
        k.dma(q, vt[:, c0:c0 + 8, 0:64], src[:, c0:c0 + 8, :], writes=[tok])


def cmp_tiles(qb, kc_sb, q_sb, tbc, V_sb, rd):
    qs = qb * 512
    tiles = []
    for ch in range(2):
        Dd = qs - 2048 * ch
        if Dd < 0:
            continue
        rows = 128 if ch == 0 else 127
        tiles.append(dict(
            rows=rows, kT=kc_sb[:, ch * 128:ch * 128 + rows],
            qfn=lambda c0, n, qs=qs: q_sb[:, qs + c0:qs + c0 + n],
            tbfn=lambda c0, n, Dd=Dd, rows=rows: tbc[0:rows, Dd + c0:Dd + c0 + n],
            cbias=None, maskfn=None, V=V_sb[0:rows, ch, :], qt_lo=0, qt_hi=4, rd=rd))
    return tiles


def nsa_phase(k, qkT, vtok, gtok, kcT, vcs, gd, consts, identf, ident, ytok_sb):
    P = k.P
    with ExitStack() as es:
        flipf = k.sb(es, "flipf", [128, 128], F32)
        rev = k.sb(es, "rev", [128, 4096], F32)
        tbc = k.sb(es, "tbc", [128, 4096], BF16)
        tbf = k.sb(es, "tbf", [128, 2432], BF16)
        cbf = k.sb(es, "cbf", [128, 1], F32)
        tbw = k.sb(es, "tbw", [128, 1408], BF16)
        q_sb = k.sb(es, "nq", [128, S], BF16)
        kc_sb = k.sb(es, "nkc", [128, 256], BF16)
        ctsf = k.sb(es, "ctsf", [128, 2, 65], F32)
        cts = k.sb(es, "cts", [128, 2, 65], BF16)
        vc_sb = k.sb(es, "nvc", [128, 2, 65], BF16)
        ksel = k.sb(es, "nksel", [128, S], BF16)
        kwin = k.sb(es, "nkwin", [128, S], BF16)
        vsel = k.sb(es, "nvsel", [128, NT, 65], BF16)
        vwin = k.sb(es, "nvwin", [128, NT, 65], BF16)
        eself = k.sb(es, "eself", [128, 1024], F32)
        imp = k.sb(es, "imp", [128, NT, 64], F32)
        keep = k.sb(es, "keep", [128, NT, 64], F32)
        addc = k.sb(es, "addc", [128, NT, 64], F32)
        gts = k.sb(es, "gts", [128, NT, 18], F32)
        sm = k.sb(es, "nsm", [128, 64], F32)
        wk = [k.sb(es, "nwk", [128, 64], F32) for _ in range(3)]
        m8 = [k.sb(es, "nm8", [128, 8], F32) for _ in range(2)]
        negq = k.sb(es, "negq", [128, 64], F32)
        acc = [k.sb(es, "nacc", [128, 64], F32) for _ in range(2)]
        ucp = [k.sb(es, "nucp", [128, 4, 65], F32) for _ in range(3)]
        cx = AttnCx(k, es, 3, ident)
        pfl = [k.ps(es, "pfl", [128, 512], F32) for _ in range(2)]

        k.dma("sp", flipf[:], consts["c_flip"], writes=["flipf"])
        k.dma("sp", ctsf[:], consts["c_cts"].rearrange("(c p) d -> p c d", p=128), writes=["ctsf"])
        k.v("dve", "tensor_copy", reads=["ctsf"], writes=["cts"], out=cts[:], in_=ctsf[:])
        for c0 in range(0, S, 1024):
            k.dma("sp", eself[64:128, :], consts["c_esel"][:, c0:c0 + 1024], writes=["eself"])
            k.v("dve", "tensor_copy", reads=["eself"], writes=["esel"], out=ksel[64:128, c0:c0 + 1024], in_=eself[64:128, :])
        k.v("dve", "memset", writes=["nneg"], ap=q_sb[64:128, :], constant=0.0)
        k.v("dve", "memset", writes=["nkc0"], ap=kc_sb[64:128, :], constant=0.0)
        k.v("dve", "memset", writes=["nkwin0"], ap=kwin[64:128, :], constant=0.0)
        dma_split(k, "pool", keep, consts["c_keep"].rearrange("(c p) d -> p c d", p=128), "keep")
        dma_split(k, "pool", addc, consts["c_add"].rearrange("(c p) d -> p c d", p=128), "addc")
        dma_split(k, "pool", gts, gtok.rearrange("(c p) d -> p c d", p=128), "gts")

        for g in range(2):
            k.dma("sp", kc_sb[0:64, :], kcT[g], writes=["nkc"])
            for r in range(3):
                h = g * 3 + r
                k.dma("sp", q_sb[0:64, :], qkT[R_NSAQ + h * 64:R_NSAQ + (h + 1) * 64, :], writes=["nq"])
                make_tb(k, "cmp", gd, h, flipf, rev, tbc, pfl, "tbc")
                for qb in range(8):
                    tiles = cmp_tiles(qb, kc_sb, q_sb, tbc, cts, ["nkc", "nkc0", "nneg", "nq", "tbc", "cts"])
                    run_branch(k, cx, tiles, 0, ("aU", 0))

                    def post1(qb=qb, r=r):
                        U = cx.U[0]
                        k.v("dve", "tensor_scalar", reads=[("aU", 0)], writes=["nsm"], out=sm[:, 0:4],
                            in0=U[:, :, 64], scalar1=1e-30, scalar2=None, op0=ALU.max)
                        k.v("dve", "reciprocal", reads=["nsm"], writes=["nsm2"], out=sm[:, 4:8], in_=sm[:, 0:4])
                        for qt in range(4):
                            tl = qb * 4 + qt
                            if r == 0:
                                k.v("dve", "tensor_scalar", reads=[("aU", 0), "nsm2"], writes=[("imp", tl)],
                                    out=imp[:, tl, :], in0=U[:, qt, 0:64], scalar1=sm[:, 4 + qt:5 + qt],
                                    scalar2=None, op0=ALU.mult)
                            else:
                                k.v("dve", "scalar_tensor_tensor", reads=[("aU", 0), "nsm2", ("imp", tl)],
                                    writes=[("imp", tl)], out=imp[:, tl, :], in0=U[:, qt, 0:64],
                                    scalar=sm[:, 4 + qt:5 + qt], in1=imp[:, tl, :], op0=ALU.mult, op1=ALU.add)
                    pipe_post(k, cx, post1)
                pipe_flush(k, cx)
            for tl in range(NT):
                a, b_, c_ = wk
                k.v("dve", "tensor_tensor", reads=[("imp", tl), "keep"], writes=["nwk0"], out=a[:],
                    in0=imp[:, tl, :], in1=keep[:, tl, :], op=ALU.mult)
                k.v("dve", "tensor_tensor", reads=["nwk0", "addc"], writes=["nwk1"], out=b_[:],
                    in0=a[:], in1=addc[:, tl, :], op=ALU.add)
                k.v("dve", "max", reads=["nwk1"], writes=["nm80"], out=m8[0][:], in_=b_[:])
                k.v("dve", "match_replace", reads=["nwk1", "nm80"], writes=["nwk2"], out=c_[:],
                    in_to_replace=m8[0][:], in_values=b_[:], imm_value=-3.0e38)
                k.v("dve", "max", reads=["nwk2"], writes=["nm81"], out=m8[1][:], in_=c_[:])
                k.v("dve", "tensor_scalar", reads=["nwk1", "nm81"], writes=["nwk0"], out=a[:], in0=b_[:],
                    scalar1=m8[1][:, 7:8], scalar2=None, op0=ALU.is_ge)
                k.v("dve", "tensor_scalar", reads=["nwk0"], writes=["negq"], out=negq[:], in0=a[:],
                    scalar1=-NEGP, scalar2=NEGP, op0=ALU.mult, op1=ALU.add)
                pb = tl % 2
                k.tr(pfl[pb][0:64, 0:128], negq[:, :], identf[:], reads=["negq", "identf"], writes=[("pflip", pb)])
                k.v("dve", "tensor_copy", reads=[("pflip", pb)], writes=["nneg"],
                    out=q_sb[64:128, tl * 128:(tl + 1) * 128], in_=pfl[pb][0:64, 0:128])
            k.dma("sp", ksel[0:64, :], qkT[R_KSEL + g * 64:R_KSEL + (g + 1) * 64, :], writes=["nksel"])
            k.dma("sp", kwin[0:64, :], qkT[R_KWIN + g * 64:R_KWIN + (g + 1) * 64, :], writes=["nkwin"])
            load_v_aug(k, "pool", vsel, vtok, C_VSEL + g * 64, "nvsel")
            load_v_aug(k, "pool", vwin, vtok, C_VWIN + g * 64, "nvwin")
            k.v("dve", "memset", writes=["nvc"], ap=vc_sb[:, :, 64:65], constant=1.0)
            k.dma("pool", vc_sb[:, :, 0:64], vcs[g].rearrange("(c p) d -> p c d", p=128), writes=["nvc"])
            for r in range(3):
                h = g * 3 + r
                k.dma("sp", q_sb[0:64, :], qkT[R_NSAQ + h * 64:R_NSAQ + (h + 1) * 64, :], writes=["nq"])
                make_tb(k, "cmp", gd, h, flipf, rev, tbc, pfl, "tbc")
                make_tb(k, "full", gd, h, flipf, rev, tbf, pfl, "tbf")
                k.v("dve", "tensor_scalar", reads=["tbf"], writes=["cbf"], out=cbf[:, 0:1], in0=tbf[:, 2431:2432],
                    scalar1=SCALE, scalar2=None, op0=ALU.mult)
                make_tb(k, "win", gd, h, flipf, rev, tbw, pfl, "tbw")
                for qb in range(8):
                    qs = qb * 512
                    qfn = lambda c0, n, qs=qs: q_sb[:, qs + c0:qs + c0 + n]
                    tiles = cmp_tiles(qb, kc_sb, q_sb, tbc, vc_sb, ["nkc", "nkc0", "nneg", "nq", "tbc", "nvc"])
                    run_branch(k, cx, tiles, 0, ("aU", 0))
                    tiles = []
                    for kc in range(0, (qs + 384) // 128 + 1):
                        Dl = qs - 128 * kc
                        lo = max(0, -Dl // 128)
                        far = Dl >= 1664
                        tiles.append(dict(
                            rows=128, kT=ksel[:, kc * 128:(kc + 1) * 128], qfn=qfn,
                            tbfn=None if far else (lambda c0, n, Dl=Dl: tbf[:, Dl + 384 + c0:Dl + 384 + c0 + n]),
                            cbias=cbf[:, 0:1] if far else None,
                            maskfn=None,
                            V=vsel[:, kc, :], qt_lo=lo, qt_hi=4, rd=["nksel", "nq", "tbf", "cbf", "nvsel", "esel", "nneg"]))
                    run_branch(k, cx, tiles, 1, ("aU", 1))
                    tiles = []
                    for kc in range(max(0, (qs - 512) // 128), (qs + 384) // 128 + 1):
                        Dl = qs - 128 * kc
                        lo = max(0, -Dl // 128)
                        hi = min(4, (639 - Dl) // 128 + 1)
                        tiles.append(dict(
                            rows=128, kT=kwin[:, kc * 128:(kc + 1) * 128], qfn=qfn,
                            tbfn=lambda c0, n, Dl=Dl: tbw[:, Dl + 384 + c0:Dl + 384 + c0 + n],
                            cbias=None, maskfn=None, V=vwin[:, kc, :], qt_lo=lo, qt_hi=hi,
                            rd=["nkwin", "nkwin0", "nneg", "nq", "tbw", "nvwin"]))
                    run_branch(k, cx, tiles, 2, ("aU", 2))
                    def post3(qb=qb, h=h):
                        for br in range(3):
                            k.v("dve", "tensor_copy", reads=[("aU", br)], writes=[("ucp", br)],
                                out=ucp[br][:, :, :], in_=cx.U[br][:, :, 0:65])
                        for br in range(3):
                            U = ucp[br]
                            k.v("dve", "tensor_scalar", reads=[("ucp", br)], writes=[("nsmc", br)],
                                out=sm[:, 8 + br * 12:12 + br * 12], in0=U[:, :, 64], scalar1=1e-30, scalar2=None,
                                op0=ALU.max)
                            k.v("dve", "reciprocal", reads=[("nsmc", br)], writes=[("nsmr", br)],
                                out=sm[:, 12 + br * 12:16 + br * 12], in_=sm[:, 8 + br * 12:12 + br * 12])
                            k.v("dve", "tensor_tensor", reads=[("nsmr", br), "gts"], writes=[("nsmg", br)],
                                out=sm[:, 16 + br * 12:20 + br * 12], in0=sm[:, 12 + br * 12:16 + br * 12],
                                in1=gts[:, qb * 4:qb * 4 + 4, h * 3 + br], op=ALU.mult)
                        for qt in range(4):
                            tl = qb * 4 + qt
                            a0, a1 = acc
                            k.v("dve", "tensor_scalar", reads=[("ucp", 0), ("nsmg", 0)], writes=["nacc0"], out=a0[:],
                                in0=ucp[0][:, qt, 0:64], scalar1=sm[:, 16 + qt:17 + qt], scalar2=None, op0=ALU.mult)
                            k.v("dve", "scalar_tensor_tensor", reads=[("ucp", 1), ("nsmg", 1), "nacc0"], writes=["nacc1"],
                                out=a1[:], in0=ucp[1][:, qt, 0:64], scalar=sm[:, 28 + qt:29 + qt], in1=a0[:],
                                op0=ALU.mult, op1=ALU.add)
                            k.v("dve", "scalar_tensor_tensor", reads=[("ucp", 2), ("nsmg", 2), "nacc1"],
                                writes=[("ytok", tl)], out=ytok_sb[:, tl, h * 64:(h + 1) * 64],
                                in0=ucp[2][:, qt, 0:64], scalar=sm[:, 40 + qt:41 + qt], in1=a1[:],
                                op0=ALU.mult, op1=ALU.add)
                    pipe_post(k, cx, post3)
                pipe_flush(k, cx)
    P.barrier()


def moba_phase(k, qkT, vtok, gd, consts, identf, ident, ytok_sb):
    P = k.P
    with ExitStack() as es:
        flipf = k.sb(es, "mflipf", [128, 128], F32)
        rev = [k.sb(es, "mrev", [128, 2432], F32) for _ in range(2)]
        tbf = [k.sb(es, "mtbf", [128, 2432], BF16) for _ in range(2)]
        cbf = [k.sb(es, "mcbf", [128, 1], F32) for _ in range(2)]
        q_sb = [k.sb(es, "mq", [128, S], BF16) for _ in range(2)]
        k_sb = [k.sb(es, "mk", [128, S], BF16) for _ in range(2)]
        v_sb = [k.sb(es, "mv", [128, NT, 65], BF16) for _ in range(2)]
        ebf = k.sb(es, "ebf", [128, 1024], F32)
        valid = k.sb(es, "mvalid", [128, NT, 16], F32)
        addm = k.sb(es, "maddm", [128, NT, 16], F32)
        ownc = k.sb(es, "mown", [128, NT, 16], F32)
        km = k.sb(es, "mkm", [64, 16], F32)
        kmh = [k.sb(es, "mkmh", [64, 16], BF16) for _ in range(2)]
        kmhf = k.sb(es, "mkmhf", [64, 16], F32)
        kml = [k.sb(es, "mkml", [64, 16], BF16) for _ in range(2)]
        wk = [k.sb(es, "mwk", [128, 16], F32) for _ in range(3)]
        m8 = k.sb(es, "mm8", [128, 8], F32)
        sm = k.sb(es, "msm", [128, 8], F32)
        bq = [k.sb(es, "mbq", [128, 16], F32) for _ in range(8)]
        cx = AttnCx(k, es, 2, ident)
        pfl = [k.ps(es, "mpfl", [128, 512], F32) for _ in range(3)]
        k.dma("sp", flipf[:], consts["c_flip"], writes=["flipf"])
        for s_ in range(2):
            k.v("dve", "memset", writes=[("mneg", s_, qb) for qb in range(8)], ap=q_sb[s_][64:128, :], constant=0.0)
            k.v("dve", "memset", writes=[("eb", s_)], ap=k_sb[s_][64:128, :], constant=0.0)
            for c0 in range(0, S, 1024):
                k.dma("sp", ebf[64:80, :], consts["c_ebig"][:, c0:c0 + 1024], writes=["ebf"])
                k.v("dve", "tensor_copy", reads=["ebf"], writes=[("eb", s_)], out=k_sb[s_][64:80, c0:c0 + 1024],
                    in_=ebf[64:80, :])
        dma_split(k, "pool", valid, consts["c_mvalid"].rearrange("(c p) d -> p c d", p=128), "mvalid")
        dma_split(k, "pool", addm, consts["c_maddm"].rearrange("(c p) d -> p c d", p=128), "maddm")
        dma_split(k, "pool", ownc, consts["c_mown"].rearrange("(c p) d -> p c d", p=128), "mown")
        L_, W_, st_ = FAM["full"]

        def prep_dma(hb):
            s_ = hb % 2
            k.dma("sp", q_sb[s_][0:64, :], qkT[R_MQ + hb * 64:R_MQ + (hb + 1) * 64, :], writes=[("mq", s_)])
            k.dma("sp", k_sb[s_][0:64, :], qkT[R_MK + hb * 64:R_MK + (hb + 1) * 64, :], writes=[("mk", s_)])
            load_v_aug(k, "pool", v_sb[s_], vtok, C_MV + hb * 64, ("mv", s_))
            g = gd["full"]
            k.dma("sp", rev[s_][:, 0:W_], dap(g.tensor, g.offset + (6 + hb) * L_, [[st_, 128], [1, W_]]),
                  writes=[("mrev", s_)])

        def prep_cmp(hb):
            s_ = hb % 2
            i = 0
            for c0 in range(0, W_, 512):
                n = min(512, W_ - c0)
                b = 2
                i += 1
                k.mm(pfl[b][:, 0:n], flipf[:, :], rev[s_][:, c0:c0 + n], True, True, reads=["flipf", ("mrev", s_)],
                     writes=[("pflip", b)])
                k.v("dve", "tensor_scalar", reads=[("pflip", b)], writes=[("tbf", s_)], out=tbf[s_][:, c0:c0 + n],
                    in0=pfl[b][:, 0:n], scalar1=1.0 / SCALE, scalar2=None, op0=ALU.mult)
            k.v("dve", "tensor_scalar", reads=[("tbf", s_)], writes=[("cbf", s_)], out=cbf[s_][:, 0:1],
                in0=tbf[s_][:, 2431:2432], scalar1=SCALE, scalar2=None, op0=ALU.mult)
            k.v("dve", "tensor_reduce", reads=[("mk", s_)], writes=["mkm"], out=km[:],
                in_=k_sb[s_][0:64, :].rearrange("p (n j) -> p n j", j=256), axis=mybir.AxisListType.X, op=ALU.add)
            k.v("dve", "tensor_scalar", reads=["mkm"], writes=["mkm2"], out=km[:], in0=km[:], scalar1=1.0 / 256,
                scalar2=None, op0=ALU.mult)
            k.v("dve", "tensor_copy", reads=["mkm2"], writes=[("mkmh", s_)], out=kmh[s_][:], in_=km[:])
            k.v("dve", "tensor_copy", reads=[("mkmh", s_)], writes=["mkmhf"], out=kmhf[:], in_=kmh[s_][:])
            k.v("dve", "tensor_tensor", reads=["mkm2", "mkmhf"], writes=[("mkml", s_)], out=kml[s_][:], in0=km[:],
                in1=kmhf[:], op=ALU.subtract)

        def attend(hb):
            s_ = hb % 2
            q_, k_, v_, tb_, cb_ = q_sb[s_], k_sb[s_], v_sb[s_], tbf[s_], cbf[s_]

            def gate12(qb):
                G = pfl[0]
                for j in range(4):
                    tl = qb * 4 + j
                    k.mm(G[:, j * 16:(j + 1) * 16], q_[0:64, tl * 128:(tl + 1) * 128], kmh[s_][:, :], True, False,
                         reads=[("mq", s_), ("mkmh", s_)], writes=[("pflip", 0)])
                    k.mm(G[:, j * 16:(j + 1) * 16], q_[0:64, tl * 128:(tl + 1) * 128], kml[s_][:, :], False, True,
                         reads=[("mq", s_), ("mkml", s_)], writes=[("pflip", 0)])
                for j in range(4):
                    tl = qb * 4 + j
                    bi = (qb % 2) * 4 + j
                    a, b_, c_ = wk
                    k.v("dve", "tensor_tensor", reads=[("pflip", 0), "mvalid"], writes=["mwk0"], out=a[:],
                        in0=G[:, j * 16:(j + 1) * 16], in1=valid[:, tl, :], op=ALU.mult)
                    k.v("dve", "tensor_tensor", reads=["mwk0", "maddm"], writes=["mwk1"], out=b_[:], in0=a[:],
                        in1=addm[:, tl, :], op=ALU.add)
                    k.v("dve", "max", reads=["mwk1"], writes=["mm8"], out=m8[:], in_=b_[:])
                    k.v("dve", "tensor_scalar", reads=["mwk1", "mm8"], writes=["mwk2"], out=c_[:], in0=b_[:],
                        scalar1=m8[:, 2:3], scalar2=None, op0=ALU.is_ge)
                    k.v("dve", "tensor_tensor", reads=["mwk2", "mvalid"], writes=["mwk0"], out=a[:], in0=c_[:],
                        in1=valid[:, tl, :], op=ALU.mult)
                    k.v("dve", "scalar_tensor_tensor", reads=["mwk0", "mown"], writes=[("mbq", bi)], out=bq[bi][:],
                        in0=a[:], scalar=-NEGP, in1=ownc[:, tl, :], op0=ALU.mult, op1=ALU.add)

            def gate34(qb):
                for j in range(4):
                    bi = (qb % 2) * 4 + j
                    k.tr(pfl[1][0:16, j * 128:(j + 1) * 128], bq[bi][:, :], identf[:], reads=[("mbq", bi), "identf"],
                         writes=[("pflip", 1)])
                k.v("dve", "tensor_copy", reads=[("pflip", 1)], writes=[("mneg", s_, qb)],
                    out=q_[64:80, qb * 512:(qb + 1) * 512], in_=pfl[1][0:16, 0:512])
            gate12(0)
            gate34(0)
            gate12(1)
            for qb in range(8):
                qs = qb * 512
                qfn = lambda c0, n, qs=qs: q_[:, qs + c0:qs + c0 + n]
                tiles = []
                for kc in range(0, (qs + 384) // 128 + 1):
                    Dl = qs - 128 * kc
                    lo = max(0, -Dl // 128)
                    far = Dl >= 1664
                    tiles.append(dict(
                        rows=128, kT=k_[:, kc * 128:(kc + 1) * 128], qfn=qfn,
                        tbfn=None if far else (lambda c0, n, Dl=Dl: tb_[:, Dl + 384 + c0:Dl + 384 + c0 + n]),
                        cbias=cb_[:, 0:1] if far else None,
                        maskfn=None,
                        V=v_[:, kc, :], qt_lo=lo, qt_hi=4,
                        rd=[("mk", s_), ("mq", s_), ("tbf", s_), ("cbf", s_), ("mv", s_), ("eb", s_), ("mneg", s_, qb)]))
                ub = qb % 2

                def midm(qb=qb):
                    if qb + 1 < 8:
                        gate34(qb + 1)
                    if qb + 2 < 8:
                        gate12(qb + 2)
                run_branch(k, cx, tiles, ub, ("aU", ub), mid=midm)

                def postm(qb=qb, hb=hb, ub=ub):
                    U = cx.U[ub]
                    k.v("dve", "tensor_scalar", reads=[("aU", ub)], writes=["msm"], out=sm[:, 0:4], in0=U[:, :, 64],
                        scalar1=1e-30, scalar2=None, op0=ALU.max)
                    k.v("dve", "reciprocal", reads=["msm"], writes=["msm2"], out=sm[:, 4:8], in_=sm[:, 0:4])
                    for qt in range(4):
                        tl = qb * 4 + qt
                        k.v("dve", "tensor_scalar", reads=[("aU", ub), "msm2"], writes=[("ytok", tl)],
                            out=ytok_sb[:, tl, 384 + hb * 64:384 + (hb + 1) * 64], in0=U[:, qt, 0:64],
                            scalar1=sm[:, 4 + qt:5 + qt], scalar2=None, op0=ALU.mult)
                pipe_post(k, cx, postm)
                if hb + 1 < 4 and qb == 1:
                    prep_dma(hb + 1)
                if hb + 1 < 4 and qb == 4:
                    prep_cmp(hb + 1)
            pipe_flush(k, cx)

        prep_dma(0)
        prep_cmp(0)
        for hb in range(4):
            attend(hb)
    P.barrier()


def dil_phase(k, qkT, vtok, gd, consts, ident, ytok_sb):
    P = k.P
    with ExitStack() as es:
        flipf = k.sb(es, "dflipf", [128, 128], F32)
        rev = k.sb(es, "drev", [128, 2944], F32)
        tbs = [k.sb(es, "dtb", [128, FAM["dil%d" % g][1]], BF16) for g in range(3)]
        q_sb = [k.sb(es, "dq", [128, S], BF16) for _ in range(3)]
        k_sb = [k.sb(es, "dk", [128, S], BF16) for _ in range(3)]
        v_sb = [k.sb(es, "dv", [128, NT, 65], BF16) for _ in range(3)]
        sm = k.sb(es, "dsm", [128, 8], F32)
        cx = AttnCx(k, es, 2, ident)
        pfl = [k.ps(es, "dpfl", [128, 512], F32) for _ in range(2)]
        k.dma("sp", flipf[:], consts["c_flip"], writes=["flipf"])
        for g in range(3):
            k.v("dve", "memset", writes=[("dq0", g)], ap=q_sb[g][64:128, :], constant=0.0)
            k.v("dve", "memset", writes=[("dk0", g)], ap=k_sb[g][64:128, :], constant=0.0)
        for i in range(2):
            for g in range(3):
                hh = g * 2 + i
                k.dma("sp", q_sb[g][0:64, :], qkT[R_DQ + hh * 64:R_DQ + (hh + 1) * 64, :], writes=[("dq", g)])
                k.dma("sp", k_sb[g][0:64, :], qkT[R_DK + hh * 64:R_DK + (hh + 1) * 64, :], writes=[("dk", g)])
                load_v_aug(k, "pool", v_sb[g], vtok, C_DV + hh * 64, ("dv", g))
                make_tb(k, "dil%d" % g, gd, 10 + hh, flipf, rev, tbs[g], pfl, ("dtb", g))
            for qb in range(8):
                qs = qb * 512
                tiles = []
                for g, (W, dl) in enumerate(DIL):
                    for kc in range(max(0, (qs - W) // 128), (qs + 384) // 128 + 1):
                        Dl = qs - 128 * kc
                        lo = max(0, -Dl // 128)
                        hi = min(4, (W - Dl) // 128 + 1)
                        tiles.append(dict(
                            rows=128, kT=k_sb[g][:, kc * 128:(kc + 1) * 128],
                            qfn=lambda c0, n, g=g, qs=qs: q_sb[g][:, qs + c0:qs + c0 + n],
                            tbfn=lambda c0, n, g=g, Dl=Dl: tbs[g][:, Dl + 384 + c0:Dl + 384 + c0 + n],
                            cbias=None, maskfn=None, V=v_sb[g][:, kc, :], qt_lo=lo, qt_hi=hi,
                            rd=[("dq", g), ("dk", g), ("dq0", g), ("dk0", g), ("dv", g), ("dtb", g)]))
                ub = qb % 2
                run_branch(k, cx, tiles, ub, ("aU", ub))

                def postd(qb=qb, i=i, ub=ub):
                    U = cx.U[ub]
                    k.v("dve", "tensor_scalar", reads=[("aU", ub)], writes=["dsm"], out=sm[:, 0:4], in0=U[:, :, 64],
                        scalar1=1e-30, scalar2=None, op0=ALU.max)
                    k.v("dve", "reciprocal", reads=["dsm"], writes=["dsm2"], out=sm[:, 4:8], in_=sm[:, 0:4])
                    for qt in range(4):
                        tl = qb * 4 + qt
                        k.v("dve", "tensor_scalar", reads=[("aU", ub), "dsm2"], writes=[("ytok", tl)],
                            out=ytok_sb[:, tl, 640 + i * 64:640 + (i + 1) * 64], in0=U[:, qt, 0:64],
                            scalar1=sm[:, 4 + qt:5 + qt], scalar2=None, op0=ALU.mult)
                pipe_post(k, cx, postd)
            pipe_flush(k, cx)
    P.barrier()


def merge_phase(k, xres, mgT, w_up_a, w_up_b, w_up_c, w_o, ident, ytok_sb):
    P = k.P
    with ExitStack() as es:
        wup = k.sb(es, "wup", [128, 6, D], BF16)
        wo = k.sb(es, "wo", [128, 8, D], BF16)
        stg = [k.sb(es, "gstg", [128, D], F32) for _ in range(2)]
        yT = k.sb(es, "gyT", [128, 6, 512], BF16)
        gt = [k.sb(es, "ggt", [128, 3, 512], BF16) for _ in range(2)]
        m1 = [k.sb(es, "gm1", [128, 512], F32) for _ in range(2)]
        m2 = [k.sb(es, "gm2", [128, 512], F32) for _ in range(2)]
        m3 = [k.sb(es, "gm3", [128, 512], F32) for _ in range(2)]
        m4 = [k.sb(es, "gm4", [128, 512], F32) for _ in range(2)]
        mT = k.sb(es, "gmT", [128, 8, 512], BF16)
        xr = [k.sb(es, "gxr", [128, 512], F32) for _ in range(2)]
        ob = [k.sb(es, "gob", [128, 512], F32) for _ in range(2)]
        ptp = [k.ps(es, "gptp", [128, 1024], BF16) for _ in range(2)]
        pu = [k.ps(es, "gpu", [128, 512], F32) for _ in range(3)]
        po = [k.ps(es, "gpo", [128, 512], F32) for _ in range(2)]
        srcs = [(w_up_a, 0), (w_up_a, 1), (w_up_a, 2), (w_up_b, 0), (w_up_b, 1), (w_up_c, 0)]
        ci = 0
        for fc, (w, j) in enumerate(srcs):
            b = ci % 2
            ci += 1
            k.dma("sp" if b == 0 else "pool", stg[b][:], w[j * 128:(j + 1) * 128, :], writes=[("gstg", b)])
            k.v("dve", "tensor_copy", reads=[("gstg", b)], writes=["wup"], out=wup[:, fc, :], in_=stg[b][:])
        for kc in range(8):
            b = ci % 2
            ci += 1
            k.dma("sp" if b == 0 else "pool", stg[b][:], w_o[kc * 128:(kc + 1) * 128, :], writes=[("gstg", b)])
            k.v("dve", "tensor_copy", reads=[("gstg", b)], writes=["wo"], out=wo[:, kc, :], in_=stg[b][:])
        ui = 0
        oi = 0
        for tb8 in range(8):
            t0 = tb8 * 512
            for fc in range(6):
                pb = fc % 2
                for t in range(4):
                    tl = tb8 * 4 + t
                    k.tr(ptp[pb][:, t * 128:(t + 1) * 128], ytok_sb[:, tl, fc * 128:(fc + 1) * 128], ident[:],
                         reads=[("ytok", tl), "ident"], writes=[("gptp", pb)])
                if fc % 2 == 0:
                    k.act(yT[:, fc, :], ptp[pb][:, 0:512], AF.Copy, reads=[("gptp", pb)], writes=[("gyT", fc)])
                else:
                    k.v("dve", "tensor_copy", reads=[("gptp", pb)], writes=[("gyT", fc)], out=yT[:, fc, :],
                        in_=ptp[pb][:, 0:512])
            for cc in range(8):
                b = ui % 2
                ui += 1
                k.dma("sp", gt[b][:], mgT.rearrange("(b r) t -> r b t", b=3)[cc * 128:(cc + 1) * 128, :, t0:t0 + 512],
                      writes=[("ggt", b)])
                groups = ((0, (0, 1, 2)), (1, (3, 4)), (2, (5,)))
                for br, fcs in groups:
                    for j, fc in enumerate(fcs):
                        k.mm(pu[br][:], wup[:, fc, cc * 128:(cc + 1) * 128], yT[:, fc, :], j == 0, j == len(fcs) - 1,
                             reads=[("gyT", fc), "wup"], writes=[("gpu", br)])
                k.v("dve", "tensor_tensor", reads=[("gpu", 0), ("ggt", b)], writes=[("gm1", b)], out=m1[b][:],
                    in0=pu[0][:], in1=gt[b][:, 0, :], op=ALU.mult)
                k.v("dve", "tensor_tensor", reads=[("gpu", 1), ("ggt", b)], writes=[("gm2", b)], out=m2[b][:],
                    in0=pu[1][:], in1=gt[b][:, 1, :], op=ALU.mult)
                k.v("dve", "tensor_tensor", reads=[("gpu", 2), ("ggt", b)], writes=[("gm3", b)], out=m3[b][:],
                    in0=pu[2][:], in1=gt[b][:, 2, :], op=ALU.mult)
                k.v("dve", "tensor_tensor", reads=[("gm1", b), ("gm2", b)], writes=[("gm4", b)], out=m4[b][:],
                    in0=m1[b][:], in1=m2[b][:], op=ALU.add)
                k.v("dve", "tensor_tensor", reads=[("gm4", b), ("gm3", b)], writes=[("gmT", cc)], out=mT[:, cc, :],
                    in0=m4[b][:], in1=m3[b][:], op=ALU.add)
            allm = [("gmT", cc) for cc in range(8)]
            for t in range(4):
                r0 = t0 + t * 128
                for hh in range(2):
                    b = oi % 2
                    oi += 1
                    k.dma("pool", xr[b][:], xres[r0:r0 + 128, hh * 512:(hh + 1) * 512], writes=[("gxr", b)])
                    for cc in range(8):
                        k.mm(po[b][:], mT[:, cc, t * 128:(t + 1) * 128], wo[:, cc, hh * 512:(hh + 1) * 512],
                             cc == 0, cc == 7, reads=allm + ["wo"], writes=[("gpo", b)])
                    k.v("dve", "tensor_tensor", reads=[("gpo", b), ("gxr", b)], writes=[("gob", b)], out=ob[b][:],
                        in0=po[b][:], in1=xr[b][:], op=ALU.add)
                    k.dma("sp", xres[r0:r0 + 128, hh * 512:(hh + 1) * 512], ob[b][:], reads=[("gob", b)])
    P.barrier()


W_QK_COLS = np.concatenate([np.arange(0, 384), np.arange(384, 512), np.arange(512, 640), np.arange(640, 768),
                            np.arange(896, 1024), np.arange(1170, 1426), np.arange(1426, 1682),
                            np.arange(1938, 2322), np.arange(2322, 2706)])
W_V_COLS = np.concatenate([np.arange(768, 896), np.arange(1024, 1152), np.arange(1682, 1938),
                           np.arange(2706, 3090), np.arange(1152, 1170)])
W_MG_COLS = np.arange(3090, 6162)


def build(stop_after=None, debug=False):
    nc = bass.Bass("TRN2", target_bir_lowering=False)

    def inp(name, shape):
        return nc.dram_tensor(name, list(shape), F32, kind="ExternalInput").ap()

    def scr(name, shape, dt):
        return nc.dram_tensor(name, list(shape), dt, kind="ExternalOutput" if debug else "Internal").ap()

    x = inp("x", [S, D])
    rel_bias = inp("rel_bias", [32, 16])
    ffn_norm = [inp("ffn1_norm", [2, D]), inp("ffn2_norm", [2, D])]
    ffn_wg = [inp("ffn1_w_gate", [2, D, DFF]), inp("ffn2_w_gate", [2, D, DFF])]
    ffn_wu = [inp("ffn1_w_up", [2, D, DFF]), inp("ffn2_w_up", [2, D, DFF])]
    ffn_wd = [inp("ffn1_w_down", [2, DFF, D]), inp("ffn2_w_down", [2, DFF, D])]
    mix_norm = inp("mix_norm", [2, D])
    w_qk = inp("w_qk", [2, D, NQK])
    w_v = inp("w_v", [2, D, NVG])
    w_mg = inp("w_mg", [2, D, 3 * D])
    pe_k = inp("nsa_pe_k", [2, 32, 64])
    pe_v = inp("nsa_pe_v", [2, 32, 64])
    phi_k1 = inp("nsa_phi_k1", [2, 2048, 256])
    phi_k2 = inp("nsa_phi_k2", [2, 256, 64])
    phi_v1 = inp("nsa_phi_v1", [2, 2048, 256])
    phi_v2 = inp("nsa_phi_v2", [2, 256, 64])
    w_up_a = inp("w_up_a", [2, 384, D])
    w_up_b = inp("w_up_b", [2, 256, D])
    w_up_c = inp("w_up_c", [2, 128, D])
    w_o = inp("w_o", [2, D, D])
    final_norm = inp("final_norm", [1, D])
    consts = {nm: inp(nm, shp) for nm, shp in CONST_SHAPES.items()}
    y = nc.dram_tensor("y", [S, D], F32, kind="ExternalOutput").ap()
    xres = scr("xres", [S, D], F32)
    qkT = scr("qkT", [NQK, S], BF16)
    vtok = scr("vtok", [S, NVC], BF16)
    gtok = scr("gtok", [S, 18], F32)
    mgT = scr("mgT", [3 * D, S], BF16)
    kcT = scr("kcT", [2, 64, 256], BF16)
    vcs = scr("vcs", [2, 256, 64], BF16)
    gd = {fam: scr("gd_" + fam, [16, L], F32) for fam, (L, _, _) in FAM.items()}
    ydbg = scr("ydbg", [S, 768], BF16) if debug else None

    P = Prog(nc)
    k = K(nc, P)
    with ExitStack() as es:
        identf = k.sb(es, "identf", [128, 128], F32)
        ident = k.sb(es, "ident", [128, 128], BF16)
        k.dma("sp", identf[:], consts["c_ident"], writes=["identf"])
        k.v("dve", "tensor_copy", reads=["identf"], writes=["ident"], out=ident[:], in_=identf[:])

        def dump_y(ytok_sb):
            if debug:
                yv = ydbg.rearrange("(c p) d -> p c d", p=128)
                for c0 in range(0, NT, 8):
                    k.dma("sp", yv[:, c0:c0 + 8, :], ytok_sb[:, c0:c0 + 8, :], reads=[("ytok", t) for t in range(NT)])
                P.barrier()

        ystack = []

        def dump_y_dummy():
            pass

        def run():
            stages = stop_after
            bias_gen_phase(k, rel_bias, consts, gd)
            for l in range(2):
                ffn_phase(k, x if l == 0 else xres, xres, ffn_norm[0][l:l + 1, :], ffn_wg[0][l], ffn_wu[0][l],
                          ffn_wd[0][l], ident)
                if stages == "ffn1":
                    return
                proj_phase(k, xres, mix_norm[l:l + 1, :], w_qk[l], w_v[l], w_mg[l], ident, qkT, vtok, gtok, mgT)
                if stages == "proj":
                    return
                compress_phase(k, qkT, pe_k[l], pe_v[l], phi_k1[l], phi_k2[l], phi_v1[l], phi_v2[l], kcT, vcs, identf)
                if stages == "cmp":
                    return
                ys = ExitStack()
                ytok_sb = k.sb(ys, "ytok", [128, NT, 768], BF16)
                ystack.append(ys)
                if stages in (None, "nsa", "attn", "merge", "l0"):
                    nsa_phase(k, qkT, vtok, gtok, kcT, vcs, gd, consts, identf, ident, ytok_sb)
                if stages == "nsa":
                    dump_y(ytok_sb)
                    return
                if stages in (None, "moba", "attn", "merge", "l0"):
                    moba_phase(k, qkT, vtok, gd, consts, identf, ident, ytok_sb)
                if stages == "moba":
                    dump_y(ytok_sb)
                    return
                dil_phase(k, qkT, vtok, gd, consts, ident, ytok_sb)
                if stages in ("dil", "attn"):
                    dump_y(ytok_sb)
                    return
                merge_phase(k, xres, mgT, w_up_a[l], w_up_b[l], w_up_c[l], w_o[l], ident, ytok_sb)
                ystack.pop().close()
                if stages == "merge":
                    return
                ffn_phase(k, xres, xres, ffn_norm[1][l:l + 1, :], ffn_wg[1][l], ffn_wu[1][l], ffn_wd[1][l], ident)
                if stages == "l0":
                    return
            final_norm_phase(k, xres, y, final_norm)
        run()
        P.barrier()
        with ExitStack() as es2:
            P.emit(es2)
        while ystack:
            ystack.pop().close()
    return nc


def make_in_maps(inputs, cores):
    consts = host_consts()
    w_in = np.asarray(inputs["w_in"])
    shared = dict(consts)
    shared["w_qk"] = np.ascontiguousarray(w_in[:, :, W_QK_COLS])
    shared["w_v"] = np.ascontiguousarray(w_in[:, :, W_V_COLS])
    shared["w_mg"] = np.ascontiguousarray(w_in[:, :, W_MG_COLS])
    for nm in ("rel_bias", "ffn1_norm", "ffn2_norm", "ffn1_w_gate", "ffn2_w_gate", "ffn1_w_up", "ffn2_w_up",
               "ffn1_w_down", "ffn2_w_down", "mix_norm", "nsa_pe_k", "nsa_pe_v", "nsa_phi_k1", "nsa_phi_k2",
               "nsa_phi_v1", "nsa_phi_v2", "w_up_a", "w_up_b", "w_up_c", "w_o"):
        shared[nm] = np.ascontiguousarray(np.asarray(inputs[nm], dtype=np.float32))
    shared["final_norm"] = np.ascontiguousarray(np.asarray(inputs["final_norm"], dtype=np.float32)).reshape(1, D)
    maps = []
    for b in cores:
        m = dict(shared)
        m["x"] = np.ascontiguousarray(np.asarray(inputs["x"][b], dtype=np.float32))
        maps.append(m)
    return maps


def kernel(**inputs):
    nc = build()
    in_maps = make_in_maps(inputs, range(8))
    res = run_bass_kernel_spmd(nc, in_maps, core_ids=list(range(8)))
    return np.stack([np.asarray(r["y"]) for r in res.results], axis=0).astype(np.float32)
```

```python
import numpy as np
from contextlib import ExitStack
import concourse.bass as bass
import concourse.mybir as mybir
from concourse.bass_utils import run_bass_kernel_spmd

F32 = mybir.dt.float32
BF16 = mybir.dt.bfloat16
AF = mybir.ActivationFunctionType
ALU = mybir.AluOpType

S = 4096
D = 1024
DFF = 2816
NFC = DFF // 128
NT = S // 128
EPS = 1e-6
NEGM = -30000.0
NDS = 16


class _Op:
    __slots__ = ("eng", "fn", "deps", "dma", "signal", "sig", "dsem", "dval", "bar")


class Prog:
    ENG = ("pe", "act", "dve", "pool", "sp")

    def __init__(self, nc):
        self.nc = nc
        self.ops = []
        self.lastw = {}
        self.readers = {}
        self.last_on = {e: None for e in self.ENG}

    def op(self, eng, fn, reads=(), writes=(), dma=False):
        o = _Op()
        o.eng, o.fn, o.dma, o.signal, o.bar = eng, fn, dma, False, None
        deps = set()
        for r in reads:
            w = self.lastw.get(r)
            if w is not None:
                deps.add(w)
        for w_ in writes:
            w = self.lastw.get(w_)
            if w is not None:
                deps.add(w)
            for rd in self.readers.get(w_, ()):
                deps.add(rd)
        idx = len(self.ops)
        for w_ in writes:
            self.lastw[w_] = idx
            self.readers[w_] = []
        for r in reads:
            self.readers.setdefault(r, []).append(idx)
        deps.discard(idx)
        best = {}
        pruned = []
        for d in deps:
            p = self.ops[d]
            if p.dma:
                pruned.append(d)
            elif best.get(p.eng, -1) < d:
                best[p.eng] = d
        pruned.extend(best.values())
        o.deps = sorted(pruned)
        for d in o.deps:
            self.ops[d].signal = True
        self.ops.append(o)
        self.last_on[eng] = idx
        return idx

    def barrier(self):
        o = _Op()
        o.eng, o.fn, o.dma, o.signal, o.deps = None, None, False, False, []
        o.bar = dict(self.last_on)
        for e, i in o.bar.items():
            if i is not None and not self.ops[i].dma:
                self.ops[i].signal = True
        self.ops.append(o)

    def emit(self, es):
        nc = self.nc
        E = {"pe": nc.tensor, "act": nc.scalar, "dve": nc.vector, "pool": nc.gpsimd, "sp": nc.sync}
        sem = {e: es.enter_context(nc.semaphore("s_" + e)) for e in self.ENG}
        dsem = {q: [es.enter_context(nc.semaphore("d_%s%d" % (q, i))) for i in range(NDS)]
                for q in ("sp", "pool")}
        cnt = {e: 0 for e in self.ENG}
        dcnt = {"sp": 0, "pool": 0}
        seen = {e: {} for e in self.ENG}

        def wait(e, s, v):
            key = id(s)
            if seen[e].get(key, 0) < v:
                E[e].wait_ge(s, v)
                seen[e][key] = v

        for o in self.ops:
            if o.bar is not None:
                for q in ("sp", "pool"):
                    k = dcnt[q]
                    if k == 0:
                        continue
                    for i in range(min(k, NDS)):
                        uses = (k - 1 - i) // NDS + 1
                        wait(q, dsem[q][i], 16 * uses)
                    E[q].sem_inc(sem[q], 1)
                    cnt[q] += 1
                for f in self.ENG:
                    for e in self.ENG:
                        if e != f and cnt[e] > 0:
                            wait(f, sem[e], cnt[e])
                continue
            e = o.eng
            for d in o.deps:
                p = self.ops[d]
                if p.dma:
                    wait(e, p.dsem, p.dval)
                else:
                    if p.eng == e and e == "pe":
                        continue
                    wait(e, sem[p.eng], p.sig)
            if o.dma:
                k = dcnt[e]
                s = dsem[e][k % NDS]
                v = 16 * (k // NDS + 1)
                if k >= NDS:
                    wait(e, s, v - 16)
                ins = o.fn()
                ins.then_inc(s, 16)
                o.dsem, o.dval = s, v
                dcnt[e] = k + 1
            else:
                ins = o.fn()
                if o.signal:
                    cnt[e] += 1
                    o.sig = cnt[e]
                    ins.then_inc(sem[e], 1)
        self.ops = []


class K:
    def __init__(self, nc, P):
        self.nc, self.P = nc, P
        self.uid = 0

    def name(self, s):
        self.uid += 1
        return "%s_%d" % (s, self.uid)

    def sb(self, es, nm, shape, dt):
        return es.enter_context(self.nc.sbuf_tensor(self.name(nm), list(shape), dt))

    def ps(self, es, nm, shape, dt):
        return es.enter_context(self.nc.psum_tensor(self.name(nm), list(shape), dt))

    def dma(self, q, out, in_, reads=(), writes=(), slow=False):
        nc = self.nc
        eng = nc.sync if q == "sp" else nc.gpsimd
        if slow:
            return self.P.op(q, lambda: eng.dma_start(out=out, in_=in_, allow_slow_non_contiguous=True),
                             reads, writes, dma=True)
        return self.P.op(q, lambda: eng.dma_start(out=out, in_=in_), reads, writes, dma=True)

    def mm(self, out, lhsT, rhs, start, stop, reads=(), writes=()):
        nc = self.nc
        return self.P.op("pe", lambda: nc.tensor.matmul(out, lhsT=lhsT, rhs=rhs, start=start, stop=stop),
                         reads, writes)

    def tr(self, out, in_, ident, reads=(), writes=()):
        nc = self.nc
        return self.P.op("pe", lambda: nc.tensor.transpose(out, in_, ident), reads, writes)

    def act(self, out, in_, func, reads=(), writes=(), **kw):
        nc = self.nc
        return self.P.op("act", lambda: nc.scalar.activation(out=out, in_=in_, func=func, **kw), reads, writes)

    def v(self, eng, name, reads=(), writes=(), **kw):
        nc = self.nc
        e = nc.vector if eng == "dve" else nc.gpsimd
        return self.P.op(eng, lambda: getattr(e, name)(**kw), reads, writes)


def dap(t, offset, pattern):
    return bass.AP(tensor=t, offset=offset, ap=[list(p) for p in pattern])


def cast(k, sel, out, in_, reads, writes):
    if sel % 2 == 0:
        k.act(out, in_, AF.Copy, reads=reads, writes=writes)
    else:
        k.v("dve", "tensor_copy", reads=reads, writes=writes, out=out, in_=in_)


def ffn_phase(k, xsrc, xdst, w_norm, w_gate, w_up, w_down, ident):
    nc, P = k.nc, k.P
    G = 512
    with ExitStack() as es:
        wg = k.sb(es, "wg", [128, 8, DFF], BF16)
        wu = k.sb(es, "wu", [128, 8, DFF], BF16)
        wd = k.sb(es, "wd", [128, NFC, D], BF16)
        HW = DFF // 2
        NSTG = 3
        stg = [k.sb(es, "stg", [128, HW], F32) for _ in range(NSTG)]
        gain = k.sb(es, "gain", [128, D], F32)
        xin = [k.sb(es, "xin", [128, D], F32) for _ in range(2)]
        hn = [k.sb(es, "hn", [128, D], BF16) for _ in range(2)]
        hT = k.sb(es, "hT", [128, 8, G], BF16)
        aT = k.sb(es, "aT", [128, NFC, G], BF16)
        sg = [k.sb(es, "sg", [128, G], F32) for _ in range(2)]
        xr = [k.sb(es, "xr", [128, 512], F32) for _ in range(2)]
        ob = [k.sb(es, "ob", [128, 512], F32) for _ in range(2)]
        st = [k.sb(es, "st", [128, 4], F32) for _ in range(2)]
        pgu = [k.ps(es, "pgu", [128, 512], F32) for _ in range(4)]
        ptp = [k.ps(es, "ptp", [128, 1024], BF16) for _ in range(2)]
        pdn = [k.ps(es, "pdn", [128, 512], F32) for _ in range(2)]

        k.dma("sp", gain[:], dap(w_norm.tensor, w_norm.offset, [[0, 128], [1, D]]), writes=["gain"])
        wgv = w_gate.rearrange("(kc p) f -> p kc f", p=128)
        wuv = w_up.rearrange("(kc p) f -> p kc f", p=128)
        wdv = w_down.rearrange("(fc p) d -> p fc d", p=128)
        wq = []
        cnt = [0]

        def chunk(dst_ap, src_ap, width, tok):
            def go():
                ci = cnt[0]
                cnt[0] += 1
                b = ci % NSTG
                k.dma("sp" if ci % 2 == 0 else "pool", stg[b][:, 0:width], src_ap, writes=[("stg", b)])
                cast(k, ci, dst_ap, stg[b][:, 0:width], [("stg", b)], [tok])
            return go
        for hh in range(2):
            for c in range(8):
                for (dst, srcv, nm) in ((wg, wgv, "wg"), (wu, wuv, "wu")):
                    wq.append(chunk(dst[:, c, hh * HW:(hh + 1) * HW], srcv[:, c, hh * HW:(hh + 1) * HW], HW,
                                    (nm, c, hh)))
        for c in range(NFC):
            wq.append(chunk(wd[:, c, :], wdv[:, c, :], D, ("wd", c)))

        def emit_w(n):
            for _ in range(n):
                if wq:
                    wq.pop(0)()

        gi = 0
        di = 0
        for g in range(S // G):
            for t in range(G // 128):
                r0 = g * G + t * 128
                b = t % 2
                k.dma("sp", xin[b][:], xsrc[r0:r0 + 128, :], writes=[("xin", b)])
                k.v("dve", "scalar_tensor_tensor", reads=[("xin", b)], writes=[("hn", b), ("st", b)],
                    out=hn[b][:], in0=xin[b][:], scalar=1.0, in1=xin[b][:], op0=ALU.mult, op1=ALU.mult,
                    accum_out=st[b][:, 0:1])
                k.act(st[b][:, 1:2], st[b][:, 0:1], AF.Sqrt, reads=[("st", b)], writes=[("st1", b)],
                      scale=1.0 / D, bias=EPS)
                k.v("dve", "reciprocal", reads=[("st1", b)], writes=[("st2", b)],
                    out=st[b][:, 2:3], in_=st[b][:, 1:2])
                k.v("dve", "scalar_tensor_tensor", reads=[("xin", b), ("st2", b), "gain"], writes=[("hn", b)],
                    out=hn[b][:], in0=xin[b][:], scalar=st[b][:, 2:3], in1=gain[:], op0=ALU.mult, op1=ALU.mult)
                for kc in range(8):
                    k.tr(ptp[b][:, kc * 128:(kc + 1) * 128], hn[b][:, kc * 128:(kc + 1) * 128], ident[:],
                         reads=[("hn", b), "ident"], writes=[("ptp", b)])
                k.act(hT[:, :, t * 128:(t + 1) * 128], ptp[b][:].rearrange("p (a c) -> p a c", a=8), AF.Copy,
                      reads=[("ptp", b)], writes=[("hT", t)])
            hT_all = [("hT", t) for t in range(G // 128)]
            emit_w(16 if g == 0 else 0)
            for fc in range(NFC):
                emit_w(2)
                pb = (gi % 2) * 2
                gi += 1
                for kc in range(8):
                    k.mm(pgu[pb][:], wg[:, kc, fc * 128:(fc + 1) * 128], hT[:, kc, :], kc == 0, kc == 7,
                         reads=hT_all + [("wg", kc, fc // (NFC // 2))], writes=[("pgu", pb)])
                for kc in range(8):
                    k.mm(pgu[pb + 1][:], wu[:, kc, fc * 128:(fc + 1) * 128], hT[:, kc, :], kc == 0, kc == 7,
                         reads=hT_all + [("wu", kc, fc // (NFC // 2))], writes=[("pgu", pb + 1)])
                sb_ = fc % 2
                k.act(sg[sb_][:], pgu[pb][:], AF.Silu, reads=[("pgu", pb)], writes=[("sg", sb_)])
                k.v("dve", "tensor_tensor", reads=[("sg", sb_), ("pgu", pb + 1)], writes=[("aT", fc)],
                    out=aT[:, fc, :], in0=sg[sb_][:], in1=pgu[pb + 1][:], op=ALU.mult)
            aT_all = [("aT", fc) for fc in range(NFC)]
            for t in range(G // 128):
                r0 = g * G + t * 128
                for hh in range(2):
                    b = di % 2
                    di += 1
                    k.dma("pool", xr[b][:], xsrc[r0:r0 + 128, hh * 512:(hh + 1) * 512], writes=[("xr", b)])
                    for fc in range(NFC):
                        k.mm(pdn[b][:], aT[:, fc, t * 128:(t + 1) * 128], wd[:, fc, hh * 512:(hh + 1) * 512],
                             fc == 0, fc == NFC - 1, reads=aT_all + [("wd", fc)], writes=[("pdn", b)])
                    k.v("dve", "scalar_tensor_tensor", reads=[("pdn", b), ("xr", b)], writes=[("ob", b)],
                        out=ob[b][:], in0=pdn[b][:], scalar=0.5, in1=xr[b][:], op0=ALU.mult, op1=ALU.add)
                    k.dma("sp", xdst[r0:r0 + 128, hh * 512:(hh + 1) * 512], ob[b][:], reads=[("ob", b)])
    P.barrier()


def final_norm_phase(k, xsrc, ydst, w_norm):
    P = k.P
    with ExitStack() as es:
        gain = k.sb(es, "fgain", [128, D], F32)
        xin = [k.sb(es, "fxin", [128, D], F32) for _ in range(2)]
        yo = [k.sb(es, "fyo", [128, D], F32) for _ in range(2)]
        junk = k.sb(es, "fjunk", [128, D], BF16)
        st = [k.sb(es, "fst", [128, 4], F32) for _ in range(2)]
        k.dma("sp", gain[:], dap(w_norm.tensor, w_norm.offset, [[0, 128], [1, D]]), writes=["fgain"])
        for t in range(NT):
            b = t % 2
            r0 = t * 128
            k.dma("sp", xin[b][:], xsrc[r0:r0 + 128, :], writes=[("fxin", b)])
            k.v("dve", "scalar_tensor_tensor", reads=[("fxin", b)], writes=["fjunk", ("fst", b)],
                out=junk[:], in0=xin[b][:], scalar=1.0, in1=xin[b][:], op0=ALU.mult, op1=ALU.mult,
                accum_out=st[b][:, 0:1])
            k.act(st[b][:, 1:2], st[b][:, 0:1], AF.Sqrt, reads=[("fst", b)], writes=[("fst1", b)],
                  scale=1.0 / D, bias=EPS)
            k.v("dve", "reciprocal", reads=[("fst1", b)], writes=[("fst2", b)],
                out=st[b][:, 2:3], in_=st[b][:, 1:2])
            k.v("dve", "scalar_tensor_tensor", reads=[("fxin", b), ("fst2", b), "fgain"], writes=[("fyo", b)],
                out=yo[b][:], in0=xin[b][:], scalar=st[b][:, 2:3], in1=gain[:], op0=ALU.mult, op1=ALU.mult)
            k.dma("pool", ydst[r0:r0 + 128, :], yo[b][:], reads=[("fyo", b)])
    P.barrier()


NQK = 2176
NVC = 896
NVG = NVC + 18
R_NSAQ, R_KCMP, R_VCMP, R_KSEL, R_KWIN, R_MQ, R_MK, R_DQ, R_DK = 0, 384, 512, 640, 768, 896, 1152, 1408, 1792
C_VSEL, C_VWIN, C_MV, C_DV = 0, 128, 256, 512
FAM = {
    "full": (2560, 2432, 1),
    "win": (1536, 1408, 1),
    "cmp": (6144, 4096, 16),
    "dil0": (1152, 1024, 1),
    "dil1": (1536, 1408, 1),
    "dil2": (3072, 2944, 1),
}
DIL = ((128, 1), (512, 4), (2048, 16))
SCALE = 0.125
NEGP = -240000.0


def np_rel_bucket(dist):
    n = np.maximum(dist, 0)
    nf = np.maximum(n, 16).astype(np.float32)
    lg = (np.log(nf / np.float32(16)) / np.float32(np.log(2048 / 16)) * np.float32(16)).astype(np.float32)
    large = 16 + lg.astype(np.int32)
    large = np.minimum(large, 31)
    return np.where(n < 16, n, large)


def host_consts():
    c = {}
    c["c_ident"] = np.eye(128, dtype=np.float32)
    c["c_flip"] = np.ascontiguousarray(np.eye(128, dtype=np.float32)[::-1])
    def fam_oh(length, off, valid_fn):
        w = np.arange(length)
        d = w - off
        ok = valid_fn(d)
        b = np_rel_bucket(d)
        oh = np.zeros((33, length), np.float32)
        oh[b[ok], w[ok]] = 1.0
        oh[32, ~ok] = 1.0
        return oh
    c["oh_full"] = fam_oh(2560, 511, lambda d: d >= 0)
    c["oh_win"] = fam_oh(1536, 511, lambda d: (d >= 0) & (d < 512))
    c["oh_cmp"] = fam_oh(6144, 2063, lambda d: d >= 0)
    for g, (W, dl) in enumerate(DIL):
        c["oh_dil%d" % g] = fam_oh(FAM["dil%d" % g][0], 511, lambda d: (d >= 0) & (d <= W) & (d % dl == 0))
    c["c_negrow"] = np.full((1, 16), NEGM, np.float32)
    n_cmp = 255
    c_start = np.arange(n_cmp) * 16
    s_start = np.arange(64) * 64
    ov = (c_start[:, None] < s_start[None, :] + 64) & (c_start[:, None] + 32 > s_start[None, :])
    cts = np.zeros((256, 65), np.float32)
    cts[:255, :64] = ov
    cts[:255, 64] = 1.0
    c["c_cts"] = cts
    t = np.arange(S)
    blk = np.arange(64)
    cur = t // 64
    keep = np.ones((S, 64), np.float32)
    add = np.zeros((S, 64), np.float32)
    f0 = np.broadcast_to(blk[None, :] == 0, (S, 64))
    f1 = blk[None, :] == cur[:, None]
    f2 = blk[None, :] == cur[:, None] - 1
    fut = blk[None, :] * 64 > t[:, None]
    for f, val in ((f0, 1e4), (f2, 3e4), (f1, 2e4)):
        keep[f] = 0.0
        add[f] = val
    keep[fut] = 0.0
    add[fut] = -1e30
    c["c_keep"] = keep
    c["c_add"] = add
    c["c_esel"] = (np.arange(S)[None, :] // 64 == np.arange(64)[:, None]).astype(np.float32)
    nb = np.arange(16)
    cb = t // 256
    valid = (nb[None, :] < cb[:, None]).astype(np.float32)
    own = (nb[None, :] == cb[:, None]).astype(np.float32)
    c["c_mvalid"] = valid
    c["c_maddm"] = np.where(valid > 0, 0.0, -1e30).astype(np.float32)
    c["c_mown"] = ((own - 1.0) * (-NEGP)).astype(np.float32)
    eb = np.zeros((16, 16, 128), np.float32)
    for n in range(16):
        eb[n, n, :] = 1.0
    c["c_eb"] = eb.reshape(16, 16 * 128)
    c["c_ebig"] = (np.arange(S)[None, :] // 256 == np.arange(16)[:, None]).astype(np.float32)
    return c


CONST_SHAPES = {
    "c_ident": [128, 128], "c_flip": [128, 128], "oh_full": [33, 2560], "oh_win": [33, 1536],
    "oh_cmp": [33, 6144], "oh_dil0": [33, 1152], "oh_dil1": [33, 1536], "oh_dil2": [33, 3072],
    "c_negrow": [1, 16], "c_cts": [256, 65], "c_keep": [S, 64], "c_add": [S, 64], "c_esel": [64, S],
    "c_mvalid": [S, 16], "c_maddm": [S, 16], "c_mown": [S, 16], "c_eb": [16, 2048], "c_ebig": [16, S],
}


def bias_gen_phase(k, rel_bias, consts, gd):
    P = k.P
    with ExitStack() as es:
        tblx = k.sb(es, "tblx", [33, 16], F32)
        oh = k.sb(es, "oh", [33, 6144], F32)
        go = k.sb(es, "go", [16, 6144], F32)
        pb = [k.ps(es, "pbg", [128, 512], F32) for _ in range(2)]
        k.dma("sp", tblx[0:32, :], rel_bias, writes=["tblx"])
        k.dma("sp", tblx[32:33, :], consts["c_negrow"], writes=["tblx"])
        i = 0
        for fam, (L, _, _) in FAM.items():
            k.dma("sp", oh[:, 0:L], consts["oh_" + fam], writes=["oh"])
            for c0 in range(0, L, 512):
                b = i % 2
                i += 1
                k.mm(pb[b][0:16, :], tblx[:, :], oh[:, c0:c0 + 512], True, True,
                     reads=["tblx", "oh"], writes=[("pbg", b)])
                k.v("dve", "tensor_copy", reads=[("pbg", b)], writes=["go"], out=go[:, c0:c0 + 512],
                    in_=pb[b][0:16, :])
            k.dma("sp", gd[fam], go[:, 0:L], reads=["go"])
    P.barrier()


def make_tb(k, fam, gd, h, flipf, rev, tb, pflip, tok):
    L, W, st = FAM[fam]
    g = gd[fam]
    k.dma("sp", rev[:, 0:W], dap(g.tensor, g.offset + h * L, [[st, 128], [1, W]]), writes=["rev"])
    i = 0
    for c0 in range(0, W, 512):
        n = min(512, W - c0)
        b = i % len(pflip)
        i += 1
        k.mm(pflip[b][:, 0:n], flipf[:, :], rev[:, c0:c0 + n], True, True, reads=["flipf", "rev"],
             writes=[("pflip", b)])
        if i % 2 == 0:
            k.act(tb[:, c0:c0 + n], pflip[b][:, 0:n], AF.Copy, reads=[("pflip", b)], writes=[tok], scale=1.0 / SCALE)
        else:
            k.v("dve", "tensor_scalar", reads=[("pflip", b)], writes=[tok], out=tb[:, c0:c0 + n],
                in0=pflip[b][:, 0:n], scalar1=1.0 / SCALE, scalar2=None, op0=ALU.mult)


def proj_phase(k, xsrc, w_norm, w_qk, w_v, w_mg, ident, qkT, vtok, gtok, mgT):
    P = k.P
    with ExitStack() as es:
        hT = k.sb(es, "phT", [128, 8, S], BF16)
        gain = k.sb(es, "pgain", [128, D], F32)
        xin = [k.sb(es, "pxin", [128, D], F32) for _ in range(3)]
        hn = [k.sb(es, "phn", [128, D], BF16) for _ in range(3)]
        st = [k.sb(es, "pst", [128, 4], F32) for _ in range(3)]
        wvs = k.sb(es, "wvs", [128, NVG], F32)
        wvb = k.sb(es, "wvb", [128, 8, NVG], BF16)
        wst = [k.sb(es, "wst", [128, 8, 128], F32) for _ in range(3)]
        wb = [k.sb(es, "wb", [128, 8, 128], BF16) for _ in range(3)]
        orow = [k.sb(es, "orow", [128, S], BF16) for _ in range(2)]
        vout = [k.sb(es, "vout", [128, NVC], BF16) for _ in range(3)]
        gout = [k.sb(es, "gout", [128, 18], F32) for _ in range(3)]
        ptp = [k.ps(es, "pptp", [128, 1024], BF16) for _ in range(3)]
        pmm = [k.ps(es, "ppmm", [128, 512], F32) for _ in range(4)]
        k.dma("sp", gain[:], dap(w_norm.tensor, w_norm.offset, [[0, 128], [1, D]]), writes=["pgain"])
        wvv = w_v.rearrange("(kc p) f -> p kc f", p=128)
        for kc in range(8):
            k.dma("pool", wvs[:], wvv[:, kc, :], writes=["wvs"])
            k.v("dve", "tensor_copy", reads=["wvs"], writes=["wvb"], out=wvb[:, kc, :], in_=wvs[:])
        def stage_a(t):
                b = t % 3
                r0 = t * 128
                k.dma("sp", xin[b][:], xsrc[r0:r0 + 128, :], writes=[("pxin", b)])
                k.v("dve", "scalar_tensor_tensor", reads=[("pxin", b)], writes=[("phn", b), ("pst", b)],
                    out=hn[b][:], in0=xin[b][:], scalar=1.0, in1=xin[b][:], op0=ALU.mult, op1=ALU.mult,
                    accum_out=st[b][:, 0:1])
                k.act(st[b][:, 1:2], st[b][:, 0:1], AF.Sqrt, reads=[("pst", b)], writes=[("pst1", b)],
                      scale=1.0 / D, bias=EPS)
                k.v("dve", "reciprocal", reads=[("pst1", b)], writes=[("pst2", b)],
                    out=st[b][:, 2:3], in_=st[b][:, 1:2])
                k.v("dve", "scalar_tensor_tensor", reads=[("pxin", b), ("pst2", b), "pgain"], writes=[("phn", b)],
                    out=hn[b][:], in0=xin[b][:], scalar=st[b][:, 2:3], in1=gain[:], op0=ALU.mult, op1=ALU.mult)
                for kc in range(8):
                    k.tr(ptp[b][:, kc * 128:(kc + 1) * 128], hn[b][:, kc * 128:(kc + 1) * 128], ident[:],
                         reads=[("phn", b), "ident"], writes=[("pptp", b)])
                k.act(hT[:, :, r0:r0 + 128], ptp[b][:].rearrange("p (a c) -> p a c", a=8), AF.Copy,
                      reads=[("pptp", b)], writes=[("phT", t)])

        def stage_b(t):
                b = t % 3
                r0 = t * 128
                pa, pb_ = pmm[(t % 2) * 2], pmm[(t % 2) * 2 + 1]
                ta, tb_ = ("ppmm", (t % 2) * 2), ("ppmm", (t % 2) * 2 + 1)
                for kc in range(8):
                    k.mm(pa[:, 0:512], hT[:, kc, r0:r0 + 128], wvb[:, kc, 0:512], kc == 0, kc == 7,
                         reads=[("phT", t), "wvb"], writes=[ta])
                for kc in range(8):
                    k.mm(pb_[:, 0:NVG - 512], hT[:, kc, r0:r0 + 128], wvb[:, kc, 512:NVG], kc == 0, kc == 7,
                         reads=[("phT", t), "wvb"], writes=[tb_])
                k.act(vout[b][:, 0:512], pa[:, 0:512], AF.Copy, reads=[ta], writes=[("vout", b)])
                k.v("dve", "tensor_copy", reads=[tb_], writes=[("vout", b)], out=vout[b][:, 512:NVC],
                    in_=pb_[:, 0:NVC - 512])
                k.act(gout[b][:], pb_[:, NVC - 512:NVG - 512], AF.Sigmoid, reads=[tb_], writes=[("gout", b)])
                k.dma("pool", vtok[r0:r0 + 128, :], vout[b][:], reads=[("vout", b)])
                k.dma("pool", gtok[r0:r0 + 128, :], gout[b][:], reads=[("gout", b)])

        stage_a(0)
        stage_a(1)
        for t in range(NT):
            if t + 2 < NT:
                stage_a(t + 2)
            stage_b(t)
        allh = [("phT", t) for t in range(NT)]
        blocks = [("qk", i) for i in range(NQK // 128)] + [("mg", i) for i in range(3 * D // 128)]
        mi = 0
        def prefetch(bi):
            kind, i = blocks[bi]
            b = bi % 3
            wsrc = (w_qk if kind == "qk" else w_mg)[:, i * 128:(i + 1) * 128].rearrange("(kc p) f -> p kc f", p=128)
            k.dma("pool", wst[b][:], wsrc, writes=[("wst", b)])
            if bi % 2 == 0:
                k.v("dve", "tensor_copy", reads=[("wst", b)], writes=[("wb", b)], out=wb[b][:], in_=wst[b][:])
            else:
                k.act(wb[b][:], wst[b][:], AF.Copy, reads=[("wst", b)], writes=[("wb", b)])
        prefetch(0)
        prefetch(1)
        for bi, (kind, i) in enumerate(blocks):
            b = bi % 3
            if bi + 2 < len(blocks):
                prefetch(bi + 2)
            for tb8 in range(8):
                pi = mi % 4
                mi += 1
                for kc in range(8):
                    k.mm(pmm[pi][:], wb[b][:, kc, :], hT[:, kc, tb8 * 512:(tb8 + 1) * 512], kc == 0, kc == 7,
                         reads=allh + [("wb", b)], writes=[("ppmm", pi)])
                ob_ = bi % 2
                if kind == "mg":
                    k.act(orow[ob_][:, tb8 * 512:(tb8 + 1) * 512], pmm[pi][:], AF.Sigmoid,
                          reads=[("ppmm", pi)], writes=[("orow", ob_)])
                elif tb8 % 2 == 0:
                    k.act(orow[ob_][:, tb8 * 512:(tb8 + 1) * 512], pmm[pi][:], AF.Copy,
                          reads=[("ppmm", pi)], writes=[("orow", ob_)])
                else:
                    k.v("dve", "tensor_copy", reads=[("ppmm", pi)], writes=[("orow", ob_)],
                        out=orow[ob_][:, tb8 * 512:(tb8 + 1) * 512], in_=pmm[pi][:])
            dst = (qkT if kind == "qk" else mgT)[i * 128:(i + 1) * 128, :]
            k.dma("sp", dst, orow[bi % 2][:], reads=[("orow", bi % 2)])
    P.barrier()


def compress_phase(k, qkT, pe_k, pe_v, phi_k1, phi_k2, phi_v1, phi_v2, kcT, vcs, identf):
    P = k.P
    C1 = 1.5957691216057308
    with ExitStack() as es:
        src = k.sb(es, "csrc", [128, S], BF16)
        w1s = k.sb(es, "w1s", [128, 8, 256], F32)
        w1b = k.sb(es, "w1b", [128, 32, 256], BF16)
        w2s = k.sb(es, "w2s", [128, 2, 64], F32)
        w2b = k.sb(es, "w2b", [128, 2, 64], BF16)
        pes = k.sb(es, "pes", [128, 64], F32)
        peb = k.sb(es, "peb", [128, 32], BF16)
        bias = k.sb(es, "cbias", [128, 2], F32)
        xa = k.sb(es, "cxa", [128, 256], F32)
        xb_ = k.sb(es, "cxb", [128, 256], F32)
        xc = k.sb(es, "cxc", [128, 256], F32)
        hid = [k.sb(es, "chid", [128, 256], BF16) for _ in range(2)]
        ko = k.sb(es, "cko", [64, 256], BF16)
        vo = k.sb(es, "cvo", [128, 64], BF16)
        ph = [k.ps(es, "cph", [128, 512], F32) for _ in range(2)]
        pbias = k.ps(es, "cpb", [128, 512], F32)
        po = k.ps(es, "cpo", [128, 512], F32)
        for which, (r0, pe, w1, w2) in enumerate(((R_KCMP, pe_k, phi_k1, phi_k2), (R_VCMP, pe_v, phi_v1, phi_v2))):
            k.dma("sp", src[:], qkT[r0:r0 + 128, :], writes=["csrc"])
            w1v = w1.rearrange("(l d) h -> d l h", d=64)
            for half in range(2):
                for l0 in range(0, 32, 8):
                    k.dma("pool", w1s[half * 64:(half + 1) * 64, :, :], w1v[:, l0:l0 + 8, :], writes=["w1s"])
                    k.v("dve", "tensor_copy", reads=["w1s"], writes=["w1b"],
                        out=w1b[half * 64:(half + 1) * 64, l0:l0 + 8, :], in_=w1s[half * 64:(half + 1) * 64, :, :])
            k.dma("sp", pes[0:32, 0:64], pe, writes=["pes"])
            k.tr(pbias[0:64, 64:96], pes[0:32, 0:64], identf[0:32, 0:32], reads=["pes", "identf"], writes=["cpb"])
            k.v("dve", "tensor_copy", reads=["cpb"], writes=["peb"], out=peb[0:64, :], in_=pbias[0:64, 64:96])
            k.dma("sp", w2s[:], w2.rearrange("(hc p) d -> p hc d", p=128), writes=["w2s"])
            k.v("dve", "tensor_copy", reads=["w2s"], writes=["w2b"], out=w2b[:], in_=w2s[:])
            for hc in range(2):
                for l in range(32):
                    k.mm(pbias[:, hc:hc + 1], w1b[0:64, l, hc * 128:(hc + 1) * 128], peb[0:64, l:l + 1],
                         l == 0, l == 31, reads=["w1b", "peb"], writes=["cpb"])
            k.v("dve", "tensor_copy", reads=["cpb"], writes=["cbias"], out=bias[:], in_=pbias[:, 0:2])
            for g in range(2):
                p0 = g * 64
                for hc in range(2):
                    for l in range(32):
                        k.mm(ph[hc][:, 0:255], w1b[p0:p0 + 64, l, hc * 128:(hc + 1) * 128],
                             src[p0:p0 + 64, l:l + 16 * 254 + 1:16], l == 0, l == 31,
                             reads=["w1b", "csrc"], writes=[("cph", hc)])
                    k.v("dve", "tensor_scalar", reads=[("cph", hc), "cbias"], writes=["cxa"], out=xa[:, 0:255],
                        in0=ph[hc][:, 0:255], scalar1=bias[:, hc:hc + 1], scalar2=None, op0=ALU.add)
                    k.v("dve", "tensor_tensor", reads=["cxa"], writes=["cxb"], out=xb_[:, 0:255], in0=xa[:, 0:255],
                        in1=xa[:, 0:255], op=ALU.mult)
                    k.v("dve", "tensor_scalar", reads=["cxb"], writes=["cxc"], out=xc[:, 0:255], in0=xb_[:, 0:255],
                        scalar1=0.044715, scalar2=1.0, op0=ALU.mult, op1=ALU.add)
                    k.v("dve", "tensor_tensor", reads=["cxc", "cxa"], writes=["cxb"], out=xb_[:, 0:255],
                        in0=xc[:, 0:255], in1=xa[:, 0:255], op=ALU.mult)
                    k.act(xc[:, 0:255], xb_[:, 0:255], AF.Sigmoid, reads=["cxb"], writes=["cxc"], scale=C1)
                    k.v("dve", "tensor_tensor", reads=["cxc", "cxa"], writes=[("chid", hc)], out=hid[hc][:, 0:255],
                        in0=xc[:, 0:255], in1=xa[:, 0:255], op=ALU.mult)
                hh = [("chid", 0), ("chid", 1)]
                if which == 0:
                    for hc in range(2):
                        k.mm(po[0:64, 0:255], w2b[:, hc, :], hid[hc][:, 0:255], hc == 0, hc == 1,
                             reads=hh + ["w2b"], writes=["cpo"])
                    k.v("dve", "tensor_copy", reads=["cpo"], writes=["cko"], out=ko[:, 0:255], in_=po[0:64, 0:255])
                    k.dma("sp", kcT[g, :, 0:255], ko[:, 0:255], reads=["cko"])
                else:
                    for ch in range(2):
                        rows = 128 if ch == 0 else 127
                        for hc in range(2):
                            k.mm(po[0:rows, 0:64], hid[hc][:, ch * 128:ch * 128 + rows], w2b[:, hc, :],
                                 hc == 0, hc == 1, reads=hh + ["w2b"], writes=["cpo"])
                        k.v("dve", "tensor_copy", reads=["cpo"], writes=["cvo"], out=vo[0:rows, :], in_=po[0:rows, 0:64])
                        k.dma("sp", vcs[g, ch * 128:ch * 128 + rows, :], vo[0:rows, :], reads=["cvo"])
    P.barrier()


FILL_CNT = 0
FILL_N = 512


class AttnCx:
    def __init__(self, k, es, nU, ident):
        self.ident = ident
        if FILL_CNT > 0:
            self.fill = k.ps(es, "afill", [128, 512], F32)
            self.fsrc = k.sb(es, "afsrc", [128, 512], BF16)
            k.v("dve", "memset", writes=["afsrc"], ap=self.fsrc[:], constant=0.0)
        self.S = [k.ps(es, "aS", [128, 512], F32) for _ in range(3)]
        self.U = [k.ps(es, "aU", [128, 4, 128], F32) for _ in range(nU)]
        self.E = [k.sb(es, "aE", [128, 512], BF16) for _ in range(5)]
        self.fifo = []
        self.npv = 0
        self.si = self.li = self.ei = 0
        self.zl = k.sb(es, "azl", [1, 128], BF16)
        self.zr = k.sb(es, "azr", [1, 512], BF16)
        k.v("dve", "memset", writes=["azl"], ap=self.zl[:], constant=0.0)
        k.v("dve", "memset", writes=["azr"], ap=self.zr[:], constant=0.0)


def emit_scores(k, cx, rows, kT, q, n, tb, cbias, mask, rd):
    si = cx.si % 3
    cx.si += 1
    Sb = cx.S[si]
    k.mm(Sb[0:rows, 0:n], kT, q, True, tb is None, reads=rd, writes=[("aS", si)])
    if tb is not None:
        k.mm(Sb[0:rows, 0:n], cx.ident[0:rows, 0:rows], tb, False, True, reads=rd + ["ident"], writes=[("aS", si)])
    ei = cx.ei % 5
    cx.ei += 1
    Eb = cx.E[ei]
    if tb is not None:
        k.act(Eb[0:rows, 0:n], Sb[0:rows, 0:n], AF.Exp, reads=[("aS", si)], writes=[("aE", ei)], scale=SCALE)
    else:
        k.act(Eb[0:rows, 0:n], Sb[0:rows, 0:n], AF.Exp, reads=[("aS", si)] + rd, writes=[("aE", ei)],
              scale=SCALE, bias=cbias)
    return ei


LOOKAHEAD = 3


def pipe_drain(k, cx, keep):
    while cx.npv > keep or (keep == 0 and cx.fifo):
        act = cx.fifo.pop(0)
        if act[0] == "pv":
            cx.npv -= 1
        act[1]()


def run_branch(k, cx, tiles, ui, utok, mid=None):
    last = {}
    for i, t in enumerate(tiles):
        for qt in range(t["qt_lo"], t["qt_hi"]):
            last[qt] = i
    U = cx.U[ui]

    def zero():
        k.mm(U[:].rearrange("p a b -> p (a b)"), cx.zl[0:1, :], cx.zr[0:1, :], True, False,
             reads=["azl", "azr"], writes=[utok])
    cx.fifo.append(("zero", zero))
    for i, t in enumerate(tiles):
        lo, hi = t["qt_lo"], t["qt_hi"]
        n = (hi - lo) * 128
        rows = t["rows"]
        ei = emit_scores(k, cx, rows, t["kT"], t["qfn"](lo * 128, n), n,
                         t["tbfn"](lo * 128, n) if t["tbfn"] is not None else None, t["cbias"],
                         t["maskfn"](lo * 128, n) if t["maskfn"] is not None else None, t["rd"])

        def pv(i=i, t=t, lo=lo, hi=hi, rows=rows, ei=ei):
            Eb = cx.E[ei]
            for qt in range(lo, hi):
                c0 = (qt - lo) * 128
                k.mm(U[:, qt, 0:65], Eb[0:rows, c0:c0 + 128], t["V"], False, last[qt] == i,
                     reads=[("aE", ei)] + t["rd"], writes=[utok])
        cx.fifo.append(("pv", pv))
        cx.npv += 1
        pipe_drain(k, cx, LOOKAHEAD)
        if mid is not None and i == (len(tiles) - 1) // 2:
            mid()


def pipe_post(k, cx, fn):
    cx.fifo.append(("post", fn))


def pipe_flush(k, cx):
    pipe_drain(k, cx, 0)


def dma_split(k, q, dst, src, tok, step=8):
    for c0 in range(0, NT, step):
        k.dma(q, dst[:, c0:c0 + step, :], src[:, c0:c0 + step, :], writes=[tok])


def load_v_aug(k, q, vt, vtok, col0, tok):
    k.v("dve", "memset", writes=[tok], ap=vt[:, :, 64:65], constant=1.0)
    src = vtok[:, col0:col0 + 64].rearrange("(c p) d -> p c d", p=128)
    for c0 in range(0, NT, 8):
        k.dma(q, vt[:, c0:c0 + 8, 0:64], src[:, c0:c0 + 8, :], writes=[tok])


def cmp_tiles(qb, kc_sb, q_sb, tbc, V_sb, rd):
    qs = qb * 512
    tiles = []
    for ch in range(2):
        Dd = qs - 2048 * ch
        if Dd < 0:
            continue
        rows = 128 if ch == 0 else 127
        tiles.append(dict(
            rows=rows, kT=kc_sb[:, ch * 128:ch * 128 + rows],
            qfn=lambda c0, n, qs=qs: q_sb[:, qs + c0:qs + c0 + n],
            tbfn=lambda c0, n, Dd=Dd, rows=rows: tbc[0:rows, Dd + c0:Dd + c0 + n],
            cbias=None, maskfn=None, V=V_sb[0:rows, ch, :], qt_lo=0, qt_hi=4, rd=rd))
    return tiles


def nsa_phase(k, qkT, vtok, gtok, kcT, vcs, gd, consts, identf, ident, ytok_sb):
    P = k.P
    with ExitStack() as es:
        flipf = k.sb(es, "flipf", [128, 128], F32)
        rev = k.sb(es, "rev", [128, 4096], F32)
        tbc = k.sb(es, "tbc", [128, 4096], BF16)
        tbf = k.sb(es, "tbf", [128, 2432], BF16)
        cbf = k.sb(es, "cbf", [128, 1], F32)
        tbw = k.sb(es, "tbw", [128, 1408], BF16)
        q_sb = k.sb(es, "nq", [128, S], BF16)
        kc_sb = k.sb(es, "nkc", [128, 256], BF16)
        ctsf = k.sb(es, "ctsf", [128, 2, 65], F32)
        cts = k.sb(es, "cts", [128, 2, 65], BF16)
        vc_sb = k.sb(es, "nvc", [128, 2, 65], BF16)
        ksel = k.sb(es, "nksel", [128, S], BF16)
        kwin = k.sb(es, "nkwin", [128, S], BF16)
        vsel = k.sb(es, "nvsel", [128, NT, 65], BF16)
        vwin = k.sb(es, "nvwin", [128, NT, 65], BF16)
        eself = k.sb(es, "eself", [128, 1024], F32)
        imp = k.sb(es, "imp", [128, NT, 64], F32)
        keep = k.sb(es, "keep", [128, NT, 64], F32)
        addc = k.sb(es, "addc", [128, NT, 64], F32)
        gts = k.sb(es, "gts", [128, NT, 18], F32)
        sm = k.sb(es, "nsm", [128, 64], F32)
        wk = [k.sb(es, "nwk", [128, 64], F32) for _ in range(3)]
        m8 = [k.sb(es, "nm8", [128, 8], F32) for _ in range(2)]
        negq = [k.sb(es, "negq", [128, 64], F32) for _ in range(8)]
        acc = [k.sb(es, "nacc", [128, 64], F32) for _ in range(2)]
        ucp = [k.sb(es, "nucp", [128, 4, 65], F32) for _ in range(3)]
        cx = AttnCx(k, es, 3, ident)
        pfl = [k.ps(es, "pfl", [128, 512], F32) for _ in range(2)]

        k.dma("sp", flipf[:], consts["c_flip"], writes=["flipf"])
        k.dma("sp", ctsf[:], consts["c_cts"].rearrange("(c p) d -> p c d", p=128), writes=["ctsf"])
        k.v("dve", "tensor_copy", reads=["ctsf"], writes=["cts"], out=cts[:], in_=ctsf[:])
        for c0 in range(0, S, 1024):
            k.dma("sp", eself[64:128, :], consts["c_esel"][:, c0:c0 + 1024], writes=["eself"])
            k.v("dve", "tensor_copy", reads=["eself"], writes=["esel"], out=ksel[64:128, c0:c0 + 1024], in_=eself[64:128, :])
        k.v("dve", "memset", writes=["nneg0"] + [("nneg", qb) for qb in range(8)], ap=q_sb[64:128, :], constant=0.0)
        k.v("dve", "memset", writes=["nkc0"], ap=kc_sb[64:128, :], constant=0.0)
        k.v("dve", "memset", writes=["nkwin0"], ap=kwin[64:128, :], constant=0.0)
        dma_split(k, "pool", keep, consts["c_keep"].rearrange("(c p) d -> p c d", p=128), "keep")
        dma_split(k, "pool", addc, consts["c_add"].rearrange("(c p) d -> p c d", p=128), "addc")
        dma_split(k, "pool", gts, gtok.rearrange("(c p) d -> p c d", p=128), "gts")

        for g in range(2):
            k.dma("sp", kc_sb[0:64, :], kcT[g], writes=["nkc"])
            for r in range(3):
                h = g * 3 + r
                k.dma("sp", q_sb[0:64, :], qkT[R_NSAQ + h * 64:R_NSAQ + (h + 1) * 64, :], writes=["nq"])
                make_tb(k, "cmp", gd, h, flipf, rev, tbc, pfl, "tbc")
                for qb in range(8):
                    tiles = cmp_tiles(qb, kc_sb, q_sb, tbc, cts, ["nkc", "nkc0", "nneg0", "nq", "tbc", "cts"])
                    run_branch(k, cx, tiles, 0, ("aU", 0))

                    def post1(qb=qb, r=r):
                        U = cx.U[0]
                        k.v("dve", "tensor_scalar", reads=[("aU", 0)], writes=["nsm"], out=sm[:, 0:4],
                            in0=U[:, :, 64], scalar1=1e-30, scalar2=None, op0=ALU.max)
                        k.v("dve", "reciprocal", reads=["nsm"], writes=["nsm2"], out=sm[:, 4:8], in_=sm[:, 0:4])
                        for qt in range(4):
                            tl = qb * 4 + qt
                            if r == 0:
                                k.v("dve", "tensor_scalar", reads=[("aU", 0), "nsm2"], writes=[("imp", tl)],
                                    out=imp[:, tl, :], in0=U[:, qt, 0:64], scalar1=sm[:, 4 + qt:5 + qt],
                                    scalar2=None, op0=ALU.mult)
                            else:
                                k.v("dve", "scalar_tensor_tensor", reads=[("aU", 0), "nsm2", ("imp", tl)],
                                    writes=[("imp", tl)], out=imp[:, tl, :], in0=U[:, qt, 0:64],
                                    scalar=sm[:, 4 + qt:5 + qt], in1=imp[:, tl, :], op0=ALU.mult, op1=ALU.add)
                    pipe_post(k, cx, post1)
                pipe_flush(k, cx)
            def sel12(qb):
                for j in range(4):
                    tl = qb * 4 + j
                    bi = (qb % 2) * 4 + j
                    a, b_, c_ = wk
                    k.v("dve", "tensor_tensor", reads=[("imp", tl), "keep"], writes=["nwk0"], out=a[:],
                        in0=imp[:, tl, :], in1=keep[:, tl, :], op=ALU.mult)
                    k.v("dve", "tensor_tensor", reads=["nwk0", "addc"], writes=["nwk1"], out=b_[:],
                        in0=a[:], in1=addc[:, tl, :], op=ALU.add)
                    k.v("dve", "max", reads=["nwk1"], writes=["nm80"], out=m8[0][:], in_=b_[:])
                    k.v("dve", "match_replace", reads=["nwk1", "nm80"], writes=["nwk2"], out=c_[:],
                        in_to_replace=m8[0][:], in_values=b_[:], imm_value=-3.0e38)
                    k.v("dve", "max", reads=["nwk2"], writes=["nm81"], out=m8[1][:], in_=c_[:])
                    k.v("dve", "tensor_scalar", reads=["nwk1", "nm81"], writes=["nwk0"], out=a[:], in0=b_[:],
                        scalar1=m8[1][:, 7:8], scalar2=None, op0=ALU.is_ge)
                    k.v("dve", "tensor_scalar", reads=["nwk0"], writes=[("negq", bi)], out=negq[bi][:], in0=a[:],
                        scalar1=-NEGP, scalar2=NEGP, op0=ALU.mult, op1=ALU.add)

            def sel34(qb):
                for j in range(4):
                    bi = (qb % 2) * 4 + j
                    k.tr(pfl[1][0:64, j * 128:(j + 1) * 128], negq[bi][:, :], identf[:], reads=[("negq", bi), "identf"],
                         writes=[("pflip", 1)])
                k.v("dve", "tensor_copy", reads=[("pflip", 1)], writes=[("nneg", qb)],
                    out=q_sb[64:128, qb * 512:(qb + 1) * 512], in_=pfl[1][0:64, 0:512])
            k.dma("sp", ksel[0:64, :], qkT[R_KSEL + g * 64:R_KSEL + (g + 1) * 64, :], writes=["nksel"])
            k.dma("sp", kwin[0:64, :], qkT[R_KWIN + g * 64:R_KWIN + (g + 1) * 64, :], writes=["nkwin"])
            load_v_aug(k, "pool", vsel, vtok, C_VSEL + g * 64, "nvsel")
            load_v_aug(k, "pool", vwin, vtok, C_VWIN + g * 64, "nvwin")
            k.v("dve", "memset", writes=["nvc"], ap=vc_sb[:, :, 64:65], constant=1.0)
            k.dma("pool", vc_sb[:, :, 0:64], vcs[g].rearrange("(c p) d -> p c d", p=128), writes=["nvc"])
            for r in range(3):
                h = g * 3 + r
                k.dma("sp", q_sb[0:64, :], qkT[R_NSAQ + h * 64:R_NSAQ + (h + 1) * 64, :], writes=["nq"])
                make_tb(k, "cmp", gd, h, flipf, rev, tbc, pfl, "tbc")
                make_tb(k, "full", gd, h, flipf, rev, tbf, pfl, "tbf")
                k.v("dve", "tensor_scalar", reads=["tbf"], writes=["cbf"], out=cbf[:, 0:1], in0=tbf[:, 2431:2432],
                    scalar1=SCALE, scalar2=None, op0=ALU.mult)
                make_tb(k, "win", gd, h, flipf, rev, tbw, pfl, "tbw")
                if r == 0:
                    sel12(0)
                    sel34(0)
                    sel12(1)
                for qb in range(8):
                    qs = qb * 512
                    qfn = lambda c0, n, qs=qs: q_sb[:, qs + c0:qs + c0 + n]
                    tiles = cmp_tiles(qb, kc_sb, q_sb, tbc, vc_sb, ["nkc", "nkc0", "nneg0", "nq", "tbc", "nvc"])
                    run_branch(k, cx, tiles, 0, ("aU", 0))
                    tiles = []
                    for kc in range(0, (qs + 384) // 128 + 1):
                        Dl = qs - 128 * kc
                        lo = max(0, -Dl // 128)
                        far = Dl >= 1664
                        tiles.append(dict(
                            rows=128, kT=ksel[:, kc * 128:(kc + 1) * 128], qfn=qfn,
                            tbfn=None if far else (lambda c0, n, Dl=Dl: tbf[:, Dl + 384 + c0:Dl + 384 + c0 + n]),
                            cbias=cbf[:, 0:1] if far else None,
                            maskfn=None,
                            V=vsel[:, kc, :], qt_lo=lo, qt_hi=4, rd=["nksel", "nq", "tbf", "cbf", "nvsel", "esel", ("nneg", qb)]))
                    def mids(qb=qb):
                        if qb + 1 < 8:
                            sel34(qb + 1)
                        if qb + 2 < 8:
                            sel12(qb + 2)
                    run_branch(k, cx, tiles, 1, ("aU", 1), mid=mids if r == 0 else None)
                    tiles = []
                    for kc in range(max(0, (qs - 512) // 128), (qs + 384) // 128 + 1):
                        Dl = qs - 128 * kc
                        lo = max(0, -Dl // 128)
                        hi = min(4, (639 - Dl) // 128 + 1)
                        tiles.append(dict(
                            rows=128, kT=kwin[:, kc * 128:(kc + 1) * 128], qfn=qfn,
                            tbfn=lambda c0, n, Dl=Dl: tbw[:, Dl + 384 + c0:Dl + 384 + c0 + n],
                            cbias=None, maskfn=None, V=vwin[:, kc, :], qt_lo=lo, qt_hi=hi,
                            rd=["nkwin", "nkwin0", "nneg0", "nq", "tbw", "nvwin"]))
                    run_branch(k, cx, tiles, 2, ("aU", 2))
                    def post3(qb=qb, h=h):
                        for br in range(3):
                            k.v("dve", "tensor_copy", reads=[("aU", br)], writes=[("ucp", br)],
                                out=ucp[br][:, :, :], in_=cx.U[br][:, :, 0:65])
                        for br in range(3):
                            U = ucp[br]
                            k.v("dve", "tensor_scalar", reads=[("ucp", br)], writes=[("nsmc", br)],
                                out=sm[:, 8 + br * 12:12 + br * 12], in0=U[:, :, 64], scalar1=1e-30, scalar2=None,
                                op0=ALU.max)
                            k.v("dve", "reciprocal", reads=[("nsmc", br)], writes=[("nsmr", br)],
                                out=sm[:, 12 + br * 12:16 + br * 12], in_=sm[:, 8 + br * 12:12 + br * 12])
                            k.v("dve", "tensor_tensor", reads=[("nsmr", br), "gts"], writes=[("nsmg", br)],
                                out=sm[:, 16 + br * 12:20 + br * 12], in0=sm[:, 12 + br * 12:16 + br * 12],
                                in1=gts[:, qb * 4:qb * 4 + 4, h * 3 + br], op=ALU.mult)
                        for qt in range(4):
                            tl = qb * 4 + qt
                            a0, a1 = acc
                            k.v("dve", "tensor_scalar", reads=[("ucp", 0), ("nsmg", 0)], writes=["nacc0"], out=a0[:],
                                in0=ucp[0][:, qt, 0:64], scalar1=sm[:, 16 + qt:17 + qt], scalar2=None, op0=ALU.mult)
                            k.v("dve", "scalar_tensor_tensor", reads=[("ucp", 1), ("nsmg", 1), "nacc0"], writes=["nacc1"],
                                out=a1[:], in0=ucp[1][:, qt, 0:64], scalar=sm[:, 28 + qt:29 + qt], in1=a0[:],
                                op0=ALU.mult, op1=ALU.add)
                            k.v("dve", "scalar_tensor_tensor", reads=[("ucp", 2), ("nsmg", 2), "nacc1"],
                                writes=[("ytok", tl)], out=ytok_sb[:, tl, h * 64:(h + 1) * 64],
                                in0=ucp[2][:, qt, 0:64], scalar=sm[:, 40 + qt:41 + qt], in1=a1[:],
                                op0=ALU.mult, op1=ALU.add)
                    pipe_post(k, cx, post3)
                pipe_flush(k, cx)
    P.barrier()


def moba_phase(k, qkT, vtok, gd, consts, identf, ident, ytok_sb):
    P = k.P
    with ExitStack() as es:
        flipf = k.sb(es, "mflipf", [128, 128], F32)
        rev = [k.sb(es, "mrev", [128, 2432], F32) for _ in range(2)]
        tbf = [k.sb(es, "mtbf", [128, 2432], BF16) for _ in range(2)]
        cbf = [k.sb(es, "mcbf", [128, 1], F32) for _ in range(2)]
        q_sb = [k.sb(es, "mq", [128, S], BF16) for _ in range(2)]
        k_sb = [k.sb(es, "mk", [128, S], BF16) for _ in range(2)]
        v_sb = [k.sb(es, "mv", [128, NT, 65], BF16) for _ in range(2)]
        ebf = k.sb(es, "ebf", [128, 1024], F32)
        valid = k.sb(es, "mvalid", [128, NT, 16], F32)
        addm = k.sb(es, "maddm", [128, NT, 16], F32)
        ownc = k.sb(es, "mown", [128, NT, 16], F32)
        km = k.sb(es, "mkm", [64, 16], F32)
        kmh = [k.sb(es, "mkmh", [64, 16], BF16) for _ in range(2)]
        kmhf = k.sb(es, "mkmhf", [64, 16], F32)
        kml = [k.sb(es, "mkml", [64, 16], BF16) for _ in range(2)]
        wk = [k.sb(es, "mwk", [128, 16], F32) for _ in range(3)]
        m8 = k.sb(es, "mm8", [128, 8], F32)
        sm = k.sb(es, "msm", [128, 8], F32)
        bq = [k.sb(es, "mbq", [128, 16], F32) for _ in range(8)]
        cx = AttnCx(k, es, 2, ident)
        pfl = [k.ps(es, "mpfl", [128, 512], F32) for _ in range(3)]
        k.dma("sp", flipf[:], consts["c_flip"], writes=["flipf"])
        for s_ in range(2):
            k.v("dve", "memset", writes=[("mneg", s_, qb) for qb in range(8)], ap=q_sb[s_][64:128, :], constant=0.0)
            k.v("dve", "memset", writes=[("eb", s_)], ap=k_sb[s_][64:128, :], constant=0.0)
            for c0 in range(0, S, 1024):
                k.dma("sp", ebf[64:80, :], consts["c_ebig"][:, c0:c0 + 1024], writes=["ebf"])
                k.v("dve", "tensor_copy", reads=["ebf"], writes=[("eb", s_)], out=k_sb[s_][64:80, c0:c0 + 1024],
                    in_=ebf[64:80, :])
        dma_split(k, "pool", valid, consts["c_mvalid"].rearrange("(c p) d -> p c d", p=128), "mvalid")
        dma_split(k, "pool", addm, consts["c_maddm"].rearrange("(c p) d -> p c d", p=128), "maddm")
        dma_split(k, "pool", ownc, consts["c_mown"].rearrange("(c p) d -> p c d", p=128), "mown")
        L_, W_, st_ = FAM["full"]

        def prep_dma(hb):
            s_ = hb % 2
            k.dma("sp", q_sb[s_][0:64, :], qkT[R_MQ + hb * 64:R_MQ + (hb + 1) * 64, :], writes=[("mq", s_)])
            k.dma("sp", k_sb[s_][0:64, :], qkT[R_MK + hb * 64:R_MK + (hb + 1) * 64, :], writes=[("mk", s_)])
            load_v_aug(k, "pool", v_sb[s_], vtok, C_MV + hb * 64, ("mv", s_))
            g = gd["full"]
            k.dma("sp", rev[s_][:, 0:W_], dap(g.tensor, g.offset + (6 + hb) * L_, [[st_, 128], [1, W_]]),
                  writes=[("mrev", s_)])

        def prep_cmp(hb):
            s_ = hb % 2
            i = 0
            for c0 in range(0, W_, 512):
                n = min(512, W_ - c0)
                b = 2
                i += 1
                k.mm(pfl[b][:, 0:n], flipf[:, :], rev[s_][:, c0:c0 + n], True, True, reads=["flipf", ("mrev", s_)],
                     writes=[("pflip", b)])
                k.v("dve", "tensor_scalar", reads=[("pflip", b)], writes=[("tbf", s_)], out=tbf[s_][:, c0:c0 + n],
                    in0=pfl[b][:, 0:n], scalar1=1.0 / SCALE, scalar2=None, op0=ALU.mult)
            k.v("dve", "tensor_scalar", reads=[("tbf", s_)], writes=[("cbf", s_)], out=cbf[s_][:, 0:1],
                in0=tbf[s_][:, 2431:2432], scalar1=SCALE, scalar2=None, op0=ALU.mult)
            k.v("dve", "tensor_reduce", reads=[("mk", s_)], writes=["mkm"], out=km[:],
                in_=k_sb[s_][0:64, :].rearrange("p (n j) -> p n j", j=256), axis=mybir.AxisListType.X, op=ALU.add)
            k.v("dve", "tensor_scalar", reads=["mkm"], writes=["mkm2"], out=km[:], in0=km[:], scalar1=1.0 / 256,
                scalar2=None, op0=ALU.mult)
            k.v("dve", "tensor_copy", reads=["mkm2"], writes=[("mkmh", s_)], out=kmh[s_][:], in_=km[:])
            k.v("dve", "tensor_copy", reads=[("mkmh", s_)], writes=["mkmhf"], out=kmhf[:], in_=kmh[s_][:])
            k.v("dve", "tensor_tensor", reads=["mkm2", "mkmhf"], writes=[("mkml", s_)], out=kml[s_][:], in0=km[:],
                in1=kmhf[:], op=ALU.subtract)

        def attend(hb):
            s_ = hb % 2
            q_, k_, v_, tb_, cb_ = q_sb[s_], k_sb[s_], v_sb[s_], tbf[s_], cbf[s_]

            def gate12(qb):
                G = pfl[0]
                for j in range(4):
                    tl = qb * 4 + j
                    k.mm(G[:, j * 16:(j + 1) * 16], q_[0:64, tl * 128:(tl + 1) * 128], kmh[s_][:, :], True, False,
                         reads=[("mq", s_), ("mkmh", s_)], writes=[("pflip", 0)])
                    k.mm(G[:, j * 16:(j + 1) * 16], q_[0:64, tl * 128:(tl + 1) * 128], kml[s_][:, :], False, True,
                         reads=[("mq", s_), ("mkml", s_)], writes=[("pflip", 0)])
                for j in range(4):
                    tl = qb * 4 + j
                    bi = (qb % 2) * 4 + j
                    a, b_, c_ = wk
                    k.v("dve", "tensor_tensor", reads=[("pflip", 0), "mvalid"], writes=["mwk0"], out=a[:],
                        in0=G[:, j * 16:(j + 1) * 16], in1=valid[:, tl, :], op=ALU.mult)
                    k.v("dve", "tensor_tensor", reads=["mwk0", "maddm"], writes=["mwk1"], out=b_[:], in0=a[:],
                        in1=addm[:, tl, :], op=ALU.add)
                    k.v("dve", "max", reads=["mwk1"], writes=["mm8"], out=m8[:], in_=b_[:])
                    k.v("dve", "tensor_scalar", reads=["mwk1", "mm8"], writes=["mwk2"], out=c_[:], in0=b_[:],
                        scalar1=m8[:, 2:3], scalar2=None, op0=ALU.is_ge)
                    k.v("dve", "tensor_tensor", reads=["mwk2", "mvalid"], writes=["mwk0"], out=a[:], in0=c_[:],
                        in1=valid[:, tl, :], op=ALU.mult)
                    k.v("dve", "scalar_tensor_tensor", reads=["mwk0", "mown"], writes=[("mbq", bi)], out=bq[bi][:],
                        in0=a[:], scalar=-NEGP, in1=ownc[:, tl, :], op0=ALU.mult, op1=ALU.add)

            def gate34(qb):
                for j in range(4):
                    bi = (qb % 2) * 4 + j
                    k.tr(pfl[1][0:16, j * 128:(j + 1) * 128], bq[bi][:, :], identf[:], reads=[("mbq", bi), "identf"],
                         writes=[("pflip", 1)])
                k.v("dve", "tensor_copy", reads=[("pflip", 1)], writes=[("mneg", s_, qb)],
                    out=q_[64:80, qb * 512:(qb + 1) * 512], in_=pfl[1][0:16, 0:512])
            gate12(0)
            gate34(0)
            gate12(1)
            for qb in range(8):
                qs = qb * 512
                qfn = lambda c0, n, qs=qs: q_[:, qs + c0:qs + c0 + n]
                tiles = []
                for kc in range(0, (qs + 384) // 128 + 1):
                    Dl = qs - 128 * kc
                    lo = max(0, -Dl // 128)
                    far = Dl >= 1664
                    tiles.append(dict(
                        rows=128, kT=k_[:, kc * 128:(kc + 1) * 128], qfn=qfn,
                        tbfn=None if far else (lambda c0, n, Dl=Dl: tb_[:, Dl + 384 + c0:Dl + 384 + c0 + n]),
                        cbias=cb_[:, 0:1] if far else None,
                        maskfn=None,
                        V=v_[:, kc, :], qt_lo=lo, qt_hi=4,
                        rd=[("mk", s_), ("mq", s_), ("tbf", s_), ("cbf", s_), ("mv", s_), ("eb", s_), ("mneg", s_, qb)]))
                ub = qb % 2

                def midm(qb=qb):
                    if qb + 1 < 8:
                        gate34(qb + 1)
                    if qb + 2 < 8:
                        gate12(qb + 2)
                run_branch(k, cx, tiles, ub, ("aU", ub), mid=midm)

                def postm(qb=qb, hb=hb, ub=ub):
                    U = cx.U[ub]
                    k.v("dve", "tensor_scalar", reads=[("aU", ub)], writes=["msm"], out=sm[:, 0:4], in0=U[:, :, 64],
                        scalar1=1e-30, scalar2=None, op0=ALU.max)
                    k.v("dve", "reciprocal", reads=["msm"], writes=["msm2"], out=sm[:, 4:8], in_=sm[:, 0:4])
                    for qt in range(4):
                        tl = qb * 4 + qt
                        k.v("dve", "tensor_scalar", reads=[("aU", ub), "msm2"], writes=[("ytok", tl)],
                            out=ytok_sb[:, tl, 384 + hb * 64:384 + (hb + 1) * 64], in0=U[:, qt, 0:64],
                            scalar1=sm[:, 4 + qt:5 + qt], scalar2=None, op0=ALU.mult)
                pipe_post(k, cx, postm)
                if hb + 1 < 4 and qb == 1:
                    prep_dma(hb + 1)
                if hb + 1 < 4 and qb == 4:
                    prep_cmp(hb + 1)
            pipe_flush(k, cx)

        prep_dma(0)
        prep_cmp(0)
        for hb in range(4):
            attend(hb)
    P.barrier()


def dil_phase(k, qkT, vtok, gd, consts, ident, ytok_sb):
    P = k.P
    with ExitStack() as es:
        flipf = k.sb(es, "dflipf", [128, 128], F32)
        rev = k.sb(es, "drev", [128, 2944], F32)
        tbs = [k.sb(es, "dtb", [128, FAM["dil%d" % g][1]], BF16) for g in range(3)]
        q_sb = [k.sb(es, "dq", [128, S], BF16) for _ in range(3)]
        k_sb = [k.sb(es, "dk", [128, S], BF16) for _ in range(3)]
        v_sb = [k.sb(es, "dv", [128, NT, 65], BF16) for _ in range(3)]
        sm = k.sb(es, "dsm", [128, 8], F32)
        cx = AttnCx(k, es, 2, ident)
        pfl = [k.ps(es, "dpfl", [128, 512], F32) for _ in range(2)]
        k.dma("sp", flipf[:], consts["c_flip"], writes=["flipf"])
        for g in range(3):
            k.v("dve", "memset", writes=[("dq0", g)], ap=q_sb[g][64:128, :], constant=0.0)
            k.v("dve", "memset", writes=[("dk0", g)], ap=k_sb[g][64:128, :], constant=0.0)
        for i in range(2):
            for g in range(3):
                hh = g * 2 + i
                k.dma("sp", q_sb[g][0:64, :], qkT[R_DQ + hh * 64:R_DQ + (hh + 1) * 64, :], writes=[("dq", g)])
                k.dma("sp", k_sb[g][0:64, :], qkT[R_DK + hh * 64:R_DK + (hh + 1) * 64, :], writes=[("dk", g)])
                load_v_aug(k, "pool", v_sb[g], vtok, C_DV + hh * 64, ("dv", g))
                make_tb(k, "dil%d" % g, gd, 10 + hh, flipf, rev, tbs[g], pfl, ("dtb", g))
            for qb in range(8):
                qs = qb * 512
                tiles = []
                for g, (W, dl) in enumerate(DIL):
                    for kc in range(max(0, (qs - W) // 128), (qs + 384) // 128 + 1):
                        Dl = qs - 128 * kc
                        lo = max(0, -Dl // 128)
                        hi = min(4, (W - Dl) // 128 + 1)
                        tiles.append(dict(
                            rows=128, kT=k_sb[g][:, kc * 128:(kc + 1) * 128],
                            qfn=lambda c0, n, g=g, qs=qs: q_sb[g][:, qs + c0:qs + c0 + n],
                            tbfn=lambda c0, n, g=g, Dl=Dl: tbs[g][:, Dl + 384 + c0:Dl + 384 + c0 + n],
                            cbias=None, maskfn=None, V=v_sb[g][:, kc, :], qt_lo=lo, qt_hi=hi,
                            rd=[("dq", g), ("dk", g), ("dq0", g), ("dk0", g), ("dv", g), ("dtb", g)]))
                ub = qb % 2
                run_branch(k, cx, tiles, ub, ("aU", ub))

                def postd(qb=qb, i=i, ub=ub):
                    U = cx.U[ub]
                    k.v("dve", "tensor_scalar", reads=[("aU", ub)], writes=["dsm"], out=sm[:, 0:4], in0=U[:, :, 64],
                        scalar1=1e-30, scalar2=None, op0=ALU.max)
                    k.v("dve", "reciprocal", reads=["dsm"], writes=["dsm2"], out=sm[:, 4:8], in_=sm[:, 0:4])
                    for qt in range(4):
                        tl = qb * 4 + qt
                        k.v("dve", "tensor_scalar", reads=[("aU", ub), "dsm2"], writes=[("ytok", tl)],
                            out=ytok_sb[:, tl, 640 + i * 64:640 + (i + 1) * 64], in0=U[:, qt, 0:64],
                            scalar1=sm[:, 4 + qt:5 + qt], scalar2=None, op0=ALU.mult)
                pipe_post(k, cx, postd)
            pipe_flush(k, cx)
    P.barrier()


def merge_phase(k, xres, mgT, w_up_a, w_up_b, w_up_c, w_o, ident, ytok_sb):
    P = k.P
    with ExitStack() as es:
        wup = k.sb(es, "wup", [128, 6, D], BF16)
        wo = k.sb(es, "wo", [128, 8, D], BF16)
        stg = [k.sb(es, "gstg", [128, D], F32) for _ in range(2)]
        yT = k.sb(es, "gyT", [128, 6, 512], BF16)
        gt = [k.sb(es, "ggt", [128, 3, 512], BF16) for _ in range(2)]
        m1 = [k.sb(es, "gm1", [128, 512], F32) for _ in range(2)]
        m2 = [k.sb(es, "gm2", [128, 512], F32) for _ in range(2)]
        m3 = [k.sb(es, "gm3", [128, 512], F32) for _ in range(2)]
        m4 = [k.sb(es, "gm4", [128, 512], F32) for _ in range(2)]
        mT = k.sb(es, "gmT", [128, 8, 512], BF16)
        xr = [k.sb(es, "gxr", [128, 512], F32) for _ in range(2)]
        ob = [k.sb(es, "gob", [128, 512], F32) for _ in range(2)]
        ptp = [k.ps(es, "gptp", [128, 1024], BF16) for _ in range(2)]
        pu = [k.ps(es, "gpu", [128, 512], F32) for _ in range(3)]
        po = [k.ps(es, "gpo", [128, 512], F32) for _ in range(2)]
        srcs = [(w_up_a, 0), (w_up_a, 1), (w_up_a, 2), (w_up_b, 0), (w_up_b, 1), (w_up_c, 0)]
        ci = 0
        for fc, (w, j) in enumerate(srcs):
            b = ci % 2
            ci += 1
            k.dma("sp" if b == 0 else "pool", stg[b][:], w[j * 128:(j + 1) * 128, :], writes=[("gstg", b)])
            k.v("dve", "tensor_copy", reads=[("gstg", b)], writes=["wup"], out=wup[:, fc, :], in_=stg[b][:])
        for kc in range(8):
            b = ci % 2
            ci += 1
            k.dma("sp" if b == 0 else "pool", stg[b][:], w_o[kc * 128:(kc + 1) * 128, :], writes=[("gstg", b)])
            k.v("dve", "tensor_copy", reads=[("gstg", b)], writes=["wo"], out=wo[:, kc, :], in_=stg[b][:])
        ui = 0
        oi = 0
        for tb8 in range(8):
            t0 = tb8 * 512
            for fc in range(6):
                pb = fc % 2
                for t in range(4):
                    tl = tb8 * 4 + t
                    k.tr(ptp[pb][:, t * 128:(t + 1) * 128], ytok_sb[:, tl, fc * 128:(fc + 1) * 128], ident[:],
                         reads=[("ytok", tl), "ident"], writes=[("gptp", pb)])
                if fc % 2 == 0:
                    k.act(yT[:, fc, :], ptp[pb][:, 0:512], AF.Copy, reads=[("gptp", pb)], writes=[("gyT", fc)])
                else:
                    k.v("dve", "tensor_copy", reads=[("gptp", pb)], writes=[("gyT", fc)], out=yT[:, fc, :],
                        in_=ptp[pb][:, 0:512])
            for cc in range(8):
                b = ui % 2
                ui += 1
                k.dma("sp", gt[b][:], mgT.rearrange("(b r) t -> r b t", b=3)[cc * 128:(cc + 1) * 128, :, t0:t0 + 512],
                      writes=[("ggt", b)])
                groups = ((0, (0, 1, 2)), (1, (3, 4)), (2, (5,)))
                for br, fcs in groups:
                    for j, fc in enumerate(fcs):
                        k.mm(pu[br][:], wup[:, fc, cc * 128:(cc + 1) * 128], yT[:, fc, :], j == 0, j == len(fcs) - 1,
                             reads=[("gyT", fc), "wup"], writes=[("gpu", br)])
                k.v("dve", "tensor_tensor", reads=[("gpu", 0), ("ggt", b)], writes=[("gm1", b)], out=m1[b][:],
                    in0=pu[0][:], in1=gt[b][:, 0, :], op=ALU.mult)
                k.v("dve", "tensor_tensor", reads=[("gpu", 1), ("ggt", b)], writes=[("gm2", b)], out=m2[b][:],
                    in0=pu[1][:], in1=gt[b][:, 1, :], op=ALU.mult)
                k.v("dve", "tensor_tensor", reads=[("gpu", 2), ("ggt", b)], writes=[("gm3", b)], out=m3[b][:],
                    in0=pu[2][:], in1=gt[b][:, 2, :], op=ALU.mult)
                k.v("dve", "tensor_tensor", reads=[("gm1", b), ("gm2", b)], writes=[("gm4", b)], out=m4[b][:],
                    in0=m1[b][:], in1=m2[b][:], op=ALU.add)
                k.v("dve", "tensor_tensor", reads=[("gm4", b), ("gm3", b)], writes=[("gmT", cc)], out=mT[:, cc, :],
                    in0=m4[b][:], in1=m3[b][:], op=ALU.add)
            allm = [("gmT", cc) for cc in range(8)]
            for t in range(4):
                r0 = t0 + t * 128
                for hh in range(2):
                    b = oi % 2
                    oi += 1
                    k.dma("pool", xr[b][:], xres[r0:r0 + 128, hh * 512:(hh + 1) * 512], writes=[("gxr", b)])
                    for cc in range(8):
                        k.mm(po[b][:], mT[:, cc, t * 128:(t + 1) * 128], wo[:, cc, hh * 512:(hh + 1) * 512],
                             cc == 0, cc == 7, reads=allm + ["wo"], writes=[("gpo", b)])
                    k.v("dve", "tensor_tensor", reads=[("gpo", b), ("gxr", b)], writes=[("gob", b)], out=ob[b][:],
                        in0=po[b][:], in1=xr[b][:], op=ALU.add)
                    k.dma("sp", xres[r0:r0 + 128, hh * 512:(hh + 1) * 512], ob[b][:], reads=[("gob", b)])
    P.barrier()


W_QK_COLS = np.concatenate([np.arange(0, 384), np.arange(384, 512), np.arange(512, 640), np.arange(640, 768),
                            np.arange(896, 1024), np.arange(1170, 1426), np.arange(1426, 1682),
                            np.arange(1938, 2322), np.arange(2322, 2706)])
W_V_COLS = np.concatenate([np.arange(768, 896), np.arange(1024, 1152), np.arange(1682, 1938),
                           np.arange(2706, 3090), np.arange(1152, 1170)])
W_MG_COLS = np.arange(3090, 6162)


def build(stop_after=None, debug=False):
    nc = bass.Bass("TRN2", target_bir_lowering=False)

    def inp(name, shape):
        return nc.dram_tensor(name, list(shape), F32, kind="ExternalInput").ap()

    def scr(name, shape, dt):
        return nc.dram_tensor(name, list(shape), dt, kind="ExternalOutput" if debug else "Internal").ap()

    x = inp("x", [S, D])
    rel_bias = inp("rel_bias", [32, 16])
    ffn_norm = [inp("ffn1_norm", [2, D]), inp("ffn2_norm", [2, D])]
    ffn_wg = [inp("ffn1_w_gate", [2, D, DFF]), inp("ffn2_w_gate", [2, D, DFF])]
    ffn_wu = [inp("ffn1_w_up", [2, D, DFF]), inp("ffn2_w_up", [2, D, DFF])]
    ffn_wd = [inp("ffn1_w_down", [2, DFF, D]), inp("ffn2_w_down", [2, DFF, D])]
    mix_norm = inp("mix_norm", [2, D])
    w_qk = inp("w_qk", [2, D, NQK])
    w_v = inp("w_v", [2, D, NVG])
    w_mg = inp("w_mg", [2, D, 3 * D])
    pe_k = inp("nsa_pe_k", [2, 32, 64])
    pe_v = inp("nsa_pe_v", [2, 32, 64])
    phi_k1 = inp("nsa_phi_k1", [2, 2048, 256])
    phi_k2 = inp("nsa_phi_k2", [2, 256, 64])
    phi_v1 = inp("nsa_phi_v1", [2, 2048, 256])
    phi_v2 = inp("nsa_phi_v2", [2, 256, 64])
    w_up_a = inp("w_up_a", [2, 384, D])
    w_up_b = inp("w_up_b", [2, 256, D])
    w_up_c = inp("w_up_c", [2, 128, D])
    w_o = inp("w_o", [2, D, D])
    final_norm = inp("final_norm", [1, D])
    consts = {nm: inp(nm, shp) for nm, shp in CONST_SHAPES.items()}
    y = nc.dram_tensor("y", [S, D], F32, kind="ExternalOutput").ap()
    xres = scr("xres", [S, D], F32)
    qkT = scr("qkT", [NQK, S], BF16)
    vtok = scr("vtok", [S, NVC], BF16)
    gtok = scr("gtok", [S, 18], F32)
    mgT = scr("mgT", [3 * D, S], BF16)
    kcT = scr("kcT", [2, 64, 256], BF16)
    vcs = scr("vcs", [2, 256, 64], BF16)
    gd = {fam: scr("gd_" + fam, [16, L], F32) for fam, (L, _, _) in FAM.items()}
    ydbg = scr("ydbg", [S, 768], BF16) if debug else None

    P = Prog(nc)
    k = K(nc, P)
    with ExitStack() as es:
        identf = k.sb(es, "identf", [128, 128], F32)
        ident = k.sb(es, "ident", [128, 128], BF16)
        k.dma("sp", identf[:], consts["c_ident"], writes=["identf"])
        k.v("dve", "tensor_copy", reads=["identf"], writes=["ident"], out=ident[:], in_=identf[:])

        def dump_y(ytok_sb):
            if debug:
                yv = ydbg.rearrange("(c p) d -> p c d", p=128)
                for c0 in range(0, NT, 8):
                    k.dma("sp", yv[:, c0:c0 + 8, :], ytok_sb[:, c0:c0 + 8, :], reads=[("ytok", t) for t in range(NT)])
                P.barrier()

        ystack = []

        def dump_y_dummy():
            pass

        def run():
            stages = stop_after
            bias_gen_phase(k, rel_bias, consts, gd)
            for l in range(2):
                ffn_phase(k, x if l == 0 else xres, xres, ffn_norm[0][l:l + 1, :], ffn_wg[0][l], ffn_wu[0][l],
                          ffn_wd[0][l], ident)
                if stages == "ffn1":
                    return
                proj_phase(k, xres, mix_norm[l:l + 1, :], w_qk[l], w_v[l], w_mg[l], ident, qkT, vtok, gtok, mgT)
                if stages == "proj":
                    return
                compress_phase(k, qkT, pe_k[l], pe_v[l], phi_k1[l], phi_k2[l], phi_v1[l], phi_v2[l], kcT, vcs, identf)
                if stages == "cmp":
                    return
                ys = ExitStack()
                ytok_sb = k.sb(ys, "ytok", [128, NT, 768], BF16)
                ystack.append(ys)
                if stages in (None, "nsa", "attn", "merge", "l0"):
                    nsa_phase(k, qkT, vtok, gtok, kcT, vcs, gd, consts, identf, ident, ytok_sb)
                if stages == "nsa":
                    dump_y(ytok_sb)
                    return
                if stages in (None, "moba", "attn", "merge", "l0"):
                    moba_phase(k, qkT, vtok, gd, consts, identf, ident, ytok_sb)
                if stages == "moba":
                    dump_y(ytok_sb)
                    return
                dil_phase(k, qkT, vtok, gd, consts, ident, ytok_sb)
                if stages in ("dil", "attn"):
                    dump_y(ytok_sb)
                    return
                merge_phase(k, xres, mgT, w_up_a[l], w_up_b[l], w_up_c[l], w_o[l], ident, ytok_sb)
                ystack.pop().close()
                if stages == "merge":
                    return
                ffn_phase(k, xres, xres, ffn_norm[1][l:l + 1, :], ffn_wg[1][l], ffn_wu[1][l], ffn_wd[1][l], ident)
                if stages == "l0":
                    return
            final_norm_phase(k, xres, y, final_norm)
        run()
        P.barrier()
        with ExitStack() as es2:
            P.emit(es2)
        while ystack:
            ystack.pop().close()
    return nc


def make_in_maps(inputs, cores):
    consts = host_consts()
    w_in = np.asarray(inputs["w_in"])
    shared = dict(consts)
    shared["w_qk"] = np.ascontiguousarray(w_in[:, :, W_QK_COLS])
    shared["w_v"] = np.ascontiguousarray(w_in[:, :, W_V_COLS])
    shared["w_mg"] = np.ascontiguousarray(w_in[:, :, W_MG_COLS])
    for nm in ("rel_bias", "ffn1_norm", "ffn2_norm", "ffn1_w_gate", "ffn2_w_gate", "ffn1_w_up", "ffn2_w_up",
               "ffn1_w_down", "ffn2_w_down", "mix_norm", "nsa_pe_k", "nsa_pe_v", "nsa_phi_k1", "nsa_phi_k2",
               "nsa_phi_v1", "nsa_phi_v2", "w_up_a", "w_up_b", "w_up_c", "w_o"):
        shared[nm] = np.ascontiguousarray(np.asarray(inputs[nm], dtype=np.float32))
    shared["final_norm"] = np.ascontiguousarray(np.asarray(inputs["final_norm"], dtype=np.float32)).reshape(1, D)
    maps = []
    for b in cores:
        m = dict(shared)
        m["x"] = np.ascontiguousarray(np.asarray(inputs["x"][b], dtype=np.float32))
        maps.append(m)
    return maps


def kernel(**inputs):
    nc = build()
    in_maps = make_in_maps(inputs, range(8))
    res = run_bass_kernel_spmd(nc, in_maps, core_ids=list(range(8)))
    return np.stack([np.asarray(r["y"]) for r in res.results], axis=0).astype(np.float32)
```
